# Optimizing a Trainium2 kernel written in Bass

```python
import math
import jax, jax.numpy as jnp
from jax import lax
import numpy as np

D_MODEL = 1024
BATCH = 2
SEQ = 16384
DEPTH = 2

N_MIXERS = 2
N_GDN_LAYERS = (DEPTH + 1) // 2
N_SSD_LAYERS = DEPTH // 2
CONV_K = 4
EPS = 1e-6

GDN_H_QK = 8
GDN_H_V = 16
GDN_DK = 128
GDN_DV = 128
GDN_QK_W = GDN_H_QK * GDN_DK
GDN_V_W = GDN_H_V * GDN_DV
GDN_CONV_C = 2 * GDN_QK_W + GDN_V_W
GDN_IN_W = GDN_CONV_C + GDN_V_W + 2 * GDN_H_V
GDN_CHUNK = 64

SSD_D_INNER = 2 * D_MODEL
SSD_HEADDIM = 64
SSD_H = SSD_D_INNER // SSD_HEADDIM
SSD_N = 128
SSD_G = 8
SSD_R = SSD_H // SSD_G
SSD_CONV_C = SSD_D_INNER + 2 * SSD_G * SSD_N
SSD_IN_W = SSD_D_INNER + SSD_CONV_C + SSD_H
SSD_CHUNK = 128

kernel_name = "hybrid_gdn_mamba2_interleaved"


def _rmsnorm(x, w):
    xf = x.astype(jnp.float32)
    y = xf * lax.rsqrt(jnp.mean(xf * xf, axis=-1, keepdims=True) + EPS)
    return (y * w.astype(jnp.float32)).astype(x.dtype)


def _l2norm(x):
    xf = x.astype(jnp.float32)
    return xf * lax.rsqrt(jnp.sum(xf * xf, axis=-1, keepdims=True) + EPS)


def _causal_conv(x, w):
    k = w.shape[0]
    c = x.shape[-1]
    return lax.conv_general_dilated(
        x, w[:, None, :].astype(x.dtype), (1,), [(k - 1, 0)],
        dimension_numbers=("NWC", "WIO", "NWC"), feature_group_count=c)


def _chunk_gated_delta_rule(q, k, v, g, beta):
    b, t, h, dk = q.shape
    dv = v.shape[-1]
    c = GDN_CHUNK
    n = t // c
    f32 = jnp.float32
    def blk(a):
        return a.astype(f32).reshape(b, n, c, h, a.shape[-1]).transpose(0, 3, 1, 2, 4)
    q, k, v = blk(q), blk(k), blk(v)
    g = g.astype(f32).reshape(b, n, c, h).transpose(0, 3, 1, 2)
    beta = beta.astype(f32).reshape(b, n, c, h).transpose(0, 3, 1, 2)
    gc = jnp.cumsum(g, axis=-1)
    tril = jnp.tril(jnp.ones((c, c), dtype=bool))
    strict = jnp.tril(jnp.ones((c, c), dtype=bool), k=-1)
    L = jnp.exp(jnp.where(tril, gc[..., :, None] - gc[..., None, :], -jnp.inf))
    kb = k * beta[..., None]
    vb = v * beta[..., None]
    kkt = jnp.einsum("bhncd,bhnsd->bhncs", kb, k) * L
    a_mat = jnp.where(strict, kkt, 0.0) + jnp.eye(c, dtype=f32)
    rhs = jnp.concatenate([vb, kb * jnp.exp(gc)[..., None]], axis=-1)
    sol = lax.linalg.triangular_solve(a_mat, rhs, left_side=True, lower=True,
                                      unit_diagonal=True)
    u, w = sol[..., :dv], sol[..., dv:]
    attn = jnp.einsum("bhncd,bhnsd->bhncs", q, k) * L
    q_dec = q * jnp.exp(gc)[..., None]
    k_dec = k * jnp.exp(gc[..., -1:] - gc)[..., None]
    g_last = jnp.exp(gc[..., -1])

    def step(S, inp):
        qd, kd, u_c, w_c, a_c, gl = inp
        v_new = u_c - jnp.einsum("bhcd,bhde->bhce", w_c, S)
        o = jnp.einsum("bhcd,bhde->bhce", qd, S) + jnp.einsum("bhcs,bhse->bhce", a_c, v_new)
        S = S * gl[..., None, None] + jnp.einsum("bhcd,bhce->bhde", kd, v_new)
        return S, o

    mv = lambda a: jnp.moveaxis(a, 2, 0)
    S0 = jnp.zeros((b, h, dk, dv), f32)
    _, o = lax.scan(step, S0, (mv(q_dec), mv(k_dec), mv(u), mv(w), mv(attn), mv(g_last)))
    return o.transpose(1, 0, 3, 2, 4).reshape(b, t, h, dv)


def _gated_deltanet(hid, w_in, conv_w, a_log, dt_bias, norm_w, w_out):
    b, t, _ = hid.shape
    proj = hid @ w_in
    qkv, z, b_raw, a_raw = jnp.split(
        proj, [GDN_CONV_C, GDN_CONV_C + GDN_V_W, GDN_CONV_C + GDN_V_W + GDN_H_V], axis=-1)
    qkv = jax.nn.silu(_causal_conv(qkv, conv_w))
    q, k, v = jnp.split(qkv, [GDN_QK_W, 2 * GDN_QK_W], axis=-1)
    rep = GDN_H_V // GDN_H_QK
    q = jnp.repeat(_l2norm(q.reshape(b, t, GDN_H_QK, GDN_DK)) * (GDN_DK ** -0.5), rep, axis=2)
    k = jnp.repeat(_l2norm(k.reshape(b, t, GDN_H_QK, GDN_DK)), rep, axis=2)
    v = v.reshape(b, t, GDN_H_V, GDN_DV)
    beta = jax.nn.sigmoid(b_raw.astype(jnp.float32))
    g = -jnp.exp(a_log.astype(jnp.float32)) * jax.nn.softplus(
        a_raw.astype(jnp.float32) + dt_bias.astype(jnp.float32))
    o = _chunk_gated_delta_rule(q, k, v, g, beta).astype(hid.dtype)
    o = _rmsnorm(o, norm_w) * jax.nn.silu(z.reshape(b, t, GDN_H_V, GDN_DV))
    return o.reshape(b, t, GDN_V_W) @ w_out


def _ssd_chunked(X, adt, Bm, Cm):
    b, t, h, p = X.shape
    q = SSD_CHUNK
    n = t // q
    f32 = jnp.float32
    X = X.astype(f32).reshape(b, n, q, SSD_G, SSD_R, p)
    adt = adt.astype(f32).reshape(b, n, q, SSD_G, SSD_R).transpose(0, 3, 4, 1, 2)
    Bm = Bm.astype(f32).reshape(b, n, q, SSD_G, SSD_N)
    Cm = Cm.astype(f32).reshape(b, n, q, SSD_G, SSD_N)
    acs = jnp.cumsum(adt, axis=-1)
    tril = jnp.tril(jnp.ones((q, q), dtype=bool))
    Lm = jnp.exp(jnp.where(tril, acs[..., :, None] - acs[..., None, :], -jnp.inf))
    cb = jnp.einsum("bnlgd,bnsgd->bgnls", Cm, Bm)
    y_diag = jnp.einsum("bgrnls,bnsgrp->bnlgrp", cb[:, :, None] * Lm, X)
    decay_states = jnp.exp(acs[..., -1:] - acs)
    states = jnp.einsum("bnsgd,bgrns,bnsgrp->bngrpd", Bm, decay_states, X)
    chunk_decay = jnp.exp(acs[..., -1])

    def step(S, inp):
        st, dec = inp
        return S * dec[..., None, None] + st, S

    S0 = jnp.zeros((b, SSD_G, SSD_R, p, SSD_N), f32)
    _, prev = lax.scan(step, S0, (jnp.moveaxis(states, 1, 0), jnp.moveaxis(chunk_decay, 3, 0)))
    prev = jnp.moveaxis(prev, 0, 1)
    y_off = jnp.einsum("bnlgd,bngrpd,bgrnl->bnlgrp", Cm, prev, jnp.exp(acs))
    return (y_diag + y_off).reshape(b, t, h, p)


def _mamba2(hid, w_in, conv_w, conv_b, dt_bias, a_log, d_skip, norm_w, w_out):
    b, t, _ = hid.shape
    proj = hid @ w_in
    z, xbc, dt = jnp.split(proj, [SSD_D_INNER, SSD_D_INNER + SSD_CONV_C], axis=-1)
    xbc = jax.nn.silu(_causal_conv(xbc, conv_w) + conv_b)
    xs, Bm, Cm = jnp.split(xbc, [SSD_D_INNER, SSD_D_INNER + SSD_G * SSD_N], axis=-1)
    xs = xs.reshape(b, t, SSD_H, SSD_HEADDIM)
    Bm = Bm.reshape(b, t, SSD_G, SSD_N)
    Cm = Cm.reshape(b, t, SSD_G, SSD_N)
    dt = jax.nn.softplus(dt.astype(jnp.float32) + dt_bias.astype(jnp.float32))
    A = -jnp.exp(a_log.astype(jnp.float32))
    y = _ssd_chunked(xs.astype(jnp.float32) * dt[..., None], A * dt, Bm, Cm)
    y = (y + xs.astype(jnp.float32) * d_skip.astype(jnp.float32)[:, None]).astype(hid.dtype)
    y = y.reshape(b, t, SSD_D_INNER) * jax.nn.silu(z)
    gs = SSD_D_INNER // SSD_G
    y = _rmsnorm(y.reshape(b, t, SSD_G, gs), norm_w.reshape(SSD_G, gs)).reshape(b, t, SSD_D_INNER)
    return y @ w_out


def _dt_bias_init(key, n):
    dt = jnp.exp(jax.random.uniform(key, (n,), minval=math.log(1e-3), maxval=math.log(1e-1)))
    return dt + jnp.log(-jnp.expm1(-dt))


def setup_inputs(seed: int = 0) -> dict:
    key = jax.random.key(seed)
    ks = jax.random.split(key, 20)
    nA, nB = N_GDN_LAYERS, N_SSD_LAYERS
    nrm = jax.random.normal
    return {
        "x": nrm(ks[0], (BATCH, SEQ, D_MODEL), jnp.float32),
        "norm_w": 1.0 + 0.01 * nrm(ks[1], (DEPTH, D_MODEL), jnp.float32),
        "gdn_w_in": nrm(ks[2], (nA, D_MODEL, GDN_IN_W), jnp.float32) * D_MODEL ** -0.5,
        "gdn_conv_w": nrm(ks[3], (nA, CONV_K, GDN_CONV_C), jnp.float32) * 0.5,
        "gdn_a_log": jnp.log(jax.random.uniform(ks[4], (nA, GDN_H_V), minval=1.0, maxval=16.0)),
        "gdn_dt_bias": jnp.stack([_dt_bias_init(kk, GDN_H_V) for kk in jax.random.split(ks[5], nA)]),
        "gdn_norm_w": 1.0 + 0.01 * nrm(ks[6], (nA, GDN_DV), jnp.float32),
        "gdn_w_out": nrm(ks[7], (nA, GDN_V_W, D_MODEL), jnp.float32) * GDN_V_W ** -0.5,
        "ssd_w_in": nrm(ks[8], (nB, D_MODEL, SSD_IN_W), jnp.float32) * D_MODEL ** -0.5,
        "ssd_conv_w": nrm(ks[9], (nB, CONV_K, SSD_CONV_C), jnp.float32) * 0.5,
        "ssd_conv_b": 0.01 * nrm(ks[10], (nB, SSD_CONV_C), jnp.float32),
        "ssd_dt_bias": jnp.stack([_dt_bias_init(kk, SSD_H) for kk in jax.random.split(ks[11], nB)]),
        "ssd_a_log": jnp.log(jax.random.uniform(ks[12], (nB, SSD_H), minval=1.0, maxval=16.0)),
        "ssd_d": 1.0 + 0.01 * nrm(ks[13], (nB, SSD_H), jnp.float32),
        "ssd_norm_w": 1.0 + 0.01 * nrm(ks[14], (nB, SSD_D_INNER), jnp.float32),
        "ssd_w_out": nrm(ks[15], (nB, SSD_D_INNER, D_MODEL), jnp.float32) * SSD_D_INNER ** -0.5,
        "final_norm_w": 1.0 + 0.01 * nrm(ks[16], (D_MODEL,), jnp.float32),
    }


def reference(x, norm_w, gdn_w_in, gdn_conv_w, gdn_a_log, gdn_dt_bias, gdn_norm_w, gdn_w_out,
              ssd_w_in, ssd_conv_w, ssd_conv_b, ssd_dt_bias, ssd_a_log, ssd_d, ssd_norm_w,
              ssd_w_out, final_norm_w):
    for i in range(DEPTH):
        hid = _rmsnorm(x, norm_w[i])
        j = i // N_MIXERS
        if i % N_MIXERS == 0:
            x = x + _gated_deltanet(hid, gdn_w_in[j], gdn_conv_w[j], gdn_a_log[j],
                                    gdn_dt_bias[j], gdn_norm_w[j], gdn_w_out[j])
        else:
            x = x + _mamba2(hid, ssd_w_in[j], ssd_conv_w[j], ssd_conv_b[j], ssd_dt_bias[j],
                            ssd_a_log[j], ssd_d[j], ssd_norm_w[j], ssd_w_out[j])
    return _rmsnorm(x, final_norm_w)
```

```python
import numpy as np
from contextlib import ExitStack
import concourse.bass as bass
import concourse.mybir as mybir
from concourse.bass_utils import run_bass_kernel_spmd

F32 = mybir.dt.float32
BF16 = mybir.dt.bfloat16
AF = mybir.ActivationFunctionType
ALU = mybir.AluOpType

NDMASEM = 24
EPS = 1e-6
C = 128
DM = 1024
INW = 6176


class Buf:
    __slots__ = ("name", "lw", "rd")

    def __init__(self, name):
        self.name = name
        self.lw = None
        self.rd = []


class Prog:
    ENGS = ("pe", "act", "dve", "pool", "sp")

    def __init__(self, nc):
        self.nc = nc
        self.ops = []
        self.ndma = 0

    def op(self, eng, fn, reads=(), writes=(), dma=False):
        idx = len(self.ops)
        deps = set()
        for b in reads:
            if b.lw is not None:
                deps.add(b.lw)
        for b in writes:
            if b.lw is not None:
                deps.add(b.lw)
            deps.update(b.rd)
        for b in reads:
            b.rd.append(idx)
        for b in writes:
            b.lw = idx
            b.rd = []
        d = dict(eng=eng, fn=fn, deps=deps, dma=dma, dmaidx=None)
        if dma:
            d["dmaidx"] = self.ndma
            self.ndma += 1
        self.ops.append(d)
        return idx

    def emit(self):
        nc = self.nc
        ops = self.ops
        needed = [False] * len(ops)
        for i, o in enumerate(ops):
            for d in o["deps"]:
                po = ops[d]
                if po["dma"]:
                    continue
                if po["eng"] == "pe" and o["eng"] == "pe" and not o["dma"]:
                    continue
                needed[d] = True
        with ExitStack() as es:
            esem = {e: es.enter_context(nc.semaphore("s_" + e)) for e in self.ENGS}
            dsem = [es.enter_context(nc.semaphore("d_%d" % i)) for i in range(NDMASEM)]
            cnt = {e: 0 for e in self.ENGS}
            ev = [None] * len(ops)
            dma_by_idx = {}
            for i, o in enumerate(ops):
                if o["dma"]:
                    k = o["dmaidx"]
                    ev[i] = (dsem[k % NDMASEM], 16 * (k // NDMASEM + 1))
                    dma_by_idx[k] = i
                elif needed[i]:
                    cnt[o["eng"]] += 1
                    ev[i] = (esem[o["eng"]], cnt[o["eng"]])
            per_eng = {e: [] for e in self.ENGS}
            for i, o in enumerate(ops):
                per_eng[o["eng"]].append(i)
            block = es.enter_context(nc.Block())

            def make(ename, handle_name):
                lst = per_eng[ename]
                if not lst:
                    return

                def body(h):
                    waited = {}
                    for i in lst:
                        o = ops[i]
                        evs = []
                        for d in sorted(o["deps"]):
                            po = ops[d]
                            if (not po["dma"]) and po["eng"] == "pe" and ename == "pe" and not o["dma"]:
                                continue
                            evs.append(ev[d])
                        if o["dma"] and o["dmaidx"] >= NDMASEM:
                            evs.append(ev[dma_by_idx[o["dmaidx"] - NDMASEM]])
                        for (s, v) in evs:
                            key = id(s)
                            if waited.get(key, 0) >= v:
                                continue
                            waited[key] = v
                            h.wait_ge(s, v)
                        ins = o["fn"](h)
                        if ev[i] is not None:
                            s, v = ev[i]
                            ins.then_inc(s, 16 if o["dma"] else 1)
                    for i in lst:
                        o = ops[i]
                        if o["dma"]:
                            s, v = ev[i]
                            if waited.get(id(s), 0) < v:
                                waited[id(s)] = v
                                h.wait_ge(s, v)
                getattr(block, handle_name)(body)

            make("sp", "sync")
            make("pe", "tensor")
            make("act", "scalar")
            make("dve", "vector")
            make("pool", "gpsimd")


LEVELS = [1, 2, 4, 8, 16, 32, 64]


def host_consts():
    i = np.arange(128)
    ident = np.eye(128, dtype=np.float32)
    masks = np.zeros((128, 14, 128), np.float32)
    for li, l in enumerate(LEVELS):
        blk = i // (2 * l)
        half = (i // l) % 2
        M = (blk[:, None] == blk[None, :]) & (half[:, None] == 1) & (half[None, :] == 0)
        masks[:, li, :] = M
        masks[:, 7 + li, :] = M.T
    maskneg = np.where(i[None, :] >= i[:, None], 0.0, -30000.0).astype(np.float32)
    maskneg4 = np.tile(maskneg, (1, 4))
    masksu = (i[None, :] > i[:, None]).astype(np.float32)
    return dict(c_ident=ident, c_masks=masks.reshape(128, 14 * 128), c_maskneg4=maskneg4, c_masksu=masksu)


def build(layer, NCH, final_norm, dbg=None):
    gdn = layer == "gdn"
    H = 16 if gdn else 32
    DV = 2048 // H
    NHG = H // 4
    FW = 4 * DV
    CO = 0 if gdn else 2048
    ZO = 4096 if gdn else 0
    SO = 6144
    QO, KO, VO = (0, 8, 16) if gdn else (24, 16, 0)
    nc = bass.Bass("TRN2", target_bir_lowering=False)

    def din(name, shape, dt=F32):
        return nc.dram_tensor(name, shape, dt, kind="ExternalInput").ap()

    def dout(name, shape, dt=F32):
        return nc.dram_tensor(name, shape, dt, kind="ExternalOutput").ap()

    x_d = din("x", [(NCH + 1) * 128, DM])
    win_d = din("w_in", [DM, INW])
    wout_d = din("w_out", [2048, DM])
    normw_d = din("normw_bc", [128, DM])
    diag_d = din("diag", [8, 128, 16 * 128])
    sin_d = din("s_in", [128, 2048])
    alog_d = din("a_log", [H, 1])
    dtb_d = din("dt_bias", [H, 1])
    ident_d = din("c_ident", [128, 128])
    masks_d = din("c_masks", [128, 14 * 128])
    maskneg_d = din("c_maskneg4", [128, 512])
    masksu_d = din("c_masksu", [128, 128])
    if gdn:
        gnw_d = din("gnw_bc", [128, 128])
    else:
        convb_d = din("conv_b", [128, 32])
        dskip_d = din("dskip_bc", [128, 32])
        snw_d = din("snw_bc", [128, 2048])
    if final_norm:
        fnw_d = din("fnw_bc", [128, DM])
    xo_d = dout("xo", [NCH * 128, DM])
    sout_d = dout("s_out", [128, 2048])
    dbg_d = {}
    if dbg:
        for nm, (shp, dt_) in dbg.items():
            dbg_d[nm] = dout("dbg_" + nm, shp, dt_)
    diag_s = nc.dram_tensor("diag_s", [8, 128, 16 * 128], BF16, kind="Internal").ap()
    wout_s = nc.dram_tensor("wout_s", [16, 128, DM], BF16, kind="Internal").ap()
    gc_s = [nc.dram_tensor("gc_s%d" % i, [H, 128], F32, kind="Internal").ap() for i in range(2)]
    gl_s = [nc.dram_tensor("gl_s%d" % i, [H, 1], F32, kind="Internal").ap() for i in range(2)]

    P = Prog(nc)
    es = ExitStack()
    with es:
        def sb(name, shape, dt=F32):
            return es.enter_context(nc.sbuf_tensor(name, shape, dt))

        Wb = sb("Wb", [128, 8, INW], BF16)
        xt = [sb("xt%d" % i, [128, DM]) for i in range(2)]
        hid = sb("hid", [128, DM], BF16)
        hidT = sb("hidT", [128, 8, 128], BF16)
        normw = sb("normw", [128, DM])
        Pbuf = [sb("Pbuf%d" % i, [128, 4, 131], BF16) for i in range(2)]
        diag = [sb("diag%d" % i, [128, 4, 128], BF16) for i in range(3)]
        convT = sb("convT", [128, 32, 128], BF16)
        zs = sb("zs", [128, 2048], BF16)
        carry = sb("carry", [128, 32, 3], BF16)
        identf = sb("identf", [128, 128])
        identb = sb("identb", [128, 128], BF16)
        onesb = sb("onesb", [128, 128], BF16)
        maskneg = sb("maskneg", [128, 512], BF16)
        onesrow = sb("onesrow", [1, 128])
        negonesrow = sb("negonesrow", [1, 128])
        onesH = sb("onesH", [H, 128])
        alog = sb("alog", [H, 1])
        negA = sb("negA", [H, 1])
        dtb = sb("dtb", [H, 1])
        ss = sb("ss", [128, 4])
        smF = sb("smF", [H, 6, 128])
        gcrow4 = [sb("gcrow4_%d" % i, [1, 512]) for i in range(2)]
        glrow = sb("glrow", [1, H])
        smT = sb("smT", [128, 3 * H])
        tokS = sb("tokS", [128, 4, H])
        glbc = sb("glbc", [128, H])
        vtok = sb("vtok", [128, 2048], BF16)
        S = sb("S", [128, 2048])
        Sb = sb("Sb", [128, 2048], BF16)
        o_t = [sb("o_t%d" % i, [128, 512]) for i in range(2)]
        ya = sb("ya", [128, 2048], BF16)
        yT = sb("yT", [128, 16, 128], BF16)
        wo = [sb("wo%d" % i, [128, DM], BF16) for i in range(3)]
        nrm = sb("nrm", [128, 8])
        LT = sb("LT", [128, 4, 128], BF16)
        attnT = sb("attnT", [128, 4, 128], BF16)
        tmpf = sb("tmpf", [128, 512])
        vnew = sb("vnew", [128, 512], BF16)
        if gdn:
            masks = sb("masks", [128, 14, 128], BF16)
            masksu = sb("masksu", [128, 128], BF16)
            gnw = sb("gnw", [128, 128])
            sq4 = sb("sq4", [128, 512], BF16)
            ke = sb("ke", [128, 4, 128], BF16)
            kd = sb("kd", [128, 4, 128], BF16)
            LTs = sb("LTs", [128, 4, 128], BF16)
            NTm = sb("NTm", [128, 4, 128], BF16)
            Nn = sb("Nn", [128, 4, 128], BF16)
            Tm = [sb("Tm%d" % i, [128, 4, 128], BF16) for i in range(2)]
            Ym = [sb("Ym%d" % i, [128, 4, 128], BF16) for i in range(2)]
            Zm = sb("Zm", [128, 4, 128], BF16)
            Zpm = sb("Zpm", [128, 4, 128], BF16)
            ident4 = sb("ident4", [128, 4, 128], BF16)
            ub = sb("ub", [128, 512])
            wT = sb("wT", [128, 4, 128], BF16)
        else:
            ktok = sb("ktok", [128, 8, 128], BF16)
            vdec = sb("vdec", [128, 256], BF16)
            convb = sb("convb", [128, 32])
            dskip = sb("dskip", [128, 32])
            snw = sb("snw", [128, 2048], BF16)
        if final_norm:
            fnw = sb("fnw", [128, DM])
        ps = [es.enter_context(nc.psum_tensor("ps%d" % i, [128, 512], F32)) for i in range(8)]
        psb = [p[:].bitcast(BF16) for p in ps]

        B = {}

        def b(n):
            if n not in B:
                B[n] = Buf(n)
            return B[n]

        PSB = [b("ps%d" % i) for i in range(8)]

        def dma(eng, out, in_, reads, writes):
            P.op(eng, lambda h: h.dma_start(out=out, in_=in_), reads=reads, writes=writes, dma=True)

        dma("sp", identf[:], ident_d[:, :], [], [b("identf")])
        dma("pool", identb[:], ident_d[:, :], [], [b("identb")])
        dma("pool", maskneg[:], maskneg_d[:, :], [], [b("maskneg")])
        dma("sp", normw[:], normw_d[:, :], [], [b("normw")])
        dma("sp", alog[:], alog_d[:, :], [], [b("alog")])
        dma("sp", dtb[:], dtb_d[:, :], [], [b("dtb")])
        dma("sp", S[:], sin_d[:, :], [], [b("S%d" % i) for i in range(NHG)])
        if gdn:
            dma("pool", masks[:].rearrange("p a b -> p (a b)"), masks_d[:, :], [], [b("masks")])
            dma("pool", masksu[:], masksu_d[:, :], [], [b("masksu")])
            dma("sp", gnw[:], gnw_d[:, :], [], [b("gnw")])
        else:
            dma("sp", convb[:], convb_d[:, :], [], [b("convb")])
            dma("sp", dskip[:], dskip_d[:, :], [], [b("dskip")])
            dma("pool", snw[:], snw_d[:, :], [], [b("snw")])
        if final_norm:
            dma("sp", fnw[:], fnw_d[:, :], [], [b("fnw")])
        P.op("dve", lambda h: h.memset(onesb[:], 1.0), writes=[b("onesb")])
        P.op("dve", lambda h: h.memset(onesrow[:], 1.0), writes=[b("onesrow")])
        P.op("dve", lambda h: h.memset(negonesrow[:], -1.0), writes=[b("negonesrow")])
        P.op("dve", lambda h: h.memset(onesH[:], 1.0), writes=[b("onesH")])
        P.op("dve", lambda h: h.memset(carry[:], 0.0), writes=[b("carry")])
        P.op("dve", lambda h: h.memset(ss[:], 0.0), writes=[b("ss")])
        P.op("dve", lambda h: h.memset(nrm[:], 0.0), writes=[b("nrm")])
        P.op("act", lambda h: h.activation(out=negA[:], in_=alog[:], func=AF.Exp), reads=[b("alog")], writes=[b("negA")])
        P.op("dve", lambda h: h.tensor_scalar(out=negA[:], in0=negA[:], scalar1=-1.0, scalar2=None, op0=ALU.mult),
             reads=[b("negA")], writes=[b("negA")])
        P.op("act", lambda h: h.activation(out=Sb[:], in_=S[:], func=AF.Copy),
             reads=[b("S%d" % i) for i in range(NHG)], writes=[b("Sb%d" % i) for i in range(NHG)])
        if gdn:
            for i in range(4):
                P.op("dve", (lambda i: lambda h: h.tensor_copy(out=ident4[:, i, :], in_=identb[:]))(i),
                     reads=[b("identb")], writes=[b("ident4")])
        for k in range(8):
            for (f0, f1) in [(0, 2048), (2048, 4096), (4096, INW)]:
                dma("pool", Wb[:, k, f0:f1], win_d[k * 128:(k + 1) * 128, f0:f1], [], [b("Wb")])
        for kt2 in range(8):
            dma("pool", zs[:].rearrange("p (a c) -> p a c", a=2),
                wout_d[kt2 * 256:(kt2 + 1) * 256, :].rearrange("(a p) c -> p a c", a=2), [], [b("zs")])
            dma("sp", wout_s[kt2 * 2:(kt2 + 1) * 2].rearrange("a p c -> p a c"),
                zs[:].rearrange("p (a c) -> p a c", a=2), [b("zs")], [b("wout_s")])
        for c4 in range(8):
            dma("pool", zs[:], diag_d[c4], [], [b("zs")])
            dma("sp", diag_s[c4], zs[:], [b("zs")], [b("diag_s")])

        def mm(out, lhsT, rhs, start, stop, reads, writes):
            P.op("pe", lambda h: h.matmul(out, lhsT=lhsT, rhs=rhs, start=start, stop=stop), reads=reads, writes=writes)

        def tr(out, in_, ident, reads, writes):
            P.op("pe", lambda h: h.transpose(out=out, in_=in_, identity=ident), reads=reads, writes=writes)

        def act(out, in_, func, reads, writes, **kw):
            P.op("act", lambda h: h.activation(out=out, in_=in_, func=func, **kw), reads=reads, writes=writes)

        def tt(out, in0, in1, op, reads, writes, eng="dve"):
            P.op(eng, lambda h: h.tensor_tensor(out=out, in0=in0, in1=in1, op=op), reads=reads, writes=writes)

        def ts(out, in0, s1, s2, op0, op1, reads, writes, eng="dve"):
            if op1 is None:
                P.op(eng, lambda h: h.tensor_scalar(out=out, in0=in0, scalar1=s1, scalar2=None, op0=op0), reads=reads, writes=writes)
            else:
                P.op(eng, lambda h: h.tensor_scalar(out=out, in0=in0, scalar1=s1, scalar2=s2, op0=op0, op1=op1), reads=reads, writes=writes)

        def stt(out, in0, scalar, in1, op0, op1, reads, writes):
            P.op("dve", lambda h: h.scalar_tensor_tensor(out=out, in0=in0, scalar=scalar, in1=in1, op0=op0, op1=op1),
                 reads=reads, writes=writes)

        def memset(ap, val, writes):
            P.op("dve", lambda h: h.memset(ap, val), writes=writes)

        def recip(out, in_, reads, writes):
            P.op("dve", lambda h: h.reciprocal(out=out, in_=in_), reads=reads, writes=writes)

        def cp(out, in_, reads, writes):
            P.op("dve", lambda h: h.tensor_copy(out=out, in_=in_), reads=reads, writes=writes)

        def dbg_out(name, src_ap, reads, ci):
            if dbg and name in dbg_d and ci == dbg_chunk:
                dma("sp", dbg_d[name][:, :], src_ap, reads, [b("dbgo_" + name)])

        dbg_chunk = NCH
        wo_cnt = [0]
        dg_cnt = [0]
        G4 = lambda ap: ap.rearrange("p (a b) -> p a b", a=4)

        for ci in range(NCH + 1):
            halo = ci == 0
            xb_ = xt[ci % 2]
            XB = b("xt%d" % (ci % 2))
            dma("sp", xb_[:], x_d[ci * 128:(ci + 1) * 128, :], [], [XB])
            memset(ss[:, 0:1], 0.0, [b("ss")])
            act(hid[:], xb_[:], AF.Square, [XB], [b("hid"), b("ss")], accum_out=ss[:, 0:1])
            act(ss[:, 1:2], ss[:, 0:1], AF.Sqrt, [b("ss")], [b("ss")], scale=1.0 / DM, bias=EPS)
            recip(ss[:, 2:3], ss[:, 1:2], [b("ss")], [b("ss")])
            stt(hid[:], xb_[:], ss[:, 2:3], normw[:], ALU.mult, ALU.mult, [XB, b("ss"), b("normw")], [b("hid")])
            for k in range(8):
                tr(psb[0][:, k * 128:(k + 1) * 128], hid[:, k * 128:(k + 1) * 128], identb[:], [b("hid"), b("identb")], [PSB[0]])
            act(hidT[:].rearrange("p a b -> p (a b)"), psb[0][:, 0:1024], AF.Copy, [PSB[0]], [b("hidT")])
            for c4 in range(8):
                pa = 1 + (c4 % 2)
                pc = 3 + (c4 % 2)
                pbuf = Pbuf[c4 % 2]
                PB = b("Pbuf%d" % (c4 % 2))
                for i in range(4):
                    f0 = CO + (c4 * 4 + i) * 128
                    for k in range(8):
                        mm(ps[pa][:, i * 128:(i + 1) * 128], Wb[:, k, f0:f0 + 128], hidT[:, k, :], k == 0, k == 7,
                           [b("Wb"), b("hidT")], [PSB[pa]])
                cp(pbuf[:, :, 0:3], carry[:, c4 * 4:(c4 + 1) * 4, :], [b("carry")], [PB])
                act(pbuf[:, :, 3:131], G4(ps[pa][:]), AF.Copy, [PSB[pa]], [PB])
                cp(carry[:, c4 * 4:(c4 + 1) * 4, :], pbuf[:, :, 128:131], [PB], [b("carry")])
                if halo:
                    continue
                for i in range(4):
                    ct = c4 * 4 + i
                    di = dg_cnt[0] % 3
                    dg_cnt[0] += 1
                    dg = diag[di]
                    DG = b("diag%d" % di)
                    dma("sp", dg[:].rearrange("p a b -> p (a b)"), diag_s[c4][:, i * 512:(i + 1) * 512], [b("diag_s")], [DG])
                    for j in range(4):
                        mm(ps[pc][:, i * 128:(i + 1) * 128], dg[:, j, :], pbuf[:, i, j:j + 128], j == 0, j == 3, [DG, PB], [PSB[pc]])
                    if not gdn:
                        act(convT[:, ct, :], ps[pc][:, i * 128:(i + 1) * 128], AF.Silu, [PSB[pc], b("convb")], [b("convT%d" % c4)],
                            bias=convb[:, ct:ct + 1])
                if gdn:
                    act(convT[:, c4 * 4:(c4 + 1) * 4, :], G4(ps[pc][:]), AF.Silu, [PSB[pc]], [b("convT%d" % c4)])
            if halo:
                continue
            dbg_out("convT", convT[:].rearrange("p a b -> p (a b)"), [b("convT%d" % i) for i in range(8)], ci)
            for f in range(4):
                pz = 5 + (f % 2)
                for k in range(8):
                    mm(ps[pz][:, :], hidT[:, k, :], Wb[:, k, ZO + f * 512:ZO + (f + 1) * 512], k == 0, k == 7,
                       [b("Wb"), b("hidT")], [PSB[pz]])
                act(zs[:, f * 512:(f + 1) * 512], ps[pz][:, :], AF.Silu, [PSB[pz]], [b("zs")])
            SM = b("smF")
            if gdn:
                for k in range(8):
                    mm(ps[7][0:16, 0:128], Wb[:, k, SO:SO + 16], hidT[:, k, :], k == 0, k == 7, [b("Wb"), b("hidT")], [PSB[7]])
                for k in range(8):
                    mm(ps[7][0:16, 128:256], Wb[:, k, SO + 16:SO + 32], hidT[:, k, :], k == 0, k == 7, [b("Wb"), b("hidT")], [PSB[7]])
                act(smF[:, 0, :], ps[7][0:16, 0:128], AF.Sigmoid, [PSB[7]], [SM])
                act(smF[:, 1, :], ps[7][0:16, 128:256], AF.Exp, [PSB[7], b("dtb")], [SM], bias=dtb[:, 0:1])
            else:
                for k in range(8):
                    mm(ps[7][0:32, 0:128], Wb[:, k, SO:SO + 32], hidT[:, k, :], k == 0, k == 7, [b("Wb"), b("hidT")], [PSB[7]])
                act(smF[:, 1, :], ps[7][0:32, 0:128], AF.Exp, [PSB[7], b("dtb")], [SM], bias=dtb[:, 0:1])
            act(smF[:, 2, :], smF[:, 1, :], AF.Ln, [SM], [SM], bias=1.0)
            if not gdn:
                cp(smF[:, 0, :], smF[:, 2, :], [SM], [SM])
            ts(smF[:, 3, :], smF[:, 2, :], negA[:, 0:1], None, ALU.mult, None, [SM, b("negA")], [SM])
            P.op("dve", lambda h: h.tensor_tensor_scan(out=smF[:, 4, :], data0=onesH[:], data1=smF[:, 3, :], initial=0.0,
                                                       op0=ALU.mult, op1=ALU.add), reads=[SM, b("onesH")], writes=[SM])
            ts(smF[:, 5, :], smF[:, 4, :], -1.0, smF[:, 4, 127:128], ALU.mult, ALU.add, [SM], [SM])
            gcs, GCS = gc_s[ci % 2], b("gc_s%d" % (ci % 2))
            gls, GLS = gl_s[ci % 2], b("gl_s%d" % (ci % 2))
            dma("sp", gcs[:, :], smF[:, 4, :], [SM], [GCS])
            dma("sp", gls[:, :], smF[:, 4, 127:128], [SM], [GLS])
            dma("sp", glrow[0:1, :], gls.rearrange("h o -> o h"), [GLS], [b("glrow")])
            for t_, src in enumerate([0, 4, 5]):
                tr(ps[7][:, 256 + t_ * H:256 + (t_ + 1) * H], smF[:, src, :], identf[0:H, 0:H], [SM, b("identf")], [PSB[7]])
            cp(smT[:], ps[7][:, 256:256 + 3 * H], [PSB[7]], [b("smT")])
            TS = b("tokS")
            act(tokS[:, 0, :], smT[:, H:2 * H], AF.Exp, [b("smT")], [TS])
            act(tokS[:, 1, :], smT[:, 2 * H:3 * H], AF.Exp, [b("smT")], [TS])
            if gdn:
                ts(tokS[:, 3, :], smT[:, 0:H], -1.0, None, ALU.mult, None, [b("smT")], [TS])
            else:
                tt(tokS[:, 2, :], smT[:, 0:H], tokS[:, 1, :], ALU.mult, [b("smT"), TS], [TS])
            mm(ps[7][:, 384:384 + H], onesrow[0:1, 0:128], glrow[0:1, :], True, True, [b("onesrow"), b("glrow")], [PSB[7]])
            act(glbc[:], ps[7][:, 384:384 + H], AF.Exp, [PSB[7]], [b("glbc")])
            dbg_out("smT", smT[:], [b("smT")], ci)
            if gdn:
                for g4 in range(4):
                    CB = b("convT%d" % g4)
                    cv = convT[:, g4 * 4:(g4 + 1) * 4, :].rearrange("p a b -> p (a b)")
                    tt(sq4[:], cv, cv, ALU.mult, [CB], [b("sq4")])
                    mm(ps[1][:, :], onesb[:], sq4[:], True, True, [b("onesb"), b("sq4")], [PSB[1]])
                    act(tmpf[:], ps[1][:, :], AF.Sqrt, [PSB[1]], [b("tmpf")], bias=EPS)
                    recip(tmpf[:], tmpf[:], [b("tmpf")], [b("tmpf")])
                    stt(cv, cv, (128.0 ** -0.5) if g4 < 2 else 1.0, tmpf[:], ALU.mult, ALU.mult, [CB, b("tmpf")], [CB])
            dbg_out("qkn", convT[:, 0:16, :].rearrange("p a b -> p (a b)"), [b("convT%d" % i) for i in range(4)], ci)
            for half in range(2):
                pv = 1 + half
                for i in range(8):
                    ct = VO + half * 8 + i
                    tr(psb[pv][:, i * 128:(i + 1) * 128], convT[:, ct, :], identb[:], [b("convT%d" % (ct // 4)), b("identb")], [PSB[pv]])
                act(vtok[:, half * 1024:(half + 1) * 1024], psb[pv][:, 0:1024], AF.Copy, [PSB[pv]], [b("vtok")])
            for i in range(8):
                ct = KO + i
                tr(psb[3][:, i * 128:(i + 1) * 128], convT[:, ct, :], identb[:], [b("convT%d" % (ct // 4)), b("identb")], [PSB[3]])
            if not gdn:
                act(ktok[:].rearrange("p a b -> p (a b)"), psb[3][:, 0:1024], AF.Copy, [PSB[3]], [b("ktok")])
            for hg in range(NHG):
                h0 = hg * 4
                SB_, SBb = b("S%d" % hg), b("Sb%d" % hg)
                oh = o_t[hg % 2]
                OB = b("o_t%d" % (hg % 2))
                gr = gcrow4[hg % 2]
                GR = b("gcrow4_%d" % (hg % 2))
                dma("sp", gr[0:1, :], gcs[h0:h0 + 4, :].rearrange("(o h) c -> o (h c)", o=1), [GCS], [GR])
                mm(ps[4][:, :], onesrow[0:1, 0:128], gr[0:1, :], True, False, [b("onesrow"), GR], [PSB[4]])
                for hh in range(4):
                    mm(ps[4][:, hh * 128:(hh + 1) * 128], gr[0:1, hh * 128:(hh + 1) * 128], negonesrow[0:1, :], False, False,
                       [b("negonesrow"), GR], [PSB[4]])
                mm(ps[4][:, :], identb[:], maskneg[:], False, True, [b("identb"), b("maskneg")], [PSB[4]])
                act(LT[:].rearrange("p a b -> p (a b)"), ps[4][:, :], AF.Exp, [PSB[4]], [b("LT")])
                if gdn:
                    kin = psb[3][:, hg * 256:(hg + 1) * 256].rearrange("p (g d) -> p g d", g=2).unsqueeze(2).broadcast_to([128, 2, 2, 128])
                    for (dst, DB, row) in [(ke, b("ke"), 0), (kd, b("kd"), 1)]:
                        tt(dst[:].rearrange("p (g r) d -> p g r d", g=2), kin,
                           tokS[:, row, h0:h0 + 4].rearrange("p (g r) -> p g r", g=2).unsqueeze(3).broadcast_to([128, 2, 2, 128]),
                           ALU.mult, [PSB[3], TS], [DB])
                    for qq in range(2):
                        g = hg * 2 + qq
                        mm(ps[5][:, qq * 128:(qq + 1) * 128], convT[:, KO + g, :], convT[:, KO + g, :], True, True,
                           [b("convT%d" % ((KO + g) // 4))], [PSB[5]])
                        mm(ps[5][:, 256 + qq * 128:256 + (qq + 1) * 128], convT[:, KO + g, :], convT[:, QO + g, :], True, True,
                           [b("convT%d" % ((KO + g) // 4)), b("convT%d" % ((QO + g) // 4))], [PSB[5]])
                    tt(LTs[:], LT[:], masksu[:].unsqueeze(1).broadcast_to([128, 4, 128]), ALU.mult, [b("LT"), b("masksu")], [b("LTs")])
                    for hh in range(4):
                        stt(NTm[:, hh, :], ps[5][:, (hh // 2) * 128:(hh // 2 + 1) * 128], tokS[:, 3, h0 + hh:h0 + hh + 1], LTs[:, hh, :],
                            ALU.mult, ALU.mult, [PSB[5], TS, b("LTs")], [b("NTm")])
                    tt(attnT[:].rearrange("p (q r) d -> p q r d", q=2),
                       ps[5][:, 256:512].rearrange("p (q d) -> p q d", q=2).unsqueeze(2).broadcast_to([128, 2, 2, 128]),
                       LT[:].rearrange("p (q r) d -> p q r d", q=2), ALU.mult, [PSB[5], b("LT")], [b("attnT")])
                else:
                    g = hg
                    mm(ps[5][:, 0:128], convT[:, KO + g, :], convT[:, QO + g, :], True, True,
                       [b("convT%d" % ((KO + g) // 4)), b("convT%d" % ((QO + g) // 4))], [PSB[5]])
                    tt(attnT[:], ps[5][:, 0:128].unsqueeze(1).broadcast_to([128, 4, 128]), LT[:], ALU.mult, [PSB[5], b("LT")], [b("attnT")])
                if gdn:
                    for hh in range(4):
                        tr(psb[6][:, hh * 128:(hh + 1) * 128], NTm[:, hh, :], identb[:], [b("NTm"), b("identb")], [PSB[6]])
                    act(Nn[:].rearrange("p a b -> p (a b)"), psb[6][:, 0:512], AF.Copy, [PSB[6]], [b("Nn")])
                    cur = 0
                    Tc, Yc = ident4, ident4
                    TCB, YCB = b("ident4"), b("ident4")
                    for li in range(7):
                        Tn_, Yn_ = Tm[cur], Ym[cur]
                        TNB, YNB = b("Tm%d" % cur), b("Ym%d" % cur)
                        last = li == 6
                        Ml = masks[:, li, :].unsqueeze(1).broadcast_to([128, 4, 128])
                        MlT = masks[:, 7 + li, :].unsqueeze(1).broadcast_to([128, 4, 128])
                        if not last:
                            for hh in range(4):
                                mm(ps[4][:, hh * 128:(hh + 1) * 128], NTm[:, hh, :], Tc[:, hh, :], True, True, [b("NTm"), TCB], [PSB[4]])
                            tt(Zm[:], G4(ps[4][:, :]), Ml, ALU.mult, [PSB[4], b("masks")], [b("Zm")])
                        for hh in range(4):
                            mm(ps[5][:, hh * 128:(hh + 1) * 128], Nn[:, hh, :], Yc[:, hh, :], True, True, [b("Nn"), YCB], [PSB[5]])
                        tt(Zpm[:], G4(ps[5][:, :]), MlT, ALU.mult, [PSB[5], b("masks")], [b("Zpm")])
                        if not last:
                            for hh in range(4):
                                mm(ps[6][:, hh * 128:(hh + 1) * 128], Yc[:, hh, :], Zm[:, hh, :], True, True, [YCB, b("Zm")], [PSB[6]])
                            tt(Tn_[:], Tc[:], G4(ps[6][:, :]), ALU.add, [PSB[6], TCB], [TNB])
                        for hh in range(4):
                            mm(ps[7][:, hh * 128:(hh + 1) * 128], Tc[:, hh, :], Zpm[:, hh, :], True, True, [TCB, b("Zpm")], [PSB[7]])
                        tt(Yn_[:], Yc[:], G4(ps[7][:, :]), ALU.add, [PSB[7], YCB], [YNB])
                        if not last:
                            Tc, TCB = Tn_, TNB
                        Yc, YCB = Yn_, YNB
                        cur ^= 1
                    for hh in range(4):
                        hd = h0 + hh
                        mm(ps[4][:, hh * 128:(hh + 1) * 128], Yc[:, hh, :], vtok[:, hd * 128:(hd + 1) * 128], True, True,
                           [YCB, b("vtok")], [PSB[4]])
                    tt(G4(ub[:]), G4(ps[4][:, :]), smT[:, h0:h0 + 4].unsqueeze(2).broadcast_to([128, 4, 128]), ALU.mult,
                       [PSB[4], b("smT")], [b("ub")])
                    for hh in range(4):
                        mm(ps[5][:, hh * 128:(hh + 1) * 128], ke[:, hh, :], Yc[:, hh, :], True, True, [b("ke"), YCB], [PSB[5]])
                    act(wT[:].rearrange("p a b -> p (a b)"), ps[5][:, :], AF.Copy, [PSB[5]], [b("wT")])
                    for hh in range(4):
                        hd = h0 + hh
                        mm(ps[6][:, hh * 128:(hh + 1) * 128], wT[:, hh, :], Sb[:, hd * 128:(hd + 1) * 128], True, True, [b("wT"), SBb], [PSB[6]])
                    tt(G4(tmpf[:]), G4(ps[6][:, :]), tokS[:, 3, h0:h0 + 4].unsqueeze(2).broadcast_to([128, 4, 128]), ALU.mult,
                       [PSB[6], TS], [b("tmpf")])
                    tt(vnew[:], tmpf[:], ub[:], ALU.add, [b("tmpf"), b("ub")], [b("vnew")])
                else:
                    xin = G4(vtok[:, h0 * DV:(h0 + 4) * DV])
                    tt(G4(vnew[:, 0:FW]), xin, smT[:, h0:h0 + 4].unsqueeze(2).broadcast_to([128, 4, DV]),
                       ALU.mult, [b("vtok"), b("smT")], [b("vnew")])
                    tt(G4(vdec[:, 0:FW]), xin, tokS[:, 2, h0:h0 + 4].unsqueeze(2).broadcast_to([128, 4, DV]),
                       ALU.mult, [b("vtok"), TS], [b("vdec")])
                if gdn:
                    for hh in range(4):
                        hd = h0 + hh
                        g = hd // 2
                        mm(ps[4][:, hh * 128:(hh + 1) * 128], convT[:, QO + g, :], Sb[:, hd * 128:(hd + 1) * 128], True, True,
                           [b("convT%d" % ((QO + g) // 4)), SBb], [PSB[4]])
                else:
                    mm(ps[4][:, 0:FW], convT[:, QO + hg, :], Sb[:, h0 * DV:(h0 + 4) * DV], True, True,
                       [b("convT%d" % ((QO + hg) // 4)), SBb], [PSB[4]])
                for hh in range(4):
                    mm(ps[5][:, hh * DV:(hh + 1) * DV], attnT[:, hh, :], vnew[:, hh * DV:(hh + 1) * DV], True, True,
                       [b("attnT"), b("vnew")], [PSB[5]])
                tt(G4(tmpf[:, 0:FW]), G4(ps[4][:, 0:FW]), tokS[:, 0, h0:h0 + 4].unsqueeze(2).broadcast_to([128, 4, DV]), ALU.mult,
                   [PSB[4], TS], [b("tmpf")])
                tt(oh[:, 0:FW], tmpf[:, 0:FW], ps[5][:, 0:FW], ALU.add, [b("tmpf"), PSB[5]], [OB])
                if gdn:
                    for hh in range(4):
                        mm(ps[6][:, hh * 128:(hh + 1) * 128], kd[:, hh, :], vnew[:, hh * 128:(hh + 1) * 128], True, True,
                           [b("kd"), b("vnew")], [PSB[6]])
                else:
                    mm(ps[6][:, 0:FW], ktok[:, hg, :], vdec[:, 0:FW], True, True, [b("ktok"), b("vdec")], [PSB[6]])
                tt(G4(tmpf[:, 0:FW]), G4(S[:, hg * FW:(hg + 1) * FW]), glbc[:, h0:h0 + 4].unsqueeze(2).broadcast_to([128, 4, DV]),
                   ALU.mult, [SB_, b("glbc")], [b("tmpf")])
                tt(S[:, hg * FW:(hg + 1) * FW], tmpf[:, 0:FW], ps[6][:, 0:FW], ALU.add, [b("tmpf"), PSB[6]], [SB_])
                act(Sb[:, hg * FW:(hg + 1) * FW], S[:, hg * FW:(hg + 1) * FW], AF.Copy, [SB_], [SBb])
                dbg_out("o%d" % hg, oh[:, 0:FW], [OB], ci)
                ysl = ya[:, hg * FW:(hg + 1) * FW]
                zsl = zs[:, hg * FW:(hg + 1) * FW]
                jk = yT[:].rearrange("p a b -> p (a b)")[:, 0:FW]
                memset(nrm[:, 0:4], 0.0, [b("nrm")])
                if gdn:
                    for hh in range(4):
                        act(jk[:, hh * 128:(hh + 1) * 128], oh[:, hh * 128:(hh + 1) * 128], AF.Square, [OB], [b("yT"), b("nrm")],
                            accum_out=nrm[:, hh:hh + 1])
                    act(nrm[:, 4:8], nrm[:, 0:4], AF.Sqrt, [b("nrm")], [b("nrm")], scale=1.0 / 128, bias=EPS)
                    recip(nrm[:, 4:8], nrm[:, 4:8], [b("nrm")], [b("nrm")])
                    for hh in range(4):
                        act(ysl[:, hh * 128:(hh + 1) * 128], oh[:, hh * 128:(hh + 1) * 128], AF.Copy, [OB, b("nrm")], [b("ya")],
                            scale=nrm[:, 4 + hh:5 + hh])
                    tt(G4(ysl), G4(ysl), gnw[:].unsqueeze(1).broadcast_to([128, 4, 128]), ALU.mult, [b("ya"), b("gnw")], [b("ya")])
                    tt(ysl, ysl, zsl, ALU.mult, [b("ya"), b("zs")], [b("ya")])
                else:
                    tt(G4(tmpf[:, 0:FW]), G4(vtok[:, h0 * DV:(h0 + 4) * DV]), dskip[:, h0:h0 + 4].unsqueeze(2).broadcast_to([128, 4, DV]),
                       ALU.mult, [b("vtok"), b("dskip")], [b("tmpf")])
                    tt(oh[:, 0:FW], oh[:, 0:FW], tmpf[:, 0:FW], ALU.add, [OB, b("tmpf")], [OB])
                    tt(oh[:, 0:FW], oh[:, 0:FW], zsl, ALU.mult, [OB, b("zs")], [OB])
                    act(jk, oh[:, 0:FW], AF.Square, [OB], [b("yT"), b("nrm")], accum_out=nrm[:, 0:1])
                    act(nrm[:, 4:5], nrm[:, 0:1], AF.Sqrt, [b("nrm")], [b("nrm")], scale=1.0 / 256, bias=EPS)
                    recip(nrm[:, 4:5], nrm[:, 4:5], [b("nrm")], [b("nrm")])
                    act(ysl, oh[:, 0:FW], AF.Copy, [OB, b("nrm")], [b("ya")], scale=nrm[:, 4:5])
                    tt(ysl, ysl, snw[:, hg * FW:(hg + 1) * FW], ALU.mult, [b("ya"), b("snw")], [b("ya")])
            dbg_out("y", ya[:], [b("ya")], ci)
            for half in range(2):
                pv = 1 + half
                for i in range(8):
                    kt = half * 8 + i
                    tr(psb[pv][:, i * 128:(i + 1) * 128], ya[:, kt * 128:(kt + 1) * 128], identb[:], [b("ya"), b("identb")], [PSB[pv]])
                act(yT[:, half * 8:(half + 1) * 8, :].rearrange("p a b -> p (a b)"), psb[pv][:, 0:1024], AF.Copy, [PSB[pv]], [b("yT")])
            for kt in range(16):
                wi = wo_cnt[0] % 3
                wo_cnt[0] += 1
                WB = b("wo%d" % wi)
                dma("sp", wo[wi][:], wout_s[kt], [b("wout_s")], [WB])
                for n in range(2):
                    mm(ps[3 + n][:, :], yT[:, kt, :], wo[wi][:, n * 512:(n + 1) * 512], kt == 0, kt == 15, [b("yT"), WB], [PSB[3 + n]])
            for n in range(2):
                tt(xb_[:, n * 512:(n + 1) * 512], xb_[:, n * 512:(n + 1) * 512], ps[3 + n][:, :], ALU.add, [XB, PSB[3 + n]], [XB])
            if final_norm:
                memset(ss[:, 0:1], 0.0, [b("ss")])
                act(hid[:], xb_[:], AF.Square, [XB], [b("hid"), b("ss")], accum_out=ss[:, 0:1])
                act(ss[:, 1:2], ss[:, 0:1], AF.Sqrt, [b("ss")], [b("ss")], scale=1.0 / DM, bias=EPS)
                recip(ss[:, 2:3], ss[:, 1:2], [b("ss")], [b("ss")])
                stt(xb_[:], xb_[:], ss[:, 2:3], fnw[:], ALU.mult, ALU.mult, [XB, b("ss"), b("fnw")], [XB])
            dma("pool", xo_d[(ci - 1) * 128:ci * 128, :], xb_[:], [XB], [b("xo%d" % ci)])
        dma("pool", sout_d[:, :], S[:], [b("S%d" % i) for i in range(NHG)], [b("sout")])
        P.emit()
    return nc


def conv_diag(conv_w):
    cw = np.asarray(conv_w, np.float32).reshape(4, 8, 4, 128)
    d = np.zeros((8, 128, 4, 4, 128), np.float32)
    for p in range(128):
        d[:, p, :, :, p] = np.transpose(cw[:, :, :, p], (1, 2, 0))
    return d.reshape(8, 128, 16 * 128)


def bc(v, n=128):
    v = np.asarray(v, np.float32).reshape(1, -1)
    return np.ascontiguousarray(np.broadcast_to(v, (n, v.shape[1])))


def layer_inputs(layer, x_with_halo, s_in, norm_w, w_in, conv_w, w_out, a_log, dt_bias, gdn_norm_w=None,
                 conv_b=None, d_skip=None, ssd_norm_w=None, final_norm_w=None):
    H = 16 if layer == "gdn" else 32
    m = dict(host_consts())
    m.update(x=np.ascontiguousarray(x_with_halo, dtype=np.float32), w_in=np.ascontiguousarray(w_in, dtype=np.float32),
             w_out=np.ascontiguousarray(w_out, dtype=np.float32), normw_bc=bc(norm_w), diag=conv_diag(conv_w),
             s_in=np.ascontiguousarray(s_in, dtype=np.float32),
             a_log=np.asarray(a_log, np.float32).reshape(H, 1).copy(), dt_bias=np.asarray(dt_bias, np.float32).reshape(H, 1).copy())
    if layer == "gdn":
        m["gnw_bc"] = bc(gdn_norm_w)
    else:
        m["conv_b"] = np.ascontiguousarray(np.asarray(conv_b, np.float32).reshape(32, 128).T)
        m["dskip_bc"] = bc(d_skip)
        m["snw_bc"] = bc(ssd_norm_w)
    if final_norm_w is not None:
        m["fnw_bc"] = bc(final_norm_w)
    return m


NSEG = 4
SEG = 4096
NCHUNK = SEG // 128


def kernel(x, norm_w, gdn_w_in, gdn_conv_w, gdn_a_log, gdn_dt_bias, gdn_norm_w, gdn_w_out,
           ssd_w_in, ssd_conv_w, ssd_conv_b, ssd_dt_bias, ssd_a_log, ssd_d, ssd_norm_w, ssd_w_out, final_norm_w):
    x = np.asarray(x, np.float32)
    Bn = x.shape[0]
    z128 = np.zeros((128, DM), np.float32)
    cur = x
    for layer in ("gdn", "ssd"):
        nc = build(layer, NCHUNK, layer == "ssd")
        nxt = np.empty_like(x)
        S = [np.zeros((128, 2048), np.float32) for _ in range(Bn)]
        for k in range(NSEG):
            maps = []
            for bi in range(Bn):
                halo = z128 if k == 0 else cur[bi, k * SEG - 128:k * SEG]
                xh = np.concatenate([halo, cur[bi, k * SEG:(k + 1) * SEG]], axis=0)
                if layer == "gdn":
                    maps.append(layer_inputs("gdn", xh, S[bi], norm_w[0], gdn_w_in[0], gdn_conv_w[0], gdn_w_out[0], gdn_a_log[0],
                                             gdn_dt_bias[0], gdn_norm_w=gdn_norm_w[0]))
                else:
                    maps.append(layer_inputs("ssd", xh, S[bi], norm_w[1], ssd_w_in[0], ssd_conv_w[0], ssd_w_out[0], ssd_a_log[0],
                                             ssd_dt_bias[0], conv_b=ssd_conv_b[0], d_skip=ssd_d[0], ssd_norm_w=ssd_norm_w[0],
                                             final_norm_w=final_norm_w))
            res = run_bass_kernel_spmd(nc, maps, core_ids=list(range(Bn)))
            for bi in range(Bn):
                nxt[bi, k * SEG:(k + 1) * SEG] = res.results[bi]["xo"]
                S[bi] = res.results[bi]["s_out"]
        cur = nxt
    return cur
```

```python
import numpy as np
from contextlib import ExitStack
import concourse.bass as bass
import concourse.mybir as mybir
from concourse.bass_utils import run_bass_kernel_spmd

F32 = mybir.dt.float32
BF16 = mybir.dt.bfloat16
AF = mybir.ActivationFunctionType
ALU = mybir.AluOpType

NDMASEM = 24
EPS = 1e-6
C = 128
DM = 1024
INW = 6176


class Buf:
    __slots__ = ("name", "lw", "rd")

    def __init__(self, name):
        self.name = name
        self.lw = None
        self.rd = []


class Prog:
    ENGS = ("pe", "act", "dve", "pool", "sp")

    def __init__(self, nc):
        self.nc = nc
        self.ops = []
        self.ndma = 0

    def op(self, eng, fn, reads=(), writes=(), dma=False):
        idx = len(self.ops)
        deps = set()
        for b in reads:
            if b.lw is not None:
                deps.add(b.lw)
        for b in writes:
            if b.lw is not None:
                deps.add(b.lw)
            deps.update(b.rd)
        for b in reads:
            b.rd.append(idx)
        for b in writes:
            b.lw = idx
            b.rd = []
        d = dict(eng=eng, fn=fn, deps=deps, dma=dma, dmaidx=None)
        if dma:
            d["dmaidx"] = self.ndma
            self.ndma += 1
        self.ops.append(d)
        return idx

    def emit(self):
        nc = self.nc
        ops = self.ops
        needed = [False] * len(ops)
        for i, o in enumerate(ops):
            for d in o["deps"]:
                po = ops[d]
                if po["dma"]:
                    continue
                if po["eng"] == "pe" and o["eng"] == "pe" and not o["dma"]:
                    continue
                needed[d] = True
        with ExitStack() as es:
            esem = {e: es.enter_context(nc.semaphore("s_" + e)) for e in self.ENGS}
            dsem = [es.enter_context(nc.semaphore("d_%d" % i)) for i in range(NDMASEM)]
            cnt = {e: 0 for e in self.ENGS}
            ev = [None] * len(ops)
            dma_by_idx = {}
            for i, o in enumerate(ops):
                if o["dma"]:
                    k = o["dmaidx"]
                    ev[i] = (dsem[k % NDMASEM], 16 * (k // NDMASEM + 1))
                    dma_by_idx[k] = i
                elif needed[i]:
                    cnt[o["eng"]] += 1
                    ev[i] = (esem[o["eng"]], cnt[o["eng"]])
            per_eng = {e: [] for e in self.ENGS}
            for i, o in enumerate(ops):
                per_eng[o["eng"]].append(i)
            block = es.enter_context(nc.Block())

            def make(ename, handle_name):
                lst = per_eng[ename]
                if not lst:
                    return

                def body(h):
                    waited = {}
                    for i in lst:
                        o = ops[i]
                        evs = []
                        for d in sorted(o["deps"]):
                            po = ops[d]
                            if (not po["dma"]) and po["eng"] == "pe" and ename == "pe" and not o["dma"]:
                                continue
                            evs.append(ev[d])
                        if o["dma"] and o["dmaidx"] >= NDMASEM:
                            evs.append(ev[dma_by_idx[o["dmaidx"] - NDMASEM]])
                        for (s, v) in evs:
                            key = id(s)
                            if waited.get(key, 0) >= v:
                                continue
                            waited[key] = v
                            h.wait_ge(s, v)
                        ins = o["fn"](h)
                        if ev[i] is not None:
                            s, v = ev[i]
                            ins.then_inc(s, 16 if o["dma"] else 1)
                    for i in lst:
                        o = ops[i]
                        if o["dma"]:
                            s, v = ev[i]
                            if waited.get(id(s), 0) < v:
                                waited[id(s)] = v
                                h.wait_ge(s, v)
                getattr(block, handle_name)(body)

            make("sp", "sync")
            make("pe", "tensor")
            make("act", "scalar")
            make("dve", "vector")
            make("pool", "gpsimd")


LEVELS = [1, 2, 4, 8, 16, 32, 64]


def host_consts():
    i = np.arange(128)
    ident = np.eye(128, dtype=np.float32)
    masks = np.zeros((128, 14, 128), np.float32)
    for li, l in enumerate(LEVELS):
        blk = i // (2 * l)
        half = (i // l) % 2
        M = (blk[:, None] == blk[None, :]) & (half[:, None] == 1) & (half[None, :] == 0)
        masks[:, li, :] = M
        masks[:, 7 + li, :] = M.T
    maskneg = np.where(i[None, :] >= i[:, None], 0.0, -30000.0).astype(np.float32)
    maskneg4 = np.tile(maskneg, (1, 4))
    masksu = (i[None, :] > i[:, None]).astype(np.float32)
    return dict(c_ident=ident, c_masks=masks.reshape(128, 14 * 128), c_maskneg4=maskneg4, c_masksu=masksu)


def build_fused(NCH, layers=("gdn", "ssd"), dbg=None, n_in_rows=None):
    nc = bass.Bass("TRN2", target_bir_lowering=False)

    def din(name, shape, dt=F32):
        return nc.dram_tensor(name, shape, dt, kind="ExternalInput").ap()

    def dout(name, shape, dt=F32):
        return nc.dram_tensor(name, shape, dt, kind="ExternalOutput").ap()

    x_d = din("x", [(NCH + 1) * 128, DM])
    ident_d = din("c_ident", [128, 128])
    masks_d = din("c_masks", [128, 14 * 128])
    maskneg_d = din("c_maskneg4", [128, 512])
    masksu_d = din("c_masksu", [128, 128])
    LD = {}
    for layer in layers:
        pf = layer[0] + "_"
        Hh = 16 if layer == "gdn" else 32
        d = dict(w_in=din(pf + "w_in", [DM, INW]), w_out=din(pf + "w_out", [2048, DM]), normw=din(pf + "normw_bc", [128, DM]),
                 diag=din(pf + "diag", [8, 128, 16 * 128]), a_log=din(pf + "a_log", [Hh, 1]), dt_bias=din(pf + "dt_bias", [Hh, 1]))
        if layer == "gdn":
            d["gnw"] = din(pf + "gnw_bc", [128, 128])
        else:
            d["convb"] = din(pf + "conv_b", [128, 32])
            d["dskip"] = din(pf + "dskip_bc", [128, 32])
            d["snw"] = din(pf + "snw_bc", [128, 2048])
        LD[layer] = d
    fnw_d = din("fnw_bc", [128, DM])
    xo_d = dout("xo", [NCH * 128, DM])
    dbg_d = {}
    if dbg:
        for nm, (shp, dt_) in dbg.items():
            dbg_d[nm] = dout("dbg_" + nm, shp, dt_)
    diag_s = nc.dram_tensor("diag_s", [8, 128, 16 * 128], BF16, kind="Internal").ap()
    wout_s = nc.dram_tensor("wout_s", [16, 128, DM], BF16, kind="Internal").ap()
    gc_s_full = [nc.dram_tensor("gc_s%d" % i, [32, 128], F32, kind="Internal").ap() for i in range(2)]
    gl_s_full = [nc.dram_tensor("gl_s%d" % i, [32, 1], F32, kind="Internal").ap() for i in range(2)]
    xmid_s = nc.dram_tensor("xmid_s", [(NCH + 1) * 128, DM], F32, kind="Internal").ap()

    P = Prog(nc)
    es = ExitStack()
    with es:
        def sb(name, shape, dt=F32):
            return es.enter_context(nc.sbuf_tensor(name, shape, dt))

        Wb = sb("Wb", [128, 8, INW], BF16)
        xt = [sb("xt%d" % i, [128, DM]) for i in range(2)]
        hid = sb("hid", [128, DM], BF16)
        hidT = sb("hidT", [128, 8, 128], BF16)
        normw = sb("normw", [128, DM])
        fnw = sb("fnw", [128, DM])
        Pbuf = [sb("Pbuf%d" % i, [128, 4, 131], BF16) for i in range(2)]
        diag = [sb("diag%d" % i, [128, 4, 128], BF16) for i in range(3)]
        convT = sb("convT", [128, 32, 128], BF16)
        zs = sb("zs", [128, 2048], BF16)
        carry = sb("carry", [128, 32, 3], BF16)
        identf = sb("identf", [128, 128])
        identb = sb("identb", [128, 128], BF16)
        onesb = sb("onesb", [128, 128], BF16)
        maskneg = sb("maskneg", [128, 512], BF16)
        onesrow = sb("onesrow", [1, 128])
        negonesrow = sb("negonesrow", [1, 128])
        onesH_f = sb("onesH", [32, 128])
        alog_f = sb("alog", [32, 1])
        negA_f = sb("negA", [32, 1])
        dtb_f = sb("dtb", [32, 1])
        ss = sb("ss", [128, 4])
        smF_f = sb("smF", [32, 6, 128])
        gcrow4 = [sb("gcrow4_%d" % i, [1, 512]) for i in range(2)]
        glrow_f = sb("glrow", [1, 32])
        smT_f = sb("smT", [128, 96])
        tokS_f = sb("tokS", [128, 4, 32])
        glbc_f = sb("glbc", [128, 32])
        vtok = sb("vtok", [128, 2048], BF16)
        S = sb("S", [128, 2048])
        Sb = sb("Sb", [128, 2048], BF16)
        o_t = [sb("o_t%d" % i, [128, 512]) for i in range(1)]
        ya = sb("ya", [128, 2048], BF16)
        yT = sb("yT", [128, 16, 128], BF16)
        wo = [sb("wo%d" % i, [128, DM], BF16) for i in range(2)]
        NWO = 2
        nrm = sb("nrm", [128, 8])
        LT = sb("LT", [128, 4, 128], BF16)
        attnT = sb("attnT", [128, 4, 128], BF16)
        tmpf = sb("tmpf", [128, 512])
        vnew = sb("vnew", [128, 512], BF16)
        UN = 20736 // 2
        U = sb("U", [128, UN], BF16)
        ps = [es.enter_context(nc.psum_tensor("ps%d" % i, [128, 512], F32)) for i in range(8)]
        psb = [p[:].bitcast(BF16) for p in ps]

        B = {}

        def b(n):
            if n not in B:
                B[n] = Buf(n)
            return B[n]

        PSB = [b("ps%d" % i) for i in range(8)]

        def dma(eng, out, in_, reads, writes):
            P.op(eng, lambda h: h.dma_start(out=out, in_=in_), reads=reads, writes=writes, dma=True)


        dma("sp", identf[:], ident_d[:, :], [], [b("identf")])
        dma("pool", identb[:], ident_d[:, :], [], [b("identb")])
        dma("pool", maskneg[:], maskneg_d[:, :], [], [b("maskneg")])
        dma("sp", fnw[:], fnw_d[:, :], [], [b("fnw")])
        P.op("dve", lambda h: h.memset(onesb[:], 1.0), writes=[b("onesb")])
        P.op("dve", lambda h: h.memset(onesrow[:], 1.0), writes=[b("onesrow")])
        P.op("dve", lambda h: h.memset(negonesrow[:], -1.0), writes=[b("negonesrow")])
        P.op("dve", lambda h: h.memset(onesH_f[:], 1.0), writes=[b("onesH")])
        P.op("dve", lambda h: h.memset(ss[:], 0.0), writes=[b("ss")])
        P.op("dve", lambda h: h.memset(nrm[:], 0.0), writes=[b("nrm")])
        P.op("dve", lambda h: h.memset(xt[1][:], 0.0), writes=[b("xt1")])
        dma("sp", xmid_s[0:128, :], xt[1][:], [b("xt1")], [b("xmid0")])

        def mm(out, lhsT, rhs, start, stop, reads, writes):
            P.op("pe", lambda h: h.matmul(out, lhsT=lhsT, rhs=rhs, start=start, stop=stop), reads=reads, writes=writes)

        def tr(out, in_, ident, reads, writes):
            P.op("pe", lambda h: h.transpose(out=out, in_=in_, identity=ident), reads=reads, writes=writes)

        def act(out, in_, func, reads, writes, **kw):
            P.op("act", lambda h: h.activation(out=out, in_=in_, func=func, **kw), reads=reads, writes=writes)

        def tt(out, in0, in1, op, reads, writes, eng="dve"):
            P.op(eng, lambda h: h.tensor_tensor(out=out, in0=in0, in1=in1, op=op), reads=reads, writes=writes)

        def ts(out, in0, s1, s2, op0, op1, reads, writes, eng="dve"):
            if op1 is None:
                P.op(eng, lambda h: h.tensor_scalar(out=out, in0=in0, scalar1=s1, scalar2=None, op0=op0), reads=reads, writes=writes)
            else:
                P.op(eng, lambda h: h.tensor_scalar(out=out, in0=in0, scalar1=s1, scalar2=s2, op0=op0, op1=op1), reads=reads, writes=writes)

        def stt(out, in0, scalar, in1, op0, op1, reads, writes):
            P.op("dve", lambda h: h.scalar_tensor_tensor(out=out, in0=in0, scalar=scalar, in1=in1, op0=op0, op1=op1),
                 reads=reads, writes=writes)

        def memset(ap, val, writes):
            P.op("dve", lambda h: h.memset(ap, val), writes=writes)

        def recip(out, in_, reads, writes):
            P.op("dve", lambda h: h.reciprocal(out=out, in_=in_), reads=reads, writes=writes)

        def cp(out, in_, reads, writes):
            P.op("dve", lambda h: h.tensor_copy(out=out, in_=in_), reads=reads, writes=writes)

        def dbg_out(name, src_ap, reads, ci):
            if dbg and name in dbg_d and ci == dbg_chunk:
                dma("sp", dbg_d[name][:, :], src_ap, reads, [b("dbgo_" + name)])

        dbg_chunk = NCH
        wo_cnt = [0]
        dg_cnt = [0]
        G4 = lambda ap: ap.rearrange("p (a b) -> p a b", a=4)


        GDN_BUFS = ["masks", "masksu", "gnw", "sq4", "ke", "kd", "LTs", "NTm", "Nn", "Tm0", "Tm1", "Ym0", "Ym1", "Zm", "Zpm",
                    "ident4", "ub", "wT"]
        SSD_BUFS = ["ktok", "vdec", "convb", "dskip", "snw"]
        wo_cnt = [0]
        dg_cnt = [0]
        G4 = lambda ap: ap.rearrange("p (a b) -> p a b", a=4)

        def carve_factory():
            off = [0]

            def carve(nbytes, dt=BF16):
                n = nbytes // 2
                ap = U[:, off[0]:off[0] + n]
                off[0] += n
                assert off[0] <= UN
                if dt == F32:
                    ap = ap.bitcast(F32)
                return ap
            return carve

        for lidx, layer in enumerate(layers):
            gdn = layer == "gdn"
            first_layer = lidx == 0
            final_norm = lidx == len(layers) - 1
            H = 16 if gdn else 32
            DV = 2048 // H
            NHG = H // 4
            FW = 4 * DV
            CO = 0 if gdn else 2048
            ZO = 4096 if gdn else 0
            SO = 6144
            QO, KO, VO = (0, 8, 16) if gdn else (24, 16, 0)
            D_ = LD[layer]
            src_d = x_d if first_layer else xmid_s
            smF = smF_f[0:H]
            onesH = onesH_f[0:H]
            alog, negA, dtb = alog_f[0:H], negA_f[0:H], dtb_f[0:H]
            glrow = glrow_f[:, 0:H]
            smT = smT_f[:, 0:3 * H]
            tokS = tokS_f[:, :, 0:H]
            glbc = glbc_f[:, 0:H]
            gc_s = [g[0:H] for g in gc_s_full]
            gl_s = [g[0:H] for g in gl_s_full]
            carve = carve_factory()
            if lidx > 0:
                prev_b = GDN_BUFS if layers[lidx - 1] == "gdn" else SSD_BUFS
                cur_b = GDN_BUFS if gdn else SSD_BUFS
                P.op("dve", lambda h: h.memset(ss[:, 3:4], 0.0), reads=[b(n) for n in prev_b], writes=[b(n) for n in cur_b] + [b("ss")])
            if gdn:
                masks = carve(3584).rearrange("p (a b) -> p a b", a=14)
                masksu = carve(256)
                gnw = carve(512, F32)
                sq4 = carve(1024)
                ke = G4(carve(1024))
                kd = G4(carve(1024))
                LTs = G4(carve(1024))
                NTm = G4(carve(1024))
                Nn = G4(carve(1024))
                Tm = [G4(carve(1024)) for i in range(2)]
                Ym = [G4(carve(1024)) for i in range(2)]
                Zm = G4(carve(1024))
                Zpm = G4(carve(1024))
                ident4 = G4(carve(1024))
                ub = carve(2048, F32)
                wT = G4(carve(1024))
            else:
                ktok = carve(2048).rearrange("p (a b) -> p a b", a=8)
                vdec = carve(512)
                convb = carve(128, F32)
                dskip = carve(128, F32)
                snw = carve(4096)
            dma("sp", normw[:], D_["normw"][:, :], [], [b("normw")])
            dma("sp", alog[:], D_["a_log"][:, :], [], [b("alog")])
            dma("sp", dtb[:], D_["dt_bias"][:, :], [], [b("dtb")])
            if gdn:
                dma("pool", masks[:].rearrange("p a b -> p (a b)"), masks_d[:, :], [], [b("masks")])
                dma("pool", masksu[:], masksu_d[:, :], [], [b("masksu")])
                dma("sp", gnw[:], D_["gnw"][:, :], [], [b("gnw")])
            else:
                dma("sp", convb[:], D_["convb"][:, :], [], [b("convb")])
                dma("sp", dskip[:], D_["dskip"][:, :], [], [b("dskip")])
                dma("pool", snw[:], D_["snw"][:, :], [], [b("snw")])
            P.op("dve", lambda h: h.memset(carry[:], 0.0), writes=[b("carry")])
            P.op("dve", lambda h: h.memset(S[:], 0.0), writes=[b("S%d" % i) for i in range(8)])
            P.op("dve", lambda h: h.memset(Sb[:], 0.0), writes=[b("Sb%d" % i) for i in range(8)])
            P.op("act", (lambda negA, alog: lambda h: h.activation(out=negA[:], in_=alog[:], func=AF.Exp))(negA, alog),
                 reads=[b("alog")], writes=[b("negA")])
            P.op("dve", (lambda negA: lambda h: h.tensor_scalar(out=negA[:], in0=negA[:], scalar1=-1.0, scalar2=None, op0=ALU.mult))(negA),
                 reads=[b("negA")], writes=[b("negA")])
            if gdn:
                for i in range(4):
                    P.op("dve", (lambda i, ident4: lambda h: h.tensor_copy(out=ident4[:, i, :], in_=identb[:]))(i, ident4),
                         reads=[b("identb")], writes=[b("ident4")])
            for k in range(8):
                for (f0, f1) in [(0, 2048), (2048, 4096), (4096, INW)]:
                    dma("pool", Wb[:, k, f0:f1], D_["w_in"][k * 128:(k + 1) * 128, f0:f1], [], [b("Wb")])
            for kt2 in range(8):
                dma("pool", zs[:].rearrange("p (a c) -> p a c", a=2),
                    D_["w_out"][kt2 * 256:(kt2 + 1) * 256, :].rearrange("(a p) c -> p a c", a=2), [], [b("zs")])
                dma("sp", wout_s[kt2 * 2:(kt2 + 1) * 2].rearrange("a p c -> p a c"),
                    zs[:].rearrange("p (a c) -> p a c", a=2), [b("zs")], [b("wout_s")])
            for c4 in range(8):
                dma("pool", zs[:], D_["diag"][c4], [], [b("zs")])
                dma("sp", diag_s[c4], zs[:], [b("zs")], [b("diag_s")])
            dbg_chunk = NCH
            for ci in range(NCH + 1):
                halo = ci == 0
                xb_ = xt[ci % 2]
                XB = b("xt%d" % (ci % 2))
                dma("sp", xb_[:], src_d[ci * 128:(ci + 1) * 128, :], [] if first_layer else [b("xmid%d" % ci)], [XB])
                memset(ss[:, 0:1], 0.0, [b("ss")])
                act(hid[:], xb_[:], AF.Square, [XB], [b("hid"), b("ss")], accum_out=ss[:, 0:1])
                act(ss[:, 1:2], ss[:, 0:1], AF.Sqrt, [b("ss")], [b("ss")], scale=1.0 / DM, bias=EPS)
                recip(ss[:, 2:3], ss[:, 1:2], [b("ss")], [b("ss")])
                stt(hid[:], xb_[:], ss[:, 2:3], normw[:], ALU.mult, ALU.mult, [XB, b("ss"), b("normw")], [b("hid")])
                for k in range(8):
                    tr(psb[0][:, k * 128:(k + 1) * 128], hid[:, k * 128:(k + 1) * 128], identb[:], [b("hid"), b("identb")], [PSB[0]])
                act(hidT[:].rearrange("p a b -> p (a b)"), psb[0][:, 0:1024], AF.Copy, [PSB[0]], [b("hidT")])
                for c4 in range(8):
                    pa = 1 + (c4 % 2)
                    pc = 3 + (c4 % 2)
                    pbuf = Pbuf[c4 % 2]
                    PB = b("Pbuf%d" % (c4 % 2))
                    for i in range(4):
                        f0 = CO + (c4 * 4 + i) * 128
                        for k in range(8):
                            mm(ps[pa][:, i * 128:(i + 1) * 128], Wb[:, k, f0:f0 + 128], hidT[:, k, :], k == 0, k == 7,
                               [b("Wb"), b("hidT")], [PSB[pa]])
                    cp(pbuf[:, :, 0:3], carry[:, c4 * 4:(c4 + 1) * 4, :], [b("carry")], [PB])
                    act(pbuf[:, :, 3:131], G4(ps[pa][:]), AF.Copy, [PSB[pa]], [PB])
                    cp(carry[:, c4 * 4:(c4 + 1) * 4, :], pbuf[:, :, 128:131], [PB], [b("carry")])
                    if halo:
                        continue
                    for i in range(4):
                        ct = c4 * 4 + i
                        di = dg_cnt[0] % 3
                        dg_cnt[0] += 1
                        dg = diag[di]
                        DG = b("diag%d" % di)
                        dma("sp", dg[:].rearrange("p a b -> p (a b)"), diag_s[c4][:, i * 512:(i + 1) * 512], [b("diag_s")], [DG])
                        for j in range(4):
                            mm(ps[pc][:, i * 128:(i + 1) * 128], dg[:, j, :], pbuf[:, i, j:j + 128], j == 0, j == 3, [DG, PB], [PSB[pc]])
                        if not gdn:
                            act(convT[:, ct, :], ps[pc][:, i * 128:(i + 1) * 128], AF.Silu, [PSB[pc], b("convb")], [b("convT%d" % c4)],
                                bias=convb[:, ct:ct + 1])
                    if gdn:
                        act(convT[:, c4 * 4:(c4 + 1) * 4, :], G4(ps[pc][:]), AF.Silu, [PSB[pc]], [b("convT%d" % c4)])
                if halo:
                    continue
                dbg_out("convT", convT[:].rearrange("p a b -> p (a b)"), [b("convT%d" % i) for i in range(8)], ci)
                for f in range(4):
                    pz = 5 + (f % 2)
                    for k in range(8):
                        mm(ps[pz][:, :], hidT[:, k, :], Wb[:, k, ZO + f * 512:ZO + (f + 1) * 512], k == 0, k == 7,
                           [b("Wb"), b("hidT")], [PSB[pz]])
                    act(zs[:, f * 512:(f + 1) * 512], ps[pz][:, :], AF.Silu, [PSB[pz]], [b("zs")])
                SM = b("smF")
                if gdn:
                    for k in range(8):
                        mm(ps[7][0:16, 0:128], Wb[:, k, SO:SO + 16], hidT[:, k, :], k == 0, k == 7, [b("Wb"), b("hidT")], [PSB[7]])
                    for k in range(8):
                        mm(ps[7][0:16, 128:256], Wb[:, k, SO + 16:SO + 32], hidT[:, k, :], k == 0, k == 7, [b("Wb"), b("hidT")], [PSB[7]])
                    act(smF[:, 0, :], ps[7][0:16, 0:128], AF.Sigmoid, [PSB[7]], [SM])
                    act(smF[:, 1, :], ps[7][0:16, 128:256], AF.Exp, [PSB[7], b("dtb")], [SM], bias=dtb[:, 0:1])
                else:
                    for k in range(8):
                        mm(ps[7][0:32, 0:128], Wb[:, k, SO:SO + 32], hidT[:, k, :], k == 0, k == 7, [b("Wb"), b("hidT")], [PSB[7]])
                    act(smF[:, 1, :], ps[7][0:32, 0:128], AF.Exp, [PSB[7], b("dtb")], [SM], bias=dtb[:, 0:1])
                act(smF[:, 2, :], smF[:, 1, :], AF.Ln, [SM], [SM], bias=1.0)
                if not gdn:
                    cp(smF[:, 0, :], smF[:, 2, :], [SM], [SM])
                ts(smF[:, 3, :], smF[:, 2, :], negA[:, 0:1], None, ALU.mult, None, [SM, b("negA")], [SM])
                P.op("dve", (lambda smF, onesH: lambda h: h.tensor_tensor_scan(
                    out=smF[:, 4, :], data0=onesH[:], data1=smF[:, 3, :], initial=0.0, op0=ALU.mult, op1=ALU.add))(smF, onesH),
                     reads=[SM, b("onesH")], writes=[SM])
                ts(smF[:, 5, :], smF[:, 4, :], -1.0, smF[:, 4, 127:128], ALU.mult, ALU.add, [SM], [SM])
                gcs, GCS = gc_s[ci % 2], b("gc_s%d" % (ci % 2))
                gls, GLS = gl_s[ci % 2], b("gl_s%d" % (ci % 2))
                dma("sp", gcs[:, :], smF[:, 4, :], [SM], [GCS])
                dma("sp", gls[:, :], smF[:, 4, 127:128], [SM], [GLS])
                dma("sp", glrow[0:1, :], gls.rearrange("h o -> o h"), [GLS], [b("glrow")])
                for t_, src in enumerate([0, 4, 5]):
                    tr(ps[7][:, 256 + t_ * H:256 + (t_ + 1) * H], smF[:, src, :], identf[0:H, 0:H], [SM, b("identf")], [PSB[7]])
                cp(smT[:], ps[7][:, 256:256 + 3 * H], [PSB[7]], [b("smT")])
                TS = b("tokS")
                act(tokS[:, 0, :], smT[:, H:2 * H], AF.Exp, [b("smT")], [TS])
                act(tokS[:, 1, :], smT[:, 2 * H:3 * H], AF.Exp, [b("smT")], [TS])
                if gdn:
                    ts(tokS[:, 3, :], smT[:, 0:H], -1.0, None, ALU.mult, None, [b("smT")], [TS])
                else:
                    tt(tokS[:, 2, :], smT[:, 0:H], tokS[:, 1, :], ALU.mult, [b("smT"), TS], [TS])
                mm(ps[7][:, 384:384 + H], onesrow[0:1, 0:128], glrow[0:1, :], True, True, [b("onesrow"), b("glrow")], [PSB[7]])
                act(glbc[:], ps[7][:, 384:384 + H], AF.Exp, [PSB[7]], [b("glbc")])
                dbg_out("smT", smT[:], [b("smT")], ci)
                if gdn:
                    for g4 in range(4):
                        CB = b("convT%d" % g4)
                        cv = convT[:, g4 * 4:(g4 + 1) * 4, :].rearrange("p a b -> p (a b)")
                        tt(sq4[:], cv, cv, ALU.mult, [CB], [b("sq4")])
                        mm(ps[1][:, :], onesb[:], sq4[:], True, True, [b("onesb"), b("sq4")], [PSB[1]])
                        act(tmpf[:], ps[1][:, :], AF.Sqrt, [PSB[1]], [b("tmpf")], bias=EPS)
                        recip(tmpf[:], tmpf[:], [b("tmpf")], [b("tmpf")])
                        stt(cv, cv, (128.0 ** -0.5) if g4 < 2 else 1.0, tmpf[:], ALU.mult, ALU.mult, [CB, b("tmpf")], [CB])
                dbg_out("qkn", convT[:, 0:16, :].rearrange("p a b -> p (a b)"), [b("convT%d" % i) for i in range(4)], ci)
                for half in range(2):
                    pv = 1 + half
                    for i in range(8):
                        ct = VO + half * 8 + i
                        tr(psb[pv][:, i * 128:(i + 1) * 128], convT[:, ct, :], identb[:], [b("convT%d" % (ct // 4)), b("identb")], [PSB[pv]])
                    act(vtok[:, half * 1024:(half + 1) * 1024], psb[pv][:, 0:1024], AF.Copy, [PSB[pv]], [b("vtok")])
                for i in range(8):
                    ct = KO + i
                    tr(psb[3][:, i * 128:(i + 1) * 128], convT[:, ct, :], identb[:], [b("convT%d" % (ct // 4)), b("identb")], [PSB[3]])
                if not gdn:
                    act(ktok[:].rearrange("p a b -> p (a b)"), psb[3][:, 0:1024], AF.Copy, [PSB[3]], [b("ktok")])
                for hg in range(NHG):
                    h0 = hg * 4
                    SB_, SBb = b("S%d" % hg), b("Sb%d" % hg)
                    oh = o_t[0]
                    OB = b("o_t0")
                    gr = gcrow4[hg % 2]
                    GR = b("gcrow4_%d" % (hg % 2))
                    dma("sp", gr[0:1, :], gcs[h0:h0 + 4, :].rearrange("(o h) c -> o (h c)", o=1), [GCS], [GR])
                    mm(ps[4][:, :], onesrow[0:1, 0:128], gr[0:1, :], True, False, [b("onesrow"), GR], [PSB[4]])
                    for hh in range(4):
                        mm(ps[4][:, hh * 128:(hh + 1) * 128], gr[0:1, hh * 128:(hh + 1) * 128], negonesrow[0:1, :], False, False,
                           [b("negonesrow"), GR], [PSB[4]])
                    mm(ps[4][:, :], identb[:], maskneg[:], False, True, [b("identb"), b("maskneg")], [PSB[4]])
                    act(LT[:].rearrange("p a b -> p (a b)"), ps[4][:, :], AF.Exp, [PSB[4]], [b("LT")])
                    if gdn:
                        kin = psb[3][:, hg * 256:(hg + 1) * 256].rearrange("p (g d) -> p g d", g=2).unsqueeze(2).broadcast_to([128, 2, 2, 128])
                        for (dst, DB, row) in [(ke, b("ke"), 0), (kd, b("kd"), 1)]:
                            tt(dst[:].rearrange("p (g r) d -> p g r d", g=2), kin,
                               tokS[:, row, h0:h0 + 4].rearrange("p (g r) -> p g r", g=2).unsqueeze(3).broadcast_to([128, 2, 2, 128]),
                               ALU.mult, [PSB[3], TS], [DB])
                        for qq in range(2):
                            g = hg * 2 + qq
                            mm(ps[5][:, qq * 128:(qq + 1) * 128], convT[:, KO + g, :], convT[:, KO + g, :], True, True,
                               [b("convT%d" % ((KO + g) // 4))], [PSB[5]])
                            mm(ps[5][:, 256 + qq * 128:256 + (qq + 1) * 128], convT[:, KO + g, :], convT[:, QO + g, :], True, True,
                               [b("convT%d" % ((KO + g) // 4)), b("convT%d" % ((QO + g) // 4))], [PSB[5]])
                        tt(LTs[:], LT[:], masksu[:].unsqueeze(1).broadcast_to([128, 4, 128]), ALU.mult, [b("LT"), b("masksu")], [b("LTs")])
                        for hh in range(4):
                            stt(NTm[:, hh, :], ps[5][:, (hh // 2) * 128:(hh // 2 + 1) * 128], tokS[:, 3, h0 + hh:h0 + hh + 1], LTs[:, hh, :],
                                ALU.mult, ALU.mult, [PSB[5], TS, b("LTs")], [b("NTm")])
                        tt(attnT[:].rearrange("p (q r) d -> p q r d", q=2),
                           ps[5][:, 256:512].rearrange("p (q d) -> p q d", q=2).unsqueeze(2).broadcast_to([128, 2, 2, 128]),
                           LT[:].rearrange("p (q r) d -> p q r d", q=2), ALU.mult, [PSB[5], b("LT")], [b("attnT")])
                    else:
                        g = hg
                        mm(ps[5][:, 0:128], convT[:, KO + g, :], convT[:, QO + g, :], True, True,
                           [b("convT%d" % ((KO + g) // 4)), b("convT%d" % ((QO + g) // 4))], [PSB[5]])
                        tt(attnT[:], ps[5][:, 0:128].unsqueeze(1).broadcast_to([128, 4, 128]), LT[:], ALU.mult, [PSB[5], b("LT")], [b("attnT")])
                    if gdn:
                        for hh in range(4):
                            tr(psb[6][:, hh * 128:(hh + 1) * 128], NTm[:, hh, :], identb[:], [b("NTm"), b("identb")], [PSB[6]])
                        act(Nn[:].rearrange("p a b -> p (a b)"), psb[6][:, 0:512], AF.Copy, [PSB[6]], [b("Nn")])
                        cur = 0
                        Tc, Yc = ident4, ident4
                        TCB, YCB = b("ident4"), b("ident4")
                        for li in range(7):
                            Tn_, Yn_ = Tm[cur], Ym[cur]
                            TNB, YNB = b("Tm%d" % cur), b("Ym%d" % cur)
                            last = li == 6
                            Ml = masks[:, li, :].unsqueeze(1).broadcast_to([128, 4, 128])
                            MlT = masks[:, 7 + li, :].unsqueeze(1).broadcast_to([128, 4, 128])
                            if not last:
                                for hh in range(4):
                                    mm(ps[4][:, hh * 128:(hh + 1) * 128], NTm[:, hh, :], Tc[:, hh, :], True, True, [b("NTm"), TCB], [PSB[4]])
                                tt(Zm[:], G4(ps[4][:, :]), Ml, ALU.mult, [PSB[4], b("masks")], [b("Zm")])
                            for hh in range(4):
                                mm(ps[5][:, hh * 128:(hh + 1) * 128], Nn[:, hh, :], Yc[:, hh, :], True, True, [b("Nn"), YCB], [PSB[5]])
                            tt(Zpm[:], G4(ps[5][:, :]), MlT, ALU.mult, [PSB[5], b("masks")], [b("Zpm")])
                            if not last:
                                for hh in range(4):
                                    mm(ps[6][:, hh * 128:(hh + 1) * 128], Yc[:, hh, :], Zm[:, hh, :], True, True, [YCB, b("Zm")], [PSB[6]])
                                tt(Tn_[:], Tc[:], G4(ps[6][:, :]), ALU.add, [PSB[6], TCB], [TNB])
                            for hh in range(4):
                                mm(ps[7][:, hh * 128:(hh + 1) * 128], Tc[:, hh, :], Zpm[:, hh, :], True, True, [TCB, b("Zpm")], [PSB[7]])
                            tt(Yn_[:], Yc[:], G4(ps[7][:, :]), ALU.add, [PSB[7], YCB], [YNB])
                            if not last:
                                Tc, TCB = Tn_, TNB
                            Yc, YCB = Yn_, YNB
                            cur ^= 1
                        for hh in range(4):
                            hd = h0 + hh
                            mm(ps[4][:, hh * 128:(hh + 1) * 128], Yc[:, hh, :], vtok[:, hd * 128:(hd + 1) * 128], True, True,
                               [YCB, b("vtok")], [PSB[4]])
                        tt(G4(ub[:]), G4(ps[4][:, :]), smT[:, h0:h0 + 4].unsqueeze(2).broadcast_to([128, 4, 128]), ALU.mult,
                           [PSB[4], b("smT")], [b("ub")])
                        for hh in range(4):
                            mm(ps[5][:, hh * 128:(hh + 1) * 128], ke[:, hh, :], Yc[:, hh, :], True, True, [b("ke"), YCB], [PSB[5]])
                        act(wT[:].rearrange("p a b -> p (a b)"), ps[5][:, :], AF.Copy, [PSB[5]], [b("wT")])
                        for hh in range(4):
                            hd = h0 + hh
                            mm(ps[6][:, hh * 128:(hh + 1) * 128], wT[:, hh, :], Sb[:, hd * 128:(hd + 1) * 128], True, True, [b("wT"), SBb], [PSB[6]])
                        tt(G4(tmpf[:]), G4(ps[6][:, :]), tokS[:, 3, h0:h0 + 4].unsqueeze(2).broadcast_to([128, 4, 128]), ALU.mult,
                           [PSB[6], TS], [b("tmpf")])
                        tt(vnew[:], tmpf[:], ub[:], ALU.add, [b("tmpf"), b("ub")], [b("vnew")])
                    else:
                        xin = G4(vtok[:, h0 * DV:(h0 + 4) * DV])
                        tt(G4(vnew[:, 0:FW]), xin, smT[:, h0:h0 + 4].unsqueeze(2).broadcast_to([128, 4, DV]),
                           ALU.mult, [b("vtok"), b("smT")], [b("vnew")])
                        tt(G4(vdec[:, 0:FW]), xin, tokS[:, 2, h0:h0 + 4].unsqueeze(2).broadcast_to([128, 4, DV]),
                           ALU.mult, [b("vtok"), TS], [b("vdec")])
                    if gdn:
                        for hh in range(4):
                            hd = h0 + hh
                            g = hd // 2
                            mm(ps[4][:, hh * 128:(hh + 1) * 128], convT[:, QO + g, :], Sb[:, hd * 128:(hd + 1) * 128], True, True,
                               [b("convT%d" % ((QO + g) // 4)), SBb], [PSB[4]])
                    else:
                        mm(ps[4][:, 0:FW], convT[:, QO + hg, :], Sb[:, h0 * DV:(h0 + 4) * DV], True, True,
                           [b("convT%d" % ((QO + hg) // 4)), SBb], [PSB[4]])
                    for hh in range(4):
                        mm(ps[5][:, hh * DV:(hh + 1) * DV], attnT[:, hh, :], vnew[:, hh * DV:(hh + 1) * DV], True, True,
                           [b("attnT"), b("vnew")], [PSB[5]])
                    tt(G4(tmpf[:, 0:FW]), G4(ps[4][:, 0:FW]), tokS[:, 0, h0:h0 + 4].unsqueeze(2).broadcast_to([128, 4, DV]), ALU.mult,
                       [PSB[4], TS], [b("tmpf")])
                    tt(oh[:, 0:FW], tmpf[:, 0:FW], ps[5][:, 0:FW], ALU.add, [b("tmpf"), PSB[5]], [OB])
                    if gdn:
                        for hh in range(4):
                            mm(ps[6][:, hh * 128:(hh + 1) * 128], kd[:, hh, :], vnew[:, hh * 128:(hh + 1) * 128], True, True,
                               [b("kd"), b("vnew")], [PSB[6]])
                    else:
                        mm(ps[6][:, 0:FW], ktok[:, hg, :], vdec[:, 0:FW], True, True, [b("ktok"), b("vdec")], [PSB[6]])
                    tt(G4(tmpf[:, 0:FW]), G4(S[:, hg * FW:(hg + 1) * FW]), glbc[:, h0:h0 + 4].unsqueeze(2).broadcast_to([128, 4, DV]),
                       ALU.mult, [SB_, b("glbc")], [b("tmpf")])
                    tt(S[:, hg * FW:(hg + 1) * FW], tmpf[:, 0:FW], ps[6][:, 0:FW], ALU.add, [b("tmpf"), PSB[6]], [SB_])
                    act(Sb[:, hg * FW:(hg + 1) * FW], S[:, hg * FW:(hg + 1) * FW], AF.Copy, [SB_], [SBb])
                    dbg_out("o%d" % hg, oh[:, 0:FW], [OB], ci)
                    ysl = ya[:, hg * FW:(hg + 1) * FW]
                    zsl = zs[:, hg * FW:(hg + 1) * FW]
                    jk = yT[:].rearrange("p a b -> p (a b)")[:, 0:FW]
                    memset(nrm[:, 0:4], 0.0, [b("nrm")])
                    if gdn:
                        for hh in range(4):
                            act(jk[:, hh * 128:(hh + 1) * 128], oh[:, hh * 128:(hh + 1) * 128], AF.Square, [OB], [b("yT"), b("nrm")],
                                accum_out=nrm[:, hh:hh + 1])
                        act(nrm[:, 4:8], nrm[:, 0:4], AF.Sqrt, [b("nrm")], [b("nrm")], scale=1.0 / 128, bias=EPS)
                        recip(nrm[:, 4:8], nrm[:, 4:8], [b("nrm")], [b("nrm")])
                        for hh in range(4):
                            act(ysl[:, hh * 128:(hh + 1) * 128], oh[:, hh * 128:(hh + 1) * 128], AF.Copy, [OB, b("nrm")], [b("ya")],
                                scale=nrm[:, 4 + hh:5 + hh])
                        tt(G4(ysl), G4(ysl), gnw[:].unsqueeze(1).broadcast_to([128, 4, 128]), ALU.mult, [b("ya"), b("gnw")], [b("ya")])
                        tt(ysl, ysl, zsl, ALU.mult, [b("ya"), b("zs")], [b("ya")])
                    else:
                        tt(G4(tmpf[:, 0:FW]), G4(vtok[:, h0 * DV:(h0 + 4) * DV]), dskip[:, h0:h0 + 4].unsqueeze(2).broadcast_to([128, 4, DV]),
                           ALU.mult, [b("vtok"), b("dskip")], [b("tmpf")])
                        tt(oh[:, 0:FW], oh[:, 0:FW], tmpf[:, 0:FW], ALU.add, [OB, b("tmpf")], [OB])
                        tt(oh[:, 0:FW], oh[:, 0:FW], zsl, ALU.mult, [OB, b("zs")], [OB])
                        act(jk, oh[:, 0:FW], AF.Square, [OB], [b("yT"), b("nrm")], accum_out=nrm[:, 0:1])
                        act(nrm[:, 4:5], nrm[:, 0:1], AF.Sqrt, [b("nrm")], [b("nrm")], scale=1.0 / 256, bias=EPS)
                        recip(nrm[:, 4:5], nrm[:, 4:5], [b("nrm")], [b("nrm")])
                        act(ysl, oh[:, 0:FW], AF.Copy, [OB, b("nrm")], [b("ya")], scale=nrm[:, 4:5])
                        tt(ysl, ysl, snw[:, hg * FW:(hg + 1) * FW], ALU.mult, [b("ya"), b("snw")], [b("ya")])
                dbg_out("y", ya[:], [b("ya")], ci)
                for half in range(2):
                    pv = 1 + half
                    for i in range(8):
                        kt = half * 8 + i
                        tr(psb[pv][:, i * 128:(i + 1) * 128], ya[:, kt * 128:(kt + 1) * 128], identb[:], [b("ya"), b("identb")], [PSB[pv]])
                    act(yT[:, half * 8:(half + 1) * 8, :].rearrange("p a b -> p (a b)"), psb[pv][:, 0:1024], AF.Copy, [PSB[pv]], [b("yT")])
                for kt in range(16):
                    wi = wo_cnt[0] % NWO
                    wo_cnt[0] += 1
                    WB = b("wo%d" % wi)
                    dma("sp", wo[wi][:], wout_s[kt], [b("wout_s")], [WB])
                    for n in range(2):
                        mm(ps[3 + n][:, :], yT[:, kt, :], wo[wi][:, n * 512:(n + 1) * 512], kt == 0, kt == 15, [b("yT"), WB], [PSB[3 + n]])
                for n in range(2):
                    tt(xb_[:, n * 512:(n + 1) * 512], xb_[:, n * 512:(n + 1) * 512], ps[3 + n][:, :], ALU.add, [XB, PSB[3 + n]], [XB])
                if final_norm:
                    memset(ss[:, 0:1], 0.0, [b("ss")])
                    act(hid[:], xb_[:], AF.Square, [XB], [b("hid"), b("ss")], accum_out=ss[:, 0:1])
                    act(ss[:, 1:2], ss[:, 0:1], AF.Sqrt, [b("ss")], [b("ss")], scale=1.0 / DM, bias=EPS)
                    recip(ss[:, 2:3], ss[:, 1:2], [b("ss")], [b("ss")])
                    stt(xb_[:], xb_[:], ss[:, 2:3], fnw[:], ALU.mult, ALU.mult, [XB, b("ss"), b("fnw")], [XB])
                if final_norm:
                    dma("pool", xo_d[(ci - 1) * 128:ci * 128, :], xb_[:], [XB], [b("xo%d" % ci)])
                else:
                    dma("pool", xmid_s[ci * 128:(ci + 1) * 128, :], xb_[:], [XB], [b("xmid%d" % ci)])
        P.emit()
    return nc


def conv_diag(conv_w):
    cw = np.asarray(conv_w, np.float32).reshape(4, 8, 4, 128)
    d = np.zeros((8, 128, 4, 4, 128), np.float32)
    for p in range(128):
        d[:, p, :, :, p] = np.transpose(cw[:, :, :, p], (1, 2, 0))
    return d.reshape(8, 128, 16 * 128)


def bc(v, n=128):
    v = np.asarray(v, np.float32).reshape(1, -1)
    return np.ascontiguousarray(np.broadcast_to(v, (n, v.shape[1])))


def layer_inputs(layer, x_with_halo, s_in, norm_w, w_in, conv_w, w_out, a_log, dt_bias, gdn_norm_w=None,
                 conv_b=None, d_skip=None, ssd_norm_w=None, final_norm_w=None):
    H = 16 if layer == "gdn" else 32
    m = dict(host_consts())
    m.update(x=np.ascontiguousarray(x_with_halo, dtype=np.float32), w_in=np.ascontiguousarray(w_in, dtype=np.float32),
             w_out=np.ascontiguousarray(w_out, dtype=np.float32), normw_bc=bc(norm_w), diag=conv_diag(conv_w),
             s_in=np.ascontiguousarray(s_in, dtype=np.float32),
             a_log=np.asarray(a_log, np.float32).reshape(H, 1).copy(), dt_bias=np.asarray(dt_bias, np.float32).reshape(H, 1).copy())
    if layer == "gdn":
        m["gnw_bc"] = bc(gdn_norm_w)
    else:
        m["conv_b"] = np.ascontiguousarray(np.asarray(conv_b, np.float32).reshape(32, 128).T)
        m["dskip_bc"] = bc(d_skip)
        m["snw_bc"] = bc(ssd_norm_w)
    if final_norm_w is not None:
        m["fnw_bc"] = bc(final_norm_w)
    return m


def fused_inputs(x_with_halo, p):
    m = dict(host_consts())
    m["x"] = np.ascontiguousarray(x_with_halo, dtype=np.float32)
    f32 = lambda a: np.ascontiguousarray(np.asarray(a, np.float32))
    m.update(g_w_in=f32(p["gdn_w_in"][0]), g_w_out=f32(p["gdn_w_out"][0]), g_normw_bc=bc(p["norm_w"][0]),
             g_diag=conv_diag(p["gdn_conv_w"][0]), g_a_log=f32(p["gdn_a_log"][0]).reshape(16, 1),
             g_dt_bias=f32(p["gdn_dt_bias"][0]).reshape(16, 1), g_gnw_bc=bc(p["gdn_norm_w"][0]))
    m.update(s_w_in=f32(p["ssd_w_in"][0]), s_w_out=f32(p["ssd_w_out"][0]), s_normw_bc=bc(p["norm_w"][1]),
             s_diag=conv_diag(p["ssd_conv_w"][0]), s_a_log=f32(p["ssd_a_log"][0]).reshape(32, 1),
             s_dt_bias=f32(p["ssd_dt_bias"][0]).reshape(32, 1),
             s_conv_b=np.ascontiguousarray(f32(p["ssd_conv_b"][0]).reshape(32, 128).T), s_dskip_bc=bc(p["ssd_d"][0]),
             s_snw_bc=bc(p["ssd_norm_w"][0]))
    m["fnw_bc"] = bc(p["final_norm_w"])
    return m


def kernel(**inputs):
    x = np.asarray(inputs["x"], np.float32)
    Bn, T, _ = x.shape
    NCH = T // 128
    nc = build_fused(NCH)
    z128 = np.zeros((128, DM), np.float32)
    maps = [fused_inputs(np.concatenate([z128, x[bi]], axis=0), inputs) for bi in range(Bn)]
    res = run_bass_kernel_spmd(nc, maps, core_ids=list(range(Bn)))
    return np.stack([res.results[bi]["xo"] for bi in range(Bn)], axis=0)
```

```python
import numpy as np
from contextlib import ExitStack
import concourse.bass as bass
import concourse.mybir as mybir
from concourse.bass_utils import run_bass_kernel_spmd

F32 = mybir.dt.float32
BF16 = mybir.dt.bfloat16
AF = mybir.ActivationFunctionType
ALU = mybir.AluOpType

NDMASEM = 24
EPS = 1e-6
C = 128
DM = 1024
INW = 6176


class Buf:
    __slots__ = ("name", "lw", "rd")

    def __init__(self, name):
        self.name = name
        self.lw = None
        self.rd = []


class Prog:
    ENGS = ("pe", "act", "dve", "pool", "sp")

    def __init__(self, nc):
        self.nc = nc
        self.ops = []
        self.ndma = 0

    def op(self, eng, fn, reads=(), writes=(), dma=False):
        idx = len(self.ops)
        deps = set()
        for b in reads:
            if b.lw is not None:
                deps.add(b.lw)
        for b in writes:
            if b.lw is not None:
                deps.add(b.lw)
            deps.update(b.rd)
        for b in reads:
            b.rd.append(idx)
        for b in writes:
            b.lw = idx
            b.rd = []
        d = dict(eng=eng, fn=fn, deps=deps, dma=dma, dmaidx=None)
        if dma:
            d["dmaidx"] = self.ndma
            self.ndma += 1
        self.ops.append(d)
        return idx

    def emit(self):
        nc = self.nc
        ops = self.ops
        needed = [False] * len(ops)
        for i, o in enumerate(ops):
            for d in o["deps"]:
                po = ops[d]
                if po["dma"]:
                    continue
                if po["eng"] == "pe" and o["eng"] == "pe" and not o["dma"]:
                    continue
                needed[d] = True
        with ExitStack() as es:
            esem = {e: es.enter_context(nc.semaphore("s_" + e)) for e in self.ENGS}
            dsem = [es.enter_context(nc.semaphore("d_%d" % i)) for i in range(NDMASEM)]
            cnt = {e: 0 for e in self.ENGS}
            ev = [None] * len(ops)
            dma_by_idx = {}
            for i, o in enumerate(ops):
                if o["dma"]:
                    k = o["dmaidx"]
                    ev[i] = (dsem[k % NDMASEM], 16 * (k // NDMASEM + 1))
                    dma_by_idx[k] = i
                elif needed[i]:
                    cnt[o["eng"]] += 1
                    ev[i] = (esem[o["eng"]], cnt[o["eng"]])
            per_eng = {e: [] for e in self.ENGS}
            for i, o in enumerate(ops):
                per_eng[o["eng"]].append(i)
            block = es.enter_context(nc.Block())

            def make(ename, handle_name):
                lst = per_eng[ename]
                if not lst:
                    return

                def body(h):
                    waited = {}
                    for i in lst:
                        o = ops[i]
                        evs = []
                        for d in sorted(o["deps"]):
                            po = ops[d]
                            if (not po["dma"]) and po["eng"] == "pe" and ename == "pe" and not o["dma"]:
                                continue
                            evs.append(ev[d])
                        if o["dma"] and o["dmaidx"] >= NDMASEM:
                            evs.append(ev[dma_by_idx[o["dmaidx"] - NDMASEM]])
                        for (s, v) in evs:
                            key = id(s)
                            if waited.get(key, 0) >= v:
                                continue
                            waited[key] = v
                            h.wait_ge(s, v)
                        ins = o["fn"](h)
                        if ev[i] is not None:
                            s, v = ev[i]
                            ins.then_inc(s, 16 if o["dma"] else 1)
                    for i in lst:
                        o = ops[i]
                        if o["dma"]:
                            s, v = ev[i]
                            if waited.get(id(s), 0) < v:
                                waited[id(s)] = v
                                h.wait_ge(s, v)
                getattr(block, handle_name)(body)

            make("sp", "sync")
            make("pe", "tensor")
            make("act", "scalar")
            make("dve", "vector")
            make("pool", "gpsimd")


LEVELS = [1, 2, 4, 8, 16, 32, 64]


def host_consts():
    i = np.arange(128)
    ident = np.eye(128, dtype=np.float32)
    masks = np.zeros((128, 14, 128), np.float32)
    for li, l in enumerate(LEVELS):
        blk = i // (2 * l)
        half = (i // l) % 2
        M = (blk[:, None] == blk[None, :]) & (half[:, None] == 1) & (half[None, :] == 0)
        masks[:, li, :] = M
        masks[:, 7 + li, :] = M.T
    maskneg = np.where(i[None, :] >= i[:, None], 0.0, -30000.0).astype(np.float32)
    maskneg4 = np.tile(maskneg, (1, 4))
    masksu = (i[None, :] > i[:, None]).astype(np.float32)
    return dict(c_ident=ident, c_masks=masks.reshape(128, 14 * 128), c_maskneg4=maskneg4, c_masksu=masksu)


def build_par(NCH, layers=("gdn", "ssd"), dbg=None, RG=((0, 1, 2, 3), (4, 5, 6, 7))):
    nc = bass.Bass("TRN2", target_bir_lowering=False)

    def din(name, shape, dt=F32):
        return nc.dram_tensor(name, shape, dt, kind="ExternalInput").ap()

    def dout(name, shape, dt=F32):
        return nc.dram_tensor(name, shape, dt, kind="ExternalOutput").ap()

    x_d = din("x", [(NCH + 1) * 128, DM])
    ident_d = din("c_ident", [128, 128])
    masks_d = din("c_masks", [128, 14 * 128])
    maskneg_d = din("c_maskneg4", [128, 512])
    masksu_d = din("c_masksu", [128, 128])
    LD = {}
    for layer in layers:
        pf = layer[0] + "_"
        Hh = 16 if layer == "gdn" else 32
        d = dict(w_in=din(pf + "w_in", [DM, INW]), w_out=din(pf + "w_out", [2048, DM]), normw=din(pf + "normw_bc", [128, DM]),
                 diag=din(pf + "diag", [8, 128, 16 * 128]), a_log=din(pf + "a_log", [Hh, 1]), dt_bias=din(pf + "dt_bias", [Hh, 1]))
        if layer == "gdn":
            d["gnw"] = din(pf + "gnw_bc", [128, 128])
        else:
            d["convb"] = din(pf + "conv_b", [128, 32])
            d["dskip"] = din(pf + "dskip_bc", [128, 32])
            d["snw"] = din(pf + "snw_bc", [128, 2048])
        LD[layer] = d
    fnw_d = din("fnw_bc", [128, DM])
    mprev_d = din("mprev", [128, 4])
    RGL = [list(g) for g in RG]
    xo_d = dout("xo", [NCH * 128, DM])
    dbg_d = {}
    if dbg:
        for nm, (shp, dt_) in dbg.items():
            dbg_d[nm] = dout("dbg_" + nm, shp, dt_)
    diag_s = nc.dram_tensor("diag_s", [8, 128, 16 * 128], BF16, kind="Internal").ap()
    wout_s = nc.dram_tensor("wout_s", [16, 128, DM], BF16, kind="Internal").ap()
    gc_s_full = [nc.dram_tensor("gc_s%d" % i, [32, 128], F32, kind="Internal").ap() for i in range(2)]
    gl_s_full = [nc.dram_tensor("gl_s%d" % i, [32, 1], F32, kind="Internal").ap() for i in range(2)]
    xmid_s = nc.dram_tensor("xmid_s", [(NCH + 1) * 128, DM], F32, kind="Internal").ap()
    PWG, PWS = 2336, 1040
    prod_s = nc.dram_tensor("prod_s", [NCH, 128, 4 * PWG], BF16, kind="Internal").ap()
    zs_s = nc.dram_tensor("zs_s", [NCH, 128, 2048], BF16, kind="Internal").ap()
    srcS = nc.dram_tensor("srcS", [128, 2048], F32, kind="Internal").ap()
    gatS = [nc.dram_tensor("gatS%d" % i, [512, 2048], F32, kind="Internal").ap() for i in range(3)]
    hal_src = nc.dram_tensor("hal_src", [128, DM], F32, kind="Internal").ap()
    hal_gat = nc.dram_tensor("hal_gat", [512, DM], F32, kind="Internal").ap()

    P = Prog(nc)
    es = ExitStack()
    with es:
        def sb(name, shape, dt=F32):
            return es.enter_context(nc.sbuf_tensor(name, shape, dt))

        Wb = sb("Wb", [128, 8, INW], BF16)
        xt = [sb("xt%d" % i, [128, DM]) for i in range(2)]
        hid = sb("hid", [128, DM], BF16)
        hidT = sb("hidT", [128, 8, 128], BF16)
        normw = sb("normw", [128, DM])
        fnw = sb("fnw", [128, DM])
        Pbuf = [sb("Pbuf%d" % i, [128, 4, 131], BF16) for i in range(2)]
        diag = [sb("diag%d" % i, [128, 4, 128], BF16) for i in range(3)]
        convT = sb("convT", [128, 32, 128], BF16)
        zs = sb("zs", [128, 2048], BF16)
        carry = sb("carry", [128, 32, 3], BF16)
        identf = sb("identf", [128, 128])
        identb = sb("identb", [128, 128], BF16)
        onesb = sb("onesb", [128, 128], BF16)
        maskneg = sb("maskneg", [128, 512], BF16)
        onesrow = sb("onesrow", [1, 128])
        negonesrow = sb("negonesrow", [1, 128])
        onesH_f = sb("onesH", [32, 128])
        alog_f = sb("alog", [32, 1])
        negA_f = sb("negA", [32, 1])
        dtb_f = sb("dtb", [32, 1])
        ss = sb("ss", [128, 4])
        smF_f = sb("smF", [32, 6, 128])
        gcrow4 = [sb("gcrow4_%d" % i, [1, 512]) for i in range(2)]
        glrow_f = sb("glrow", [1, 32])
        smT_f = sb("smT", [128, 96])
        tokS_f = sb("tokS", [128, 4, 32])
        glbc_f = sb("glbc", [128, 32])
        vtok = sb("vtok", [128, 2048], BF16)
        S = sb("S", [128, 2048])
        Sb = sb("Sb", [128, 2048], BF16)
        o_t = [sb("o_t%d" % i, [128, 512]) for i in range(1)]
        ya = sb("ya", [128, 2048], BF16)
        yT = sb("yT", [128, 16, 128], BF16)
        wo = [sb("wo%d" % i, [128, DM], BF16) for i in range(2)]
        NWO = 2
        nrm = sb("nrm", [128, 8])
        LT = sb("LT", [128, 4, 128], BF16)
        tmpf = sb("tmpf", [128, 512])
        vnew = sb("vnew", [128, 512], BF16)
        UN = 21312 // 2
        mprev = sb("mprev_sb", [128, 4])
        U = sb("U", [128, UN], BF16)
        ps = [es.enter_context(nc.psum_tensor("ps%d" % i, [128, 512], F32)) for i in range(8)]
        psb = [p[:].bitcast(BF16) for p in ps]

        B = {}

        def b(n):
            if n not in B:
                B[n] = Buf(n)
            return B[n]

        PSB = [b("ps%d" % i) for i in range(8)]

        def dma(eng, out, in_, reads, writes):
            P.op(eng, lambda h: h.dma_start(out=out, in_=in_), reads=reads, writes=writes, dma=True)


        dma("sp", identf[:], ident_d[:, :], [], [b("identf")])
        dma("pool", identb[:], ident_d[:, :], [], [b("identb")])
        dma("pool", maskneg[:], maskneg_d[:, :], [], [b("maskneg")])
        dma("sp", fnw[:], fnw_d[:, :], [], [b("fnw")])
        dma("sp", mprev[:], mprev_d[:, :], [], [b("mprev")])
        P.op("dve", lambda h: h.memset(onesb[:], 1.0), writes=[b("onesb")])
        P.op("dve", lambda h: h.memset(onesrow[:], 1.0), writes=[b("onesrow")])
        P.op("dve", lambda h: h.memset(negonesrow[:], -1.0), writes=[b("negonesrow")])
        P.op("dve", lambda h: h.memset(onesH_f[:], 1.0), writes=[b("onesH")])
        P.op("dve", lambda h: h.memset(ss[:], 0.0), writes=[b("ss")])
        P.op("dve", lambda h: h.memset(nrm[:], 0.0), writes=[b("nrm")])

        def mm(out, lhsT, rhs, start, stop, reads, writes):
            P.op("pe", lambda h: h.matmul(out, lhsT=lhsT, rhs=rhs, start=start, stop=stop), reads=reads, writes=writes)

        def tr(out, in_, ident, reads, writes):
            P.op("pe", lambda h: h.transpose(out=out, in_=in_, identity=ident), reads=reads, writes=writes)

        def act(out, in_, func, reads, writes, **kw):
            P.op("act", lambda h: h.activation(out=out, in_=in_, func=func, **kw), reads=reads, writes=writes)

        def tt(out, in0, in1, op, reads, writes, eng="dve"):
            P.op(eng, lambda h: h.tensor_tensor(out=out, in0=in0, in1=in1, op=op), reads=reads, writes=writes)

        def ts(out, in0, s1, s2, op0, op1, reads, writes, eng="dve"):
            if op1 is None:
                P.op(eng, lambda h: h.tensor_scalar(out=out, in0=in0, scalar1=s1, scalar2=None, op0=op0), reads=reads, writes=writes)
            else:
                P.op(eng, lambda h: h.tensor_scalar(out=out, in0=in0, scalar1=s1, scalar2=s2, op0=op0, op1=op1), reads=reads, writes=writes)

        def stt(out, in0, scalar, in1, op0, op1, reads, writes):
            P.op("dve", lambda h: h.scalar_tensor_tensor(out=out, in0=in0, scalar=scalar, in1=in1, op0=op0, op1=op1),
                 reads=reads, writes=writes)

        def memset(ap, val, writes):
            P.op("dve", lambda h: h.memset(ap, val), writes=writes)

        def recip(out, in_, reads, writes):
            P.op("dve", lambda h: h.reciprocal(out=out, in_=in_), reads=reads, writes=writes)

        def cp(out, in_, reads, writes):
            P.op("dve", lambda h: h.tensor_copy(out=out, in_=in_), reads=reads, writes=writes)

        def dbg_out(name, src_ap, reads, ci):
            if dbg and name in dbg_d and ci == dbg_chunk:
                dma("sp", dbg_d[name][:, :], src_ap, reads, [b("dbgo_" + name)])

        dbg_chunk = NCH
        wo_cnt = [0]
        dg_cnt = [0]
        G4 = lambda ap: ap.rearrange("p (a b) -> p a b", a=4)


        GDN_BUFS = ["masks", "masksu", "gnw", "sq4", "ke", "kd", "LTs", "NTm", "Nn", "Tm0", "Tm1", "Ym0", "Ym1", "Zm", "Zpm",
                    "ident4", "ub", "wT"]
        SSD_BUFS = ["ktok", "vdec", "convb", "dskip", "snw"]
        wo_cnt = [0]
        dg_cnt = [0]
        G4 = lambda ap: ap.rearrange("p (a b) -> p a b", a=4)

        def carve_factory():
            off = [0]

            def carve(nbytes, dt=BF16):
                n = nbytes // 2
                ap = U[:, off[0]:off[0] + n]
                off[0] += n
                assert off[0] <= UN
                if dt == F32:
                    ap = ap.bitcast(F32)
                return ap
            return carve

        for lidx, layer in enumerate(layers):
            gdn = layer == "gdn"
            first_layer = lidx == 0
            final_norm = lidx == len(layers) - 1
            H = 16 if gdn else 32
            DV = 2048 // H
            NHG = H // 4
            FW = 4 * DV
            CO = 0 if gdn else 2048
            ZO = 4096 if gdn else 0
            SO = 6144
            QO, KO, VO = (0, 8, 16) if gdn else (24, 16, 0)
            D_ = LD[layer]
            src_d = x_d if first_layer else xmid_s
            smF = smF_f[0:H]
            onesH = onesH_f[0:H]
            alog, negA, dtb = alog_f[0:H], negA_f[0:H], dtb_f[0:H]
            glrow = glrow_f[:, 0:H]
            smT = smT_f[:, 0:3 * H]
            tokS = tokS_f[:, :, 0:H]
            glbc = glbc_f[:, 0:H]
            gc_s = [g[0:H] for g in gc_s_full]
            gl_s = [g[0:H] for g in gl_s_full]
            carve = carve_factory()
            PA_BUFS = ["masks", "masksu", "sq4", "ke", "LTs", "NTm", "Nn", "Tm0", "Tm1", "Ym0", "Ym1", "Zm", "Zpm", "ident4", "stg",
                       "ktok", "convb", "dskip", "LTa"]
            RB_BUFS = ["rb0", "rb1", "rb2"]
            if lidx > 0:
                P.op("dve", lambda h: h.memset(ss[:, 3:4], 0.0), reads=[b(n) for n in RB_BUFS + ["gnw", "snw"]],
                     writes=[b(n) for n in PA_BUFS + ["gnw", "snw"]] + [b("ss")])
            PW = PWG if gdn else PWS
            SC = 1568 if gdn else 400
            if gdn:
                gnw = carve(512, F32)
            else:
                snw = carve(4096)
            carve_rb = carve_factory()
            carve_rb(512 if gdn else 4096)
            rbuf = [carve_rb(4672) for i in range(3)]
            if gdn:
                masks = carve(3584).rearrange("p (a b) -> p a b", a=14)
                masksu = carve(256)
                sq4 = carve(1024)
                ke = G4(carve(1024))
                LTs = G4(carve(1024))
                NTm = G4(carve(1024))
                Nn = G4(carve(1024))
                Tm = [G4(carve(1024)) for i in range(2)]
                Ym = [G4(carve(1024)) for i in range(2)]
                Zm = G4(carve(1024))
                Zpm = G4(carve(1024))
                ident4 = G4(carve(1024))
                stg = carve(4672)
            else:
                ktok = carve(2048).rearrange("p (a b) -> p a b", a=8)
                convb = carve(128, F32)
                dskip = carve(128, F32)
                stg = carve(2080)
                LTa = G4(carve(1024))
            if gdn:
                LTa = None

            def views(blk):
                w_ = blk.shape[1]
                if gdn:
                    return dict(wT=G4(blk[:, 0:512]), kd=G4(blk[:, 512:1024]), ub=blk[:, 1024:1536],
                                sm=blk[:, 1536:1560].bitcast(F32), attnT=G4(blk[:, 1568:2080]) if w_ >= 2080 else None,
                                qT=blk[:, 2080:2336].rearrange("p (a b) -> p a b", a=2) if w_ >= 2336 else None)
                return dict(ktok=blk[:, 0:128], vdec=blk[:, 128:384], sm=blk[:, 384:400].bitcast(F32),
                            o0=blk[:, 400:912].bitcast(F32) if w_ >= 912 else None, CT=blk[:, 912:1040] if w_ >= 1040 else None)

            def s_chain(v, hg, VB, full, ps_a=4, ps_b=5, ps_c=6):
                h0 = hg * 4
                SB_, SBb = b("S%d" % hg), b("Sb%d" % hg)
                OB = b("o_t0")
                oh = o_t[0]
                sm = v["sm"]
                if gdn:
                    for hh in range(4):
                        hd = h0 + hh
                        mm(ps[ps_c][:, hh * 128:(hh + 1) * 128], v["wT"][:, hh, :], Sb[:, hd * 128:(hd + 1) * 128], True, True, VB + [SBb], [PSB[ps_c]])
                    tt(G4(tmpf[:]), G4(ps[ps_c][:, :]), sm[:, 0:4].unsqueeze(2).broadcast_to([128, 4, 128]), ALU.mult,
                       [PSB[ps_c]] + VB, [b("tmpf")])
                    tt(vnew[:], tmpf[:], v["ub"], ALU.add, [b("tmpf")] + VB, [b("vnew")])
                    if full:
                        for hh in range(4):
                            hd = h0 + hh
                            mm(ps[ps_a][:, hh * 128:(hh + 1) * 128], v["qT"][:, hh // 2, :], Sb[:, hd * 128:(hd + 1) * 128], True, True,
                               VB + [SBb], [PSB[ps_a]])
                        for hh in range(4):
                            mm(ps[ps_b][:, hh * 128:(hh + 1) * 128], v["attnT"][:, hh, :], vnew[:, hh * 128:(hh + 1) * 128], True, True,
                               VB + [b("vnew")], [PSB[ps_b]])
                        tt(G4(tmpf[:]), G4(ps[ps_a][:, :]), sm[:, 8:12].unsqueeze(2).broadcast_to([128, 4, 128]), ALU.mult,
                           [PSB[ps_a]] + VB, [b("tmpf")])
                        tt(oh[:, 0:FW], tmpf[:, 0:FW], ps[ps_b][:, 0:FW], ALU.add, [b("tmpf"), PSB[ps_b]], [OB])
                    for hh in range(4):
                        mm(ps[ps_c][:, hh * 128:(hh + 1) * 128], v["kd"][:, hh, :], vnew[:, hh * 128:(hh + 1) * 128], True, True,
                           VB + [b("vnew")], [PSB[ps_c]])
                    gl = sm[:, 4:8]
                else:
                    if full:
                        mm(ps[ps_a][:, 0:FW], v["CT"], Sb[:, h0 * DV:(h0 + 4) * DV], True, True, VB + [SBb], [PSB[ps_a]])
                        tt(G4(tmpf[:, 0:FW]), G4(ps[ps_a][:, 0:FW]), sm[:, 4:8].unsqueeze(2).broadcast_to([128, 4, DV]), ALU.mult,
                           [PSB[ps_a]] + VB, [b("tmpf")])
                        tt(oh[:, 0:FW], tmpf[:, 0:FW], v["o0"], ALU.add, [b("tmpf")] + VB, [OB])
                    mm(ps[ps_c][:, 0:FW], v["ktok"], v["vdec"], True, True, VB, [PSB[ps_c]])
                    gl = sm[:, 0:4]
                tt(G4(tmpf[:, 0:FW]), G4(S[:, hg * FW:(hg + 1) * FW]), gl.unsqueeze(2).broadcast_to([128, 4, DV]),
                   ALU.mult, [SB_] + VB, [b("tmpf")])
                tt(S[:, hg * FW:(hg + 1) * FW], tmpf[:, 0:FW], ps[ps_c][:, 0:FW], ALU.add, [b("tmpf"), PSB[ps_c]], [SB_])
                act(Sb[:, hg * FW:(hg + 1) * FW], S[:, hg * FW:(hg + 1) * FW], AF.Copy, [SB_], [SBb])

            def blk_d(l, hg, lo, hi):
                return prod_s[l][:, hg * PW + lo:hg * PW + hi]

            dma("sp", normw[:], D_["normw"][:, :], [], [b("normw")])
            dma("sp", alog[:], D_["a_log"][:, :], [], [b("alog")])
            dma("sp", dtb[:], D_["dt_bias"][:, :], [], [b("dtb")])
            if gdn:
                dma("pool", masks[:].rearrange("p a b -> p (a b)"), masks_d[:, :], [], [b("masks")])
                dma("pool", masksu[:], masksu_d[:, :], [], [b("masksu")])
                dma("sp", gnw[:], D_["gnw"][:, :], [], [b("gnw")])
            else:
                dma("sp", convb[:], D_["convb"][:, :], [], [b("convb")])
                dma("sp", dskip[:], D_["dskip"][:, :], [], [b("dskip")])
                dma("pool", snw[:], D_["snw"][:, :], [], [b("snw")])
            P.op("dve", lambda h: h.memset(carry[:], 0.0), writes=[b("carry")])
            P.op("dve", lambda h: h.memset(S[:], 0.0), writes=[b("S%d" % i) for i in range(8)])
            P.op("dve", lambda h: h.memset(Sb[:], 0.0), writes=[b("Sb%d" % i) for i in range(8)])
            P.op("act", (lambda negA, alog: lambda h: h.activation(out=negA[:], in_=alog[:], func=AF.Exp))(negA, alog),
                 reads=[b("alog")], writes=[b("negA")])
            P.op("dve", (lambda negA: lambda h: h.tensor_scalar(out=negA[:], in0=negA[:], scalar1=-1.0, scalar2=None, op0=ALU.mult))(negA),
                 reads=[b("negA")], writes=[b("negA")])
            if gdn:
                for i in range(4):
                    P.op("dve", (lambda i, ident4: lambda h: h.tensor_copy(out=ident4[:, i, :], in_=identb[:]))(i, ident4),
                         reads=[b("identb")], writes=[b("ident4")])
            for k in range(8):
                for (f0, f1) in [(0, 2048), (2048, 4096), (4096, INW)]:
                    dma("pool", Wb[:, k, f0:f1], D_["w_in"][k * 128:(k + 1) * 128, f0:f1], [], [b("Wb")])
            for kt2 in range(8):
                dma("pool", zs[:].rearrange("p (a c) -> p a c", a=2),
                    D_["w_out"][kt2 * 256:(kt2 + 1) * 256, :].rearrange("(a p) c -> p a c", a=2), [], [b("zs")])
                dma("sp", wout_s[kt2 * 2:(kt2 + 1) * 2].rearrange("a p c -> p a c"),
                    zs[:].rearrange("p (a c) -> p a c", a=2), [b("zs")], [b("wout_s")])
            for c4 in range(8):
                dma("pool", zs[:], D_["diag"][c4], [], [b("zs")])
                dma("sp", diag_s[c4], zs[:], [b("zs")], [b("diag_s")])
            dbg_chunk = NCH
            for ci in range(NCH + 1):
                halo = ci == 0
                xb_ = xt[ci % 2]
                XB = b("xt%d" % (ci % 2))
                dma("sp", xb_[:], src_d[ci * 128:(ci + 1) * 128, :], [] if first_layer else [b("xmid%d" % ci)], [XB])
                memset(ss[:, 0:1], 0.0, [b("ss")])
                act(hid[:], xb_[:], AF.Square, [XB], [b("hid"), b("ss")], accum_out=ss[:, 0:1])
                act(ss[:, 1:2], ss[:, 0:1], AF.Sqrt, [b("ss")], [b("ss")], scale=1.0 / DM, bias=EPS)
                recip(ss[:, 2:3], ss[:, 1:2], [b("ss")], [b("ss")])
                stt(hid[:], xb_[:], ss[:, 2:3], normw[:], ALU.mult, ALU.mult, [XB, b("ss"), b("normw")], [b("hid")])
                for k in range(8):
                    tr(psb[0][:, k * 128:(k + 1) * 128], hid[:, k * 128:(k + 1) * 128], identb[:], [b("hid"), b("identb")], [PSB[0]])
                act(hidT[:].rearrange("p a b -> p (a b)"), psb[0][:, 0:1024], AF.Copy, [PSB[0]], [b("hidT")])
                for c4 in range(8):
                    pa = 1 + (c4 % 2)
                    pc = 3 + (c4 % 2)
                    pbuf = Pbuf[c4 % 2]
                    PB = b("Pbuf%d" % (c4 % 2))
                    for i in range(4):
                        f0 = CO + (c4 * 4 + i) * 128
                        for k in range(8):
                            mm(ps[pa][:, i * 128:(i + 1) * 128], Wb[:, k, f0:f0 + 128], hidT[:, k, :], k == 0, k == 7,
                               [b("Wb"), b("hidT")], [PSB[pa]])
                    cp(pbuf[:, :, 0:3], carry[:, c4 * 4:(c4 + 1) * 4, :], [b("carry")], [PB])
                    act(pbuf[:, :, 3:131], G4(ps[pa][:]), AF.Copy, [PSB[pa]], [PB])
                    cp(carry[:, c4 * 4:(c4 + 1) * 4, :], pbuf[:, :, 128:131], [PB], [b("carry")])
                    if halo:
                        continue
                    for i in range(4):
                        ct = c4 * 4 + i
                        di = dg_cnt[0] % 3
                        dg_cnt[0] += 1
                        dg = diag[di]
                        DG = b("diag%d" % di)
                        dma("sp", dg[:].rearrange("p a b -> p (a b)"), diag_s[c4][:, i * 512:(i + 1) * 512], [b("diag_s")], [DG])
                        for j in range(4):
                            mm(ps[pc][:, i * 128:(i + 1) * 128], dg[:, j, :], pbuf[:, i, j:j + 128], j == 0, j == 3, [DG, PB], [PSB[pc]])
                        if not gdn:
                            act(convT[:, ct, :], ps[pc][:, i * 128:(i + 1) * 128], AF.Silu, [PSB[pc], b("convb")], [b("convT%d" % c4)],
                                bias=convb[:, ct:ct + 1])
                    if gdn:
                        act(convT[:, c4 * 4:(c4 + 1) * 4, :], G4(ps[pc][:]), AF.Silu, [PSB[pc]], [b("convT%d" % c4)])
                if halo:
                    continue
                dbg_out("convT", convT[:].rearrange("p a b -> p (a b)"), [b("convT%d" % i) for i in range(8)], ci)
                for f in range(4):
                    pz = 5 + (f % 2)
                    for k in range(8):
                        mm(ps[pz][:, :], hidT[:, k, :], Wb[:, k, ZO + f * 512:ZO + (f + 1) * 512], k == 0, k == 7,
                           [b("Wb"), b("hidT")], [PSB[pz]])
                    act(zs[:, f * 512:(f + 1) * 512], ps[pz][:, :], AF.Silu, [PSB[pz]], [b("zs")])
                SM = b("smF")
                if gdn:
                    for k in range(8):
                        mm(ps[7][0:16, 0:128], Wb[:, k, SO:SO + 16], hidT[:, k, :], k == 0, k == 7, [b("Wb"), b("hidT")], [PSB[7]])
                    for k in range(8):
                        mm(ps[7][0:16, 128:256], Wb[:, k, SO + 16:SO + 32], hidT[:, k, :], k == 0, k == 7, [b("Wb"), b("hidT")], [PSB[7]])
                    act(smF[:, 0, :], ps[7][0:16, 0:128], AF.Sigmoid, [PSB[7]], [SM])
                    act(smF[:, 1, :], ps[7][0:16, 128:256], AF.Exp, [PSB[7], b("dtb")], [SM], bias=dtb[:, 0:1])
                else:
                    for k in range(8):
                        mm(ps[7][0:32, 0:128], Wb[:, k, SO:SO + 32], hidT[:, k, :], k == 0, k == 7, [b("Wb"), b("hidT")], [PSB[7]])
                    act(smF[:, 1, :], ps[7][0:32, 0:128], AF.Exp, [PSB[7], b("dtb")], [SM], bias=dtb[:, 0:1])
                act(smF[:, 2, :], smF[:, 1, :], AF.Ln, [SM], [SM], bias=1.0)
                if not gdn:
                    cp(smF[:, 0, :], smF[:, 2, :], [SM], [SM])
                ts(smF[:, 3, :], smF[:, 2, :], negA[:, 0:1], None, ALU.mult, None, [SM, b("negA")], [SM])
                P.op("dve", (lambda smF, onesH: lambda h: h.tensor_tensor_scan(
                    out=smF[:, 4, :], data0=onesH[:], data1=smF[:, 3, :], initial=0.0, op0=ALU.mult, op1=ALU.add))(smF, onesH),
                     reads=[SM, b("onesH")], writes=[SM])
                ts(smF[:, 5, :], smF[:, 4, :], -1.0, smF[:, 4, 127:128], ALU.mult, ALU.add, [SM], [SM])
                gcs, GCS = gc_s[ci % 2], b("gc_s%d" % (ci % 2))
                gls, GLS = gl_s[ci % 2], b("gl_s%d" % (ci % 2))
                dma("sp", gcs[:, :], smF[:, 4, :], [SM], [GCS])
                dma("sp", gls[:, :], smF[:, 4, 127:128], [SM], [GLS])
                dma("sp", glrow[0:1, :], gls.rearrange("h o -> o h"), [GLS], [b("glrow")])
                for t_, src in enumerate([0, 4, 5]):
                    tr(ps[7][:, 256 + t_ * H:256 + (t_ + 1) * H], smF[:, src, :], identf[0:H, 0:H], [SM, b("identf")], [PSB[7]])
                cp(smT[:], ps[7][:, 256:256 + 3 * H], [PSB[7]], [b("smT")])
                TS = b("tokS")
                act(tokS[:, 0, :], smT[:, H:2 * H], AF.Exp, [b("smT")], [TS])
                act(tokS[:, 1, :], smT[:, 2 * H:3 * H], AF.Exp, [b("smT")], [TS])
                if gdn:
                    ts(tokS[:, 3, :], smT[:, 0:H], -1.0, None, ALU.mult, None, [b("smT")], [TS])
                else:
                    tt(tokS[:, 2, :], smT[:, 0:H], tokS[:, 1, :], ALU.mult, [b("smT"), TS], [TS])
                mm(ps[7][:, 384:384 + H], onesrow[0:1, 0:128], glrow[0:1, :], True, True, [b("onesrow"), b("glrow")], [PSB[7]])
                act(glbc[:], ps[7][:, 384:384 + H], AF.Exp, [PSB[7]], [b("glbc")])
                dbg_out("smT", smT[:], [b("smT")], ci)
                if gdn:
                    for g4 in range(4):
                        CB = b("convT%d" % g4)
                        cv = convT[:, g4 * 4:(g4 + 1) * 4, :].rearrange("p a b -> p (a b)")
                        tt(sq4[:], cv, cv, ALU.mult, [CB], [b("sq4")])
                        mm(ps[1][:, :], onesb[:], sq4[:], True, True, [b("onesb"), b("sq4")], [PSB[1]])
                        act(tmpf[:], ps[1][:, :], AF.Sqrt, [PSB[1]], [b("tmpf")], bias=EPS)
                        recip(tmpf[:], tmpf[:], [b("tmpf")], [b("tmpf")])
                        stt(cv, cv, (128.0 ** -0.5) if g4 < 2 else 1.0, tmpf[:], ALU.mult, ALU.mult, [CB, b("tmpf")], [CB])
                dbg_out("qkn", convT[:, 0:16, :].rearrange("p a b -> p (a b)"), [b("convT%d" % i) for i in range(4)], ci)
                for half in range(2):
                    pv = 1 + half
                    for i in range(8):
                        ct = VO + half * 8 + i
                        tr(psb[pv][:, i * 128:(i + 1) * 128], convT[:, ct, :], identb[:], [b("convT%d" % (ct // 4)), b("identb")], [PSB[pv]])
                    act(vtok[:, half * 1024:(half + 1) * 1024], psb[pv][:, 0:1024], AF.Copy, [PSB[pv]], [b("vtok")])
                for i in range(8):
                    ct = KO + i
                    tr(psb[3][:, i * 128:(i + 1) * 128], convT[:, ct, :], identb[:], [b("convT%d" % (ct // 4)), b("identb")], [PSB[3]])
                if not gdn:
                    act(ktok[:].rearrange("p a b -> p (a b)"), psb[3][:, 0:1024], AF.Copy, [PSB[3]], [b("ktok")])
                for hg in range(NHG):
                    h0 = hg * 4
                    SV = views(stg)
                    attnT = SV["attnT"] if gdn else LTa
                    gr = gcrow4[hg % 2]
                    GR = b("gcrow4_%d" % (hg % 2))
                    dma("sp", gr[0:1, :], gcs[h0:h0 + 4, :].rearrange("(o h) c -> o (h c)", o=1), [GCS], [GR])
                    mm(ps[4][:, :], onesrow[0:1, 0:128], gr[0:1, :], True, False, [b("onesrow"), GR], [PSB[4]])
                    for hh in range(4):
                        mm(ps[4][:, hh * 128:(hh + 1) * 128], gr[0:1, hh * 128:(hh + 1) * 128], negonesrow[0:1, :], False, False,
                           [b("negonesrow"), GR], [PSB[4]])
                    mm(ps[4][:, :], identb[:], maskneg[:], False, True, [b("identb"), b("maskneg")], [PSB[4]])
                    act(LT[:].rearrange("p a b -> p (a b)"), ps[4][:, :], AF.Exp, [PSB[4]], [b("LT")])
                    if gdn:
                        kin = psb[3][:, hg * 256:(hg + 1) * 256].rearrange("p (g d) -> p g d", g=2).unsqueeze(2).broadcast_to([128, 2, 2, 128])
                        for (dst, DB, row) in [(ke, b("ke"), 0), (SV["kd"], b("stg"), 1)]:
                            tt(dst[:].rearrange("p (g r) d -> p g r d", g=2), kin,
                               tokS[:, row, h0:h0 + 4].rearrange("p (g r) -> p g r", g=2).unsqueeze(3).broadcast_to([128, 2, 2, 128]),
                               ALU.mult, [PSB[3], TS], [DB])
                        for qq in range(2):
                            g = hg * 2 + qq
                            mm(ps[5][:, qq * 128:(qq + 1) * 128], convT[:, KO + g, :], convT[:, KO + g, :], True, True,
                               [b("convT%d" % ((KO + g) // 4))], [PSB[5]])
                            mm(ps[5][:, 256 + qq * 128:256 + (qq + 1) * 128], convT[:, KO + g, :], convT[:, QO + g, :], True, True,
                               [b("convT%d" % ((KO + g) // 4)), b("convT%d" % ((QO + g) // 4))], [PSB[5]])
                        tt(LTs[:], LT[:], masksu[:].unsqueeze(1).broadcast_to([128, 4, 128]), ALU.mult, [b("LT"), b("masksu")], [b("LTs")])
                        for hh in range(4):
                            stt(NTm[:, hh, :], ps[5][:, (hh // 2) * 128:(hh // 2 + 1) * 128], tokS[:, 3, h0 + hh:h0 + hh + 1], LTs[:, hh, :],
                                ALU.mult, ALU.mult, [PSB[5], TS, b("LTs")], [b("NTm")])
                        tt(attnT[:].rearrange("p (q r) d -> p q r d", q=2),
                           ps[5][:, 256:512].rearrange("p (q d) -> p q d", q=2).unsqueeze(2).broadcast_to([128, 2, 2, 128]),
                           LT[:].rearrange("p (q r) d -> p q r d", q=2), ALU.mult, [PSB[5], b("LT")], [b("stg") if gdn else b("LTa")])
                    else:
                        g = hg
                        mm(ps[5][:, 0:128], convT[:, KO + g, :], convT[:, QO + g, :], True, True,
                           [b("convT%d" % ((KO + g) // 4)), b("convT%d" % ((QO + g) // 4))], [PSB[5]])
                        tt(attnT[:], ps[5][:, 0:128].unsqueeze(1).broadcast_to([128, 4, 128]), LT[:], ALU.mult, [PSB[5], b("LT")], [b("stg") if gdn else b("LTa")])
                    if gdn:
                        for hh in range(4):
                            tr(psb[6][:, hh * 128:(hh + 1) * 128], NTm[:, hh, :], identb[:], [b("NTm"), b("identb")], [PSB[6]])
                        act(Nn[:].rearrange("p a b -> p (a b)"), psb[6][:, 0:512], AF.Copy, [PSB[6]], [b("Nn")])
                        cur = 0
                        Tc, Yc = ident4, ident4
                        TCB, YCB = b("ident4"), b("ident4")
                        for li in range(7):
                            Tn_, Yn_ = Tm[cur], Ym[cur]
                            TNB, YNB = b("Tm%d" % cur), b("Ym%d" % cur)
                            last = li == 6
                            Ml = masks[:, li, :].unsqueeze(1).broadcast_to([128, 4, 128])
                            MlT = masks[:, 7 + li, :].unsqueeze(1).broadcast_to([128, 4, 128])
                            if not last:
                                for hh in range(4):
                                    mm(ps[4][:, hh * 128:(hh + 1) * 128], NTm[:, hh, :], Tc[:, hh, :], True, True, [b("NTm"), TCB], [PSB[4]])
                                tt(Zm[:], G4(ps[4][:, :]), Ml, ALU.mult, [PSB[4], b("masks")], [b("Zm")])
                            for hh in range(4):
                                mm(ps[5][:, hh * 128:(hh + 1) * 128], Nn[:, hh, :], Yc[:, hh, :], True, True, [b("Nn"), YCB], [PSB[5]])
                            tt(Zpm[:], G4(ps[5][:, :]), MlT, ALU.mult, [PSB[5], b("masks")], [b("Zpm")])
                            if not last:
                                for hh in range(4):
                                    mm(ps[6][:, hh * 128:(hh + 1) * 128], Yc[:, hh, :], Zm[:, hh, :], True, True, [YCB, b("Zm")], [PSB[6]])
                                tt(Tn_[:], Tc[:], G4(ps[6][:, :]), ALU.add, [PSB[6], TCB], [TNB])
                            for hh in range(4):
                                mm(ps[7][:, hh * 128:(hh + 1) * 128], Tc[:, hh, :], Zpm[:, hh, :], True, True, [TCB, b("Zpm")], [PSB[7]])
                            tt(Yn_[:], Yc[:], G4(ps[7][:, :]), ALU.add, [PSB[7], YCB], [YNB])
                            if not last:
                                Tc, TCB = Tn_, TNB
                            Yc, YCB = Yn_, YNB
                            cur ^= 1
                        for hh in range(4):
                            hd = h0 + hh
                            mm(ps[4][:, hh * 128:(hh + 1) * 128], Yc[:, hh, :], vtok[:, hd * 128:(hd + 1) * 128], True, True,
                               [YCB, b("vtok")], [PSB[4]])
                        tt(G4(SV["ub"]), G4(ps[4][:, :]), smT[:, h0:h0 + 4].unsqueeze(2).broadcast_to([128, 4, 128]), ALU.mult,
                           [PSB[4], b("smT")], [b("stg")])
                        for hh in range(4):
                            mm(ps[5][:, hh * 128:(hh + 1) * 128], ke[:, hh, :], Yc[:, hh, :], True, True, [b("ke"), YCB], [PSB[5]])
                        act(SV["wT"].rearrange("p a b -> p (a b)"), ps[5][:, :], AF.Copy, [PSB[5]], [b("stg")])
                        cp(SV["sm"][:, 0:4], tokS[:, 3, h0:h0 + 4], [TS], [b("stg")])
                        cp(SV["sm"][:, 4:8], glbc[:, h0:h0 + 4], [b("glbc")], [b("stg")])
                        cp(SV["sm"][:, 8:12], tokS[:, 0, h0:h0 + 4], [TS], [b("stg")])
                        dma("sp", blk_d(ci - 1, hg, 0, 2080), stg[:, 0:2080], [b("stg")], [b("prod%d_%d" % (ci - 1, hg))])
                        dma("sp", blk_d(ci - 1, hg, 2080, 2336).rearrange("p (a b) -> p a b", a=2), convT[:, QO + 2 * hg:QO + 2 * hg + 2, :],
                            [b("convT%d" % ((QO + 2 * hg) // 4))], [b("prodq%d_%d" % (ci - 1, hg))])
                    else:
                        xin = G4(vtok[:, h0 * DV:(h0 + 4) * DV])
                        tt(G4(vnew[:, 0:FW]), xin, smT[:, h0:h0 + 4].unsqueeze(2).broadcast_to([128, 4, DV]),
                           ALU.mult, [b("vtok"), b("smT")], [b("vnew")])
                        tt(G4(SV["vdec"]), xin, tokS[:, 2, h0:h0 + 4].unsqueeze(2).broadcast_to([128, 4, DV]),
                           ALU.mult, [b("vtok"), TS], [b("stg")])
                        for hh in range(4):
                            mm(ps[5][:, hh * DV:(hh + 1) * DV], attnT[:, hh, :], vnew[:, hh * DV:(hh + 1) * DV], True, True,
                               [b("LTa"), b("vnew")], [PSB[5]])
                        tt(G4(tmpf[:, 0:FW]), xin, dskip[:, h0:h0 + 4].unsqueeze(2).broadcast_to([128, 4, DV]),
                           ALU.mult, [b("vtok"), b("dskip")], [b("tmpf")])
                        tt(SV["o0"], tmpf[:, 0:FW], ps[5][:, 0:FW], ALU.add, [b("tmpf"), PSB[5]], [b("stg")])
                        cp(SV["sm"][:, 0:4], glbc[:, h0:h0 + 4], [b("glbc")], [b("stg")])
                        cp(SV["sm"][:, 4:8], tokS[:, 0, h0:h0 + 4], [TS], [b("stg")])
                        dma("sp", blk_d(ci - 1, hg, 128, 912), stg[:, 128:912], [b("stg")], [b("prod%d_%d" % (ci - 1, hg))])
                        dma("sp", blk_d(ci - 1, hg, 0, 128), ktok[:, hg, :], [b("ktok")], [b("prodk%d_%d" % (ci - 1, hg))])
                        dma("sp", blk_d(ci - 1, hg, 912, 1040), convT[:, QO + hg, :], [b("convT%d" % ((QO + hg) // 4))],
                            [b("prodq%d_%d" % (ci - 1, hg))])
                    if gdn:
                        s_chain(SV, hg, [b("stg")], False)
                    else:
                        s_chain(dict(ktok=ktok[:, hg, :], vdec=SV["vdec"], sm=SV["sm"]), hg, [b("stg"), b("ktok")], False)
                dma("sp", zs_s[ci - 1], zs[:], [b("zs")], [b("zs_s%d" % (ci - 1))])

            P.op("dve", lambda h: h.memset(ss[:, 3:4], 0.0), reads=[b(n) for n in PA_BUFS], writes=[b(n) for n in RB_BUFS] + [b("ss")])
            SALL = [b("S%d" % i) for i in range(8)]
            SBALL = [b("Sb%d" % i) for i in range(8)]
            rb_cnt = [0]

            def load_blk(l, hg, width):
                i = rb_cnt[0] % 3
                rb_cnt[0] += 1
                RB = b("rb%d" % i)
                deps = [b("prod%d_%d" % (l, hg))]
                if not gdn:
                    deps.append(b("prodk%d_%d" % (l, hg)))
                if width > SC:
                    deps.append(b("prodq%d_%d" % (l, hg)))
                dma("sp", rbuf[i][:, 0:width], blk_d(l, hg, 0, width), deps, [RB])
                return rbuf[i][:, 0:width], RB

            for rnd in range(3):
                dma("sp", srcS[:, :], S[:], SALL, [b("srcS")])
                P.op("pool", (lambda rnd: lambda h: h.collective_compute("AllGather", ALU.bypass, replica_groups=RGL,
                                                                        ins=[srcS[:, :]], outs=[gatS[rnd][:, :]]))(rnd),
                     reads=[b("srcS")], writes=[b("gatS%d" % rnd)])
                if rnd == 2:
                    break
                dma("sp", S[:], gatS[rnd][rnd * 128:(rnd + 1) * 128, :], [b("gatS%d" % rnd)], SALL)
                act(Sb[:], S[:], AF.Copy, SALL, SBALL)
                for l in range(NCH):
                    for hg in range(NHG):
                        blk, RB = load_blk(l, hg, SC)
                        s_chain(views(blk), hg, [RB], False)
            for pc_ in range(4):
                sl = slice(pc_ * 512, (pc_ + 1) * 512)
                SP_ = [b("S%d" % i) for i in range(8) if (i * FW) // 512 == pc_]
                for j in range(3):
                    dma("sp", tmpf[:], gatS[j][j * 128:(j + 1) * 128, sl], [b("gatS%d" % j)], [b("tmpf")])
                    if j == 0:
                        ts(S[:, sl], tmpf[:], mprev[:, 0:1], None, ALU.mult, None, [b("tmpf"), b("mprev")], SP_)
                    else:
                        stt(S[:, sl], tmpf[:], mprev[:, j:j + 1], S[:, sl], ALU.mult, ALU.add, [b("tmpf"), b("mprev")] + SP_, SP_)
            act(Sb[:], S[:], AF.Copy, SALL, SBALL)
            for ci in range(1, NCH + 1):
                l = ci - 1
                xb_ = xt[ci % 2]
                XB = b("xt%d" % (ci % 2))
                dma("sp", xb_[:], src_d[ci * 128:(ci + 1) * 128, :], [] if first_layer else [b("xmid%d" % ci)], [XB])
                dma("sp", zs[:], zs_s[l], [b("zs_s%d" % l)], [b("zs")])
                for hg in range(NHG):
                    h0 = hg * 4
                    oh = o_t[0]
                    OB = b("o_t0")
                    blk, RB = load_blk(l, hg, PW)
                    s_chain(views(blk), hg, [RB], True)
                    ysl = ya[:, hg * FW:(hg + 1) * FW]
                    zsl = zs[:, hg * FW:(hg + 1) * FW]
                    jk = yT[:].rearrange("p a b -> p (a b)")[:, 0:FW]
                    memset(nrm[:, 0:4], 0.0, [b("nrm")])
                    if gdn:
                        for hh in range(4):
                            act(jk[:, hh * 128:(hh + 1) * 128], oh[:, hh * 128:(hh + 1) * 128], AF.Square, [OB], [b("yT"), b("nrm")],
                                accum_out=nrm[:, hh:hh + 1])
                        act(nrm[:, 4:8], nrm[:, 0:4], AF.Sqrt, [b("nrm")], [b("nrm")], scale=1.0 / 128, bias=EPS)
                        recip(nrm[:, 4:8], nrm[:, 4:8], [b("nrm")], [b("nrm")])
                        for hh in range(4):
                            act(ysl[:, hh * 128:(hh + 1) * 128], oh[:, hh * 128:(hh + 1) * 128], AF.Copy, [OB, b("nrm")], [b("ya")],
                                scale=nrm[:, 4 + hh:5 + hh])
                        tt(G4(ysl), G4(ysl), gnw[:].unsqueeze(1).broadcast_to([128, 4, 128]), ALU.mult, [b("ya"), b("gnw")], [b("ya")])
                        tt(ysl, ysl, zsl, ALU.mult, [b("ya"), b("zs")], [b("ya")])
                    else:
                        tt(oh[:, 0:FW], oh[:, 0:FW], zsl, ALU.mult, [OB, b("zs")], [OB])
                        act(jk, oh[:, 0:FW], AF.Square, [OB], [b("yT"), b("nrm")], accum_out=nrm[:, 0:1])
                        act(nrm[:, 4:5], nrm[:, 0:1], AF.Sqrt, [b("nrm")], [b("nrm")], scale=1.0 / 256, bias=EPS)
                        recip(nrm[:, 4:5], nrm[:, 4:5], [b("nrm")], [b("nrm")])
                        act(ysl, oh[:, 0:FW], AF.Copy, [OB, b("nrm")], [b("ya")], scale=nrm[:, 4:5])
                        tt(ysl, ysl, snw[:, hg * FW:(hg + 1) * FW], ALU.mult, [b("ya"), b("snw")], [b("ya")])
                dbg_out("y", ya[:], [b("ya")], ci)
                for half in range(2):
                    pv = 1 + half
                    for i in range(8):
                        kt = half * 8 + i
                        tr(psb[pv][:, i * 128:(i + 1) * 128], ya[:, kt * 128:(kt + 1) * 128], identb[:], [b("ya"), b("identb")], [PSB[pv]])
                    act(yT[:, half * 8:(half + 1) * 8, :].rearrange("p a b -> p (a b)"), psb[pv][:, 0:1024], AF.Copy, [PSB[pv]], [b("yT")])
                for kt in range(16):
                    wi = wo_cnt[0] % NWO
                    wo_cnt[0] += 1
                    WB = b("wo%d" % wi)
                    dma("sp", wo[wi][:], wout_s[kt], [b("wout_s")], [WB])
                    for n in range(2):
                        mm(ps[3 + n][:, :], yT[:, kt, :], wo[wi][:, n * 512:(n + 1) * 512], kt == 0, kt == 15, [b("yT"), WB], [PSB[3 + n]])
                for n in range(2):
                    tt(xb_[:, n * 512:(n + 1) * 512], xb_[:, n * 512:(n + 1) * 512], ps[3 + n][:, :], ALU.add, [XB, PSB[3 + n]], [XB])
                if final_norm:
                    memset(ss[:, 0:1], 0.0, [b("ss")])
                    act(hid[:], xb_[:], AF.Square, [XB], [b("hid"), b("ss")], accum_out=ss[:, 0:1])
                    act(ss[:, 1:2], ss[:, 0:1], AF.Sqrt, [b("ss")], [b("ss")], scale=1.0 / DM, bias=EPS)
                    recip(ss[:, 2:3], ss[:, 1:2], [b("ss")], [b("ss")])
                    stt(xb_[:], xb_[:], ss[:, 2:3], fnw[:], ALU.mult, ALU.mult, [XB, b("ss"), b("fnw")], [XB])
                if final_norm:
                    dma("sp", xo_d[(ci - 1) * 128:ci * 128, :], xb_[:], [XB], [b("xo%d" % ci)])
                else:
                    dma("sp", xmid_s[ci * 128:(ci + 1) * 128, :], xb_[:], [XB], [b("xmid%d" % ci)])
                    if ci == NCH:
                        dma("sp", hal_src[:, :], xb_[:], [XB], [b("hal_src")])
            if not final_norm:
                P.op("pool", lambda h: h.collective_compute("AllGather", ALU.bypass, replica_groups=RGL, ins=[hal_src[:, :]], outs=[hal_gat[:, :]]),
                     reads=[b("hal_src")], writes=[b("hal_gat")])
                for j in range(3):
                    dma("sp", xt[0][:], hal_gat[j * 128:(j + 1) * 128, :], [b("hal_gat")], [b("xt0")])
                    if j == 0:
                        ts(xt[1][:], xt[0][:], mprev[:, 0:1], None, ALU.mult, None, [b("xt0"), b("mprev")], [b("xt1")])
                    else:
                        stt(xt[1][:], xt[0][:], mprev[:, j:j + 1], xt[1][:], ALU.mult, ALU.add, [b("xt0"), b("mprev"), b("xt1")], [b("xt1")])
                dma("sp", xmid_s[0:128, :], xt[1][:], [b("xt1")], [b("xmid0")])
        P.emit()
    return nc


def conv_diag(conv_w):
    cw = np.asarray(conv_w, np.float32).reshape(4, 8, 4, 128)
    d = np.zeros((8, 128, 4, 4, 128), np.float32)
    for p in range(128):
        d[:, p, :, :, p] = np.transpose(cw[:, :, :, p], (1, 2, 0))
    return d.reshape(8, 128, 16 * 128)


def bc(v, n=128):
    v = np.asarray(v, np.float32).reshape(1, -1)
    return np.ascontiguousarray(np.broadcast_to(v, (n, v.shape[1])))


def layer_inputs(layer, x_with_halo, s_in, norm_w, w_in, conv_w, w_out, a_log, dt_bias, gdn_norm_w=None,
                 conv_b=None, d_skip=None, ssd_norm_w=None, final_norm_w=None):
    H = 16 if layer == "gdn" else 32
    m = dict(host_consts())
    m.update(x=np.ascontiguousarray(x_with_halo, dtype=np.float32), w_in=np.ascontiguousarray(w_in, dtype=np.float32),
             w_out=np.ascontiguousarray(w_out, dtype=np.float32), normw_bc=bc(norm_w), diag=conv_diag(conv_w),
             s_in=np.ascontiguousarray(s_in, dtype=np.float32),
             a_log=np.asarray(a_log, np.float32).reshape(H, 1).copy(), dt_bias=np.asarray(dt_bias, np.float32).reshape(H, 1).copy())
    if layer == "gdn":
        m["gnw_bc"] = bc(gdn_norm_w)
    else:
        m["conv_b"] = np.ascontiguousarray(np.asarray(conv_b, np.float32).reshape(32, 128).T)
        m["dskip_bc"] = bc(d_skip)
        m["snw_bc"] = bc(ssd_norm_w)
    if final_norm_w is not None:
        m["fnw_bc"] = bc(final_norm_w)
    return m


def fused_inputs(x_with_halo, p):
    m = dict(host_consts())
    m["x"] = np.ascontiguousarray(x_with_halo, dtype=np.float32)
    f32 = lambda a: np.ascontiguousarray(np.asarray(a, np.float32))
    m.update(g_w_in=f32(p["gdn_w_in"][0]), g_w_out=f32(p["gdn_w_out"][0]), g_normw_bc=bc(p["norm_w"][0]),
             g_diag=conv_diag(p["gdn_conv_w"][0]), g_a_log=f32(p["gdn_a_log"][0]).reshape(16, 1),
             g_dt_bias=f32(p["gdn_dt_bias"][0]).reshape(16, 1), g_gnw_bc=bc(p["gdn_norm_w"][0]))
    m.update(s_w_in=f32(p["ssd_w_in"][0]), s_w_out=f32(p["ssd_w_out"][0]), s_normw_bc=bc(p["norm_w"][1]),
             s_diag=conv_diag(p["ssd_conv_w"][0]), s_a_log=f32(p["ssd_a_log"][0]).reshape(32, 1),
             s_dt_bias=f32(p["ssd_dt_bias"][0]).reshape(32, 1),
             s_conv_b=np.ascontiguousarray(f32(p["ssd_conv_b"][0]).reshape(32, 128).T), s_dskip_bc=bc(p["ssd_d"][0]),
             s_snw_bc=bc(p["ssd_norm_w"][0]))
    m["fnw_bc"] = bc(p["final_norm_w"])
    return m


def par_inputs(x_with_halo, r, p):
    m = fused_inputs(x_with_halo, p)
    mp = np.zeros((128, 4), np.float32)
    if r >= 1:
        mp[:, r - 1] = 1.0
    m["mprev"] = mp
    return m


def kernel(**inputs):
    x = np.asarray(inputs["x"], np.float32)
    Bn, T, _ = x.shape
    NSEG = 4
    SEG = T // NSEG
    NCH = SEG // 128
    RG = tuple(tuple(range(bi * NSEG, (bi + 1) * NSEG)) for bi in range(Bn))
    nc = build_par(NCH, RG=RG)
    z128 = np.zeros((128, DM), np.float32)
    maps = []
    for bi in range(Bn):
        for r in range(NSEG):
            halo = z128 if r == 0 else x[bi, r * SEG - 128:r * SEG]
            maps.append(par_inputs(np.concatenate([halo, x[bi, r * SEG:(r + 1) * SEG]], axis=0), r, inputs))
    res = run_bass_kernel_spmd(nc, maps, core_ids=list(range(Bn * NSEG)))
    out = np.empty_like(x)
    for bi in range(Bn):
        for r in range(NSEG):
            out[bi, r * SEG:(r + 1) * SEG] = res.results[bi * NSEG + r]["xo"]
    return out
```

```python
import numpy as np
from contextlib import ExitStack
import concourse.bass as bass
import concourse.mybir as mybir
from concourse.bass_utils import run_bass_kernel_spmd

F32 = mybir.dt.float32
BF16 = mybir.dt.bfloat16
AF = mybir.ActivationFunctionType
ALU = mybir.AluOpType

NDMASEM = 24
PROFILE_LINES = None
PROFILE_NAMES = {}
EPS = 1e-6
C = 128
DM = 1024
INW = 6176


class Buf:
    __slots__ = ("name", "lw", "rd")

    def __init__(self, name):
        self.name = name
        self.lw = None
        self.rd = []


class Prog:
    ENGS = ("pe", "act", "dve", "pool", "sp")

    def __init__(self, nc):
        self.nc = nc
        self.ops = []
        self.ndma = 0

    def op(self, eng, fn, reads=(), writes=(), dma=False):
        idx = len(self.ops)
        deps = set()
        for b in reads:
            if b.lw is not None:
                deps.add(b.lw)
        for b in writes:
            if b.lw is not None:
                deps.add(b.lw)
            deps.update(b.rd)
        key = None if dma else eng
        for b in reads:
            if key is not None:
                b.rd = [r for r in b.rd if self.ops[r]["dma"] or self.ops[r]["eng"] != key]
            b.rd.append(idx)
        for b in writes:
            b.lw = idx
            b.rd = []
        d = dict(eng=eng, fn=fn, deps=deps, dma=dma, dmaidx=None)
        if PROFILE_LINES is not None:
            import sys as _sys
            f = _sys._getframe(1)
            while f is not None and f.f_code.co_name not in ("build_par", "build_fused", "build"):
                f = f.f_back
            PROFILE_LINES.append((eng, dma, f.f_lineno if f is not None else 0))
            d["line"] = f.f_lineno if f is not None else 0
        if dma:
            d["dmaidx"] = self.ndma
            self.ndma += 1
        self.ops.append(d)
        return idx

    def emit(self):
        nc = self.nc
        ops = self.ops
        needed = [False] * len(ops)
        for i, o in enumerate(ops):
            for d in o["deps"]:
                po = ops[d]
                if po["dma"]:
                    continue
                if po["eng"] == "pe" and o["eng"] == "pe" and not o["dma"]:
                    continue
                needed[d] = True
        with ExitStack() as es:
            esem = {e: es.enter_context(nc.semaphore("s_" + e)) for e in self.ENGS}
            dsem = [es.enter_context(nc.semaphore("d_%d" % i)) for i in range(NDMASEM)]
            cnt = {e: 0 for e in self.ENGS}
            ev = [None] * len(ops)
            dma_by_idx = {}
            for i, o in enumerate(ops):
                if o["dma"]:
                    k = o["dmaidx"]
                    ev[i] = (dsem[k % NDMASEM], 16 * (k // NDMASEM + 1))
                    dma_by_idx[k] = i
                elif needed[i]:
                    cnt[o["eng"]] += 1
                    ev[i] = (esem[o["eng"]], cnt[o["eng"]])
            per_eng = {e: [] for e in self.ENGS}
            for i, o in enumerate(ops):
                per_eng[o["eng"]].append(i)
            block = es.enter_context(nc.Block())

            def make(ename, handle_name):
                lst = per_eng[ename]
                if not lst:
                    return

                def body(h):
                    waited = {}
                    for i in lst:
                        o = ops[i]
                        evs = []
                        for d in sorted(o["deps"]):
                            po = ops[d]
                            if (not po["dma"]) and po["eng"] == "pe" and ename == "pe" and not o["dma"]:
                                continue
                            evs.append(ev[d])
                        if o["dma"] and o["dmaidx"] >= NDMASEM:
                            evs.append(ev[dma_by_idx[o["dmaidx"] - NDMASEM]])
                        for (s, v) in evs:
                            key = id(s)
                            if waited.get(key, 0) >= v:
                                continue
                            waited[key] = v
                            h.wait_ge(s, v)
                        ins = o["fn"](h)
                        if PROFILE_LINES is not None:
                            try:
                                PROFILE_NAMES[ins.ins.name] = o.get("line", 0)
                            except Exception:
                                pass
                        if ev[i] is not None:
                            s, v = ev[i]
                            ins.then_inc(s, 16 if o["dma"] else 1)
                    for i in lst:
                        o = ops[i]
                        if o["dma"]:
                            s, v = ev[i]
                            if waited.get(id(s), 0) < v:
                                waited[id(s)] = v
                                h.wait_ge(s, v)
                getattr(block, handle_name)(body)

            make("sp", "sync")
            make("pe", "tensor")
            make("act", "scalar")
            make("dve", "vector")
            make("pool", "gpsimd")


LEVELS = [1, 2, 4, 8, 16, 32, 64]


def host_consts():
    i = np.arange(128)
    ident = np.eye(128, dtype=np.float32)
    masks = np.zeros((128, 14, 128), np.float32)
    for li, l in enumerate(LEVELS):
        blk = i // (2 * l)
        half = (i // l) % 2
        M = (blk[:, None] == blk[None, :]) & (half[:, None] == 1) & (half[None, :] == 0)
        masks[:, li, :] = M
        masks[:, 7 + li, :] = M.T
    maskneg = np.where(i[None, :] >= i[:, None], 0.0, -30000.0).astype(np.float32)
    maskneg4 = np.tile(maskneg, (1, 4))
    masksu = (i[None, :] > i[:, None]).astype(np.float32)
    return dict(c_ident=ident, c_masks=masks.reshape(128, 14 * 128), c_maskneg4=maskneg4, c_masksu=masksu)


def build_par(NCH, layers=("gdn", "ssd"), dbg=None, RG=((0, 1, 2, 3), (4, 5, 6, 7))):
    nc = bass.Bass("TRN2", target_bir_lowering=False)

    def din(name, shape, dt=F32):
        return nc.dram_tensor(name, shape, dt, kind="ExternalInput").ap()

    def dout(name, shape, dt=F32):
        return nc.dram_tensor(name, shape, dt, kind="ExternalOutput").ap()

    x_d = din("x", [(NCH + 1) * 128, DM])
    ident_d = din("c_ident", [128, 128])
    masks_d = din("c_masks", [128, 14 * 128])
    maskneg_d = din("c_maskneg4", [128, 512])
    masksu_d = din("c_masksu", [128, 128])
    LD = {}
    for layer in layers:
        pf = layer[0] + "_"
        Hh = 16 if layer == "gdn" else 32
        d = dict(w_in=din(pf + "w_in", [DM, INW]), w_out=din(pf + "w_out", [2048, DM]), normw=din(pf + "normw_bc", [128, DM]),
                 diag=din(pf + "diag", [8, 128, 16 * 128]), a_log=din(pf + "a_log", [Hh, 1]), dt_bias=din(pf + "dt_bias", [Hh, 1]))
        if layer == "gdn":
            d["gnw"] = din(pf + "gnw_bc", [128, 128])
        else:
            d["convb"] = din(pf + "conv_b", [128, 32])
            d["dskip"] = din(pf + "dskip_bc", [128, 32])
            d["snw"] = din(pf + "snw_bc", [128, 2048])
        LD[layer] = d
    fnw_d = din("fnw_bc", [128, DM])
    mprev_d = din("mprev", [128, 4])
    RGL = [list(g) for g in RG]
    xo_d = dout("xo", [NCH * 128, DM])
    dbg_d = {}
    if dbg:
        for nm, (shp, dt_) in dbg.items():
            dbg_d[nm] = dout("dbg_" + nm, shp, dt_)
    diag_s = nc.dram_tensor("diag_s", [8, 128, 16 * 128], BF16, kind="Internal").ap()
    wout_s = nc.dram_tensor("wout_s", [16, 128, DM], BF16, kind="Internal").ap()
    gc_s_full = [nc.dram_tensor("gc_s%d" % i, [32, 128], F32, kind="Internal").ap() for i in range(2)]
    gl_s_full = [nc.dram_tensor("gl_s%d" % i, [32, 1], F32, kind="Internal").ap() for i in range(2)]
    xmid_s = nc.dram_tensor("xmid_s", [(NCH + 1) * 128, DM], F32, kind="Internal").ap()
    PWG, PWS = 2336, 1040
    prod_s = nc.dram_tensor("prod_s", [NCH, 128, 4 * PWG], BF16, kind="Internal").ap()
    zs_s = nc.dram_tensor("zs_s", [NCH, 128, 2048], BF16, kind="Internal").ap()
    srcS = nc.dram_tensor("srcS", [128, 2048], F32, kind="Internal").ap()
    gatS = [nc.dram_tensor("gatS%d" % i, [512, 2048], F32, kind="Internal").ap() for i in range(3)]
    hal_src = nc.dram_tensor("hal_src", [128, DM], F32, kind="Internal").ap()
    hal_gat = nc.dram_tensor("hal_gat", [512, DM], F32, kind="Internal").ap()

    P = Prog(nc)
    es = ExitStack()
    with es:
        def sb(name, shape, dt=F32):
            return es.enter_context(nc.sbuf_tensor(name, shape, dt))

        Wb = sb("Wb", [128, 8, INW], BF16)
        xt = [sb("xt%d" % i, [128, DM]) for i in range(2)]
        hid = sb("hid", [128, DM], BF16)
        hidT = sb("hidT", [128, 8, 128], BF16)
        normw = sb("normw", [128, DM])
        fnw = sb("fnw", [128, DM])
        Pbuf = [sb("Pbuf%d" % i, [128, 4, 131], BF16) for i in range(2)]
        diag = [sb("diag%d" % i, [128, 4, 128], BF16) for i in range(3)]
        convT = sb("convT", [128, 32, 128], BF16)
        zs = sb("zs", [128, 2048], BF16)
        carry = sb("carry", [128, 32, 3], BF16)
        identf = sb("identf", [128, 128])
        identb = sb("identb", [128, 128], BF16)
        onesb = sb("onesb", [128, 128], BF16)
        maskneg = sb("maskneg", [128, 512], BF16)
        onesrow = sb("onesrow", [1, 128])
        negonesrow = sb("negonesrow", [1, 128])
        onesH_f = sb("onesH", [32, 128])
        alog_f = sb("alog", [32, 1])
        negA_f = sb("negA", [32, 1])
        dtb_f = sb("dtb", [32, 1])
        ss = sb("ss", [128, 4])
        smF_f = sb("smF", [32, 6, 128])
        gcrow4 = [sb("gcrow4_%d" % i, [1, 512]) for i in range(2)]
        glrow_f = sb("glrow", [1, 32])
        smT_f = sb("smT", [128, 96])
        tokS_f = sb("tokS", [128, 4, 32])
        glbc_f = sb("glbc", [128, 32])
        vtok = sb("vtok", [128, 2048], BF16)
        S = sb("S", [128, 2048])
        Sb = sb("Sb", [128, 2048], BF16)
        o_t = [sb("o_t%d" % i, [128, 512]) for i in range(1)]
        ya = sb("ya", [128, 2048], BF16)
        yT = sb("yT", [128, 16, 128], BF16)
        wo = [sb("wo%d" % i, [128, DM], BF16) for i in range(2)]
        NWO = 2
        nrm = sb("nrm", [128, 8])
        LT = sb("LT", [128, 4, 128], BF16)
        tmpf = sb("tmpf", [128, 512])
        vnew = sb("vnew", [128, 512], BF16)
        UN = 23232 // 2
        mprev = sb("mprev_sb", [128, 4])
        U = sb("U", [128, UN], BF16)
        ps = [es.enter_context(nc.psum_tensor("ps%d" % i, [128, 512], F32)) for i in range(8)]
        psb = [p[:].bitcast(BF16) for p in ps]

        B = {}

        def b(n):
            if n not in B:
                B[n] = Buf(n)
            return B[n]

        PSB = [b("ps%d" % i) for i in range(8)]

        def dma(eng, out, in_, reads, writes):
            P.op(eng, lambda h: h.dma_start(out=out, in_=in_), reads=reads, writes=writes, dma=True)


        dma("sp", identf[:], ident_d[:, :], [], [b("identf")])
        dma("pool", identb[:], ident_d[:, :], [], [b("identb")])
        dma("pool", maskneg[:], maskneg_d[:, :], [], [b("maskneg")])
        dma("sp", fnw[:], fnw_d[:, :], [], [b("fnw")])
        dma("sp", mprev[:], mprev_d[:, :], [], [b("mprev")])
        P.op("dve", lambda h: h.memset(onesb[:], 1.0), writes=[b("onesb")])
        P.op("dve", lambda h: h.memset(onesrow[:], 1.0), writes=[b("onesrow")])
        P.op("dve", lambda h: h.memset(negonesrow[:], -1.0), writes=[b("negonesrow")])
        P.op("dve", lambda h: h.memset(onesH_f[:], 1.0), writes=[b("onesH")])
        P.op("dve", lambda h: h.memset(ss[:], 0.0), writes=[b("ss")])
        P.op("dve", lambda h: h.memset(nrm[:], 0.0), writes=[b("nrm")])

        def mm(out, lhsT, rhs, start, stop, reads, writes):
            P.op("pe", lambda h: h.matmul(out, lhsT=lhsT, rhs=rhs, start=start, stop=stop), reads=reads, writes=writes)

        def tr(out, in_, ident, reads, writes):
            P.op("pe", lambda h: h.transpose(out=out, in_=in_, identity=ident), reads=reads, writes=writes)

        def act(out, in_, func, reads, writes, **kw):
            P.op("act", lambda h: h.activation(out=out, in_=in_, func=func, **kw), reads=reads, writes=writes)

        def tt(out, in0, in1, op, reads, writes, eng="dve"):
            P.op(eng, lambda h: h.tensor_tensor(out=out, in0=in0, in1=in1, op=op), reads=reads, writes=writes)

        def ts(out, in0, s1, s2, op0, op1, reads, writes, eng="dve"):
            if op1 is None:
                P.op(eng, lambda h: h.tensor_scalar(out=out, in0=in0, scalar1=s1, scalar2=None, op0=op0), reads=reads, writes=writes)
            else:
                P.op(eng, lambda h: h.tensor_scalar(out=out, in0=in0, scalar1=s1, scalar2=s2, op0=op0, op1=op1), reads=reads, writes=writes)

        def stt(out, in0, scalar, in1, op0, op1, reads, writes):
            P.op("dve", lambda h: h.scalar_tensor_tensor(out=out, in0=in0, scalar=scalar, in1=in1, op0=op0, op1=op1),
                 reads=reads, writes=writes)

        def memset(ap, val, writes):
            P.op("dve", lambda h: h.memset(ap, val), writes=writes)

        def recip(out, in_, reads, writes):
            P.op("dve", lambda h: h.reciprocal(out=out, in_=in_), reads=reads, writes=writes)

        def cp(out, in_, reads, writes):
            P.op("dve", lambda h: h.tensor_copy(out=out, in_=in_), reads=reads, writes=writes)

        def dbg_out(name, src_ap, reads, ci):
            if dbg and name in dbg_d and ci == dbg_chunk:
                dma("sp", dbg_d[name][:, :], src_ap, reads, [b("dbgo_" + name)])

        dbg_chunk = NCH
        wo_cnt = [0]
        dg_cnt = [0]
        G4 = lambda ap: ap.rearrange("p (a b) -> p a b", a=4)


        GDN_BUFS = ["masks", "masksu", "gnw", "sq4", "ke", "kd", "LTs", "NTm", "Nn", "Tm0", "Tm1", "Ym0", "Ym1", "Zm", "Zpm",
                    "ident4", "ub", "wT"]
        SSD_BUFS = ["ktok", "vdec", "convb", "dskip", "snw"]
        wo_cnt = [0]
        dg_cnt = [0]
        G4 = lambda ap: ap.rearrange("p (a b) -> p a b", a=4)

        def carve_factory():
            off = [0]

            def carve(nbytes, dt=BF16):
                n = nbytes // 2
                ap = U[:, off[0]:off[0] + n]
                off[0] += n
                assert off[0] <= UN
                if dt == F32:
                    ap = ap.bitcast(F32)
                return ap
            return carve

        for lidx, layer in enumerate(layers):
            gdn = layer == "gdn"
            first_layer = lidx == 0
            final_norm = lidx == len(layers) - 1
            H = 16 if gdn else 32
            DV = 2048 // H
            NHG = H // 4
            FW = 4 * DV
            CO = 0 if gdn else 2048
            ZO = 4096 if gdn else 0
            SO = 6144
            QO, KO, VO = (0, 8, 16) if gdn else (24, 16, 0)
            D_ = LD[layer]
            src_d = x_d if first_layer else xmid_s
            smF = smF_f[0:H]
            onesH = onesH_f[0:H]
            alog, negA, dtb = alog_f[0:H], negA_f[0:H], dtb_f[0:H]
            glrow = glrow_f[:, 0:H]
            smT = smT_f[:, 0:3 * H]
            tokS = tokS_f[:, :, 0:H]
            glbc = glbc_f[:, 0:H]
            gc_s = [g[0:H] for g in gc_s_full]
            gl_s = [g[0:H] for g in gl_s_full]
            carve = carve_factory()
            PA_BUFS = ["masks", "masksu", "sq4", "ke", "LTs", "NTm", "Nn", "Tm0", "Tm1", "Ym0", "Ym1", "Zm", "Zpm", "ident4", "stg",
                       "ktok", "convb", "dskip", "LTa"]
            RB_BUFS = ["rb0", "rb1", "rb2", "tmpfB", "vnewB", "o_tB"]
            if lidx > 0:
                P.op("dve", lambda h: h.memset(ss[:, 3:4], 0.0), reads=[b(n) for n in RB_BUFS + ["gnw", "snw"]],
                     writes=[b(n) for n in PA_BUFS + ["gnw", "snw"]] + [b("ss")])
            PW = PWG if gdn else PWS
            SC = 1568 if gdn else 400
            if gdn:
                gnw = carve(512, F32)
            else:
                snw = carve(4096)
            carve_rb = carve_factory()
            carve_rb(512 if gdn else 4096)
            rbuf = [carve_rb(4672) for i in range(3)]
            tmpf_b = carve_rb(2048, F32)
            vnew_b = carve_rb(1024)
            o_t_b = carve_rb(2048, F32)
            tmpf_a, vnew_a = tmpf, vnew
            if gdn:
                masks = carve(3584).rearrange("p (a b) -> p a b", a=14)
                masksu = carve(256)
                sq4 = carve(1024)
                ke = G4(carve(1024))
                LTs = G4(carve(1024))
                NTm = G4(carve(1024))
                Nn = G4(carve(1024))
                Tm = [G4(carve(1024)) for i in range(2)]
                Ym = [G4(carve(1024)) for i in range(2)]
                Zm = G4(carve(1024))
                Zpm = G4(carve(1024))
                ident4 = G4(carve(1024))
                stg = carve(4672)
            else:
                ktok = carve(2048).rearrange("p (a b) -> p a b", a=8)
                convb = carve(128, F32)
                dskip = carve(128, F32)
                stg = carve(2080)
                LTa = G4(carve(1024))
            if gdn:
                LTa = None

            def views(blk):
                w_ = blk.shape[1]
                if gdn:
                    return dict(wT=G4(blk[:, 0:512]), kd=G4(blk[:, 512:1024]), ub=blk[:, 1024:1536],
                                sm=blk[:, 1536:1560].bitcast(F32), attnT=G4(blk[:, 1568:2080]) if w_ >= 2080 else None,
                                qT=blk[:, 2080:2336].rearrange("p (a b) -> p a b", a=2) if w_ >= 2336 else None)
                return dict(ktok=blk[:, 0:128], vdec=blk[:, 128:384], sm=blk[:, 384:400].bitcast(F32),
                            o0=blk[:, 400:912].bitcast(F32) if w_ >= 912 else None, CT=blk[:, 912:1040] if w_ >= 1040 else None)

            def s_chain(v, hg, VB, full, par=0):
                h0 = hg * 4
                SB_, SBb = b("S%d" % hg), b("Sb%d" % hg)
                ps_a, ps_b, ps_c = (4, 5, 6) if par == 0 else (7, 0, 1)
                tmpf, TMB = (tmpf_a, b("tmpf")) if par == 0 else (tmpf_b, b("tmpfB"))
                vnew, VNB = (vnew_a, b("vnew")) if par == 0 else (vnew_b, b("vnewB"))
                oh, OB = (o_t[0], b("o_t0")) if par == 0 else (o_t_b, b("o_tB"))
                sm = v["sm"]
                if gdn:
                    for hh in range(4):
                        hd = h0 + hh
                        mm(ps[ps_c][:, hh * 128:(hh + 1) * 128], v["wT"][:, hh, :], Sb[:, hd * 128:(hd + 1) * 128], True, True, VB + [SBb], [PSB[ps_c]])
                    tt(G4(tmpf[:]), G4(ps[ps_c][:, :]), sm[:, 0:4].unsqueeze(2).broadcast_to([128, 4, 128]), ALU.mult,
                       [PSB[ps_c]] + VB, [TMB])
                    tt(vnew[:], tmpf[:], v["ub"], ALU.add, [TMB] + VB, [VNB])
                    if full:
                        for hh in range(4):
                            hd = h0 + hh
                            mm(ps[ps_a][:, hh * 128:(hh + 1) * 128], v["qT"][:, hh // 2, :], Sb[:, hd * 128:(hd + 1) * 128], True, True,
                               VB + [SBb], [PSB[ps_a]])
                        for hh in range(4):
                            mm(ps[ps_b][:, hh * 128:(hh + 1) * 128], v["attnT"][:, hh, :], vnew[:, hh * 128:(hh + 1) * 128], True, True,
                               VB + [VNB], [PSB[ps_b]])
                        tt(G4(tmpf[:]), G4(ps[ps_a][:, :]), sm[:, 8:12].unsqueeze(2).broadcast_to([128, 4, 128]), ALU.mult,
                           [PSB[ps_a]] + VB, [TMB])
                        tt(oh[:, 0:FW], tmpf[:, 0:FW], ps[ps_b][:, 0:FW], ALU.add, [TMB, PSB[ps_b]], [OB])
                    for hh in range(4):
                        mm(ps[ps_c][:, hh * 128:(hh + 1) * 128], v["kd"][:, hh, :], vnew[:, hh * 128:(hh + 1) * 128], True, True,
                           VB + [VNB], [PSB[ps_c]])
                    gl = sm[:, 4:8]
                else:
                    if full:
                        mm(ps[ps_a][:, 0:FW], v["CT"], Sb[:, h0 * DV:(h0 + 4) * DV], True, True, VB + [SBb], [PSB[ps_a]])
                        tt(G4(tmpf[:, 0:FW]), G4(ps[ps_a][:, 0:FW]), sm[:, 4:8].unsqueeze(2).broadcast_to([128, 4, DV]), ALU.mult,
                           [PSB[ps_a]] + VB, [TMB])
                        tt(oh[:, 0:FW], tmpf[:, 0:FW], v["o0"], ALU.add, [TMB] + VB, [OB])
                    mm(ps[ps_c][:, 0:FW], v["ktok"], v["vdec"], True, True, VB, [PSB[ps_c]])
                    gl = sm[:, 0:4]
                tt(G4(tmpf[:, 0:FW]), G4(S[:, hg * FW:(hg + 1) * FW]), gl.unsqueeze(2).broadcast_to([128, 4, DV]),
                   ALU.mult, [SB_] + VB, [TMB])
                tt(S[:, hg * FW:(hg + 1) * FW], tmpf[:, 0:FW], ps[ps_c][:, 0:FW], ALU.add, [TMB, PSB[ps_c]], [SB_])
                act(Sb[:, hg * FW:(hg + 1) * FW], S[:, hg * FW:(hg + 1) * FW], AF.Copy, [SB_], [SBb])

            def blk_d(l, hg, lo, hi):
                return prod_s[l][:, hg * PW + lo:hg * PW + hi]

            dma("sp", normw[:], D_["normw"][:, :], [], [b("normw")])
            dma("sp", alog[:], D_["a_log"][:, :], [], [b("alog")])
            dma("sp", dtb[:], D_["dt_bias"][:, :], [], [b("dtb")])
            if gdn:
                dma("pool", masks[:].rearrange("p a b -> p (a b)"), masks_d[:, :], [], [b("masks")])
                dma("pool", masksu[:], masksu_d[:, :], [], [b("masksu")])
                dma("sp", gnw[:], D_["gnw"][:, :], [], [b("gnw")])
            else:
                dma("sp", convb[:], D_["convb"][:, :], [], [b("convb")])
                dma("sp", dskip[:], D_["dskip"][:, :], [], [b("dskip")])
                dma("pool", snw[:], D_["snw"][:, :], [], [b("snw")])
            P.op("dve", lambda h: h.memset(carry[:], 0.0), writes=[b("carry")])
            P.op("dve", lambda h: h.memset(S[:], 0.0), writes=[b("S%d" % i) for i in range(8)])
            P.op("dve", lambda h: h.memset(Sb[:], 0.0), writes=[b("Sb%d" % i) for i in range(8)])
            P.op("act", (lambda negA, alog: lambda h: h.activation(out=negA[:], in_=alog[:], func=AF.Exp))(negA, alog),
                 reads=[b("alog")], writes=[b("negA")])
            P.op("dve", (lambda negA: lambda h: h.tensor_scalar(out=negA[:], in0=negA[:], scalar1=-1.0, scalar2=None, op0=ALU.mult))(negA),
                 reads=[b("negA")], writes=[b("negA")])
            if gdn:
                for i in range(4):
                    P.op("dve", (lambda i, ident4: lambda h: h.tensor_copy(out=ident4[:, i, :], in_=identb[:]))(i, ident4),
                         reads=[b("identb")], writes=[b("ident4")])
            for k in range(8):
                for (f0, f1) in [(0, 2048), (2048, 4096), (4096, INW)]:
                    dma("pool", Wb[:, k, f0:f1], D_["w_in"][k * 128:(k + 1) * 128, f0:f1], [], [b("Wb")])
            for kt2 in range(8):
                dma("pool", zs[:].rearrange("p (a c) -> p a c", a=2),
                    D_["w_out"][kt2 * 256:(kt2 + 1) * 256, :].rearrange("(a p) c -> p a c", a=2), [], [b("zs")])
                dma("sp", wout_s[kt2 * 2:(kt2 + 1) * 2].rearrange("a p c -> p a c"),
                    zs[:].rearrange("p (a c) -> p a c", a=2), [b("zs")], [b("wout_s")])
            for c4 in range(8):
                dma("pool", zs[:], D_["diag"][c4], [], [b("zs")])
                dma("sp", diag_s[c4], zs[:], [b("zs")], [b("diag_s")])
            dbg_chunk = NCH
            for ci in range(NCH + 1):
                halo = ci == 0
                xb_ = xt[ci % 2]
                XB = b("xt%d" % (ci % 2))
                dma("sp", xb_[:], src_d[ci * 128:(ci + 1) * 128, :], [] if first_layer else [b("xmid%d" % ci)], [XB])
                memset(ss[:, 0:1], 0.0, [b("ss")])
                act(hid[:], xb_[:], AF.Square, [XB], [b("hid"), b("ss")], accum_out=ss[:, 0:1])
                act(ss[:, 1:2], ss[:, 0:1], AF.Sqrt, [b("ss")], [b("ss")], scale=1.0 / DM, bias=EPS)
                recip(ss[:, 2:3], ss[:, 1:2], [b("ss")], [b("ss")])
                stt(hid[:], xb_[:], ss[:, 2:3], normw[:], ALU.mult, ALU.mult, [XB, b("ss"), b("normw")], [b("hid")])
                for k in range(8):
                    tr(psb[0][:, k * 128:(k + 1) * 128], hid[:, k * 128:(k + 1) * 128], identb[:], [b("hid"), b("identb")], [PSB[0]])
                act(hidT[:].rearrange("p a b -> p (a b)"), psb[0][:, 0:1024], AF.Copy, [PSB[0]], [b("hidT")])
                def inproj_group(c4):
                    pa = 1 + (c4 % 2)
                    for i in range(4):
                        f0 = CO + (c4 * 4 + i) * 128
                        for k in range(8):
                            mm(ps[pa][:, i * 128:(i + 1) * 128], Wb[:, k, f0:f0 + 128], hidT[:, k, :], k == 0, k == 7,
                               [b("Wb"), b("hidT")], [PSB[pa]])

                inproj_group(0)
                for c4 in range(8):
                    pa = 1 + (c4 % 2)
                    pc = 3 + (c4 % 2)
                    pbuf = Pbuf[c4 % 2]
                    PB = b("Pbuf%d" % (c4 % 2))
                    if c4 + 1 < 8:
                        inproj_group(c4 + 1)
                    cp(pbuf[:, :, 0:3], carry[:, c4 * 4:(c4 + 1) * 4, :], [b("carry")], [PB])
                    act(pbuf[:, :, 3:131], G4(ps[pa][:]), AF.Copy, [PSB[pa]], [PB])
                    cp(carry[:, c4 * 4:(c4 + 1) * 4, :], pbuf[:, :, 128:131], [PB], [b("carry")])
                    if halo:
                        continue
                    for i in range(4):
                        ct = c4 * 4 + i
                        di = dg_cnt[0] % 3
                        dg_cnt[0] += 1
                        dg = diag[di]
                        DG = b("diag%d" % di)
                        dma("sp", dg[:].rearrange("p a b -> p (a b)"), diag_s[c4][:, i * 512:(i + 1) * 512], [b("diag_s")], [DG])
                        for j in range(4):
                            mm(ps[pc][:, i * 128:(i + 1) * 128], dg[:, j, :], pbuf[:, i, j:j + 128], j == 0, j == 3, [DG, PB], [PSB[pc]])
                        if not gdn:
                            act(convT[:, ct, :], ps[pc][:, i * 128:(i + 1) * 128], AF.Silu, [PSB[pc], b("convb")], [b("convT%d" % c4)],
                                bias=convb[:, ct:ct + 1])
                    if gdn:
                        act(convT[:, c4 * 4:(c4 + 1) * 4, :], G4(ps[pc][:]), AF.Silu, [PSB[pc]], [b("convT%d" % c4)])
                if halo:
                    continue
                dbg_out("convT", convT[:].rearrange("p a b -> p (a b)"), [b("convT%d" % i) for i in range(8)], ci)
                for f in range(4):
                    pz = 5 + (f % 2)
                    for k in range(8):
                        mm(ps[pz][:, :], hidT[:, k, :], Wb[:, k, ZO + f * 512:ZO + (f + 1) * 512], k == 0, k == 7,
                           [b("Wb"), b("hidT")], [PSB[pz]])
                    act(zs[:, f * 512:(f + 1) * 512], ps[pz][:, :], AF.Silu, [PSB[pz]], [b("zs")])
                SM = b("smF")
                if gdn:
                    for k in range(8):
                        mm(ps[7][0:16, 0:128], Wb[:, k, SO:SO + 16], hidT[:, k, :], k == 0, k == 7, [b("Wb"), b("hidT")], [PSB[7]])
                    for k in range(8):
                        mm(ps[7][0:16, 128:256], Wb[:, k, SO + 16:SO + 32], hidT[:, k, :], k == 0, k == 7, [b("Wb"), b("hidT")], [PSB[7]])
                    act(smF[:, 0, :], ps[7][0:16, 0:128], AF.Sigmoid, [PSB[7]], [SM])
                    act(smF[:, 1, :], ps[7][0:16, 128:256], AF.Exp, [PSB[7], b("dtb")], [SM], bias=dtb[:, 0:1])
                else:
                    for k in range(8):
                        mm(ps[7][0:32, 0:128], Wb[:, k, SO:SO + 32], hidT[:, k, :], k == 0, k == 7, [b("Wb"), b("hidT")], [PSB[7]])
                    act(smF[:, 1, :], ps[7][0:32, 0:128], AF.Exp, [PSB[7], b("dtb")], [SM], bias=dtb[:, 0:1])
                act(smF[:, 2, :], smF[:, 1, :], AF.Ln, [SM], [SM], bias=1.0)
                if not gdn:
                    cp(smF[:, 0, :], smF[:, 2, :], [SM], [SM])
                ts(smF[:, 3, :], smF[:, 2, :], negA[:, 0:1], None, ALU.mult, None, [SM, b("negA")], [SM])
                P.op("dve", (lambda smF, onesH: lambda h: h.tensor_tensor_scan(
                    out=smF[:, 4, :], data0=onesH[:], data1=smF[:, 3, :], initial=0.0, op0=ALU.mult, op1=ALU.add))(smF, onesH),
                     reads=[SM, b("onesH")], writes=[SM])
                ts(smF[:, 5, :], smF[:, 4, :], -1.0, smF[:, 4, 127:128], ALU.mult, ALU.add, [SM], [SM])
                gcs, GCS = gc_s[ci % 2], b("gc_s%d" % (ci % 2))
                gls, GLS = gl_s[ci % 2], b("gl_s%d" % (ci % 2))
                dma("pool", gcs[:, :], smF[:, 4, :], [SM], [GCS])
                dma("pool", gls[:, :], smF[:, 4, 127:128], [SM], [GLS])
                dma("sp", glrow[0:1, :], gls.rearrange("h o -> o h"), [GLS], [b("glrow")])
                for t_, src in enumerate([0, 4, 5]):
                    tr(ps[7][:, 256 + t_ * H:256 + (t_ + 1) * H], smF[:, src, :], identf[0:H, 0:H], [SM, b("identf")], [PSB[7]])
                cp(smT[:], ps[7][:, 256:256 + 3 * H], [PSB[7]], [b("smT")])
                TS = b("tokS")
                act(tokS[:, 0, :], smT[:, H:2 * H], AF.Exp, [b("smT")], [TS])
                act(tokS[:, 1, :], smT[:, 2 * H:3 * H], AF.Exp, [b("smT")], [TS])
                if gdn:
                    ts(tokS[:, 3, :], smT[:, 0:H], -1.0, None, ALU.mult, None, [b("smT")], [TS])
                else:
                    tt(tokS[:, 2, :], smT[:, 0:H], tokS[:, 1, :], ALU.mult, [b("smT"), TS], [TS])
                mm(ps[7][:, 384:384 + H], onesrow[0:1, 0:128], glrow[0:1, :], True, True, [b("onesrow"), b("glrow")], [PSB[7]])
                act(glbc[:], ps[7][:, 384:384 + H], AF.Exp, [PSB[7]], [b("glbc")])
                dbg_out("smT", smT[:], [b("smT")], ci)
                if gdn:
                    for g4 in range(4):
                        CB = b("convT%d" % g4)
                        cv = convT[:, g4 * 4:(g4 + 1) * 4, :].rearrange("p a b -> p (a b)")
                        tt(sq4[:], cv, cv, ALU.mult, [CB], [b("sq4")])
                        mm(ps[1][:, :], onesb[:], sq4[:], True, True, [b("onesb"), b("sq4")], [PSB[1]])
                        act(tmpf[:], ps[1][:, :], AF.Sqrt, [PSB[1]], [b("tmpf")], bias=EPS)
                        recip(tmpf[:], tmpf[:], [b("tmpf")], [b("tmpf")])
                        stt(cv, cv, (128.0 ** -0.5) if g4 < 2 else 1.0, tmpf[:], ALU.mult, ALU.mult, [CB, b("tmpf")], [CB])
                dbg_out("qkn", convT[:, 0:16, :].rearrange("p a b -> p (a b)"), [b("convT%d" % i) for i in range(4)], ci)
                for half in range(2):
                    pv = 1 + half
                    for i in range(8):
                        ct = VO + half * 8 + i
                        tr(psb[pv][:, i * 128:(i + 1) * 128], convT[:, ct, :], identb[:], [b("convT%d" % (ct // 4)), b("identb")], [PSB[pv]])
                    act(vtok[:, half * 1024:(half + 1) * 1024], psb[pv][:, 0:1024], AF.Copy, [PSB[pv]], [b("vtok")])
                for i in range(8):
                    ct = KO + i
                    tr(psb[3][:, i * 128:(i + 1) * 128], convT[:, ct, :], identb[:], [b("convT%d" % (ct // 4)), b("identb")], [PSB[3]])
                if not gdn:
                    act(ktok[:].rearrange("p a b -> p (a b)"), psb[3][:, 0:1024], AF.Copy, [PSB[3]], [b("ktok")])
                for hg in range(NHG):
                    h0 = hg * 4
                    SV = views(stg)
                    attnT = SV["attnT"] if gdn else LTa
                    gr = gcrow4[hg % 2]
                    GR = b("gcrow4_%d" % (hg % 2))
                    dma("sp", gr[0:1, :], gcs[h0:h0 + 4, :].rearrange("(o h) c -> o (h c)", o=1), [GCS], [GR])
                    mm(ps[4][:, :], onesrow[0:1, 0:128], gr[0:1, :], True, False, [b("onesrow"), GR], [PSB[4]])
                    for hh in range(4):
                        mm(ps[4][:, hh * 128:(hh + 1) * 128], gr[0:1, hh * 128:(hh + 1) * 128], negonesrow[0:1, :], False, False,
                           [b("negonesrow"), GR], [PSB[4]])
                    mm(ps[4][:, :], identb[:], maskneg[:], False, True, [b("identb"), b("maskneg")], [PSB[4]])
                    act(LT[:].rearrange("p a b -> p (a b)"), ps[4][:, :], AF.Exp, [PSB[4]], [b("LT")])
                    if gdn:
                        kin = psb[3][:, hg * 256:(hg + 1) * 256].rearrange("p (g d) -> p g d", g=2).unsqueeze(2).broadcast_to([128, 2, 2, 128])
                        for (dst, DB, row) in [(ke, b("ke"), 0), (SV["kd"], b("stg"), 1)]:
                            tt(dst[:].rearrange("p (g r) d -> p g r d", g=2), kin,
                               tokS[:, row, h0:h0 + 4].rearrange("p (g r) -> p g r", g=2).unsqueeze(3).broadcast_to([128, 2, 2, 128]),
                               ALU.mult, [PSB[3], TS], [DB])
                        for qq in range(2):
                            g = hg * 2 + qq
                            mm(ps[5][:, qq * 128:(qq + 1) * 128], convT[:, KO + g, :], convT[:, KO + g, :], True, True,
                               [b("convT%d" % ((KO + g) // 4))], [PSB[5]])
                            mm(ps[5][:, 256 + qq * 128:256 + (qq + 1) * 128], convT[:, KO + g, :], convT[:, QO + g, :], True, True,
                               [b("convT%d" % ((KO + g) // 4)), b("convT%d" % ((QO + g) // 4))], [PSB[5]])
                        tt(LTs[:], LT[:], masksu[:].unsqueeze(1).broadcast_to([128, 4, 128]), ALU.mult, [b("LT"), b("masksu")], [b("LTs")])
                        for hh in range(4):
                            stt(NTm[:, hh, :], ps[5][:, (hh // 2) * 128:(hh // 2 + 1) * 128], tokS[:, 3, h0 + hh:h0 + hh + 1], LTs[:, hh, :],
                                ALU.mult, ALU.mult, [PSB[5], TS, b("LTs")], [b("NTm")])
                        tt(attnT[:].rearrange("p (q r) d -> p q r d", q=2),
                           ps[5][:, 256:512].rearrange("p (q d) -> p q d", q=2).unsqueeze(2).broadcast_to([128, 2, 2, 128]),
                           LT[:].rearrange("p (q r) d -> p q r d", q=2), ALU.mult, [PSB[5], b("LT")], [b("stg") if gdn else b("LTa")])
                    else:
                        g = hg
                        mm(ps[5][:, 0:128], convT[:, KO + g, :], convT[:, QO + g, :], True, True,
                           [b("convT%d" % ((KO + g) // 4)), b("convT%d" % ((QO + g) // 4))], [PSB[5]])
                        tt(attnT[:], ps[5][:, 0:128].unsqueeze(1).broadcast_to([128, 4, 128]), LT[:], ALU.mult, [PSB[5], b("LT")], [b("stg") if gdn else b("LTa")])
                    if gdn:
                        for hh in range(4):
                            tr(psb[6][:, hh * 128:(hh + 1) * 128], NTm[:, hh, :], identb[:], [b("NTm"), b("identb")], [PSB[6]])
                        act(Nn[:].rearrange("p a b -> p (a b)"), psb[6][:, 0:512], AF.Copy, [PSB[6]], [b("Nn")])
                        cur = 0
                        Tc, Yc = ident4, ident4
                        TCB, YCB = b("ident4"), b("ident4")
                        for li in range(7):
                            Tn_, Yn_ = Tm[cur], Ym[cur]
                            TNB, YNB = b("Tm%d" % cur), b("Ym%d" % cur)
                            last = li == 6
                            Ml = masks[:, li, :].unsqueeze(1).broadcast_to([128, 4, 128])
                            MlT = masks[:, 7 + li, :].unsqueeze(1).broadcast_to([128, 4, 128])
                            if not last:
                                for hh in range(4):
                                    mm(ps[4][:, hh * 128:(hh + 1) * 128], NTm[:, hh, :], Tc[:, hh, :], True, True, [b("NTm"), TCB], [PSB[4]])
                                tt(Zm[:], G4(ps[4][:, :]), Ml, ALU.mult, [PSB[4], b("masks")], [b("Zm")])
                            for hh in range(4):
                                mm(ps[5][:, hh * 128:(hh + 1) * 128], Nn[:, hh, :], Yc[:, hh, :], True, True, [b("Nn"), YCB], [PSB[5]])
                            tt(Zpm[:], G4(ps[5][:, :]), MlT, ALU.mult, [PSB[5], b("masks")], [b("Zpm")])
                            if not last:
                                for hh in range(4):
                                    mm(ps[6][:, hh * 128:(hh + 1) * 128], Yc[:, hh, :], Zm[:, hh, :], True, True, [YCB, b("Zm")], [PSB[6]])
                                tt(Tn_[:], Tc[:], G4(ps[6][:, :]), ALU.add, [PSB[6], TCB], [TNB])
                            for hh in range(4):
                                mm(ps[7][:, hh * 128:(hh + 1) * 128], Tc[:, hh, :], Zpm[:, hh, :], True, True, [TCB, b("Zpm")], [PSB[7]])
                            tt(Yn_[:], Yc[:], G4(ps[7][:, :]), ALU.add, [PSB[7], YCB], [YNB])
                            if not last:
                                Tc, TCB = Tn_, TNB
                            Yc, YCB = Yn_, YNB
                            cur ^= 1
                        for hh in range(4):
                            hd = h0 + hh
                            mm(ps[4][:, hh * 128:(hh + 1) * 128], Yc[:, hh, :], vtok[:, hd * 128:(hd + 1) * 128], True, True,
                               [YCB, b("vtok")], [PSB[4]])
                        tt(G4(SV["ub"]), G4(ps[4][:, :]), smT[:, h0:h0 + 4].unsqueeze(2).broadcast_to([128, 4, 128]), ALU.mult,
                           [PSB[4], b("smT")], [b("stg")])
                        for hh in range(4):
                            mm(ps[5][:, hh * 128:(hh + 1) * 128], ke[:, hh, :], Yc[:, hh, :], True, True, [b("ke"), YCB], [PSB[5]])
                        act(SV["wT"].rearrange("p a b -> p (a b)"), ps[5][:, :], AF.Copy, [PSB[5]], [b("stg")])
                        cp(SV["sm"][:, 0:4], tokS[:, 3, h0:h0 + 4], [TS], [b("stg")])
                        cp(SV["sm"][:, 4:8], glbc[:, h0:h0 + 4], [b("glbc")], [b("stg")])
                        cp(SV["sm"][:, 8:12], tokS[:, 0, h0:h0 + 4], [TS], [b("stg")])
                        dma("pool", blk_d(ci - 1, hg, 0, 2080), stg[:, 0:2080], [b("stg")], [b("prod%d_%d" % (ci - 1, hg))])
                        dma("pool", blk_d(ci - 1, hg, 2080, 2336).rearrange("p (a b) -> p a b", a=2), convT[:, QO + 2 * hg:QO + 2 * hg + 2, :],
                            [b("convT%d" % ((QO + 2 * hg) // 4))], [b("prodq%d_%d" % (ci - 1, hg))])
                    else:
                        xin = G4(vtok[:, h0 * DV:(h0 + 4) * DV])
                        tt(G4(vnew[:, 0:FW]), xin, smT[:, h0:h0 + 4].unsqueeze(2).broadcast_to([128, 4, DV]),
                           ALU.mult, [b("vtok"), b("smT")], [b("vnew")])
                        tt(G4(SV["vdec"]), xin, tokS[:, 2, h0:h0 + 4].unsqueeze(2).broadcast_to([128, 4, DV]),
                           ALU.mult, [b("vtok"), TS], [b("stg")])
                        for hh in range(4):
                            mm(ps[5][:, hh * DV:(hh + 1) * DV], attnT[:, hh, :], vnew[:, hh * DV:(hh + 1) * DV], True, True,
                               [b("LTa"), b("vnew")], [PSB[5]])
                        tt(G4(tmpf[:, 0:FW]), xin, dskip[:, h0:h0 + 4].unsqueeze(2).broadcast_to([128, 4, DV]),
                           ALU.mult, [b("vtok"), b("dskip")], [b("tmpf")])
                        tt(SV["o0"], tmpf[:, 0:FW], ps[5][:, 0:FW], ALU.add, [b("tmpf"), PSB[5]], [b("stg")])
                        cp(SV["sm"][:, 0:4], glbc[:, h0:h0 + 4], [b("glbc")], [b("stg")])
                        cp(SV["sm"][:, 4:8], tokS[:, 0, h0:h0 + 4], [TS], [b("stg")])
                        dma("pool", blk_d(ci - 1, hg, 128, 912), stg[:, 128:912], [b("stg")], [b("prod%d_%d" % (ci - 1, hg))])
                        dma("pool", blk_d(ci - 1, hg, 0, 128), ktok[:, hg, :], [b("ktok")], [b("prodk%d_%d" % (ci - 1, hg))])
                        dma("pool", blk_d(ci - 1, hg, 912, 1040), convT[:, QO + hg, :], [b("convT%d" % ((QO + hg) // 4))],
                            [b("prodq%d_%d" % (ci - 1, hg))])
                    if gdn:
                        s_chain(SV, hg, [b("stg")], False)
                    else:
                        s_chain(dict(ktok=ktok[:, hg, :], vdec=SV["vdec"], sm=SV["sm"]), hg, [b("stg"), b("ktok")], False)
                dma("pool", zs_s[ci - 1], zs[:], [b("zs")], [b("zs_s%d" % (ci - 1))])

            P.op("dve", lambda h: h.memset(ss[:, 3:4], 0.0), reads=[b(n) for n in PA_BUFS], writes=[b(n) for n in RB_BUFS] + [b("ss")])
            SALL = [b("S%d" % i) for i in range(8)]
            SBALL = [b("Sb%d" % i) for i in range(8)]
            rb_cnt = [0]

            def load_blk(l, hg, width):
                i = rb_cnt[0] % 3
                rb_cnt[0] += 1
                RB = b("rb%d" % i)
                deps = [b("prod%d_%d" % (l, hg))]
                if not gdn:
                    deps.append(b("prodk%d_%d" % (l, hg)))
                if width > SC:
                    deps.append(b("prodq%d_%d" % (l, hg)))
                dma("sp", rbuf[i][:, 0:width], blk_d(l, hg, 0, width), deps, [RB])
                return rbuf[i][:, 0:width], RB

            for rnd in range(3):
                dma("pool", srcS[:, :], S[:], SALL, [b("srcS")])
                P.op("pool", (lambda rnd: lambda h: h.collective_compute("AllGather", ALU.bypass, replica_groups=RGL,
                                                                        ins=[srcS[:, :]], outs=[gatS[rnd][:, :]]))(rnd),
                     reads=[b("srcS")], writes=[b("gatS%d" % rnd)])
                if rnd == 2:
                    break
                dma("sp", S[:], gatS[rnd][rnd * 128:(rnd + 1) * 128, :], [b("gatS%d" % rnd)], SALL)
                act(Sb[:], S[:], AF.Copy, SALL, SBALL)
                for l in range(NCH):
                    for hg in range(NHG):
                        blk, RB = load_blk(l, hg, SC)
                        s_chain(views(blk), hg, [RB], False, par=hg % 2)
            for pc_ in range(4):
                sl = slice(pc_ * 512, (pc_ + 1) * 512)
                SP_ = [b("S%d" % i) for i in range(8) if (i * FW) // 512 == pc_]
                for j in range(3):
                    dma("sp", tmpf[:], gatS[j][j * 128:(j + 1) * 128, sl], [b("gatS%d" % j)], [b("tmpf")])
                    if j == 0:
                        ts(S[:, sl], tmpf[:], mprev[:, 0:1], None, ALU.mult, None, [b("tmpf"), b("mprev")], SP_)
                    else:
                        stt(S[:, sl], tmpf[:], mprev[:, j:j + 1], S[:, sl], ALU.mult, ALU.add, [b("tmpf"), b("mprev")] + SP_, SP_)
            act(Sb[:], S[:], AF.Copy, SALL, SBALL)
            for ci in range(1, NCH + 1):
                l = ci - 1
                xb_ = xt[ci % 2]
                XB = b("xt%d" % (ci % 2))
                dma("sp", xb_[:], src_d[ci * 128:(ci + 1) * 128, :], [] if first_layer else [b("xmid%d" % ci)], [XB])
                dma("sp", zs[:], zs_s[l], [b("zs_s%d" % l)], [b("zs")])
                for hg in range(NHG):
                    h0 = hg * 4
                    oh, OB = (o_t[0], b("o_t0")) if hg % 2 == 0 else (o_t_b, b("o_tB"))
                    blk, RB = load_blk(l, hg, PW)
                    s_chain(views(blk), hg, [RB], True, par=hg % 2)
                    ysl = ya[:, hg * FW:(hg + 1) * FW]
                    zsl = zs[:, hg * FW:(hg + 1) * FW]
                    jk = yT[:].rearrange("p a b -> p (a b)")[:, 0:FW]
                    memset(nrm[:, 0:4], 0.0, [b("nrm")])
                    if gdn:
                        for hh in range(4):
                            act(jk[:, hh * 128:(hh + 1) * 128], oh[:, hh * 128:(hh + 1) * 128], AF.Square, [OB], [b("yT"), b("nrm")],
                                accum_out=nrm[:, hh:hh + 1])
                        act(nrm[:, 4:8], nrm[:, 0:4], AF.Sqrt, [b("nrm")], [b("nrm")], scale=1.0 / 128, bias=EPS)
                        recip(nrm[:, 4:8], nrm[:, 4:8], [b("nrm")], [b("nrm")])
                        for hh in range(4):
                            act(ysl[:, hh * 128:(hh + 1) * 128], oh[:, hh * 128:(hh + 1) * 128], AF.Copy, [OB, b("nrm")], [b("ya")],
                                scale=nrm[:, 4 + hh:5 + hh])
                        tt(G4(ysl), G4(ysl), gnw[:].unsqueeze(1).broadcast_to([128, 4, 128]), ALU.mult, [b("ya"), b("gnw")], [b("ya")])
                        tt(ysl, ysl, zsl, ALU.mult, [b("ya"), b("zs")], [b("ya")])
                    else:
                        tt(oh[:, 0:FW], oh[:, 0:FW], zsl, ALU.mult, [OB, b("zs")], [OB])
                        act(jk, oh[:, 0:FW], AF.Square, [OB], [b("yT"), b("nrm")], accum_out=nrm[:, 0:1])
                        act(nrm[:, 4:5], nrm[:, 0:1], AF.Sqrt, [b("nrm")], [b("nrm")], scale=1.0 / 256, bias=EPS)
                        recip(nrm[:, 4:5], nrm[:, 4:5], [b("nrm")], [b("nrm")])
                        act(ysl, oh[:, 0:FW], AF.Copy, [OB, b("nrm")], [b("ya")], scale=nrm[:, 4:5])
                        tt(ysl, ysl, snw[:, hg * FW:(hg + 1) * FW], ALU.mult, [b("ya"), b("snw")], [b("ya")])
                dbg_out("y", ya[:], [b("ya")], ci)
                for half in range(2):
                    pv = 1 + half
                    for i in range(8):
                        kt = half * 8 + i
                        tr(psb[pv][:, i * 128:(i + 1) * 128], ya[:, kt * 128:(kt + 1) * 128], identb[:], [b("ya"), b("identb")], [PSB[pv]])
                    act(yT[:, half * 8:(half + 1) * 8, :].rearrange("p a b -> p (a b)"), psb[pv][:, 0:1024], AF.Copy, [PSB[pv]], [b("yT")])
                wo_list = [(wo[0][:], [b("wo0")]), (wo[1][:], [b("wo1")])] + [
                    (convT[:, 8 * q:8 * q + 8, :].rearrange("p a b -> p (a b)"), [b("convT%d" % (2 * q)), b("convT%d" % (2 * q + 1))])
                    for q in range(4)]
                for kt in range(16):
                    wo_ap, WBL = wo_list[wo_cnt[0] % len(wo_list)]
                    wo_cnt[0] += 1
                    dma("sp", wo_ap, wout_s[kt], [b("wout_s")], WBL)
                    for n in range(2):
                        mm(ps[3 + n][:, :], yT[:, kt, :], wo_ap[:, n * 512:(n + 1) * 512], kt == 0, kt == 15, [b("yT")] + WBL, [PSB[3 + n]])
                for n in range(2):
                    tt(xb_[:, n * 512:(n + 1) * 512], xb_[:, n * 512:(n + 1) * 512], ps[3 + n][:, :], ALU.add, [XB, PSB[3 + n]], [XB])
                if final_norm:
                    memset(ss[:, 0:1], 0.0, [b("ss")])
                    act(hid[:], xb_[:], AF.Square, [XB], [b("hid"), b("ss")], accum_out=ss[:, 0:1])
                    act(ss[:, 1:2], ss[:, 0:1], AF.Sqrt, [b("ss")], [b("ss")], scale=1.0 / DM, bias=EPS)
                    recip(ss[:, 2:3], ss[:, 1:2], [b("ss")], [b("ss")])
                    stt(xb_[:], xb_[:], ss[:, 2:3], fnw[:], ALU.mult, ALU.mult, [XB, b("ss"), b("fnw")], [XB])
                if final_norm:
                    dma("pool", xo_d[(ci - 1) * 128:ci * 128, :], xb_[:], [XB], [b("xo%d" % ci)])
                else:
                    dma("pool", xmid_s[ci * 128:(ci + 1) * 128, :], xb_[:], [XB], [b("xmid%d" % ci)])
                    if ci == NCH:
                        dma("pool", hal_src[:, :], xb_[:], [XB], [b("hal_src")])
            if not final_norm:
                P.op("pool", lambda h: h.collective_compute("AllGather", ALU.bypass, replica_groups=RGL, ins=[hal_src[:, :]], outs=[hal_gat[:, :]]),
                     reads=[b("hal_src")], writes=[b("hal_gat")])
                for j in range(3):
                    dma("sp", xt[0][:], hal_gat[j * 128:(j + 1) * 128, :], [b("hal_gat")], [b("xt0")])
                    if j == 0:
                        ts(xt[1][:], xt[0][:], mprev[:, 0:1], None, ALU.mult, None, [b("xt0"), b("mprev")], [b("xt1")])
                    else:
                        stt(xt[1][:], xt[0][:], mprev[:, j:j + 1], xt[1][:], ALU.mult, ALU.add, [b("xt0"), b("mprev"), b("xt1")], [b("xt1")])
                dma("sp", xmid_s[0:128, :], xt[1][:], [b("xt1")], [b("xmid0")])
        P.emit()
    return nc


def conv_diag(conv_w):
    cw = np.asarray(conv_w, np.float32).reshape(4, 8, 4, 128)
    d = np.zeros((8, 128, 4, 4, 128), np.float32)
    for p in range(128):
        d[:, p, :, :, p] = np.transpose(cw[:, :, :, p], (1, 2, 0))
    return d.reshape(8, 128, 16 * 128)


def bc(v, n=128):
    v = np.asarray(v, np.float32).reshape(1, -1)
    return np.ascontiguousarray(np.broadcast_to(v, (n, v.shape[1])))


def layer_inputs(layer, x_with_halo, s_in, norm_w, w_in, conv_w, w_out, a_log, dt_bias, gdn_norm_w=None,
                 conv_b=None, d_skip=None, ssd_norm_w=None, final_norm_w=None):
    H = 16 if layer == "gdn" else 32
    m = dict(host_consts())
    m.update(x=np.ascontiguousarray(x_with_halo, dtype=np.float32), w_in=np.ascontiguousarray(w_in, dtype=np.float32),
             w_out=np.ascontiguousarray(w_out, dtype=np.float32), normw_bc=bc(norm_w), diag=conv_diag(conv_w),
             s_in=np.ascontiguousarray(s_in, dtype=np.float32),
             a_log=np.asarray(a_log, np.float32).reshape(H, 1).copy(), dt_bias=np.asarray(dt_bias, np.float32).reshape(H, 1).copy())
    if layer == "gdn":
        m["gnw_bc"] = bc(gdn_norm_w)
    else:
        m["conv_b"] = np.ascontiguousarray(np.asarray(conv_b, np.float32).reshape(32, 128).T)
        m["dskip_bc"] = bc(d_skip)
        m["snw_bc"] = bc(ssd_norm_w)
    if final_norm_w is not None:
        m["fnw_bc"] = bc(final_norm_w)
    return m


def fused_inputs(x_with_halo, p):
    m = dict(host_consts())
    m["x"] = np.ascontiguousarray(x_with_halo, dtype=np.float32)
    f32 = lambda a: np.ascontiguousarray(np.asarray(a, np.float32))
    m.update(g_w_in=f32(p["gdn_w_in"][0]), g_w_out=f32(p["gdn_w_out"][0]), g_normw_bc=bc(p["norm_w"][0]),
             g_diag=conv_diag(p["gdn_conv_w"][0]), g_a_log=f32(p["gdn_a_log"][0]).reshape(16, 1),
             g_dt_bias=f32(p["gdn_dt_bias"][0]).reshape(16, 1), g_gnw_bc=bc(p["gdn_norm_w"][0]))
    m.update(s_w_in=f32(p["ssd_w_in"][0]), s_w_out=f32(p["ssd_w_out"][0]), s_normw_bc=bc(p["norm_w"][1]),
             s_diag=conv_diag(p["ssd_conv_w"][0]), s_a_log=f32(p["ssd_a_log"][0]).reshape(32, 1),
             s_dt_bias=f32(p["ssd_dt_bias"][0]).reshape(32, 1),
             s_conv_b=np.ascontiguousarray(f32(p["ssd_conv_b"][0]).reshape(32, 128).T), s_dskip_bc=bc(p["ssd_d"][0]),
             s_snw_bc=bc(p["ssd_norm_w"][0]))
    m["fnw_bc"] = bc(p["final_norm_w"])
    return m


def par_inputs(x_with_halo, r, p):
    m = fused_inputs(x_with_halo, p)
    mp = np.zeros((128, 4), np.float32)
    if r >= 1:
        mp[:, r - 1] = 1.0
    m["mprev"] = mp
    return m


def kernel(**inputs):
    x = np.asarray(inputs["x"], np.float32)
    Bn, T, _ = x.shape
    NSEG = 4
    SEG = T // NSEG
    NCH = SEG // 128
    RG = tuple(tuple(range(bi * NSEG, (bi + 1) * NSEG)) for bi in range(Bn))
    nc = build_par(NCH, RG=RG)
    z128 = np.zeros((128, DM), np.float32)
    maps = []
    for bi in range(Bn):
        for r in range(NSEG):
            halo = z128 if r == 0 else x[bi, r * SEG - 128:r * SEG]
            maps.append(par_inputs(np.concatenate([halo, x[bi, r * SEG:(r + 1) * SEG]], axis=0), r, inputs))
    res = run_bass_kernel_spmd(nc, maps, core_ids=list(range(Bn * NSEG)))
    out = np.empty_like(x)
    for bi in range(Bn):
        for r in range(NSEG):
            out[bi, r * SEG:(r + 1) * SEG] = res.results[bi * NSEG + r]["xo"]
    return out
```

```python
import numpy as np
from contextlib import ExitStack
import concourse.bass as bass
import concourse.mybir as mybir
from concourse.bass_utils import run_bass_kernel_spmd

F32 = mybir.dt.float32
BF16 = mybir.dt.bfloat16
AF = mybir.ActivationFunctionType
ALU = mybir.AluOpType

NDMASEM = 24
PROFILE_LINES = None
PROFILE_NAMES = {}
EPS = 1e-6
C = 128
DM = 1024
INW = 6176


class Buf:
    __slots__ = ("name", "lw", "rd")

    def __init__(self, name):
        self.name = name
        self.lw = None
        self.rd = []


class Prog:
    ENGS = ("pe", "act", "dve", "pool", "sp")

    def __init__(self, nc):
        self.nc = nc
        self.ops = []
        self.ndma = 0

    def op(self, eng, fn, reads=(), writes=(), dma=False):
        idx = len(self.ops)
        deps = set()
        for b in reads:
            if b.lw is not None:
                deps.add(b.lw)
        for b in writes:
            if b.lw is not None:
                deps.add(b.lw)
            deps.update(b.rd)
        key = None if dma else eng
        for b in reads:
            if key is not None:
                b.rd = [r for r in b.rd if self.ops[r]["dma"] or self.ops[r]["eng"] != key]
            b.rd.append(idx)
        for b in writes:
            b.lw = idx
            b.rd = []
        d = dict(eng=eng, fn=fn, deps=deps, dma=dma, dmaidx=None)
        if PROFILE_LINES is not None:
            import sys as _sys
            f = _sys._getframe(1)
            while f is not None and f.f_code.co_name not in ("build_par", "build_fused", "build"):
                f = f.f_back
            PROFILE_LINES.append((eng, dma, f.f_lineno if f is not None else 0))
            d["line"] = f.f_lineno if f is not None else 0
        if dma:
            d["dmaidx"] = self.ndma
            self.ndma += 1
        self.ops.append(d)
        return idx

    def emit(self):
        nc = self.nc
        ops = self.ops
        needed = [False] * len(ops)
        for i, o in enumerate(ops):
            for d in o["deps"]:
                po = ops[d]
                if po["dma"]:
                    continue
                if po["eng"] == "pe" and o["eng"] == "pe" and not o["dma"]:
                    continue
                needed[d] = True
        with ExitStack() as es:
            esem = {e: es.enter_context(nc.semaphore("s_" + e)) for e in self.ENGS}
            dsem = [es.enter_context(nc.semaphore("d_%d" % i)) for i in range(NDMASEM)]
            cnt = {e: 0 for e in self.ENGS}
            ev = [None] * len(ops)
            dma_by_idx = {}
            for i, o in enumerate(ops):
                if o["dma"]:
                    k = o["dmaidx"]
                    ev[i] = (dsem[k % NDMASEM], 16 * (k // NDMASEM + 1))
                    dma_by_idx[k] = i
                elif needed[i]:
                    cnt[o["eng"]] += 1
                    ev[i] = (esem[o["eng"]], cnt[o["eng"]])
            per_eng = {e: [] for e in self.ENGS}
            for i, o in enumerate(ops):
                per_eng[o["eng"]].append(i)
            block = es.enter_context(nc.Block())

            def make(ename, handle_name):
                lst = per_eng[ename]
                if not lst:
                    return

                def body(h):
                    waited = {}
                    for i in lst:
                        o = ops[i]
                        evs = []
                        for d in sorted(o["deps"]):
                            po = ops[d]
                            if (not po["dma"]) and po["eng"] == "pe" and ename == "pe" and not o["dma"]:
                                continue
                            evs.append(ev[d])
                        if o["dma"] and o["dmaidx"] >= NDMASEM:
                            evs.append(ev[dma_by_idx[o["dmaidx"] - NDMASEM]])
                        for (s, v) in evs:
                            key = id(s)
                            if waited.get(key, 0) >= v:
                                continue
                            waited[key] = v
                            h.wait_ge(s, v)
                        ins = o["fn"](h)
                        if PROFILE_LINES is not None:
                            try:
                                PROFILE_NAMES[ins.ins.name] = o.get("line", 0)
                            except Exception:
                                pass
                        if ev[i] is not None:
                            s, v = ev[i]
                            ins.then_inc(s, 16 if o["dma"] else 1)
                    for i in lst:
                        o = ops[i]
                        if o["dma"]:
                            s, v = ev[i]
                            if waited.get(id(s), 0) < v:
                                waited[id(s)] = v
                                h.wait_ge(s, v)
                getattr(block, handle_name)(body)

            make("sp", "sync")
            make("pe", "tensor")
            make("act", "scalar")
            make("dve", "vector")
            make("pool", "gpsimd")


LEVELS = [1, 2, 4, 8, 16, 32, 64]


def host_consts():
    i = np.arange(128)
    ident = np.eye(128, dtype=np.float32)
    masks = np.zeros((128, 14, 128), np.float32)
    for li, l in enumerate(LEVELS):
        blk = i // (2 * l)
        half = (i // l) % 2
        M = (blk[:, None] == blk[None, :]) & (half[:, None] == 1) & (half[None, :] == 0)
        masks[:, li, :] = M
        masks[:, 7 + li, :] = M.T
    maskneg = np.where(i[None, :] >= i[:, None], 0.0, -30000.0).astype(np.float32)
    maskneg4 = np.tile(maskneg, (1, 4))
    masksu = (i[None, :] > i[:, None]).astype(np.float32)
    return dict(c_ident=ident, c_masks=masks.reshape(128, 14 * 128), c_maskneg4=maskneg4, c_masksu=masksu)


def build_par(NCH, layers=("gdn", "ssd"), dbg=None, RG=((0, 1, 2, 3), (4, 5, 6, 7))):
    nc = bass.Bass("TRN2", target_bir_lowering=False)

    def din(name, shape, dt=F32):
        return nc.dram_tensor(name, shape, dt, kind="ExternalInput").ap()

    def dout(name, shape, dt=F32):
        return nc.dram_tensor(name, shape, dt, kind="ExternalOutput").ap()

    x_d = din("x", [(NCH + 1) * 128, DM])
    ident_d = din("c_ident", [128, 128])
    masks_d = din("c_masks", [128, 14 * 128])
    maskneg_d = din("c_maskneg4", [128, 512])
    masksu_d = din("c_masksu", [128, 128])
    LD = {}
    for layer in layers:
        pf = layer[0] + "_"
        Hh = 16 if layer == "gdn" else 32
        d = dict(w_in=din(pf + "w_in", [DM, INW]), w_out=din(pf + "w_out", [2048, DM]), normw=din(pf + "normw_bc", [128, DM]),
                 diag=din(pf + "diag", [8, 128, 16 * 128]), a_log=din(pf + "a_log", [Hh, 1]), dt_bias=din(pf + "dt_bias", [Hh, 1]))
        if layer == "gdn":
            d["gnw"] = din(pf + "gnw_bc", [128, 128])
        else:
            d["convb"] = din(pf + "conv_b", [128, 32])
            d["dskip"] = din(pf + "dskip_bc", [128, 32])
            d["snw"] = din(pf + "snw_bc", [128, 2048])
        LD[layer] = d
    fnw_d = din("fnw_bc", [128, DM])
    mprev_d = din("mprev", [128, 4])
    RGL = [list(g) for g in RG]
    xo_d = dout("xo", [NCH * 128, DM])
    dbg_d = {}
    if dbg:
        for nm, (shp, dt_) in dbg.items():
            dbg_d[nm] = dout("dbg_" + nm, shp, dt_)
    diag_s = nc.dram_tensor("diag_s", [8, 128, 16 * 128], BF16, kind="Internal").ap()
    wout_s = nc.dram_tensor("wout_s", [16, 128, DM], BF16, kind="Internal").ap()
    gc_s_full = [nc.dram_tensor("gc_s%d" % i, [32, 128], F32, kind="Internal").ap() for i in range(2)]
    gl_s_full = [nc.dram_tensor("gl_s%d" % i, [32, 1], F32, kind="Internal").ap() for i in range(2)]
    xmid_s = nc.dram_tensor("xmid_s", [(NCH + 1) * 128, DM], F32, kind="Internal").ap()
    PWG, PWS = 2336, 1040
    prod_s = nc.dram_tensor("prod_s", [NCH, 128, 4 * PWG], BF16, kind="Internal").ap()
    zs_s = nc.dram_tensor("zs_s", [NCH, 128, 2048], BF16, kind="Internal").ap()
    srcS = nc.dram_tensor("srcS", [128, 2048], F32, kind="Internal").ap()
    gatS = [nc.dram_tensor("gatS%d" % i, [512, 2048], F32, kind="Internal").ap() for i in range(3)]
    hal_src = nc.dram_tensor("hal_src", [128, DM], F32, kind="Internal").ap()
    hal_gat = nc.dram_tensor("hal_gat", [512, DM], F32, kind="Internal").ap()

    P = Prog(nc)
    es = ExitStack()
    with es:
        def sb(name, shape, dt=F32):
            return es.enter_context(nc.sbuf_tensor(name, shape, dt))

        Wb = sb("Wb", [128, 8, INW], BF16)
        xt = [sb("xt%d" % i, [128, DM]) for i in range(2)]
        hid = sb("hid", [128, DM], BF16)
        hidT = sb("hidT", [128, 8, 128], BF16)
        normw = sb("normw", [128, DM])
        fnw = sb("fnw", [128, DM])
        Pbuf = [sb("Pbuf%d" % i, [128, 4, 131], BF16) for i in range(2)]
        diag = [sb("diag%d" % i, [128, 4, 128], BF16) for i in range(3)]
        convT = sb("convT", [128, 32, 128], BF16)
        zs = sb("zs", [128, 2048], BF16)
        carry = sb("carry", [128, 32, 3], BF16)
        identf = sb("identf", [128, 128])
        identb = sb("identb", [128, 128], BF16)
        onesb = sb("onesb", [128, 128], BF16)
        maskneg = sb("maskneg", [128, 512], BF16)
        onesrow = sb("onesrow", [1, 128])
        negonesrow = sb("negonesrow", [1, 128])
        onesH_f = sb("onesH", [32, 128])
        alog_f = sb("alog", [32, 1])
        negA_f = sb("negA", [32, 1])
        dtb_f = sb("dtb", [32, 1])
        ss = sb("ss", [128, 4])
        smF_f = sb("smF", [32, 6, 128])
        gcrow4 = [sb("gcrow4_%d" % i, [128, 512]) for i in range(2)]
        glrow_f = sb("glrow", [1, 32])
        smT_f = sb("smT", [128, 96])
        tokS_f = sb("tokS", [128, 4, 32])
        glbc_f = sb("glbc", [128, 32])
        vtok = sb("vtok", [128, 2048], BF16)
        S = sb("S", [128, 2048])
        Sb = sb("Sb", [128, 2048], BF16)
        o_t = [sb("o_t%d" % i, [128, 512]) for i in range(1)]
        ya = sb("ya", [128, 2048], BF16)
        yT = sb("yT", [128, 16, 128], BF16)
        wo = [sb("wo%d" % i, [128, DM], BF16) for i in range(2)]
        NWO = 2
        nrm = sb("nrm", [128, 8])
        LT = sb("LT", [128, 4, 128], BF16)
        tmpf = sb("tmpf", [128, 512])
        vnew = sb("vnew", [128, 512], BF16)
        UN = 23232 // 2
        mprev = sb("mprev_sb", [128, 4])
        U = sb("U", [128, UN], BF16)
        ps = [es.enter_context(nc.psum_tensor("ps%d" % i, [128, 512], F32)) for i in range(8)]
        psb = [p[:].bitcast(BF16) for p in ps]

        B = {}

        def b(n):
            if n not in B:
                B[n] = Buf(n)
            return B[n]

        PSB = [b("ps%d" % i) for i in range(8)]

        def dma(eng, out, in_, reads, writes):
            P.op(eng, lambda h: h.dma_start(out=out, in_=in_), reads=reads, writes=writes, dma=True)


        dma("sp", identf[:], ident_d[:, :], [], [b("identf")])
        dma("pool", identb[:], ident_d[:, :], [], [b("identb")])
        dma("pool", maskneg[:], maskneg_d[:, :], [], [b("maskneg")])
        dma("sp", fnw[:], fnw_d[:, :], [], [b("fnw")])
        dma("sp", mprev[:], mprev_d[:, :], [], [b("mprev")])
        P.op("dve", lambda h: h.memset(onesb[:], 1.0), writes=[b("onesb")])
        P.op("dve", lambda h: h.memset(onesrow[:], 1.0), writes=[b("onesrow")])
        P.op("dve", lambda h: h.memset(negonesrow[:], -1.0), writes=[b("negonesrow")])
        P.op("dve", lambda h: h.memset(onesH_f[:], 1.0), writes=[b("onesH")])
        P.op("dve", lambda h: h.memset(ss[:], 0.0), writes=[b("ss")])
        P.op("dve", lambda h: h.memset(nrm[:], 0.0), writes=[b("nrm")])

        def mm(out, lhsT, rhs, start, stop, reads, writes):
            P.op("pe", lambda h: h.matmul(out, lhsT=lhsT, rhs=rhs, start=start, stop=stop), reads=reads, writes=writes)

        def tr(out, in_, ident, reads, writes):
            P.op("pe", lambda h: h.transpose(out=out, in_=in_, identity=ident), reads=reads, writes=writes)

        def act(out, in_, func, reads, writes, **kw):
            P.op("act", lambda h: h.activation(out=out, in_=in_, func=func, **kw), reads=reads, writes=writes)

        def tt(out, in0, in1, op, reads, writes, eng="dve"):
            P.op(eng, lambda h: h.tensor_tensor(out=out, in0=in0, in1=in1, op=op), reads=reads, writes=writes)

        def ts(out, in0, s1, s2, op0, op1, reads, writes, eng="dve"):
            if op1 is None:
                P.op(eng, lambda h: h.tensor_scalar(out=out, in0=in0, scalar1=s1, scalar2=None, op0=op0), reads=reads, writes=writes)
            else:
                P.op(eng, lambda h: h.tensor_scalar(out=out, in0=in0, scalar1=s1, scalar2=s2, op0=op0, op1=op1), reads=reads, writes=writes)

        def stt(out, in0, scalar, in1, op0, op1, reads, writes):
            P.op("dve", lambda h: h.scalar_tensor_tensor(out=out, in0=in0, scalar=scalar, in1=in1, op0=op0, op1=op1),
                 reads=reads, writes=writes)

        def memset(ap, val, writes):
            P.op("dve", lambda h: h.memset(ap, val), writes=writes)

        def recip(out, in_, reads, writes):
            P.op("dve", lambda h: h.reciprocal(out=out, in_=in_), reads=reads, writes=writes)

        def cp(out, in_, reads, writes):
            P.op("dve", lambda h: h.tensor_copy(out=out, in_=in_), reads=reads, writes=writes)

        def dbg_out(name, src_ap, reads, ci):
            if dbg and name in dbg_d and ci == dbg_chunk:
                dma("sp", dbg_d[name][:, :], src_ap, reads, [b("dbgo_" + name)])

        dbg_chunk = NCH
        wo_cnt = [0]
        dg_cnt = [0]
        G4 = lambda ap: ap.rearrange("p (a b) -> p a b", a=4)


        GDN_BUFS = ["masks", "masksu", "gnw", "sq4", "ke", "kd", "LTs", "NTm", "Nn", "Tm0", "Tm1", "Ym0", "Ym1", "Zm", "Zpm",
                    "ident4", "ub", "wT"]
        SSD_BUFS = ["ktok", "vdec", "convb", "dskip", "snw"]
        wo_cnt = [0]
        dg_cnt = [0]
        G4 = lambda ap: ap.rearrange("p (a b) -> p a b", a=4)

        def carve_factory():
            off = [0]

            def carve(nbytes, dt=BF16):
                n = nbytes // 2
                ap = U[:, off[0]:off[0] + n]
                off[0] += n
                assert off[0] <= UN
                if dt == F32:
                    ap = ap.bitcast(F32)
                return ap
            return carve

        for lidx, layer in enumerate(layers):
            gdn = layer == "gdn"
            first_layer = lidx == 0
            final_norm = lidx == len(layers) - 1
            H = 16 if gdn else 32
            DV = 2048 // H
            NHG = H // 4
            FW = 4 * DV
            CO = 0 if gdn else 2048
            ZO = 4096 if gdn else 0
            SO = 6144
            QO, KO, VO = (0, 8, 16) if gdn else (24, 16, 0)
            D_ = LD[layer]
            src_d = x_d if first_layer else xmid_s
            smF = smF_f[0:H]
            onesH = onesH_f[0:H]
            alog, negA, dtb = alog_f[0:H], negA_f[0:H], dtb_f[0:H]
            glrow = glrow_f[:, 0:H]
            smT = smT_f[:, 0:3 * H]
            tokS = tokS_f[:, :, 0:H]
            glbc = glbc_f[:, 0:H]
            gc_s = [g[0:H] for g in gc_s_full]
            gl_s = [g[0:H] for g in gl_s_full]
            carve = carve_factory()
            PA_BUFS = ["masks", "masksu", "sq4", "ke", "LTs", "NTm", "Nn", "Tm0", "Tm1", "Ym0", "Ym1", "Zm", "Zpm", "ident4", "stg",
                       "ktok", "convb", "dskip", "LTa", "stgB", "LTaB", "LT_B"]
            RB_BUFS = ["rb0", "rb1", "rb2", "tmpfB", "vnewB", "o_tB"]
            if lidx > 0:
                P.op("dve", lambda h: h.memset(ss[:, 3:4], 0.0), reads=[b(n) for n in RB_BUFS + ["gnw", "snw"]],
                     writes=[b(n) for n in PA_BUFS + ["gnw", "snw"]] + [b("ss")])
            PW = PWG if gdn else PWS
            SC = 1568 if gdn else 400
            if gdn:
                gnw = carve(512, F32)
            else:
                snw = carve(4096)
            carve_rb = carve_factory()
            carve_rb(512 if gdn else 4096)
            rbuf = [carve_rb(4672) for i in range(3)]
            tmpf_b = carve_rb(2048, F32)
            vnew_b = carve_rb(1024)
            o_t_b = carve_rb(2048, F32)
            tmpf_a, vnew_a = tmpf, vnew
            if gdn:
                masks = carve(3584).rearrange("p (a b) -> p a b", a=14)
                masksu = carve(256)
                sq4 = carve(1024)
                ke = G4(carve(1024))
                LTs = G4(carve(1024))
                NTm = G4(carve(1024))
                Nn = G4(carve(1024))
                Tm = [G4(carve(1024)) for i in range(2)]
                Ym = [G4(carve(1024)) for i in range(2)]
                Zm = G4(carve(1024))
                Zpm = G4(carve(1024))
                ident4 = G4(carve(1024))
                stg = carve(4672)
            else:
                ktok = carve(2048).rearrange("p (a b) -> p a b", a=8)
                convb = carve(128, F32)
                dskip = carve(128, F32)
                stg = carve(2080)
                LTa = G4(carve(1024))
                stgB = carve(2080)
                LTaB = G4(carve(1024))
                LT_B = G4(carve(1024))
            if gdn:
                LTa = None

            def views(blk):
                w_ = blk.shape[1]
                if gdn:
                    return dict(wT=G4(blk[:, 0:512]), kd=G4(blk[:, 512:1024]), ub=blk[:, 1024:1536],
                                sm=blk[:, 1536:1560].bitcast(F32), attnT=G4(blk[:, 1568:2080]) if w_ >= 2080 else None,
                                qT=blk[:, 2080:2336].rearrange("p (a b) -> p a b", a=2) if w_ >= 2336 else None)
                return dict(ktok=blk[:, 0:128], vdec=blk[:, 128:384], sm=blk[:, 384:400].bitcast(F32),
                            o0=blk[:, 400:912].bitcast(F32) if w_ >= 912 else None, CT=blk[:, 912:1040] if w_ >= 1040 else None)

            def s_chain(v, hg, VB, full, par=0):
                h0 = hg * 4
                SB_, SBb = b("S%d" % hg), b("Sb%d" % hg)
                ps_a, ps_b, ps_c = (4, 5, 6) if par == 0 else (7, 0, 1)
                tmpf, TMB = (tmpf_a, b("tmpf")) if par == 0 else (tmpf_b, b("tmpfB"))
                vnew, VNB = (vnew_a, b("vnew")) if par == 0 else (vnew_b, b("vnewB"))
                oh, OB = (o_t[0], b("o_t0")) if par == 0 else (o_t_b, b("o_tB"))
                sm = v["sm"]
                if gdn:
                    for hh in range(4):
                        hd = h0 + hh
                        mm(ps[ps_c][:, hh * 128:(hh + 1) * 128], v["wT"][:, hh, :], Sb[:, hd * 128:(hd + 1) * 128], True, True, VB + [SBb], [PSB[ps_c]])
                    tt(G4(tmpf[:]), G4(ps[ps_c][:, :]), sm[:, 0:4].unsqueeze(2).broadcast_to([128, 4, 128]), ALU.mult,
                       [PSB[ps_c]] + VB, [TMB])
                    tt(vnew[:], tmpf[:], v["ub"], ALU.add, [TMB] + VB, [VNB])
                    if full:
                        for hh in range(4):
                            hd = h0 + hh
                            mm(ps[ps_a][:, hh * 128:(hh + 1) * 128], v["qT"][:, hh // 2, :], Sb[:, hd * 128:(hd + 1) * 128], True, True,
                               VB + [SBb], [PSB[ps_a]])
                        for hh in range(4):
                            mm(ps[ps_b][:, hh * 128:(hh + 1) * 128], v["attnT"][:, hh, :], vnew[:, hh * 128:(hh + 1) * 128], True, True,
                               VB + [VNB], [PSB[ps_b]])
                        tt(G4(tmpf[:]), G4(ps[ps_a][:, :]), sm[:, 8:12].unsqueeze(2).broadcast_to([128, 4, 128]), ALU.mult,
                           [PSB[ps_a]] + VB, [TMB])
                        tt(oh[:, 0:FW], tmpf[:, 0:FW], ps[ps_b][:, 0:FW], ALU.add, [TMB, PSB[ps_b]], [OB])
                    for hh in range(4):
                        mm(ps[ps_c][:, hh * 128:(hh + 1) * 128], v["kd"][:, hh, :], vnew[:, hh * 128:(hh + 1) * 128], True, True,
                           VB + [VNB], [PSB[ps_c]])
                    gl = sm[:, 4:8]
                else:
                    if full:
                        mm(ps[ps_a][:, 0:FW], v["CT"], Sb[:, h0 * DV:(h0 + 4) * DV], True, True, VB + [SBb], [PSB[ps_a]])
                        tt(G4(tmpf[:, 0:FW]), G4(ps[ps_a][:, 0:FW]), sm[:, 4:8].unsqueeze(2).broadcast_to([128, 4, DV]), ALU.mult,
                           [PSB[ps_a]] + VB, [TMB])
                        tt(oh[:, 0:FW], tmpf[:, 0:FW], v["o0"], ALU.add, [TMB] + VB, [OB])
                    mm(ps[ps_c][:, 0:FW], v["ktok"], v["vdec"], True, True, VB, [PSB[ps_c]])
                    gl = sm[:, 0:4]
                tt(G4(tmpf[:, 0:FW]), G4(S[:, hg * FW:(hg + 1) * FW]), gl.unsqueeze(2).broadcast_to([128, 4, DV]),
                   ALU.mult, [SB_] + VB, [TMB])
                tt(S[:, hg * FW:(hg + 1) * FW], tmpf[:, 0:FW], ps[ps_c][:, 0:FW], ALU.add, [TMB, PSB[ps_c]], [SB_])
                act(Sb[:, hg * FW:(hg + 1) * FW], S[:, hg * FW:(hg + 1) * FW], AF.Copy, [SB_], [SBb])

            def blk_d(l, hg, lo, hi):
                return prod_s[l][:, hg * PW + lo:hg * PW + hi]

            dma("sp", normw[:], D_["normw"][:, :], [], [b("normw")])
            dma("sp", alog[:], D_["a_log"][:, :], [], [b("alog")])
            dma("sp", dtb[:], D_["dt_bias"][:, :], [], [b("dtb")])
            if gdn:
                dma("pool", masks[:].rearrange("p a b -> p (a b)"), masks_d[:, :], [], [b("masks")])
                dma("pool", masksu[:], masksu_d[:, :], [], [b("masksu")])
                dma("sp", gnw[:], D_["gnw"][:, :], [], [b("gnw")])
            else:
                dma("sp", convb[:], D_["convb"][:, :], [], [b("convb")])
                dma("sp", dskip[:], D_["dskip"][:, :], [], [b("dskip")])
                dma("pool", snw[:], D_["snw"][:, :], [], [b("snw")])
            P.op("dve", lambda h: h.memset(carry[:], 0.0), writes=[b("carry")])
            P.op("dve", lambda h: h.memset(S[:], 0.0), writes=[b("S%d" % i) for i in range(8)])
            P.op("dve", lambda h: h.memset(Sb[:], 0.0), writes=[b("Sb%d" % i) for i in range(8)])
            P.op("act", (lambda negA, alog: lambda h: h.activation(out=negA[:], in_=alog[:], func=AF.Exp))(negA, alog),
                 reads=[b("alog")], writes=[b("negA")])
            P.op("dve", (lambda negA: lambda h: h.tensor_scalar(out=negA[:], in0=negA[:], scalar1=-1.0, scalar2=None, op0=ALU.mult))(negA),
                 reads=[b("negA")], writes=[b("negA")])
            if gdn:
                for i in range(4):
                    P.op("dve", (lambda i, ident4: lambda h: h.tensor_copy(out=ident4[:, i, :], in_=identb[:]))(i, ident4),
                         reads=[b("identb")], writes=[b("ident4")])
            for k in range(8):
                for (f0, f1) in [(0, 2048), (2048, 4096), (4096, INW)]:
                    dma("pool", Wb[:, k, f0:f1], D_["w_in"][k * 128:(k + 1) * 128, f0:f1], [], [b("Wb")])
            for kt2 in range(8):
                dma("pool", zs[:].rearrange("p (a c) -> p a c", a=2),
                    D_["w_out"][kt2 * 256:(kt2 + 1) * 256, :].rearrange("(a p) c -> p a c", a=2), [], [b("zs")])
                dma("sp", wout_s[kt2 * 2:(kt2 + 1) * 2].rearrange("a p c -> p a c"),
                    zs[:].rearrange("p (a c) -> p a c", a=2), [b("zs")], [b("wout_s")])
            for c4 in range(8):
                dma("pool", zs[:], D_["diag"][c4], [], [b("zs")])
                dma("sp", diag_s[c4], zs[:], [b("zs")], [b("diag_s")])
            dbg_chunk = NCH
            for ci in range(NCH + 1):
                halo = ci == 0
                xb_ = xt[ci % 2]
                XB = b("xt%d" % (ci % 2))
                dma("sp", xb_[:], src_d[ci * 128:(ci + 1) * 128, :], [] if first_layer else [b("xmid%d" % ci)], [XB])
                memset(ss[:, 0:1], 0.0, [b("ss")])
                act(hid[:], xb_[:], AF.Square, [XB], [b("hid"), b("ss")], accum_out=ss[:, 0:1])
                act(ss[:, 1:2], ss[:, 0:1], AF.Sqrt, [b("ss")], [b("ss")], scale=1.0 / DM, bias=EPS)
                recip(ss[:, 2:3], ss[:, 1:2], [b("ss")], [b("ss")])
                stt(hid[:], xb_[:], ss[:, 2:3], normw[:], ALU.mult, ALU.mult, [XB, b("ss"), b("normw")], [b("hid")])
                for k in range(8):
                    tr(psb[0][:, k * 128:(k + 1) * 128], hid[:, k * 128:(k + 1) * 128], identb[:], [b("hid"), b("identb")], [PSB[0]])
                act(hidT[:].rearrange("p a b -> p (a b)"), psb[0][:, 0:1024], AF.Copy, [PSB[0]], [b("hidT")])
                def inproj_group(c4):
                    pa = 1 + (c4 % 2)
                    for i in range(4):
                        f0 = CO + (c4 * 4 + i) * 128
                        for k in range(8):
                            mm(ps[pa][:, i * 128:(i + 1) * 128], Wb[:, k, f0:f0 + 128], hidT[:, k, :], k == 0, k == 7,
                               [b("Wb"), b("hidT")], [PSB[pa]])

                inproj_group(0)
                for c4 in range(8):
                    pa = 1 + (c4 % 2)
                    pc = 3 + (c4 % 2)
                    pbuf = Pbuf[c4 % 2]
                    PB = b("Pbuf%d" % (c4 % 2))
                    if c4 + 1 < 8:
                        inproj_group(c4 + 1)
                    cp(pbuf[:, :, 0:3], carry[:, c4 * 4:(c4 + 1) * 4, :], [b("carry")], [PB])
                    act(pbuf[:, :, 3:131], G4(ps[pa][:]), AF.Copy, [PSB[pa]], [PB])
                    cp(carry[:, c4 * 4:(c4 + 1) * 4, :], pbuf[:, :, 128:131], [PB], [b("carry")])
                    if halo:
                        continue
                    for i in range(4):
                        ct = c4 * 4 + i
                        di = dg_cnt[0] % 3
                        dg_cnt[0] += 1
                        dg = diag[di]
                        DG = b("diag%d" % di)
                        dma("sp", dg[:].rearrange("p a b -> p (a b)"), diag_s[c4][:, i * 512:(i + 1) * 512], [b("diag_s")], [DG])
                        for j in range(4):
                            mm(ps[pc][:, i * 128:(i + 1) * 128], dg[:, j, :], pbuf[:, i, j:j + 128], j == 0, j == 3, [DG, PB], [PSB[pc]])
                        if not gdn:
                            act(convT[:, ct, :], ps[pc][:, i * 128:(i + 1) * 128], AF.Silu, [PSB[pc], b("convb")], [b("convT%d" % c4)],
                                bias=convb[:, ct:ct + 1])
                    if gdn:
                        act(convT[:, c4 * 4:(c4 + 1) * 4, :], G4(ps[pc][:]), AF.Silu, [PSB[pc]], [b("convT%d" % c4)])
                if halo:
                    continue
                dbg_out("convT", convT[:].rearrange("p a b -> p (a b)"), [b("convT%d" % i) for i in range(8)], ci)
                for f in range(4):
                    pz = 5 + (f % 2)
                    for k in range(8):
                        mm(ps[pz][:, :], hidT[:, k, :], Wb[:, k, ZO + f * 512:ZO + (f + 1) * 512], k == 0, k == 7,
                           [b("Wb"), b("hidT")], [PSB[pz]])
                    act(zs[:, f * 512:(f + 1) * 512], ps[pz][:, :], AF.Silu, [PSB[pz]], [b("zs")])
                SM = b("smF")
                if gdn:
                    for k in range(8):
                        mm(ps[7][0:16, 0:128], Wb[:, k, SO:SO + 16], hidT[:, k, :], k == 0, k == 7, [b("Wb"), b("hidT")], [PSB[7]])
                    for k in range(8):
                        mm(ps[7][0:16, 128:256], Wb[:, k, SO + 16:SO + 32], hidT[:, k, :], k == 0, k == 7, [b("Wb"), b("hidT")], [PSB[7]])
                    act(smF[:, 0, :], ps[7][0:16, 0:128], AF.Sigmoid, [PSB[7]], [SM])
                    act(smF[:, 1, :], ps[7][0:16, 128:256], AF.Exp, [PSB[7], b("dtb")], [SM], bias=dtb[:, 0:1])
                else:
                    for k in range(8):
                        mm(ps[7][0:32, 0:128], Wb[:, k, SO:SO + 32], hidT[:, k, :], k == 0, k == 7, [b("Wb"), b("hidT")], [PSB[7]])
                    act(smF[:, 1, :], ps[7][0:32, 0:128], AF.Exp, [PSB[7], b("dtb")], [SM], bias=dtb[:, 0:1])
                act(smF[:, 2, :], smF[:, 1, :], AF.Ln, [SM], [SM], bias=1.0)
                if not gdn:
                    cp(smF[:, 0, :], smF[:, 2, :], [SM], [SM])
                ts(smF[:, 3, :], smF[:, 2, :], negA[:, 0:1], None, ALU.mult, None, [SM, b("negA")], [SM])
                P.op("dve", (lambda smF, onesH: lambda h: h.tensor_tensor_scan(
                    out=smF[:, 4, :], data0=onesH[:], data1=smF[:, 3, :], initial=0.0, op0=ALU.mult, op1=ALU.add))(smF, onesH),
                     reads=[SM, b("onesH")], writes=[SM])
                ts(smF[:, 5, :], smF[:, 4, :], -1.0, smF[:, 4, 127:128], ALU.mult, ALU.add, [SM], [SM])
                gcs, GCS = gc_s[ci % 2], b("gc_s%d" % (ci % 2))
                gls, GLS = gl_s[ci % 2], b("gl_s%d" % (ci % 2))
                dma("pool", gcs[:, :], smF[:, 4, :], [SM], [GCS])
                dma("pool", gls[:, :], smF[:, 4, 127:128], [SM], [GLS])
                dma("sp", glrow[0:1, :], gls.rearrange("h o -> o h"), [GLS], [b("glrow")])
                for t_, src in enumerate([0, 4, 5]):
                    tr(ps[7][:, 256 + t_ * H:256 + (t_ + 1) * H], smF[:, src, :], identf[0:H, 0:H], [SM, b("identf")], [PSB[7]])
                cp(smT[:], ps[7][:, 256:256 + 3 * H], [PSB[7]], [b("smT")])
                TS = b("tokS")
                act(tokS[:, 0, :], smT[:, H:2 * H], AF.Exp, [b("smT")], [TS])
                act(tokS[:, 1, :], smT[:, 2 * H:3 * H], AF.Exp, [b("smT")], [TS])
                if gdn:
                    ts(tokS[:, 3, :], smT[:, 0:H], -1.0, None, ALU.mult, None, [b("smT")], [TS])
                else:
                    tt(tokS[:, 2, :], smT[:, 0:H], tokS[:, 1, :], ALU.mult, [b("smT"), TS], [TS])
                mm(ps[7][:, 384:384 + H], onesrow[0:1, 0:128], glrow[0:1, :], True, True, [b("onesrow"), b("glrow")], [PSB[7]])
                act(glbc[:], ps[7][:, 384:384 + H], AF.Exp, [PSB[7]], [b("glbc")])
                dbg_out("smT", smT[:], [b("smT")], ci)
                if gdn:
                    for g4 in range(4):
                        CB = b("convT%d" % g4)
                        cv = convT[:, g4 * 4:(g4 + 1) * 4, :].rearrange("p a b -> p (a b)")
                        tt(sq4[:], cv, cv, ALU.mult, [CB], [b("sq4")])
                        mm(ps[1][:, :], onesb[:], sq4[:], True, True, [b("onesb"), b("sq4")], [PSB[1]])
                        act(tmpf[:], ps[1][:, :], AF.Sqrt, [PSB[1]], [b("tmpf")], bias=EPS)
                        recip(tmpf[:], tmpf[:], [b("tmpf")], [b("tmpf")])
                        stt(cv, cv, (128.0 ** -0.5) if g4 < 2 else 1.0, tmpf[:], ALU.mult, ALU.mult, [CB, b("tmpf")], [CB])
                dbg_out("qkn", convT[:, 0:16, :].rearrange("p a b -> p (a b)"), [b("convT%d" % i) for i in range(4)], ci)
                for half in range(2):
                    pv = 1 + half
                    for i in range(8):
                        ct = VO + half * 8 + i
                        tr(psb[pv][:, i * 128:(i + 1) * 128], convT[:, ct, :], identb[:], [b("convT%d" % (ct // 4)), b("identb")], [PSB[pv]])
                    act(vtok[:, half * 1024:(half + 1) * 1024], psb[pv][:, 0:1024], AF.Copy, [PSB[pv]], [b("vtok")])
                for i in range(8):
                    ct = KO + i
                    tr(psb[3][:, i * 128:(i + 1) * 128], convT[:, ct, :], identb[:], [b("convT%d" % (ct // 4)), b("identb")], [PSB[3]])
                if not gdn:
                    act(ktok[:].rearrange("p a b -> p (a b)"), psb[3][:, 0:1024], AF.Copy, [PSB[3]], [b("ktok")])
                for hg in range(NHG):
                    h0 = hg * 4
                    par = 0 if gdn else hg % 2
                    if par == 0:
                        stg_p, STG, LT_p, LTB, LTa_p, LAB = stg, b("stg"), LT, b("LT"), LTa, b("LTa")
                        vnew_p, VNB, tmpf_p, TMB, qb = vnew, b("vnew"), tmpf, b("tmpf"), 5
                    else:
                        stg_p, STG, LT_p, LTB, LTa_p, LAB = stgB, b("stgB"), LT_B, b("LT_B"), LTaB, b("LTaB")
                        vnew_p, VNB, tmpf_p, TMB, qb = vnew_b, b("vnewB"), tmpf_b, b("tmpfB"), 0
                    SV = views(stg_p)
                    attnT = SV["attnT"] if gdn else LTa_p
                    gr = gcrow4[hg % 2]
                    GR = b("gcrow4_%d" % (hg % 2))
                    dma("sp", gr[:, :], gcs[h0:h0 + 4, :].rearrange("(o h) c -> o (h c)", o=1).broadcast_to([128, 512]), [GCS], [GR])
                    tt(G4(gr[:, :]), G4(gr[:, :]), smT[:, H + h0:H + h0 + 4].unsqueeze(2).broadcast_to([128, 4, 128]), ALU.subtract,
                       [GR, b("smT")], [GR])
                    tt(gr[:, :], gr[:, :], maskneg[:], ALU.add, [GR, b("maskneg")], [GR])
                    act(LT_p[:].rearrange("p a b -> p (a b)"), gr[:, :], AF.Exp, [GR], [LTB])
                    if gdn:
                        kin = psb[3][:, hg * 256:(hg + 1) * 256].rearrange("p (g d) -> p g d", g=2).unsqueeze(2).broadcast_to([128, 2, 2, 128])
                        for (dst, DB, row) in [(ke, b("ke"), 0), (SV["kd"], b("stg"), 1)]:
                            tt(dst[:].rearrange("p (g r) d -> p g r d", g=2), kin,
                               tokS[:, row, h0:h0 + 4].rearrange("p (g r) -> p g r", g=2).unsqueeze(3).broadcast_to([128, 2, 2, 128]),
                               ALU.mult, [PSB[3], TS], [DB])
                        for qq in range(2):
                            g = hg * 2 + qq
                            mm(ps[5][:, qq * 128:(qq + 1) * 128], convT[:, KO + g, :], convT[:, KO + g, :], True, True,
                               [b("convT%d" % ((KO + g) // 4))], [PSB[5]])
                            mm(ps[5][:, 256 + qq * 128:256 + (qq + 1) * 128], convT[:, KO + g, :], convT[:, QO + g, :], True, True,
                               [b("convT%d" % ((KO + g) // 4)), b("convT%d" % ((QO + g) // 4))], [PSB[5]])
                        tt(LTs[:], LT[:], masksu[:].unsqueeze(1).broadcast_to([128, 4, 128]), ALU.mult, [b("LT"), b("masksu")], [b("LTs")])
                        for hh in range(4):
                            stt(NTm[:, hh, :], ps[5][:, (hh // 2) * 128:(hh // 2 + 1) * 128], tokS[:, 3, h0 + hh:h0 + hh + 1], LTs[:, hh, :],
                                ALU.mult, ALU.mult, [PSB[5], TS, b("LTs")], [b("NTm")])
                        tt(attnT[:].rearrange("p (q r) d -> p q r d", q=2),
                           ps[5][:, 256:512].rearrange("p (q d) -> p q d", q=2).unsqueeze(2).broadcast_to([128, 2, 2, 128]),
                           LT[:].rearrange("p (q r) d -> p q r d", q=2), ALU.mult, [PSB[5], b("LT")], [b("stg") if gdn else b("LTa")])
                    else:
                        g = hg
                        mm(ps[qb][:, 0:128], convT[:, KO + g, :], convT[:, QO + g, :], True, True,
                           [b("convT%d" % ((KO + g) // 4)), b("convT%d" % ((QO + g) // 4))], [PSB[qb]])
                        tt(attnT[:], ps[qb][:, 0:128].unsqueeze(1).broadcast_to([128, 4, 128]), LT_p[:], ALU.mult, [PSB[qb], LTB], [LAB])
                    if gdn:
                        for hh in range(4):
                            tr(psb[6][:, hh * 128:(hh + 1) * 128], NTm[:, hh, :], identb[:], [b("NTm"), b("identb")], [PSB[6]])
                        act(Nn[:].rearrange("p a b -> p (a b)"), psb[6][:, 0:512], AF.Copy, [PSB[6]], [b("Nn")])
                        cur = 0
                        Tc, Yc = ident4, ident4
                        TCB, YCB = b("ident4"), b("ident4")
                        for li in range(7):
                            Tn_, Yn_ = Tm[cur], Ym[cur]
                            TNB, YNB = b("Tm%d" % cur), b("Ym%d" % cur)
                            last = li == 6
                            Ml = masks[:, li, :].unsqueeze(1).broadcast_to([128, 4, 128])
                            MlT = masks[:, 7 + li, :].unsqueeze(1).broadcast_to([128, 4, 128])
                            if li == 0:
                                tt(Tn_[:], Nn[:], Ml, ALU.mult, [b("Nn"), b("masks")], [TNB])
                                tt(Tn_[:], Tn_[:], ident4[:], ALU.add, [TNB, b("ident4")], [TNB])
                                tt(Yn_[:], NTm[:], MlT, ALU.mult, [b("NTm"), b("masks")], [YNB])
                                tt(Yn_[:], Yn_[:], ident4[:], ALU.add, [YNB, b("ident4")], [YNB])
                                Tc, TCB, Yc, YCB = Tn_, TNB, Yn_, YNB
                                cur ^= 1
                                continue
                            if not last:
                                for hh in range(4):
                                    mm(ps[4][:, hh * 128:(hh + 1) * 128], NTm[:, hh, :], Tc[:, hh, :], True, True, [b("NTm"), TCB], [PSB[4]])
                                tt(Zm[:], G4(ps[4][:, :]), Ml, ALU.mult, [PSB[4], b("masks")], [b("Zm")])
                            for hh in range(4):
                                mm(ps[5][:, hh * 128:(hh + 1) * 128], Nn[:, hh, :], Yc[:, hh, :], True, True, [b("Nn"), YCB], [PSB[5]])
                            tt(Zpm[:], G4(ps[5][:, :]), MlT, ALU.mult, [PSB[5], b("masks")], [b("Zpm")])
                            if not last:
                                for hh in range(4):
                                    mm(ps[6][:, hh * 128:(hh + 1) * 128], Yc[:, hh, :], Zm[:, hh, :], True, True, [YCB, b("Zm")], [PSB[6]])
                                tt(Tn_[:], Tc[:], G4(ps[6][:, :]), ALU.add, [PSB[6], TCB], [TNB])
                            for hh in range(4):
                                mm(ps[7][:, hh * 128:(hh + 1) * 128], Tc[:, hh, :], Zpm[:, hh, :], True, True, [TCB, b("Zpm")], [PSB[7]])
                            tt(Yn_[:], Yc[:], G4(ps[7][:, :]), ALU.add, [PSB[7], YCB], [YNB])
                            if not last:
                                Tc, TCB = Tn_, TNB
                            Yc, YCB = Yn_, YNB
                            cur ^= 1
                        for hh in range(4):
                            hd = h0 + hh
                            mm(ps[4][:, hh * 128:(hh + 1) * 128], Yc[:, hh, :], vtok[:, hd * 128:(hd + 1) * 128], True, True,
                               [YCB, b("vtok")], [PSB[4]])
                        tt(G4(SV["ub"]), G4(ps[4][:, :]), smT[:, h0:h0 + 4].unsqueeze(2).broadcast_to([128, 4, 128]), ALU.mult,
                           [PSB[4], b("smT")], [b("stg")])
                        for hh in range(4):
                            mm(ps[5][:, hh * 128:(hh + 1) * 128], ke[:, hh, :], Yc[:, hh, :], True, True, [b("ke"), YCB], [PSB[5]])
                        act(SV["wT"].rearrange("p a b -> p (a b)"), ps[5][:, :], AF.Copy, [PSB[5]], [b("stg")])
                        cp(SV["sm"][:, 0:4], tokS[:, 3, h0:h0 + 4], [TS], [b("stg")])
                        cp(SV["sm"][:, 4:8], glbc[:, h0:h0 + 4], [b("glbc")], [b("stg")])
                        cp(SV["sm"][:, 8:12], tokS[:, 0, h0:h0 + 4], [TS], [b("stg")])
                        dma("pool", blk_d(ci - 1, hg, 0, 2080), stg[:, 0:2080], [b("stg")], [b("prod%d_%d" % (ci - 1, hg))])
                        dma("pool", blk_d(ci - 1, hg, 2080, 2336).rearrange("p (a b) -> p a b", a=2), convT[:, QO + 2 * hg:QO + 2 * hg + 2, :],
                            [b("convT%d" % ((QO + 2 * hg) // 4))], [b("prodq%d_%d" % (ci - 1, hg))])
                    else:
                        xin = G4(vtok[:, h0 * DV:(h0 + 4) * DV])
                        tt(G4(vnew_p[:, 0:FW]), xin, smT[:, h0:h0 + 4].unsqueeze(2).broadcast_to([128, 4, DV]),
                           ALU.mult, [b("vtok"), b("smT")], [VNB])
                        tt(G4(SV["vdec"]), xin, tokS[:, 2, h0:h0 + 4].unsqueeze(2).broadcast_to([128, 4, DV]),
                           ALU.mult, [b("vtok"), TS], [STG])
                        for hh in range(4):
                            mm(ps[qb][:, hh * DV:(hh + 1) * DV], attnT[:, hh, :], vnew_p[:, hh * DV:(hh + 1) * DV], True, True,
                               [LAB, VNB], [PSB[qb]])
                        tt(G4(tmpf_p[:, 0:FW]), xin, dskip[:, h0:h0 + 4].unsqueeze(2).broadcast_to([128, 4, DV]),
                           ALU.mult, [b("vtok"), b("dskip")], [TMB])
                        tt(SV["o0"], tmpf_p[:, 0:FW], ps[qb][:, 0:FW], ALU.add, [TMB, PSB[qb]], [STG])
                        cp(SV["sm"][:, 0:4], glbc[:, h0:h0 + 4], [b("glbc")], [STG])
                        cp(SV["sm"][:, 4:8], tokS[:, 0, h0:h0 + 4], [TS], [STG])
                        dma("pool", blk_d(ci - 1, hg, 128, 912), stg_p[:, 128:912], [STG], [b("prod%d_%d" % (ci - 1, hg))])
                        dma("pool", blk_d(ci - 1, hg, 0, 128), ktok[:, hg, :], [b("ktok")], [b("prodk%d_%d" % (ci - 1, hg))])
                        dma("pool", blk_d(ci - 1, hg, 912, 1040), convT[:, QO + hg, :], [b("convT%d" % ((QO + hg) // 4))],
                            [b("prodq%d_%d" % (ci - 1, hg))])
                    if gdn:
                        s_chain(SV, hg, [b("stg")], False)
                    else:
                        s_chain(dict(ktok=ktok[:, hg, :], vdec=SV["vdec"], sm=SV["sm"]), hg, [STG, b("ktok")], False, par=par)
                dma("pool", zs_s[ci - 1], zs[:], [b("zs")], [b("zs_s%d" % (ci - 1))])

            P.op("dve", lambda h: h.memset(ss[:, 3:4], 0.0), reads=[b(n) for n in PA_BUFS], writes=[b(n) for n in RB_BUFS] + [b("ss")])
            SALL = [b("S%d" % i) for i in range(8)]
            SBALL = [b("Sb%d" % i) for i in range(8)]
            rb_cnt = [0]

            def load_blk(l, hg, width):
                i = rb_cnt[0] % 3
                rb_cnt[0] += 1
                RB = b("rb%d" % i)
                deps = [b("prod%d_%d" % (l, hg))]
                if not gdn:
                    deps.append(b("prodk%d_%d" % (l, hg)))
                if width > SC:
                    deps.append(b("prodq%d_%d" % (l, hg)))
                dma("sp", rbuf[i][:, 0:width], blk_d(l, hg, 0, width), deps, [RB])
                return rbuf[i][:, 0:width], RB

            for rnd in range(3):
                dma("pool", srcS[:, :], S[:], SALL, [b("srcS")])
                P.op("pool", (lambda rnd: lambda h: h.collective_compute("AllGather", ALU.bypass, replica_groups=RGL,
                                                                        ins=[srcS[:, :]], outs=[gatS[rnd][:, :]]))(rnd),
                     reads=[b("srcS")], writes=[b("gatS%d" % rnd)])
                if rnd == 2:
                    break
                dma("sp", S[:], gatS[rnd][rnd * 128:(rnd + 1) * 128, :], [b("gatS%d" % rnd)], SALL)
                act(Sb[:], S[:], AF.Copy, SALL, SBALL)
                for l in range(NCH):
                    for hg in range(NHG):
                        blk, RB = load_blk(l, hg, SC)
                        s_chain(views(blk), hg, [RB], False, par=hg % 2)
            for pc_ in range(4):
                sl = slice(pc_ * 512, (pc_ + 1) * 512)
                SP_ = [b("S%d" % i) for i in range(8) if (i * FW) // 512 == pc_]
                for j in range(3):
                    dma("sp", tmpf[:], gatS[j][j * 128:(j + 1) * 128, sl], [b("gatS%d" % j)], [b("tmpf")])
                    if j == 0:
                        ts(S[:, sl], tmpf[:], mprev[:, 0:1], None, ALU.mult, None, [b("tmpf"), b("mprev")], SP_)
                    else:
                        stt(S[:, sl], tmpf[:], mprev[:, j:j + 1], S[:, sl], ALU.mult, ALU.add, [b("tmpf"), b("mprev")] + SP_, SP_)
            act(Sb[:], S[:], AF.Copy, SALL, SBALL)
            wo_list = [(wo[0][:], [b("wo0")]), (wo[1][:], [b("wo1")])] + [
                (convT[:, 8 * q:8 * q + 8, :].rearrange("p a b -> p (a b)"), [b("convT%d" % (2 * q)), b("convT%d" % (2 * q + 1))])
                for q in range(4)]

            def L_front():
                for half in range(2):
                    pv = 2 + half
                    for i in range(8):
                        kt = half * 8 + i
                        tr(psb[pv][:, i * 128:(i + 1) * 128], ya[:, kt * 128:(kt + 1) * 128], identb[:], [b("ya"), b("identb")], [PSB[pv]])
                    act(yT[:, half * 8:(half + 1) * 8, :].rearrange("p a b -> p (a b)"), psb[pv][:, 0:1024], AF.Copy, [PSB[pv]], [b("yT")])

            def L_mid(kts):
                for kt in kts:
                    wo_ap, WBL = wo_list[wo_cnt[0] % len(wo_list)]
                    wo_cnt[0] += 1
                    dma("sp", wo_ap, wout_s[kt], [b("wout_s")], WBL)
                    for n in range(2):
                        mm(ps[2 + n][:, :], yT[:, kt, :], wo_ap[:, n * 512:(n + 1) * 512], kt == 0, kt == 15, [b("yT")] + WBL, [PSB[2 + n]])

            def L_back(cj):
                xb_ = xt[cj % 2]
                XB = b("xt%d" % (cj % 2))
                for n in range(2):
                    tt(xb_[:, n * 512:(n + 1) * 512], xb_[:, n * 512:(n + 1) * 512], ps[2 + n][:, :], ALU.add, [XB, PSB[2 + n]], [XB])
                if final_norm:
                    memset(ss[:, 0:1], 0.0, [b("ss")])
                    act(hid[:], xb_[:], AF.Square, [XB], [b("hid"), b("ss")], accum_out=ss[:, 0:1])
                    act(ss[:, 1:2], ss[:, 0:1], AF.Sqrt, [b("ss")], [b("ss")], scale=1.0 / DM, bias=EPS)
                    recip(ss[:, 2:3], ss[:, 1:2], [b("ss")], [b("ss")])
                    stt(xb_[:], xb_[:], ss[:, 2:3], fnw[:], ALU.mult, ALU.mult, [XB, b("ss"), b("fnw")], [XB])
                    dma("pool", xo_d[(cj - 1) * 128:cj * 128, :], xb_[:], [XB], [b("xo%d" % cj)])
                else:
                    dma("pool", xmid_s[cj * 128:(cj + 1) * 128, :], xb_[:], [XB], [b("xmid%d" % cj)])
                    if cj == NCH:
                        dma("pool", hal_src[:, :], xb_[:], [XB], [b("hal_src")])

            KPH = 16 // NHG
            for ci in range(1, NCH + 1):
                l = ci - 1
                xb_ = xt[ci % 2]
                XB = b("xt%d" % (ci % 2))
                dma("sp", xb_[:], src_d[ci * 128:(ci + 1) * 128, :], [] if first_layer else [b("xmid%d" % ci)], [XB])
                dma("sp", zs[:], zs_s[l], [b("zs_s%d" % l)], [b("zs")])
                if ci > 1:
                    L_front()
                for hg in range(NHG):
                    h0 = hg * 4
                    oh, OB = (o_t[0], b("o_t0")) if hg % 2 == 0 else (o_t_b, b("o_tB"))
                    blk, RB = load_blk(l, hg, PW)
                    s_chain(views(blk), hg, [RB], True, par=hg % 2)
                    ysl = ya[:, hg * FW:(hg + 1) * FW]
                    zsl = zs[:, hg * FW:(hg + 1) * FW]
                    jk = hid[:, 0:FW]
                    memset(nrm[:, 0:4], 0.0, [b("nrm")])
                    if gdn:
                        for hh in range(4):
                            act(jk[:, hh * 128:(hh + 1) * 128], oh[:, hh * 128:(hh + 1) * 128], AF.Square, [OB], [b("hid"), b("nrm")],
                                accum_out=nrm[:, hh:hh + 1])
                        act(nrm[:, 4:8], nrm[:, 0:4], AF.Sqrt, [b("nrm")], [b("nrm")], scale=1.0 / 128, bias=EPS)
                        recip(nrm[:, 4:8], nrm[:, 4:8], [b("nrm")], [b("nrm")])
                        for hh in range(4):
                            act(ysl[:, hh * 128:(hh + 1) * 128], oh[:, hh * 128:(hh + 1) * 128], AF.Copy, [OB, b("nrm")], [b("ya")],
                                scale=nrm[:, 4 + hh:5 + hh])
                        tt(G4(ysl), G4(ysl), gnw[:].unsqueeze(1).broadcast_to([128, 4, 128]), ALU.mult, [b("ya"), b("gnw")], [b("ya")])
                        tt(ysl, ysl, zsl, ALU.mult, [b("ya"), b("zs")], [b("ya")])
                    else:
                        tt(oh[:, 0:FW], oh[:, 0:FW], zsl, ALU.mult, [OB, b("zs")], [OB])
                        act(jk, oh[:, 0:FW], AF.Square, [OB], [b("hid"), b("nrm")], accum_out=nrm[:, 0:1])
                        act(nrm[:, 4:5], nrm[:, 0:1], AF.Sqrt, [b("nrm")], [b("nrm")], scale=1.0 / 256, bias=EPS)
                        recip(nrm[:, 4:5], nrm[:, 4:5], [b("nrm")], [b("nrm")])
                        act(ysl, oh[:, 0:FW], AF.Copy, [OB, b("nrm")], [b("ya")], scale=nrm[:, 4:5])
                        tt(ysl, ysl, snw[:, hg * FW:(hg + 1) * FW], ALU.mult, [b("ya"), b("snw")], [b("ya")])
                    if ci > 1:
                        L_mid(range(hg * KPH, (hg + 1) * KPH))
                if ci > 1:
                    L_back(ci - 1)
            L_front()
            L_mid(range(16))
            L_back(NCH)
            if not final_norm:
                P.op("pool", lambda h: h.collective_compute("AllGather", ALU.bypass, replica_groups=RGL, ins=[hal_src[:, :]], outs=[hal_gat[:, :]]),
                     reads=[b("hal_src")], writes=[b("hal_gat")])
                for j in range(3):
                    dma("sp", xt[0][:], hal_gat[j * 128:(j + 1) * 128, :], [b("hal_gat")], [b("xt0")])
                    if j == 0:
                        ts(xt[1][:], xt[0][:], mprev[:, 0:1], None, ALU.mult, None, [b("xt0"), b("mprev")], [b("xt1")])
                    else:
                        stt(xt[1][:], xt[0][:], mprev[:, j:j + 1], xt[1][:], ALU.mult, ALU.add, [b("xt0"), b("mprev"), b("xt1")], [b("xt1")])
                dma("sp", xmid_s[0:128, :], xt[1][:], [b("xt1")], [b("xmid0")])
        P.emit()
    return nc


def conv_diag(conv_w):
    cw = np.asarray(conv_w, np.float32).reshape(4, 8, 4, 128)
    d = np.zeros((8, 128, 4, 4, 128), np.float32)
    for p in range(128):
        d[:, p, :, :, p] = np.transpose(cw[:, :, :, p], (1, 2, 0))
    return d.reshape(8, 128, 16 * 128)


def bc(v, n=128):
    v = np.asarray(v, np.float32).reshape(1, -1)
    return np.ascontiguousarray(np.broadcast_to(v, (n, v.shape[1])))


def layer_inputs(layer, x_with_halo, s_in, norm_w, w_in, conv_w, w_out, a_log, dt_bias, gdn_norm_w=None,
                 conv_b=None, d_skip=None, ssd_norm_w=None, final_norm_w=None):
    H = 16 if layer == "gdn" else 32
    m = dict(host_consts())
    m.update(x=np.ascontiguousarray(x_with_halo, dtype=np.float32), w_in=np.ascontiguousarray(w_in, dtype=np.float32),
             w_out=np.ascontiguousarray(w_out, dtype=np.float32), normw_bc=bc(norm_w), diag=conv_diag(conv_w),
             s_in=np.ascontiguousarray(s_in, dtype=np.float32),
             a_log=np.asarray(a_log, np.float32).reshape(H, 1).copy(), dt_bias=np.asarray(dt_bias, np.float32).reshape(H, 1).copy())
    if layer == "gdn":
        m["gnw_bc"] = bc(gdn_norm_w)
    else:
        m["conv_b"] = np.ascontiguousarray(np.asarray(conv_b, np.float32).reshape(32, 128).T)
        m["dskip_bc"] = bc(d_skip)
        m["snw_bc"] = bc(ssd_norm_w)
    if final_norm_w is not None:
        m["fnw_bc"] = bc(final_norm_w)
    return m


def fused_inputs(x_with_halo, p):
    m = dict(host_consts())
    m["x"] = np.ascontiguousarray(x_with_halo, dtype=np.float32)
    f32 = lambda a: np.ascontiguousarray(np.asarray(a, np.float32))
    m.update(g_w_in=f32(p["gdn_w_in"][0]), g_w_out=f32(p["gdn_w_out"][0]), g_normw_bc=bc(p["norm_w"][0]),
             g_diag=conv_diag(p["gdn_conv_w"][0]), g_a_log=f32(p["gdn_a_log"][0]).reshape(16, 1),
             g_dt_bias=f32(p["gdn_dt_bias"][0]).reshape(16, 1), g_gnw_bc=bc(p["gdn_norm_w"][0]))
    m.update(s_w_in=f32(p["ssd_w_in"][0]), s_w_out=f32(p["ssd_w_out"][0]), s_normw_bc=bc(p["norm_w"][1]),
             s_diag=conv_diag(p["ssd_conv_w"][0]), s_a_log=f32(p["ssd_a_log"][0]).reshape(32, 1),
             s_dt_bias=f32(p["ssd_dt_bias"][0]).reshape(32, 1),
             s_conv_b=np.ascontiguousarray(f32(p["ssd_conv_b"][0]).reshape(32, 128).T), s_dskip_bc=bc(p["ssd_d"][0]),
             s_snw_bc=bc(p["ssd_norm_w"][0]))
    m["fnw_bc"] = bc(p["final_norm_w"])
    return m


def par_inputs(x_with_halo, r, p):
    m = fused_inputs(x_with_halo, p)
    mp = np.zeros((128, 4), np.float32)
    if r >= 1:
        mp[:, r - 1] = 1.0
    m["mprev"] = mp
    return m


def kernel(**inputs):
    x = np.asarray(inputs["x"], np.float32)
    Bn, T, _ = x.shape
    NSEG = 4
    SEG = T // NSEG
    NCH = SEG // 128
    RG = tuple(tuple(range(bi * NSEG, (bi + 1) * NSEG)) for bi in range(Bn))
    nc = build_par(NCH, RG=RG)
    z128 = np.zeros((128, DM), np.float32)
    maps = []
    for bi in range(Bn):
        for r in range(NSEG):
            halo = z128 if r == 0 else x[bi, r * SEG - 128:r * SEG]
            maps.append(par_inputs(np.concatenate([halo, x[bi, r * SEG:(r + 1) * SEG]], axis=0), r, inputs))
    res = run_bass_kernel_spmd(nc, maps, core_ids=list(range(Bn * NSEG)))
    out = np.empty_like(x)
    for bi in range(Bn):
        for r in range(NSEG):
            out[bi, r * SEG:(r + 1) * SEG] = res.results[bi * NSEG + r]["xo"]
    return out
```

```python
import numpy as np
from contextlib import ExitStack
import concourse.bass as bass
import concourse.mybir as mybir
from concourse.bass_utils import run_bass_kernel_spmd

F32 = mybir.dt.float32
BF16 = mybir.dt.bfloat16
AF = mybir.ActivationFunctionType
ALU = mybir.AluOpType

NDMASEM = 24
PROFILE_LINES = None
PROFILE_NAMES = {}
EPS = 1e-6
C = 128
DM = 1024
INW = 6176


class Buf:
    __slots__ = ("name", "lw", "rd")

    def __init__(self, name):
        self.name = name
        self.lw = None
        self.rd = []


class Prog:
    ENGS = ("pe", "act", "dve", "pool", "sp")

    def __init__(self, nc):
        self.nc = nc
        self.ops = []
        self.ndma = 0

    def op(self, eng, fn, reads=(), writes=(), dma=False):
        idx = len(self.ops)
        deps = set()
        for b in reads:
            if b.lw is not None:
                deps.add(b.lw)
        for b in writes:
            if b.lw is not None:
                deps.add(b.lw)
            deps.update(b.rd)
        key = None if dma else eng
        for b in reads:
            if key is not None:
                b.rd = [r for r in b.rd if self.ops[r]["dma"] or self.ops[r]["eng"] != key]
            b.rd.append(idx)
        for b in writes:
            b.lw = idx
            b.rd = []
        d = dict(eng=eng, fn=fn, deps=deps, dma=dma, dmaidx=None)
        if PROFILE_LINES is not None:
            import sys as _sys
            f = _sys._getframe(1)
            while f is not None and f.f_code.co_name not in ("build_par", "build_fused", "build"):
                f = f.f_back
            PROFILE_LINES.append((eng, dma, f.f_lineno if f is not None else 0))
            d["line"] = f.f_lineno if f is not None else 0
        if dma:
            d["dmaidx"] = self.ndma
            self.ndma += 1
        self.ops.append(d)
        return idx

    def emit(self):
        nc = self.nc
        ops = self.ops
        needed = [False] * len(ops)
        for i, o in enumerate(ops):
            for d in o["deps"]:
                po = ops[d]
                if po["dma"]:
                    continue
                if po["eng"] == "pe" and o["eng"] == "pe" and not o["dma"]:
                    continue
                needed[d] = True
        with ExitStack() as es:
            esem = {e: es.enter_context(nc.semaphore("s_" + e)) for e in self.ENGS}
            dsem = [es.enter_context(nc.semaphore("d_%d" % i)) for i in range(NDMASEM)]
            cnt = {e: 0 for e in self.ENGS}
            ev = [None] * len(ops)
            dma_by_idx = {}
            for i, o in enumerate(ops):
                if o["dma"]:
                    k = o["dmaidx"]
                    ev[i] = (dsem[k % NDMASEM], 16 * (k // NDMASEM + 1))
                    dma_by_idx[k] = i
                elif needed[i]:
                    cnt[o["eng"]] += 1
                    ev[i] = (esem[o["eng"]], cnt[o["eng"]])
            per_eng = {e: [] for e in self.ENGS}
            for i, o in enumerate(ops):
                per_eng[o["eng"]].append(i)
            block = es.enter_context(nc.Block())

            def make(ename, handle_name):
                lst = per_eng[ename]
                if not lst:
                    return

                def body(h):
                    waited = {}
                    for i in lst:
                        o = ops[i]
                        evs = []
                        for d in sorted(o["deps"]):
                            po = ops[d]
                            if (not po["dma"]) and po["eng"] == "pe" and ename == "pe" and not o["dma"]:
                                continue
                            evs.append(ev[d])
                        if o["dma"] and o["dmaidx"] >= NDMASEM:
                            evs.append(ev[dma_by_idx[o["dmaidx"] - NDMASEM]])
                        for (s, v) in evs:
                            key = id(s)
                            if waited.get(key, 0) >= v:
                                continue
                            waited[key] = v
                            h.wait_ge(s, v)
                        ins = o["fn"](h)
                        if PROFILE_LINES is not None:
                            try:
                                PROFILE_NAMES[ins.ins.name] = o.get("line", 0)
                            except Exception:
                                pass
                        if ev[i] is not None:
                            s, v = ev[i]
                            ins.then_inc(s, 16 if o["dma"] else 1)
                    for i in lst:
                        o = ops[i]
                        if o["dma"]:
                            s, v = ev[i]
                            if waited.get(id(s), 0) < v:
                                waited[id(s)] = v
                                h.wait_ge(s, v)
                getattr(block, handle_name)(body)

            make("sp", "sync")
            make("pe", "tensor")
            make("act", "scalar")
            make("dve", "vector")
            make("pool", "gpsimd")


LEVELS = [1, 2, 4, 8, 16, 32, 64]


def host_consts():
    i = np.arange(128)
    ident = np.eye(128, dtype=np.float32)
    masks = np.zeros((128, 14, 128), np.float32)
    for li, l in enumerate(LEVELS):
        blk = i // (2 * l)
        half = (i // l) % 2
        M = (blk[:, None] == blk[None, :]) & (half[:, None] == 1) & (half[None, :] == 0)
        masks[:, li, :] = M
        masks[:, 7 + li, :] = M.T
    maskneg = np.where(i[None, :] >= i[:, None], 0.0, -30000.0).astype(np.float32)
    maskneg4 = np.tile(maskneg, (1, 4))
    masksu = (i[None, :] > i[:, None]).astype(np.float32)
    return dict(c_ident=ident, c_masks=masks.reshape(128, 14 * 128), c_maskneg4=maskneg4, c_masksu=masksu)


def build_par(NCH, layers=("gdn", "ssd"), dbg=None, RG=((0, 1, 2, 3), (4, 5, 6, 7))):
    nc = bass.Bass("TRN2", target_bir_lowering=False)

    def din(name, shape, dt=F32):
        return nc.dram_tensor(name, shape, dt, kind="ExternalInput").ap()

    def dout(name, shape, dt=F32):
        return nc.dram_tensor(name, shape, dt, kind="ExternalOutput").ap()

    x_d = din("x", [(NCH + 1) * 128, DM])
    ident_d = din("c_ident", [128, 128])
    masks_d = din("c_masks", [128, 14 * 128])
    maskneg_d = din("c_maskneg4", [128, 512])
    masksu_d = din("c_masksu", [128, 128])
    LD = {}
    for layer in layers:
        pf = layer[0] + "_"
        Hh = 16 if layer == "gdn" else 32
        d = dict(w_in=din(pf + "w_in", [DM, INW]), w_out=din(pf + "w_out", [2048, DM]), normw=din(pf + "normw_bc", [128, DM]),
                 diag=din(pf + "diag", [8, 128, 16 * 128]), a_log=din(pf + "a_log", [Hh, 1]), dt_bias=din(pf + "dt_bias", [Hh, 1]))
        if layer == "gdn":
            d["gnw"] = din(pf + "gnw_bc", [128, 128])
        else:
            d["convb"] = din(pf + "conv_b", [128, 32])
            d["dskip"] = din(pf + "dskip_bc", [128, 32])
            d["snw"] = din(pf + "snw_bc", [128, 2048])
        LD[layer] = d
    fnw_d = din("fnw_bc", [128, DM])
    mprev_d = din("mprev", [128, 4])
    RGL = [list(g) for g in RG]
    xo_d = dout("xo", [NCH * 128, DM])
    dbg_d = {}
    if dbg:
        for nm, (shp, dt_) in dbg.items():
            dbg_d[nm] = dout("dbg_" + nm, shp, dt_)
    diag_s = nc.dram_tensor("diag_s", [8, 128, 16 * 128], BF16, kind="Internal").ap()
    wout_s = nc.dram_tensor("wout_s", [16, 128, DM], BF16, kind="Internal").ap()
    gc_s_full = [nc.dram_tensor("gc_s%d" % i, [32, 128], F32, kind="Internal").ap() for i in range(2)]
    gl_s_full = [nc.dram_tensor("gl_s%d" % i, [32, 1], F32, kind="Internal").ap() for i in range(2)]
    xmid_s = nc.dram_tensor("xmid_s", [(NCH + 1) * 128, DM], F32, kind="Internal").ap()
    PWG, PWS = 2336, 1040
    prod_s = nc.dram_tensor("prod_s", [NCH, 128, 4 * PWG], BF16, kind="Internal").ap()
    zs_s = nc.dram_tensor("zs_s", [NCH, 128, 2048], BF16, kind="Internal").ap()
    srcS = nc.dram_tensor("srcS", [128, 2048], F32, kind="Internal").ap()
    gatS = [nc.dram_tensor("gatS%d" % i, [512, 2048], F32, kind="Internal").ap() for i in range(3)]
    hal_src = nc.dram_tensor("hal_src", [128, DM], F32, kind="Internal").ap()
    hal_gat = nc.dram_tensor("hal_gat", [512, DM], F32, kind="Internal").ap()

    P = Prog(nc)
    es = ExitStack()
    with es:
        def sb(name, shape, dt=F32):
            return es.enter_context(nc.sbuf_tensor(name, shape, dt))

        Wb = sb("Wb", [128, 8, INW], BF16)
        xt = [sb("xt%d" % i, [128, DM]) for i in range(2)]
        hid = sb("hid", [128, DM], BF16)
        hidT = sb("hidT", [128, 8, 128], BF16)
        normw = sb("normw", [128, DM])
        fnw = sb("fnw", [128, DM])
        Pbuf = [sb("Pbuf%d" % i, [128, 4, 131], BF16) for i in range(2)]
        diag = [sb("diag%d" % i, [128, 4, 128], BF16) for i in range(3)]
        convT = sb("convT", [128, 32, 128], BF16)
        zs = sb("zs", [128, 2048], BF16)
        carry = sb("carry", [128, 32, 3], BF16)
        identf = sb("identf", [128, 128])
        identb = sb("identb", [128, 128], BF16)
        onesb = sb("onesb", [128, 128], BF16)
        maskneg = sb("maskneg", [128, 512], BF16)
        onesrow = sb("onesrow", [1, 128])
        negonesrow = sb("negonesrow", [1, 128])
        onesH_f = sb("onesH", [32, 128])
        alog_f = sb("alog", [32, 1])
        negA_f = sb("negA", [32, 1])
        dtb_f = sb("dtb", [32, 1])
        ss = sb("ss", [128, 4])
        smF_f = sb("smF", [32, 6, 128])
        gcrow4 = [sb("gcrow4_%d" % i, [128, 512]) for i in range(2)]
        glrow_f = sb("glrow", [1, 32])
        smT_f = sb("smT", [128, 96])
        tokS_f = sb("tokS", [128, 4, 32])
        glbc_f = sb("glbc", [128, 32])
        vtok = sb("vtok", [128, 2048], BF16)
        S = sb("S", [128, 2048])
        Sb = sb("Sb", [128, 2048], BF16)
        o_t = [sb("o_t%d" % i, [128, 512]) for i in range(1)]
        ya = sb("ya", [128, 2048], BF16)
        yT = sb("yT", [128, 16, 128], BF16)
        wo = [sb("wo%d" % i, [128, DM], BF16) for i in range(2)]
        NWO = 2
        nrm = sb("nrm", [128, 8])
        LT = sb("LT", [128, 4, 128], BF16)
        tmpf = sb("tmpf", [128, 512])
        vnew = sb("vnew", [128, 512], BF16)
        UN = 23232 // 2
        mprev = sb("mprev_sb", [128, 4])
        U = sb("U", [128, UN], BF16)
        ps = [es.enter_context(nc.psum_tensor("ps%d" % i, [128, 512], F32)) for i in range(8)]
        psb = [p[:].bitcast(BF16) for p in ps]

        B = {}

        def b(n):
            if n not in B:
                B[n] = Buf(n)
            return B[n]

        PSB = [b("ps%d" % i) for i in range(8)]

        def dma(eng, out, in_, reads, writes):
            P.op(eng, lambda h: h.dma_start(out=out, in_=in_), reads=reads, writes=writes, dma=True)


        dma("sp", identf[:], ident_d[:, :], [], [b("identf")])
        dma("pool", identb[:], ident_d[:, :], [], [b("identb")])
        dma("pool", maskneg[:], maskneg_d[:, :], [], [b("maskneg")])
        dma("sp", fnw[:], fnw_d[:, :], [], [b("fnw")])
        dma("sp", mprev[:], mprev_d[:, :], [], [b("mprev")])
        P.op("dve", lambda h: h.memset(onesb[:], 1.0), writes=[b("onesb")])
        P.op("dve", lambda h: h.memset(onesrow[:], 1.0), writes=[b("onesrow")])
        P.op("dve", lambda h: h.memset(negonesrow[:], -1.0), writes=[b("negonesrow")])
        P.op("dve", lambda h: h.memset(onesH_f[:], 1.0), writes=[b("onesH")])
        P.op("dve", lambda h: h.memset(ss[:], 0.0), writes=[b("ss")])
        P.op("dve", lambda h: h.memset(nrm[:], 0.0), writes=[b("nrm")])

        def mm(out, lhsT, rhs, start, stop, reads, writes):
            P.op("pe", lambda h: h.matmul(out, lhsT=lhsT, rhs=rhs, start=start, stop=stop), reads=reads, writes=writes)

        def tr(out, in_, ident, reads, writes):
            P.op("pe", lambda h: h.transpose(out=out, in_=in_, identity=ident), reads=reads, writes=writes)

        def act(out, in_, func, reads, writes, **kw):
            P.op("act", lambda h: h.activation(out=out, in_=in_, func=func, **kw), reads=reads, writes=writes)

        def tt(out, in0, in1, op, reads, writes, eng="dve"):
            P.op(eng, lambda h: h.tensor_tensor(out=out, in0=in0, in1=in1, op=op), reads=reads, writes=writes)

        def ts(out, in0, s1, s2, op0, op1, reads, writes, eng="dve"):
            if op1 is None:
                P.op(eng, lambda h: h.tensor_scalar(out=out, in0=in0, scalar1=s1, scalar2=None, op0=op0), reads=reads, writes=writes)
            else:
                P.op(eng, lambda h: h.tensor_scalar(out=out, in0=in0, scalar1=s1, scalar2=s2, op0=op0, op1=op1), reads=reads, writes=writes)

        def stt(out, in0, scalar, in1, op0, op1, reads, writes):
            P.op("dve", lambda h: h.scalar_tensor_tensor(out=out, in0=in0, scalar=scalar, in1=in1, op0=op0, op1=op1),
                 reads=reads, writes=writes)

        def memset(ap, val, writes):
            P.op("dve", lambda h: h.memset(ap, val), writes=writes)

        def recip(out, in_, reads, writes):
            P.op("dve", lambda h: h.reciprocal(out=out, in_=in_), reads=reads, writes=writes)

        def cp(out, in_, reads, writes):
            P.op("dve", lambda h: h.tensor_copy(out=out, in_=in_), reads=reads, writes=writes)

        def dbg_out(name, src_ap, reads, ci):
            if dbg and name in dbg_d and ci == dbg_chunk:
                dma("sp", dbg_d[name][:, :], src_ap, reads, [b("dbgo_" + name)])

        dbg_chunk = NCH
        wo_cnt = [0]
        dg_cnt = [0]
        G4 = lambda ap: ap.rearrange("p (a b) -> p a b", a=4)


        GDN_BUFS = ["masks", "masksu", "gnw", "sq4", "ke", "kd", "LTs", "NTm", "Nn", "Tm0", "Tm1", "Ym0", "Ym1", "Zm", "Zpm",
                    "ident4", "ub", "wT"]
        SSD_BUFS = ["ktok", "vdec", "convb", "dskip", "snw"]
        wo_cnt = [0]
        dg_cnt = [0]
        G4 = lambda ap: ap.rearrange("p (a b) -> p a b", a=4)

        def carve_factory():
            off = [0]

            def carve(nbytes, dt=BF16):
                n = nbytes // 2
                ap = U[:, off[0]:off[0] + n]
                off[0] += n
                assert off[0] <= UN
                if dt == F32:
                    ap = ap.bitcast(F32)
                return ap
            return carve

        for lidx, layer in enumerate(layers):
            gdn = layer == "gdn"
            first_layer = lidx == 0
            final_norm = lidx == len(layers) - 1
            H = 16 if gdn else 32
            DV = 2048 // H
            NHG = H // 4
            FW = 4 * DV
            CO = 0 if gdn else 2048
            ZO = 4096 if gdn else 0
            SO = 6144
            QO, KO, VO = (0, 8, 16) if gdn else (24, 16, 0)
            D_ = LD[layer]
            src_d = x_d if first_layer else xmid_s
            smF = smF_f[0:H]
            onesH = onesH_f[0:H]
            alog, negA, dtb = alog_f[0:H], negA_f[0:H], dtb_f[0:H]
            glrow = glrow_f[:, 0:H]
            smT = smT_f[:, 0:3 * H]
            tokS = tokS_f[:, :, 0:H]
            glbc = glbc_f[:, 0:H]
            gc_s = [g[0:H] for g in gc_s_full]
            gl_s = [g[0:H] for g in gl_s_full]
            carve = carve_factory()
            PA_BUFS = ["masks", "masksu", "sq4", "ke", "LTs", "NTm", "Nn", "Tm0", "Tm1", "Ym0", "Ym1", "Zm", "Zpm", "ident4", "stg",
                       "ktok", "convb", "dskip", "LTa", "stgB", "LTaB", "LT_B"]
            RB_BUFS = ["rb0", "rb1", "rb2", "tmpfB", "vnewB", "o_tB"]
            if lidx > 0:
                P.op("dve", lambda h: h.memset(ss[:, 3:4], 0.0), reads=[b(n) for n in RB_BUFS + ["gnw", "snw"]],
                     writes=[b(n) for n in PA_BUFS + ["gnw", "snw"]] + [b("ss")])
            PW = PWG if gdn else PWS
            SC = 1568 if gdn else 400
            if gdn:
                gnw = carve(512, F32)
            else:
                snw = carve(4096)
            carve_rb = carve_factory()
            carve_rb(512 if gdn else 4096)
            rbuf = [carve_rb(4672) for i in range(3)]
            tmpf_b = carve_rb(2048, F32)
            vnew_b = carve_rb(1024)
            o_t_b = carve_rb(2048, F32)
            tmpf_a, vnew_a = tmpf, vnew
            if gdn:
                masks = carve(3584).rearrange("p (a b) -> p a b", a=14)
                masksu = carve(256)
                sq4 = carve(1024)
                ke = G4(carve(1024))
                LTs = G4(carve(1024))
                NTm = G4(carve(1024))
                Nn = G4(carve(1024))
                Tm = [G4(carve(1024)) for i in range(2)]
                Ym = [G4(carve(1024)) for i in range(2)]
                Zm = G4(carve(1024))
                Zpm = G4(carve(1024))
                ident4 = G4(carve(1024))
                stg = carve(4672)
            else:
                ktok = carve(2048).rearrange("p (a b) -> p a b", a=8)
                convb = carve(128, F32)
                dskip = carve(128, F32)
                stg = carve(2080)
                LTa = G4(carve(1024))
                stgB = carve(2080)
                LTaB = G4(carve(1024))
                LT_B = G4(carve(1024))
            if gdn:
                LTa = None

            def views(blk):
                w_ = blk.shape[1]
                if gdn:
                    return dict(wT=G4(blk[:, 0:512]), kd=G4(blk[:, 512:1024]), ub=blk[:, 1024:1536],
                                sm=blk[:, 1536:1560].bitcast(F32), attnT=G4(blk[:, 1568:2080]) if w_ >= 2080 else None,
                                qT=blk[:, 2080:2336].rearrange("p (a b) -> p a b", a=2) if w_ >= 2336 else None)
                return dict(ktok=blk[:, 0:128], vdec=blk[:, 128:384], sm=blk[:, 384:400].bitcast(F32),
                            o0=blk[:, 400:912].bitcast(F32) if w_ >= 912 else None, CT=blk[:, 912:1040] if w_ >= 1040 else None)

            def s_chain(v, hg, VB, full, par=0):
                h0 = hg * 4
                SB_, SBb = b("S%d" % hg), b("Sb%d" % hg)
                ps_a, ps_b, ps_c = (4, 5, 6) if par == 0 else (7, 0, 1)
                tmpf, TMB = (tmpf_a, b("tmpf")) if par == 0 else (tmpf_b, b("tmpfB"))
                vnew, VNB = (vnew_a, b("vnew")) if par == 0 else (vnew_b, b("vnewB"))
                oh, OB = (o_t[0], b("o_t0")) if par == 0 else (o_t_b, b("o_tB"))
                sm = v["sm"]
                if gdn:
                    for hh in range(4):
                        hd = h0 + hh
                        mm(ps[ps_c][:, hh * 128:(hh + 1) * 128], v["wT"][:, hh, :], Sb[:, hd * 128:(hd + 1) * 128], True, True, VB + [SBb], [PSB[ps_c]])
                    tt(G4(tmpf[:]), G4(ps[ps_c][:, :]), sm[:, 0:4].unsqueeze(2).broadcast_to([128, 4, 128]), ALU.mult,
                       [PSB[ps_c]] + VB, [TMB])
                    tt(vnew[:], tmpf[:], v["ub"], ALU.add, [TMB] + VB, [VNB])
                    if full:
                        for hh in range(4):
                            hd = h0 + hh
                            mm(ps[ps_a][:, hh * 128:(hh + 1) * 128], v["qT"][:, hh // 2, :], Sb[:, hd * 128:(hd + 1) * 128], True, True,
                               VB + [SBb], [PSB[ps_a]])
                        for hh in range(4):
                            mm(ps[ps_b][:, hh * 128:(hh + 1) * 128], v["attnT"][:, hh, :], vnew[:, hh * 128:(hh + 1) * 128], True, True,
                               VB + [VNB], [PSB[ps_b]])
                        tt(G4(tmpf[:]), G4(ps[ps_a][:, :]), sm[:, 8:12].unsqueeze(2).broadcast_to([128, 4, 128]), ALU.mult,
                           [PSB[ps_a]] + VB, [TMB])
                        tt(oh[:, 0:FW], tmpf[:, 0:FW], ps[ps_b][:, 0:FW], ALU.add, [TMB, PSB[ps_b]], [OB])
                    for hh in range(4):
                        mm(ps[ps_c][:, hh * 128:(hh + 1) * 128], v["kd"][:, hh, :], vnew[:, hh * 128:(hh + 1) * 128], True, True,
                           VB + [VNB], [PSB[ps_c]])
                    gl = sm[:, 4:8]
                else:
                    if full:
                        mm(ps[ps_a][:, 0:FW], v["CT"], Sb[:, h0 * DV:(h0 + 4) * DV], True, True, VB + [SBb], [PSB[ps_a]])
                        tt(G4(tmpf[:, 0:FW]), G4(ps[ps_a][:, 0:FW]), sm[:, 4:8].unsqueeze(2).broadcast_to([128, 4, DV]), ALU.mult,
                           [PSB[ps_a]] + VB, [TMB])
                        tt(oh[:, 0:FW], tmpf[:, 0:FW], v["o0"], ALU.add, [TMB] + VB, [OB])
                    mm(ps[ps_c][:, 0:FW], v["ktok"], v["vdec"], True, True, VB, [PSB[ps_c]])
                    gl = sm[:, 0:4]
                tt(G4(tmpf[:, 0:FW]), G4(S[:, hg * FW:(hg + 1) * FW]), gl.unsqueeze(2).broadcast_to([128, 4, DV]),
                   ALU.mult, [SB_] + VB, [TMB])
                tt(S[:, hg * FW:(hg + 1) * FW], tmpf[:, 0:FW], ps[ps_c][:, 0:FW], ALU.add, [TMB, PSB[ps_c]], [SB_])
                act(Sb[:, hg * FW:(hg + 1) * FW], S[:, hg * FW:(hg + 1) * FW], AF.Copy, [SB_], [SBb])

            def blk_d(l, hg, lo, hi):
                return prod_s[l][:, hg * PW + lo:hg * PW + hi]

            dma("sp", normw[:], D_["normw"][:, :], [], [b("normw")])
            dma("sp", alog[:], D_["a_log"][:, :], [], [b("alog")])
            dma("sp", dtb[:], D_["dt_bias"][:, :], [], [b("dtb")])
            if gdn:
                dma("pool", masks[:].rearrange("p a b -> p (a b)"), masks_d[:, :], [], [b("masks")])
                dma("pool", masksu[:], masksu_d[:, :], [], [b("masksu")])
                dma("sp", gnw[:], D_["gnw"][:, :], [], [b("gnw")])
            else:
                dma("sp", convb[:], D_["convb"][:, :], [], [b("convb")])
                dma("sp", dskip[:], D_["dskip"][:, :], [], [b("dskip")])
                dma("pool", snw[:], D_["snw"][:, :], [], [b("snw")])
            P.op("dve", lambda h: h.memset(carry[:], 0.0), writes=[b("carry")])
            P.op("dve", lambda h: h.memset(S[:], 0.0), writes=[b("S%d" % i) for i in range(8)])
            P.op("dve", lambda h: h.memset(Sb[:], 0.0), writes=[b("Sb%d" % i) for i in range(8)])
            P.op("act", (lambda negA, alog: lambda h: h.activation(out=negA[:], in_=alog[:], func=AF.Exp))(negA, alog),
                 reads=[b("alog")], writes=[b("negA")])
            P.op("dve", (lambda negA: lambda h: h.tensor_scalar(out=negA[:], in0=negA[:], scalar1=-1.0, scalar2=None, op0=ALU.mult))(negA),
                 reads=[b("negA")], writes=[b("negA")])
            if gdn:
                for i in range(4):
                    P.op("dve", (lambda i, ident4: lambda h: h.tensor_copy(out=ident4[:, i, :], in_=identb[:]))(i, ident4),
                         reads=[b("identb")], writes=[b("ident4")])
            def load_wb(lay):
                for k in range(8):
                    for (f0, f1) in [(0, 2048), (2048, 4096), (4096, INW)]:
                        dma("pool", Wb[:, k, f0:f1], LD[lay]["w_in"][k * 128:(k + 1) * 128, f0:f1], [], [b("Wb")])
            if lidx == 0:
                load_wb(layer)
            for kt2 in range(8):
                dma("pool", zs[:].rearrange("p (a c) -> p a c", a=2),
                    D_["w_out"][kt2 * 256:(kt2 + 1) * 256, :].rearrange("(a p) c -> p a c", a=2), [], [b("zs")])
                dma("sp", wout_s[kt2 * 2:(kt2 + 1) * 2].rearrange("a p c -> p a c"),
                    zs[:].rearrange("p (a c) -> p a c", a=2), [b("zs")], [b("wout_s")])
            for c4 in range(8):
                dma("pool", zs[:], D_["diag"][c4], [], [b("zs")])
                dma("sp", diag_s[c4], zs[:], [b("zs")], [b("diag_s")])
            dbg_chunk = NCH
            for ci in range(NCH + 1):
                halo = ci == 0
                xb_ = xt[ci % 2]
                XB = b("xt%d" % (ci % 2))
                dma("sp", xb_[:], src_d[ci * 128:(ci + 1) * 128, :], [] if first_layer else [b("xmid%d" % ci)], [XB])
                memset(ss[:, 0:1], 0.0, [b("ss")])
                act(hid[:], xb_[:], AF.Square, [XB], [b("hid"), b("ss")], accum_out=ss[:, 0:1])
                act(ss[:, 1:2], ss[:, 0:1], AF.Sqrt, [b("ss")], [b("ss")], scale=1.0 / DM, bias=EPS)
                recip(ss[:, 2:3], ss[:, 1:2], [b("ss")], [b("ss")])
                stt(hid[:], xb_[:], ss[:, 2:3], normw[:], ALU.mult, ALU.mult, [XB, b("ss"), b("normw")], [b("hid")])
                for k in range(8):
                    tr(psb[0][:, k * 128:(k + 1) * 128], hid[:, k * 128:(k + 1) * 128], identb[:], [b("hid"), b("identb")], [PSB[0]])
                act(hidT[:].rearrange("p a b -> p (a b)"), psb[0][:, 0:1024], AF.Copy, [PSB[0]], [b("hidT")])
                def inproj_group(c4):
                    pa = 1 + (c4 % 2)
                    for i in range(4):
                        f0 = CO + (c4 * 4 + i) * 128
                        for k in range(8):
                            mm(ps[pa][:, i * 128:(i + 1) * 128], Wb[:, k, f0:f0 + 128], hidT[:, k, :], k == 0, k == 7,
                               [b("Wb"), b("hidT")], [PSB[pa]])

                inproj_group(0)
                for c4 in range(8):
                    pa = 1 + (c4 % 2)
                    pc = 3 + (c4 % 2)
                    pbuf = Pbuf[c4 % 2]
                    PB = b("Pbuf%d" % (c4 % 2))
                    if c4 + 1 < 8:
                        inproj_group(c4 + 1)
                    cp(pbuf[:, :, 0:3], carry[:, c4 * 4:(c4 + 1) * 4, :], [b("carry")], [PB])
                    act(pbuf[:, :, 3:131], G4(ps[pa][:]), AF.Copy, [PSB[pa]], [PB])
                    cp(carry[:, c4 * 4:(c4 + 1) * 4, :], pbuf[:, :, 128:131], [PB], [b("carry")])
                    if halo:
                        continue
                    for i in range(4):
                        ct = c4 * 4 + i
                        di = dg_cnt[0] % 3
                        dg_cnt[0] += 1
                        dg = diag[di]
                        DG = b("diag%d" % di)
                        dma("sp", dg[:].rearrange("p a b -> p (a b)"), diag_s[c4][:, i * 512:(i + 1) * 512], [b("diag_s")], [DG])
                        for j in range(4):
                            mm(ps[pc][:, i * 128:(i + 1) * 128], dg[:, j, :], pbuf[:, i, j:j + 128], j == 0, j == 3, [DG, PB], [PSB[pc]])
                        if not gdn:
                            act(convT[:, ct, :], ps[pc][:, i * 128:(i + 1) * 128], AF.Silu, [PSB[pc], b("convb")], [b("convT%d" % c4)],
                                bias=convb[:, ct:ct + 1])
                    if gdn:
                        act(convT[:, c4 * 4:(c4 + 1) * 4, :], G4(ps[pc][:]), AF.Silu, [PSB[pc]], [b("convT%d" % c4)])
                if halo:
                    continue
                dbg_out("convT", convT[:].rearrange("p a b -> p (a b)"), [b("convT%d" % i) for i in range(8)], ci)
                for f in range(4):
                    pz = 5 + (f % 2)
                    if gdn:
                        CB = b("convT%d" % f)
                        cv = convT[:, f * 4:(f + 1) * 4, :].rearrange("p a b -> p (a b)")
                        tt(sq4[:], cv, cv, ALU.mult, [CB], [b("sq4")])
                    for k in range(8):
                        mm(ps[pz][:, :], hidT[:, k, :], Wb[:, k, ZO + f * 512:ZO + (f + 1) * 512], k == 0, k == 7,
                           [b("Wb"), b("hidT")], [PSB[pz]])
                    if gdn:
                        mm(ps[1][:, :], onesb[:], sq4[:], True, True, [b("onesb"), b("sq4")], [PSB[1]])
                    act(zs[:, f * 512:(f + 1) * 512], ps[pz][:, :], AF.Silu, [PSB[pz]], [b("zs")])
                    if gdn:
                        act(tmpf[:], ps[1][:, :], AF.Sqrt, [PSB[1]], [b("tmpf")], bias=EPS)
                        recip(tmpf[:], tmpf[:], [b("tmpf")], [b("tmpf")])
                        stt(cv, cv, (128.0 ** -0.5) if f < 2 else 1.0, tmpf[:], ALU.mult, ALU.mult, [CB, b("tmpf")], [CB])
                SM = b("smF")
                if gdn:
                    for k in range(8):
                        mm(ps[7][0:16, 0:128], Wb[:, k, SO:SO + 16], hidT[:, k, :], k == 0, k == 7, [b("Wb"), b("hidT")], [PSB[7]])
                    for k in range(8):
                        mm(ps[7][0:16, 128:256], Wb[:, k, SO + 16:SO + 32], hidT[:, k, :], k == 0, k == 7, [b("Wb"), b("hidT")], [PSB[7]])
                    act(smF[:, 0, :], ps[7][0:16, 0:128], AF.Sigmoid, [PSB[7]], [SM])
                    act(smF[:, 1, :], ps[7][0:16, 128:256], AF.Exp, [PSB[7], b("dtb")], [SM], bias=dtb[:, 0:1])
                else:
                    for k in range(8):
                        mm(ps[7][0:32, 0:128], Wb[:, k, SO:SO + 32], hidT[:, k, :], k == 0, k == 7, [b("Wb"), b("hidT")], [PSB[7]])
                    act(smF[:, 1, :], ps[7][0:32, 0:128], AF.Exp, [PSB[7], b("dtb")], [SM], bias=dtb[:, 0:1])
                act(smF[:, 2, :], smF[:, 1, :], AF.Ln, [SM], [SM], bias=1.0)
                if not gdn:
                    cp(smF[:, 0, :], smF[:, 2, :], [SM], [SM])
                ts(smF[:, 3, :], smF[:, 2, :], negA[:, 0:1], None, ALU.mult, None, [SM, b("negA")], [SM])
                P.op("dve", (lambda smF, onesH: lambda h: h.tensor_tensor_scan(
                    out=smF[:, 4, :], data0=onesH[:], data1=smF[:, 3, :], initial=0.0, op0=ALU.mult, op1=ALU.add))(smF, onesH),
                     reads=[SM, b("onesH")], writes=[SM])
                ts(smF[:, 5, :], smF[:, 4, :], -1.0, smF[:, 4, 127:128], ALU.mult, ALU.add, [SM], [SM])
                gcs, GCS = gc_s[ci % 2], b("gc_s%d" % (ci % 2))
                gls, GLS = gl_s[ci % 2], b("gl_s%d" % (ci % 2))
                dma("pool", gcs[:, :], smF[:, 4, :], [SM], [GCS])
                dma("pool", gls[:, :], smF[:, 4, 127:128], [SM], [GLS])
                dma("sp", glrow[0:1, :], gls.rearrange("h o -> o h"), [GLS], [b("glrow")])
                for t_, src in enumerate([0, 4, 5]):
                    tr(ps[7][:, 256 + t_ * H:256 + (t_ + 1) * H], smF[:, src, :], identf[0:H, 0:H], [SM, b("identf")], [PSB[7]])
                cp(smT[:], ps[7][:, 256:256 + 3 * H], [PSB[7]], [b("smT")])
                TS = b("tokS")
                act(tokS[:, 0, :], smT[:, H:2 * H], AF.Exp, [b("smT")], [TS])
                act(tokS[:, 1, :], smT[:, 2 * H:3 * H], AF.Exp, [b("smT")], [TS])
                if gdn:
                    ts(tokS[:, 3, :], smT[:, 0:H], -1.0, None, ALU.mult, None, [b("smT")], [TS])
                else:
                    tt(tokS[:, 2, :], smT[:, 0:H], tokS[:, 1, :], ALU.mult, [b("smT"), TS], [TS])
                mm(ps[7][:, 384:384 + H], onesrow[0:1, 0:128], glrow[0:1, :], True, True, [b("onesrow"), b("glrow")], [PSB[7]])
                act(glbc[:], ps[7][:, 384:384 + H], AF.Exp, [PSB[7]], [b("glbc")])
                dbg_out("smT", smT[:], [b("smT")], ci)
                dbg_out("qkn", convT[:, 0:16, :].rearrange("p a b -> p (a b)"), [b("convT%d" % i) for i in range(4)], ci)
                for half in range(2):
                    pv = 1 + half
                    for i in range(8):
                        ct = VO + half * 8 + i
                        tr(psb[pv][:, i * 128:(i + 1) * 128], convT[:, ct, :], identb[:], [b("convT%d" % (ct // 4)), b("identb")], [PSB[pv]])
                    act(vtok[:, half * 1024:(half + 1) * 1024], psb[pv][:, 0:1024], AF.Copy, [PSB[pv]], [b("vtok")])
                for i in range(8):
                    ct = KO + i
                    tr(psb[3][:, i * 128:(i + 1) * 128], convT[:, ct, :], identb[:], [b("convT%d" % (ct // 4)), b("identb")], [PSB[3]])
                if not gdn:
                    act(ktok[:].rearrange("p a b -> p (a b)"), psb[3][:, 0:1024], AF.Copy, [PSB[3]], [b("ktok")])
                for hg in range(NHG):
                    h0 = hg * 4
                    par = 0 if gdn else hg % 2
                    if par == 0:
                        stg_p, STG, LT_p, LTB, LTa_p, LAB = stg, b("stg"), LT, b("LT"), LTa, b("LTa")
                        vnew_p, VNB, tmpf_p, TMB, qb = vnew, b("vnew"), tmpf, b("tmpf"), 5
                    else:
                        stg_p, STG, LT_p, LTB, LTa_p, LAB = stgB, b("stgB"), LT_B, b("LT_B"), LTaB, b("LTaB")
                        vnew_p, VNB, tmpf_p, TMB, qb = vnew_b, b("vnewB"), tmpf_b, b("tmpfB"), 0
                    SV = views(stg_p)
                    attnT = SV["attnT"] if gdn else LTa_p
                    gr = gcrow4[hg % 2]
                    GR = b("gcrow4_%d" % (hg % 2))
                    dma("sp", gr[:, :], gcs[h0:h0 + 4, :].rearrange("(o h) c -> o (h c)", o=1).broadcast_to([128, 512]), [GCS], [GR])
                    tt(G4(gr[:, :]), G4(gr[:, :]), smT[:, H + h0:H + h0 + 4].unsqueeze(2).broadcast_to([128, 4, 128]), ALU.subtract,
                       [GR, b("smT")], [GR])
                    tt(gr[:, :], gr[:, :], maskneg[:], ALU.add, [GR, b("maskneg")], [GR])
                    act(LT_p[:].rearrange("p a b -> p (a b)"), gr[:, :], AF.Exp, [GR], [LTB])
                    if gdn:
                        kin = psb[3][:, hg * 256:(hg + 1) * 256].rearrange("p (g d) -> p g d", g=2).unsqueeze(2).broadcast_to([128, 2, 2, 128])
                        for (dst, DB, row) in [(ke, b("ke"), 0), (SV["kd"], b("stg"), 1)]:
                            tt(dst[:].rearrange("p (g r) d -> p g r d", g=2), kin,
                               tokS[:, row, h0:h0 + 4].rearrange("p (g r) -> p g r", g=2).unsqueeze(3).broadcast_to([128, 2, 2, 128]),
                               ALU.mult, [PSB[3], TS], [DB])
                        for qq in range(2):
                            g = hg * 2 + qq
                            mm(ps[5][:, qq * 128:(qq + 1) * 128], convT[:, KO + g, :], convT[:, KO + g, :], True, True,
                               [b("convT%d" % ((KO + g) // 4))], [PSB[5]])
                            mm(ps[5][:, 256 + qq * 128:256 + (qq + 1) * 128], convT[:, KO + g, :], convT[:, QO + g, :], True, True,
                               [b("convT%d" % ((KO + g) // 4)), b("convT%d" % ((QO + g) // 4))], [PSB[5]])
                        tt(LTs[:], LT[:], masksu[:].unsqueeze(1).broadcast_to([128, 4, 128]), ALU.mult, [b("LT"), b("masksu")], [b("LTs")])
                        for hh in range(4):
                            stt(NTm[:, hh, :], ps[5][:, (hh // 2) * 128:(hh // 2 + 1) * 128], tokS[:, 3, h0 + hh:h0 + hh + 1], LTs[:, hh, :],
                                ALU.mult, ALU.mult, [PSB[5], TS, b("LTs")], [b("NTm")])
                        tt(attnT[:].rearrange("p (q r) d -> p q r d", q=2),
                           ps[5][:, 256:512].rearrange("p (q d) -> p q d", q=2).unsqueeze(2).broadcast_to([128, 2, 2, 128]),
                           LT[:].rearrange("p (q r) d -> p q r d", q=2), ALU.mult, [PSB[5], b("LT")], [b("stg") if gdn else b("LTa")])
                    else:
                        g = hg
                        mm(ps[qb][:, 0:128], convT[:, KO + g, :], convT[:, QO + g, :], True, True,
                           [b("convT%d" % ((KO + g) // 4)), b("convT%d" % ((QO + g) // 4))], [PSB[qb]])
                        tt(attnT[:], ps[qb][:, 0:128].unsqueeze(1).broadcast_to([128, 4, 128]), LT_p[:], ALU.mult, [PSB[qb], LTB], [LAB])
                    if gdn:
                        for hh in range(4):
                            tr(psb[6][:, hh * 128:(hh + 1) * 128], NTm[:, hh, :], identb[:], [b("NTm"), b("identb")], [PSB[6]])
                        act(Nn[:].rearrange("p a b -> p (a b)"), psb[6][:, 0:512], AF.Copy, [PSB[6]], [b("Nn")])
                        cur = 0
                        Tc, Yc = ident4, ident4
                        TCB, YCB = b("ident4"), b("ident4")
                        for li in range(7):
                            Tn_, Yn_ = Tm[cur], Ym[cur]
                            TNB, YNB = b("Tm%d" % cur), b("Ym%d" % cur)
                            last = li == 6
                            Ml = masks[:, li, :].unsqueeze(1).broadcast_to([128, 4, 128])
                            MlT = masks[:, 7 + li, :].unsqueeze(1).broadcast_to([128, 4, 128])
                            if li == 0:
                                tt(Tn_[:], Nn[:], Ml, ALU.mult, [b("Nn"), b("masks")], [TNB])
                                tt(Tn_[:], Tn_[:], ident4[:], ALU.add, [TNB, b("ident4")], [TNB])
                                tt(Yn_[:], NTm[:], MlT, ALU.mult, [b("NTm"), b("masks")], [YNB])
                                tt(Yn_[:], Yn_[:], ident4[:], ALU.add, [YNB, b("ident4")], [YNB])
                                Tc, TCB, Yc, YCB = Tn_, TNB, Yn_, YNB
                                cur ^= 1
                                continue
                            if not last:
                                for hh in range(4):
                                    mm(ps[4][:, hh * 128:(hh + 1) * 128], NTm[:, hh, :], Tc[:, hh, :], True, True, [b("NTm"), TCB], [PSB[4]])
                                tt(Zm[:], G4(ps[4][:, :]), Ml, ALU.mult, [PSB[4], b("masks")], [b("Zm")])
                            for hh in range(4):
                                mm(ps[5][:, hh * 128:(hh + 1) * 128], Nn[:, hh, :], Yc[:, hh, :], True, True, [b("Nn"), YCB], [PSB[5]])
                            tt(Zpm[:], G4(ps[5][:, :]), MlT, ALU.mult, [PSB[5], b("masks")], [b("Zpm")])
                            if not last:
                                for hh in range(4):
                                    mm(ps[6][:, hh * 128:(hh + 1) * 128], Yc[:, hh, :], Zm[:, hh, :], True, True, [YCB, b("Zm")], [PSB[6]])
                                tt(Tn_[:], Tc[:], G4(ps[6][:, :]), ALU.add, [PSB[6], TCB], [TNB])
                            for hh in range(4):
                                mm(ps[7][:, hh * 128:(hh + 1) * 128], Tc[:, hh, :], Zpm[:, hh, :], True, True, [TCB, b("Zpm")], [PSB[7]])
                            tt(Yn_[:], Yc[:], G4(ps[7][:, :]), ALU.add, [PSB[7], YCB], [YNB])
                            if not last:
                                Tc, TCB = Tn_, TNB
                            Yc, YCB = Yn_, YNB
                            cur ^= 1
                        for hh in range(4):
                            hd = h0 + hh
                            mm(ps[4][:, hh * 128:(hh + 1) * 128], Yc[:, hh, :], vtok[:, hd * 128:(hd + 1) * 128], True, True,
                               [YCB, b("vtok")], [PSB[4]])
                        tt(G4(SV["ub"]), G4(ps[4][:, :]), smT[:, h0:h0 + 4].unsqueeze(2).broadcast_to([128, 4, 128]), ALU.mult,
                           [PSB[4], b("smT")], [b("stg")])
                        for hh in range(4):
                            mm(ps[5][:, hh * 128:(hh + 1) * 128], ke[:, hh, :], Yc[:, hh, :], True, True, [b("ke"), YCB], [PSB[5]])
                        act(SV["wT"].rearrange("p a b -> p (a b)"), ps[5][:, :], AF.Copy, [PSB[5]], [b("stg")])
                        cp(SV["sm"][:, 0:4], tokS[:, 3, h0:h0 + 4], [TS], [b("stg")])
                        cp(SV["sm"][:, 4:8], glbc[:, h0:h0 + 4], [b("glbc")], [b("stg")])
                        cp(SV["sm"][:, 8:12], tokS[:, 0, h0:h0 + 4], [TS], [b("stg")])
                        dma("pool", blk_d(ci - 1, hg, 0, 2080), stg[:, 0:2080], [b("stg")], [b("prod%d_%d" % (ci - 1, hg))])
                        dma("pool", blk_d(ci - 1, hg, 2080, 2336).rearrange("p (a b) -> p a b", a=2), convT[:, QO + 2 * hg:QO + 2 * hg + 2, :],
                            [b("convT%d" % ((QO + 2 * hg) // 4))], [b("prodq%d_%d" % (ci - 1, hg))])
                    else:
                        xin = G4(vtok[:, h0 * DV:(h0 + 4) * DV])
                        tt(G4(vnew_p[:, 0:FW]), xin, smT[:, h0:h0 + 4].unsqueeze(2).broadcast_to([128, 4, DV]),
                           ALU.mult, [b("vtok"), b("smT")], [VNB])
                        tt(G4(SV["vdec"]), xin, tokS[:, 2, h0:h0 + 4].unsqueeze(2).broadcast_to([128, 4, DV]),
                           ALU.mult, [b("vtok"), TS], [STG])
                        for hh in range(4):
                            mm(ps[qb][:, hh * DV:(hh + 1) * DV], attnT[:, hh, :], vnew_p[:, hh * DV:(hh + 1) * DV], True, True,
                               [LAB, VNB], [PSB[qb]])
                        tt(G4(tmpf_p[:, 0:FW]), xin, dskip[:, h0:h0 + 4].unsqueeze(2).broadcast_to([128, 4, DV]),
                           ALU.mult, [b("vtok"), b("dskip")], [TMB])
                        tt(SV["o0"], tmpf_p[:, 0:FW], ps[qb][:, 0:FW], ALU.add, [TMB, PSB[qb]], [STG])
                        cp(SV["sm"][:, 0:4], glbc[:, h0:h0 + 4], [b("glbc")], [STG])
                        cp(SV["sm"][:, 4:8], tokS[:, 0, h0:h0 + 4], [TS], [STG])
                        dma("pool", blk_d(ci - 1, hg, 128, 912), stg_p[:, 128:912], [STG], [b("prod%d_%d" % (ci - 1, hg))])
                        dma("pool", blk_d(ci - 1, hg, 0, 128), ktok[:, hg, :], [b("ktok")], [b("prodk%d_%d" % (ci - 1, hg))])
                        dma("pool", blk_d(ci - 1, hg, 912, 1040), convT[:, QO + hg, :], [b("convT%d" % ((QO + hg) // 4))],
                            [b("prodq%d_%d" % (ci - 1, hg))])
                dma("pool", zs_s[ci - 1], zs[:], [b("zs")], [b("zs_s%d" % (ci - 1))])

            P.op("dve", lambda h: h.memset(ss[:, 3:4], 0.0), reads=[b(n) for n in PA_BUFS], writes=[b(n) for n in RB_BUFS] + [b("ss")])
            SALL = [b("S%d" % i) for i in range(8)]
            SBALL = [b("Sb%d" % i) for i in range(8)]
            rb_cnt = [0]

            def load_blk(l, hg, width):
                i = rb_cnt[0] % 3
                rb_cnt[0] += 1
                RB = b("rb%d" % i)
                deps = [b("prod%d_%d" % (l, hg))]
                if not gdn:
                    deps.append(b("prodk%d_%d" % (l, hg)))
                if width > SC:
                    deps.append(b("prodq%d_%d" % (l, hg)))
                dma("sp", rbuf[i][:, 0:width], blk_d(l, hg, 0, width), deps, [RB])
                return rbuf[i][:, 0:width], RB

            for l in range(NCH):
                for hg in range(NHG):
                    blk, RB = load_blk(l, hg, SC)
                    s_chain(views(blk), hg, [RB], False, par=hg % 2)
            for rnd in range(3):
                dma("pool", srcS[:, :], S[:], SALL, [b("srcS")])
                P.op("pool", (lambda rnd: lambda h: h.collective_compute("AllGather", ALU.bypass, replica_groups=RGL,
                                                                        ins=[srcS[:, :]], outs=[gatS[rnd][:, :]]))(rnd),
                     reads=[b("srcS")], writes=[b("gatS%d" % rnd)])
                if rnd == 2:
                    break
                dma("sp", S[:], gatS[rnd][rnd * 128:(rnd + 1) * 128, :], [b("gatS%d" % rnd)], SALL)
                act(Sb[:], S[:], AF.Copy, SALL, SBALL)
                for l in range(NCH):
                    for hg in range(NHG):
                        blk, RB = load_blk(l, hg, SC)
                        s_chain(views(blk), hg, [RB], False, par=hg % 2)
            for pc_ in range(4):
                sl = slice(pc_ * 512, (pc_ + 1) * 512)
                SP_ = [b("S%d" % i) for i in range(8) if (i * FW) // 512 == pc_]
                for j in range(3):
                    dma("sp", tmpf[:], gatS[j][j * 128:(j + 1) * 128, sl], [b("gatS%d" % j)], [b("tmpf")])
                    if j == 0:
                        ts(S[:, sl], tmpf[:], mprev[:, 0:1], None, ALU.mult, None, [b("tmpf"), b("mprev")], SP_)
                    else:
                        stt(S[:, sl], tmpf[:], mprev[:, j:j + 1], S[:, sl], ALU.mult, ALU.add, [b("tmpf"), b("mprev")] + SP_, SP_)
            act(Sb[:], S[:], AF.Copy, SALL, SBALL)
            if lidx + 1 < len(layers):
                load_wb(layers[lidx + 1])
            wo_list = [(wo[0][:], [b("wo0")]), (wo[1][:], [b("wo1")])] + [
                (convT[:, 8 * q:8 * q + 8, :].rearrange("p a b -> p (a b)"), [b("convT%d" % (2 * q)), b("convT%d" % (2 * q + 1))])
                for q in range(4)]

            def L_front():
                for half in range(2):
                    pv = 2 + half
                    for i in range(8):
                        kt = half * 8 + i
                        tr(psb[pv][:, i * 128:(i + 1) * 128], ya[:, kt * 128:(kt + 1) * 128], identb[:], [b("ya"), b("identb")], [PSB[pv]])
                    act(yT[:, half * 8:(half + 1) * 8, :].rearrange("p a b -> p (a b)"), psb[pv][:, 0:1024], AF.Copy, [PSB[pv]], [b("yT")])

            def L_mid(kts):
                for kt in kts:
                    wo_ap, WBL = wo_list[wo_cnt[0] % len(wo_list)]
                    wo_cnt[0] += 1
                    dma("sp", wo_ap, wout_s[kt], [b("wout_s")], WBL)
                    for n in range(2):
                        mm(ps[2 + n][:, :], yT[:, kt, :], wo_ap[:, n * 512:(n + 1) * 512], kt == 0, kt == 15, [b("yT")] + WBL, [PSB[2 + n]])

            def L_back(cj):
                xb_ = xt[cj % 2]
                XB = b("xt%d" % (cj % 2))
                for n in range(2):
                    tt(xb_[:, n * 512:(n + 1) * 512], xb_[:, n * 512:(n + 1) * 512], ps[2 + n][:, :], ALU.add, [XB, PSB[2 + n]], [XB])
                if final_norm:
                    memset(ss[:, 0:1], 0.0, [b("ss")])
                    act(hid[:], xb_[:], AF.Square, [XB], [b("hid"), b("ss")], accum_out=ss[:, 0:1])
                    act(ss[:, 1:2], ss[:, 0:1], AF.Sqrt, [b("ss")], [b("ss")], scale=1.0 / DM, bias=EPS)
                    recip(ss[:, 2:3], ss[:, 1:2], [b("ss")], [b("ss")])
                    stt(xb_[:], xb_[:], ss[:, 2:3], fnw[:], ALU.mult, ALU.mult, [XB, b("ss"), b("fnw")], [XB])
                    dma("pool", xo_d[(cj - 1) * 128:cj * 128, :], xb_[:], [XB], [b("xo%d" % cj)])
                else:
                    dma("pool", xmid_s[cj * 128:(cj + 1) * 128, :], xb_[:], [XB], [b("xmid%d" % cj)])
                    if cj == NCH:
                        dma("pool", hal_src[:, :], xb_[:], [XB], [b("hal_src")])

            KPH = 16 // NHG
            for ci in range(1, NCH + 1):
                l = ci - 1
                xb_ = xt[ci % 2]
                XB = b("xt%d" % (ci % 2))
                dma("sp", xb_[:], src_d[ci * 128:(ci + 1) * 128, :], [] if first_layer else [b("xmid%d" % ci)], [XB])
                dma("sp", zs[:], zs_s[l], [b("zs_s%d" % l)], [b("zs")])
                if ci > 1:
                    L_front()
                for hg in range(NHG):
                    h0 = hg * 4
                    oh, OB = (o_t[0], b("o_t0")) if hg % 2 == 0 else (o_t_b, b("o_tB"))
                    blk, RB = load_blk(l, hg, PW)
                    s_chain(views(blk), hg, [RB], True, par=hg % 2)
                    ysl = ya[:, hg * FW:(hg + 1) * FW]
                    zsl = zs[:, hg * FW:(hg + 1) * FW]
                    jk = hid[:, 0:FW]
                    memset(nrm[:, 0:4], 0.0, [b("nrm")])
                    if gdn:
                        for hh in range(4):
                            act(jk[:, hh * 128:(hh + 1) * 128], oh[:, hh * 128:(hh + 1) * 128], AF.Square, [OB], [b("hid"), b("nrm")],
                                accum_out=nrm[:, hh:hh + 1])
                        act(nrm[:, 4:8], nrm[:, 0:4], AF.Sqrt, [b("nrm")], [b("nrm")], scale=1.0 / 128, bias=EPS)
                        recip(nrm[:, 4:8], nrm[:, 4:8], [b("nrm")], [b("nrm")])
                        for hh in range(4):
                            act(ysl[:, hh * 128:(hh + 1) * 128], oh[:, hh * 128:(hh + 1) * 128], AF.Copy, [OB, b("nrm")], [b("ya")],
                                scale=nrm[:, 4 + hh:5 + hh])
                        tt(G4(ysl), G4(ysl), gnw[:].unsqueeze(1).broadcast_to([128, 4, 128]), ALU.mult, [b("ya"), b("gnw")], [b("ya")])
                        tt(ysl, ysl, zsl, ALU.mult, [b("ya"), b("zs")], [b("ya")])
                    else:
                        tt(oh[:, 0:FW], oh[:, 0:FW], zsl, ALU.mult, [OB, b("zs")], [OB])
                        act(jk, oh[:, 0:FW], AF.Square, [OB], [b("hid"), b("nrm")], accum_out=nrm[:, 0:1])
                        act(nrm[:, 4:5], nrm[:, 0:1], AF.Sqrt, [b("nrm")], [b("nrm")], scale=1.0 / 256, bias=EPS)
                        recip(nrm[:, 4:5], nrm[:, 4:5], [b("nrm")], [b("nrm")])
                        act(ysl, oh[:, 0:FW], AF.Copy, [OB, b("nrm")], [b("ya")], scale=nrm[:, 4:5])
                        tt(ysl, ysl, snw[:, hg * FW:(hg + 1) * FW], ALU.mult, [b("ya"), b("snw")], [b("ya")])
                    if ci > 1:
                        L_mid(range(hg * KPH, (hg + 1) * KPH))
                if ci > 1:
                    L_back(ci - 1)
            L_front()
            L_mid(range(16))
            L_back(NCH)
            if not final_norm:
                P.op("pool", lambda h: h.collective_compute("AllGather", ALU.bypass, replica_groups=RGL, ins=[hal_src[:, :]], outs=[hal_gat[:, :]]),
                     reads=[b("hal_src")], writes=[b("hal_gat")])
                for j in range(3):
                    dma("sp", xt[0][:], hal_gat[j * 128:(j + 1) * 128, :], [b("hal_gat")], [b("xt0")])
                    if j == 0:
                        ts(xt[1][:], xt[0][:], mprev[:, 0:1], None, ALU.mult, None, [b("xt0"), b("mprev")], [b("xt1")])
                    else:
                        stt(xt[1][:], xt[0][:], mprev[:, j:j + 1], xt[1][:], ALU.mult, ALU.add, [b("xt0"), b("mprev"), b("xt1")], [b("xt1")])
                dma("sp", xmid_s[0:128, :], xt[1][:], [b("xt1")], [b("xmid0")])
        P.emit()
    return nc


def conv_diag(conv_w):
    cw = np.asarray(conv_w, np.float32).reshape(4, 8, 4, 128)
    d = np.zeros((8, 128, 4, 4, 128), np.float32)
    for p in range(128):
        d[:, p, :, :, p] = np.transpose(cw[:, :, :, p], (1, 2, 0))
    return d.reshape(8, 128, 16 * 128)


def bc(v, n=128):
    v = np.asarray(v, np.float32).reshape(1, -1)
    return np.ascontiguousarray(np.broadcast_to(v, (n, v.shape[1])))


def layer_inputs(layer, x_with_halo, s_in, norm_w, w_in, conv_w, w_out, a_log, dt_bias, gdn_norm_w=None,
                 conv_b=None, d_skip=None, ssd_norm_w=None, final_norm_w=None):
    H = 16 if layer == "gdn" else 32
    m = dict(host_consts())
    m.update(x=np.ascontiguousarray(x_with_halo, dtype=np.float32), w_in=np.ascontiguousarray(w_in, dtype=np.float32),
             w_out=np.ascontiguousarray(w_out, dtype=np.float32), normw_bc=bc(norm_w), diag=conv_diag(conv_w),
             s_in=np.ascontiguousarray(s_in, dtype=np.float32),
             a_log=np.asarray(a_log, np.float32).reshape(H, 1).copy(), dt_bias=np.asarray(dt_bias, np.float32).reshape(H, 1).copy())
    if layer == "gdn":
        m["gnw_bc"] = bc(gdn_norm_w)
    else:
        m["conv_b"] = np.ascontiguousarray(np.asarray(conv_b, np.float32).reshape(32, 128).T)
        m["dskip_bc"] = bc(d_skip)
        m["snw_bc"] = bc(ssd_norm_w)
    if final_norm_w is not None:
        m["fnw_bc"] = bc(final_norm_w)
    return m


def fused_inputs(x_with_halo, p):
    m = dict(host_consts())
    m["x"] = np.ascontiguousarray(x_with_halo, dtype=np.float32)
    f32 = lambda a: np.ascontiguousarray(np.asarray(a, np.float32))
    m.update(g_w_in=f32(p["gdn_w_in"][0]), g_w_out=f32(p["gdn_w_out"][0]), g_normw_bc=bc(p["norm_w"][0]),
             g_diag=conv_diag(p["gdn_conv_w"][0]), g_a_log=f32(p["gdn_a_log"][0]).reshape(16, 1),
             g_dt_bias=f32(p["gdn_dt_bias"][0]).reshape(16, 1), g_gnw_bc=bc(p["gdn_norm_w"][0]))
    m.update(s_w_in=f32(p["ssd_w_in"][0]), s_w_out=f32(p["ssd_w_out"][0]), s_normw_bc=bc(p["norm_w"][1]),
             s_diag=conv_diag(p["ssd_conv_w"][0]), s_a_log=f32(p["ssd_a_log"][0]).reshape(32, 1),
             s_dt_bias=f32(p["ssd_dt_bias"][0]).reshape(32, 1),
             s_conv_b=np.ascontiguousarray(f32(p["ssd_conv_b"][0]).reshape(32, 128).T), s_dskip_bc=bc(p["ssd_d"][0]),
             s_snw_bc=bc(p["ssd_norm_w"][0]))
    m["fnw_bc"] = bc(p["final_norm_w"])
    return m


def par_inputs(x_with_halo, r, p):
    m = fused_inputs(x_with_halo, p)
    mp = np.zeros((128, 4), np.float32)
    if r >= 1:
        mp[:, r - 1] = 1.0
    m["mprev"] = mp
    return m


def kernel(**inputs):
    x = np.asarray(inputs["x"], np.float32)
    Bn, T, _ = x.shape
    NSEG = 4
    SEG = T // NSEG
    NCH = SEG // 128
    RG = tuple(tuple(range(bi * NSEG, (bi + 1) * NSEG)) for bi in range(Bn))
    nc = build_par(NCH, RG=RG)
    z128 = np.zeros((128, DM), np.float32)
    maps = []
    for bi in range(Bn):
        for r in range(NSEG):
            halo = z128 if r == 0 else x[bi, r * SEG - 128:r * SEG]
            maps.append(par_inputs(np.concatenate([halo, x[bi, r * SEG:(r + 1) * SEG]], axis=0), r, inputs))
    res = run_bass_kernel_spmd(nc, maps, core_ids=list(range(Bn * NSEG)))
    out = np.empty_like(x)
    for bi in range(Bn):
        for r in range(NSEG):
            out[bi, r * SEG:(r + 1) * SEG] = res.results[bi * NSEG + r]["xo"]
    return out
```

```python
import numpy as np
from contextlib import ExitStack
import concourse.bass as bass
import concourse.mybir as mybir
from concourse.bass_utils import run_bass_kernel_spmd

F32 = mybir.dt.float32
BF16 = mybir.dt.bfloat16
AF = mybir.ActivationFunctionType
ALU = mybir.AluOpType

NDMASEM = 24
PROFILE_LINES = None
PROFILE_NAMES = {}
EPS = 1e-6
C = 128
DM = 1024
INW = 6176


class Buf:
    __slots__ = ("name", "lw", "rd")

    def __init__(self, name):
        self.name = name
        self.lw = None
        self.rd = []


class Prog:
    ENGS = ("pe", "act", "dve", "pool", "sp")

    def __init__(self, nc):
        self.nc = nc
        self.ops = []
        self.ndma = 0

    def op(self, eng, fn, reads=(), writes=(), dma=False):
        idx = len(self.ops)
        deps = set()
        for b in reads:
            if b.lw is not None:
                deps.add(b.lw)
        for b in writes:
            if b.lw is not None:
                deps.add(b.lw)
            deps.update(b.rd)
        key = None if dma else eng
        for b in reads:
            if key is not None:
                b.rd = [r for r in b.rd if self.ops[r]["dma"] or self.ops[r]["eng"] != key]
            b.rd.append(idx)
        for b in writes:
            b.lw = idx
            b.rd = []
        d = dict(eng=eng, fn=fn, deps=deps, dma=dma, dmaidx=None)
        if PROFILE_LINES is not None:
            import sys as _sys
            f = _sys._getframe(1)
            while f is not None and f.f_code.co_name not in ("build_par", "build_fused", "build"):
                f = f.f_back
            PROFILE_LINES.append((eng, dma, f.f_lineno if f is not None else 0))
            d["line"] = f.f_lineno if f is not None else 0
        if dma:
            d["dmaidx"] = self.ndma
            self.ndma += 1
        self.ops.append(d)
        return idx

    def emit(self):
        nc = self.nc
        ops = self.ops
        needed = [False] * len(ops)
        for i, o in enumerate(ops):
            for d in o["deps"]:
                po = ops[d]
                if po["dma"]:
                    continue
                if po["eng"] == "pe" and o["eng"] == "pe" and not o["dma"]:
                    continue
                needed[d] = True
        with ExitStack() as es:
            esem = {e: es.enter_context(nc.semaphore("s_" + e)) for e in self.ENGS}
            dsem = [es.enter_context(nc.semaphore("d_%d" % i)) for i in range(NDMASEM)]
            cnt = {e: 0 for e in self.ENGS}
            ev = [None] * len(ops)
            dma_by_idx = {}
            for i, o in enumerate(ops):
                if o["dma"]:
                    k = o["dmaidx"]
                    ev[i] = (dsem[k % NDMASEM], 16 * (k // NDMASEM + 1))
                    dma_by_idx[k] = i
                elif needed[i]:
                    cnt[o["eng"]] += 1
                    ev[i] = (esem[o["eng"]], cnt[o["eng"]])
            per_eng = {e: [] for e in self.ENGS}
            for i, o in enumerate(ops):
                per_eng[o["eng"]].append(i)
            block = es.enter_context(nc.Block())

            def make(ename, handle_name):
                lst = per_eng[ename]
                if not lst:
                    return

                def body(h):
                    waited = {}
                    for i in lst:
                        o = ops[i]
                        evs = []
                        for d in sorted(o["deps"]):
                            po = ops[d]
                            if (not po["dma"]) and po["eng"] == "pe" and ename == "pe" and not o["dma"]:
                                continue
                            evs.append(ev[d])
                        if o["dma"] and o["dmaidx"] >= NDMASEM:
                            evs.append(ev[dma_by_idx[o["dmaidx"] - NDMASEM]])
                        for (s, v) in evs:
                            key = id(s)
                            if waited.get(key, 0) >= v:
                                continue
                            waited[key] = v
                            h.wait_ge(s, v)
                        ins = o["fn"](h)
                        if PROFILE_LINES is not None:
                            try:
                                PROFILE_NAMES[ins.ins.name] = o.get("line", 0)
                            except Exception:
                                pass
                        if ev[i] is not None:
                            s, v = ev[i]
                            ins.then_inc(s, 16 if o["dma"] else 1)
                    for i in lst:
                        o = ops[i]
                        if o["dma"]:
                            s, v = ev[i]
                            if waited.get(id(s), 0) < v:
                                waited[id(s)] = v
                                h.wait_ge(s, v)
                getattr(block, handle_name)(body)

            make("sp", "sync")
            make("pe", "tensor")
            make("act", "scalar")
            make("dve", "vector")
            make("pool", "gpsimd")


LEVELS = [1, 2, 4, 8, 16, 32, 64]


def host_consts():
    i = np.arange(128)
    ident = np.eye(128, dtype=np.float32)
    masks = np.zeros((128, 14, 128), np.float32)
    for li, l in enumerate(LEVELS):
        blk = i // (2 * l)
        half = (i // l) % 2
        M = (blk[:, None] == blk[None, :]) & (half[:, None] == 1) & (half[None, :] == 0)
        masks[:, li, :] = M
        masks[:, 7 + li, :] = M.T
    maskneg = np.where(i[None, :] >= i[:, None], 0.0, -30000.0).astype(np.float32)
    maskneg4 = np.tile(maskneg, (1, 4))
    masksu = (i[None, :] > i[:, None]).astype(np.float32)
    return dict(c_ident=ident, c_masks=masks.reshape(128, 14 * 128), c_maskneg4=maskneg4, c_masksu=masksu)


def build_par(NCH, layers=("gdn", "ssd"), dbg=None, RG=((0, 1, 2, 3), (4, 5, 6, 7))):
    nc = bass.Bass("TRN2", target_bir_lowering=False)

    def din(name, shape, dt=F32):
        return nc.dram_tensor(name, shape, dt, kind="ExternalInput").ap()

    def dout(name, shape, dt=F32):
        return nc.dram_tensor(name, shape, dt, kind="ExternalOutput").ap()

    x_d = din("x", [(NCH + 1) * 128, DM])
    ident_d = din("c_ident", [128, 128])
    masks_d = din("c_masks", [128, 14 * 128])
    maskneg_d = din("c_maskneg4", [128, 512])
    masksu_d = din("c_masksu", [128, 128])
    LD = {}
    for layer in layers:
        pf = layer[0] + "_"
        Hh = 16 if layer == "gdn" else 32
        d = dict(w_in=din(pf + "w_in", [DM, INW]), w_out=din(pf + "w_out", [2048, DM]), normw=din(pf + "normw_bc", [128, DM]),
                 diag=din(pf + "diag", [8, 128, 16 * 128]), a_log=din(pf + "a_log", [Hh, 1]), dt_bias=din(pf + "dt_bias", [Hh, 1]))
        if layer == "gdn":
            d["gnw"] = din(pf + "gnw_bc", [128, 128])
        else:
            d["convb"] = din(pf + "conv_b", [128, 32])
            d["dskip"] = din(pf + "dskip_bc", [128, 32])
            d["snw"] = din(pf + "snw_bc", [128, 2048])
        LD[layer] = d
    fnw_d = din("fnw_bc", [128, DM])
    mprev_d = din("mprev", [128, 4])
    RGL = [list(g) for g in RG]
    xo_d = dout("xo", [NCH * 128, DM])
    dbg_d = {}
    if dbg:
        for nm, (shp, dt_) in dbg.items():
            dbg_d[nm] = dout("dbg_" + nm, shp, dt_)
    diag_s = nc.dram_tensor("diag_s", [8, 128, 16 * 128], BF16, kind="Internal").ap()
    wout_s = nc.dram_tensor("wout_s", [16, 128, DM], BF16, kind="Internal").ap()
    gc_s_full = [nc.dram_tensor("gc_s%d" % i, [32, 128], F32, kind="Internal").ap() for i in range(2)]
    gl_s_full = [nc.dram_tensor("gl_s%d" % i, [32, 1], F32, kind="Internal").ap() for i in range(2)]
    xmid_s = nc.dram_tensor("xmid_s", [(NCH + 1) * 128, DM], F32, kind="Internal").ap()
    PWG, PWS = 2336, 1040
    prod_s = nc.dram_tensor("prod_s", [NCH, 128, 4 * PWG], BF16, kind="Internal").ap()
    zs_s = nc.dram_tensor("zs_s", [NCH, 128, 2048], BF16, kind="Internal").ap()
    srcS = nc.dram_tensor("srcS", [128, 2048], F32, kind="Internal").ap()
    gatS = [nc.dram_tensor("gatS%d" % i, [512, 2048], F32, kind="Internal").ap() for i in range(3)]
    hal_src = nc.dram_tensor("hal_src", [128, DM], F32, kind="Internal").ap()
    hal_gat = nc.dram_tensor("hal_gat", [512, DM], F32, kind="Internal").ap()

    P = Prog(nc)
    es = ExitStack()
    with es:
        def sb(name, shape, dt=F32):
            return es.enter_context(nc.sbuf_tensor(name, shape, dt))

        Wb = sb("Wb", [128, 8, INW], BF16)
        xt = [sb("xt%d" % i, [128, DM]) for i in range(2)]
        hid = sb("hid", [128, DM], BF16)
        hidT = sb("hidT", [128, 8, 128], BF16)
        normw = sb("normw", [128, DM])
        fnw = sb("fnw", [128, DM])
        Pbuf = [sb("Pbuf%d" % i, [128, 4, 131], BF16) for i in range(2)]
        NDG = 6
        diag = [sb("diag%d" % i, [128, 4, 128], BF16) for i in range(NDG)]
        convT = sb("convT", [128, 32, 128], BF16)
        zs = sb("zs", [128, 2048], BF16)
        carry = sb("carry", [128, 32, 3], BF16)
        identf = sb("identf", [128, 128])
        identb = sb("identb", [128, 128], BF16)
        onesb = sb("onesb", [128, 128], BF16)
        maskneg = sb("maskneg", [128, 512], BF16)
        onesrow = sb("onesrow", [1, 128])
        negonesrow = sb("negonesrow", [1, 128])
        onesH_f = sb("onesH", [32, 128])
        alog_f = sb("alog", [32, 1])
        negA_f = sb("negA", [32, 1])
        dtb_f = sb("dtb", [32, 1])
        ss = sb("ss", [128, 4])
        smF_f = sb("smF", [32, 6, 128])
        gcrow4 = [sb("gcrow4_%d" % i, [128, 512]) for i in range(2)]
        glrow_f = sb("glrow", [1, 32])
        smT_f = sb("smT", [128, 96])
        tokS_f = sb("tokS", [128, 4, 32])
        glbc_f = sb("glbc", [128, 32])
        vtok = sb("vtok", [128, 2048], BF16)
        S = sb("S", [128, 2048])
        Sb = sb("Sb", [128, 2048], BF16)
        o_t = [sb("o_t%d" % i, [128, 512]) for i in range(1)]
        ya = sb("ya", [128, 2048], BF16)
        yT = sb("yT", [128, 16, 128], BF16)
        nrm = sb("nrm", [128, 8])
        LT = sb("LT", [128, 4, 128], BF16)
        tmpf = sb("tmpf", [128, 512])
        vnew = sb("vnew", [128, 512], BF16)
        UN = 23232 // 2
        mprev = sb("mprev_sb", [128, 4])
        U = sb("U", [128, UN], BF16)
        ps = [es.enter_context(nc.psum_tensor("ps%d" % i, [128, 512], F32)) for i in range(8)]
        psb = [p[:].bitcast(BF16) for p in ps]

        B = {}

        def b(n):
            if n not in B:
                B[n] = Buf(n)
            return B[n]

        PSB = [b("ps%d" % i) for i in range(8)]

        def dma(eng, out, in_, reads, writes):
            P.op(eng, lambda h: h.dma_start(out=out, in_=in_), reads=reads, writes=writes, dma=True)


        dma("sp", identf[:], ident_d[:, :], [], [b("identf")])
        dma("pool", identb[:], ident_d[:, :], [], [b("identb")])
        dma("pool", maskneg[:], maskneg_d[:, :], [], [b("maskneg")])
        dma("sp", fnw[:], fnw_d[:, :], [], [b("fnw")])
        dma("sp", mprev[:], mprev_d[:, :], [], [b("mprev")])
        P.op("dve", lambda h: h.memset(onesb[:], 1.0), writes=[b("onesb")])
        P.op("dve", lambda h: h.memset(onesrow[:], 1.0), writes=[b("onesrow")])
        P.op("dve", lambda h: h.memset(negonesrow[:], -1.0), writes=[b("negonesrow")])
        P.op("dve", lambda h: h.memset(onesH_f[:], 1.0), writes=[b("onesH")])
        P.op("dve", lambda h: h.memset(ss[:], 0.0), writes=[b("ss")])
        P.op("dve", lambda h: h.memset(nrm[:], 0.0), writes=[b("nrm")])

        def mm(out, lhsT, rhs, start, stop, reads, writes):
            P.op("pe", lambda h: h.matmul(out, lhsT=lhsT, rhs=rhs, start=start, stop=stop), reads=reads, writes=writes)

        def tr(out, in_, ident, reads, writes):
            P.op("pe", lambda h: h.transpose(out=out, in_=in_, identity=ident), reads=reads, writes=writes)

        def act(out, in_, func, reads, writes, **kw):
            P.op("act", lambda h: h.activation(out=out, in_=in_, func=func, **kw), reads=reads, writes=writes)

        def tt(out, in0, in1, op, reads, writes, eng="dve"):
            P.op(eng, lambda h: h.tensor_tensor(out=out, in0=in0, in1=in1, op=op), reads=reads, writes=writes)

        def ts(out, in0, s1, s2, op0, op1, reads, writes, eng="dve"):
            if op1 is None:
                P.op(eng, lambda h: h.tensor_scalar(out=out, in0=in0, scalar1=s1, scalar2=None, op0=op0), reads=reads, writes=writes)
            else:
                P.op(eng, lambda h: h.tensor_scalar(out=out, in0=in0, scalar1=s1, scalar2=s2, op0=op0, op1=op1), reads=reads, writes=writes)

        def stt(out, in0, scalar, in1, op0, op1, reads, writes):
            P.op("dve", lambda h: h.scalar_tensor_tensor(out=out, in0=in0, scalar=scalar, in1=in1, op0=op0, op1=op1),
                 reads=reads, writes=writes)

        def memset(ap, val, writes):
            P.op("dve", lambda h: h.memset(ap, val), writes=writes)

        def recip(out, in_, reads, writes):
            P.op("dve", lambda h: h.reciprocal(out=out, in_=in_), reads=reads, writes=writes)

        def cp(out, in_, reads, writes):
            P.op("dve", lambda h: h.tensor_copy(out=out, in_=in_), reads=reads, writes=writes)

        def dbg_out(name, src_ap, reads, ci):
            if dbg and name in dbg_d and ci == dbg_chunk:
                dma("sp", dbg_d[name][:, :], src_ap, reads, [b("dbgo_" + name)])

        dbg_chunk = NCH
        wo_cnt = [0]
        dg_cnt = [0]
        G4 = lambda ap: ap.rearrange("p (a b) -> p a b", a=4)


        GDN_BUFS = ["masks", "masksu", "gnw", "sq4", "ke", "kd", "LTs", "NTm", "Nn", "Tm0", "Tm1", "Ym0", "Ym1", "Zm", "Zpm",
                    "ident4", "ub", "wT"]
        SSD_BUFS = ["ktok", "vdec", "convb", "dskip", "snw"]
        wo_cnt = [0]
        dg_cnt = [0]
        G4 = lambda ap: ap.rearrange("p (a b) -> p a b", a=4)

        def carve_factory():
            off = [0]

            def carve(nbytes, dt=BF16):
                n = nbytes // 2
                ap = U[:, off[0]:off[0] + n]
                off[0] += n
                assert off[0] <= UN
                if dt == F32:
                    ap = ap.bitcast(F32)
                return ap
            return carve

        for lidx, layer in enumerate(layers):
            gdn = layer == "gdn"
            first_layer = lidx == 0
            final_norm = lidx == len(layers) - 1
            H = 16 if gdn else 32
            DV = 2048 // H
            NHG = H // 4
            FW = 4 * DV
            CO = 0 if gdn else 2048
            ZO = 4096 if gdn else 0
            SO = 6144
            QO, KO, VO = (0, 8, 16) if gdn else (24, 16, 0)
            D_ = LD[layer]
            src_d = x_d if first_layer else xmid_s
            smF = smF_f[0:H]
            onesH = onesH_f[0:H]
            alog, negA, dtb = alog_f[0:H], negA_f[0:H], dtb_f[0:H]
            glrow = glrow_f[:, 0:H]
            smT = smT_f[:, 0:3 * H]
            tokS = tokS_f[:, :, 0:H]
            glbc = glbc_f[:, 0:H]
            gc_s = [g[0:H] for g in gc_s_full]
            gl_s = [g[0:H] for g in gl_s_full]
            carve = carve_factory()
            PA_BUFS = ["masks", "masksu", "sq4", "ke", "LTs", "NTm", "Nn", "Tm0", "Tm1", "Ym0", "Ym1", "Zm", "Zpm", "ident4", "stg",
                       "ktok", "convb", "dskip", "LTa", "stgB", "LTaB", "LT_B"]
            RB_BUFS = ["rb0", "rb1", "rb2", "tmpfB", "vnewB", "o_tB"]
            if lidx > 0:
                P.op("dve", lambda h: h.memset(ss[:, 3:4], 0.0), reads=[b(n) for n in RB_BUFS + ["gnw", "snw"]],
                     writes=[b(n) for n in PA_BUFS + ["gnw", "snw"]] + [b("ss")])
            PW = PWG if gdn else PWS
            SC = 1568 if gdn else 400
            if gdn:
                gnw = carve(512, F32)
            else:
                snw = carve(4096)
            carve_rb = carve_factory()
            carve_rb(512 if gdn else 4096)
            rbuf = [carve_rb(4672) for i in range(3)]
            tmpf_b = carve_rb(2048, F32)
            vnew_b = carve_rb(1024)
            o_t_b = carve_rb(2048, F32)
            tmpf_a, vnew_a = tmpf, vnew
            if gdn:
                masks = carve(3584).rearrange("p (a b) -> p a b", a=14)
                masksu = carve(256)
                sq4 = carve(1024)
                ke = G4(carve(1024))
                LTs = G4(carve(1024))
                NTm = G4(carve(1024))
                Nn = G4(carve(1024))
                Tm = [G4(carve(1024)) for i in range(2)]
                Ym = [G4(carve(1024)) for i in range(2)]
                Zm = G4(carve(1024))
                Zpm = G4(carve(1024))
                ident4 = G4(carve(1024))
                stg = carve(4672)
            else:
                ktok = carve(2048).rearrange("p (a b) -> p a b", a=8)
                convb = carve(128, F32)
                dskip = carve(128, F32)
                stg = carve(2080)
                LTa = G4(carve(1024))
                stgB = carve(2080)
                LTaB = G4(carve(1024))
                LT_B = G4(carve(1024))
            if gdn:
                LTa = None

            def views(blk):
                w_ = blk.shape[1]
                if gdn:
                    return dict(wT=G4(blk[:, 0:512]), kd=G4(blk[:, 512:1024]), ub=blk[:, 1024:1536],
                                sm=blk[:, 1536:1560].bitcast(F32), attnT=G4(blk[:, 1568:2080]) if w_ >= 2080 else None,
                                qT=blk[:, 2080:2336].rearrange("p (a b) -> p a b", a=2) if w_ >= 2336 else None)
                return dict(ktok=blk[:, 0:128], vdec=blk[:, 128:384], sm=blk[:, 384:400].bitcast(F32),
                            o0=blk[:, 400:912].bitcast(F32) if w_ >= 912 else None, CT=blk[:, 912:1040] if w_ >= 1040 else None)

            def s_chain(v, hg, VB, full, par=0):
                h0 = hg * 4
                SB_, SBb = b("S%d" % hg), b("Sb%d" % hg)
                ps_a, ps_b, ps_c = (4, 5, 6) if par == 0 else (7, 0, 1)
                tmpf, TMB = (tmpf_a, b("tmpf")) if par == 0 else (tmpf_b, b("tmpfB"))
                vnew, VNB = (vnew_a, b("vnew")) if par == 0 else (vnew_b, b("vnewB"))
                oh, OB = (o_t[0], b("o_t0")) if par == 0 else (o_t_b, b("o_tB"))
                sm = v["sm"]
                if gdn:
                    for hh in range(4):
                        hd = h0 + hh
                        mm(ps[ps_c][:, hh * 128:(hh + 1) * 128], v["wT"][:, hh, :], Sb[:, hd * 128:(hd + 1) * 128], True, True, VB + [SBb], [PSB[ps_c]])
                    tt(G4(tmpf[:]), G4(ps[ps_c][:, :]), sm[:, 0:4].unsqueeze(2).broadcast_to([128, 4, 128]), ALU.mult,
                       [PSB[ps_c]] + VB, [TMB])
                    tt(vnew[:], tmpf[:], v["ub"], ALU.add, [TMB] + VB, [VNB])
                    if full:
                        for hh in range(4):
                            hd = h0 + hh
                            mm(ps[ps_a][:, hh * 128:(hh + 1) * 128], v["qT"][:, hh // 2, :], Sb[:, hd * 128:(hd + 1) * 128], True, True,
                               VB + [SBb], [PSB[ps_a]])
                        for hh in range(4):
                            mm(ps[ps_b][:, hh * 128:(hh + 1) * 128], v["attnT"][:, hh, :], vnew[:, hh * 128:(hh + 1) * 128], True, True,
                               VB + [VNB], [PSB[ps_b]])
                        tt(G4(tmpf[:]), G4(ps[ps_a][:, :]), sm[:, 8:12].unsqueeze(2).broadcast_to([128, 4, 128]), ALU.mult,
                           [PSB[ps_a]] + VB, [TMB])
                        tt(oh[:, 0:FW], tmpf[:, 0:FW], ps[ps_b][:, 0:FW], ALU.add, [TMB, PSB[ps_b]], [OB])
                    for hh in range(4):
                        mm(ps[ps_c][:, hh * 128:(hh + 1) * 128], v["kd"][:, hh, :], vnew[:, hh * 128:(hh + 1) * 128], True, True,
                           VB + [VNB], [PSB[ps_c]])
                    gl = sm[:, 4:8]
                else:
                    if full:
                        mm(ps[ps_a][:, 0:FW], v["CT"], Sb[:, h0 * DV:(h0 + 4) * DV], True, True, VB + [SBb], [PSB[ps_a]])
                        tt(G4(tmpf[:, 0:FW]), G4(ps[ps_a][:, 0:FW]), sm[:, 4:8].unsqueeze(2).broadcast_to([128, 4, DV]), ALU.mult,
                           [PSB[ps_a]] + VB, [TMB])
                        tt(oh[:, 0:FW], tmpf[:, 0:FW], v["o0"], ALU.add, [TMB] + VB, [OB])
                    mm(ps[ps_c][:, 0:FW], v["ktok"], v["vdec"], True, True, VB, [PSB[ps_c]])
                    gl = sm[:, 0:4]
                tt(G4(tmpf[:, 0:FW]), G4(S[:, hg * FW:(hg + 1) * FW]), gl.unsqueeze(2).broadcast_to([128, 4, DV]),
                   ALU.mult, [SB_] + VB, [TMB])
                tt(S[:, hg * FW:(hg + 1) * FW], tmpf[:, 0:FW], ps[ps_c][:, 0:FW], ALU.add, [TMB, PSB[ps_c]], [SB_])
                act(Sb[:, hg * FW:(hg + 1) * FW], S[:, hg * FW:(hg + 1) * FW], AF.Copy, [SB_], [SBb])

            def blk_d(l, hg, lo, hi):
                return prod_s[l][:, hg * PW + lo:hg * PW + hi]

            dma("sp", normw[:], D_["normw"][:, :], [], [b("normw")])
            dma("sp", alog[:], D_["a_log"][:, :], [], [b("alog")])
            dma("sp", dtb[:], D_["dt_bias"][:, :], [], [b("dtb")])
            if gdn:
                dma("pool", masks[:].rearrange("p a b -> p (a b)"), masks_d[:, :], [], [b("masks")])
                dma("pool", masksu[:], masksu_d[:, :], [], [b("masksu")])
                dma("sp", gnw[:], D_["gnw"][:, :], [], [b("gnw")])
            else:
                dma("sp", convb[:], D_["convb"][:, :], [], [b("convb")])
                dma("sp", dskip[:], D_["dskip"][:, :], [], [b("dskip")])
                dma("pool", snw[:], D_["snw"][:, :], [], [b("snw")])
            P.op("dve", lambda h: h.memset(carry[:], 0.0), writes=[b("carry")])
            P.op("dve", lambda h: h.memset(S[:], 0.0), writes=[b("S%d" % i) for i in range(8)])
            P.op("dve", lambda h: h.memset(Sb[:], 0.0), writes=[b("Sb%d" % i) for i in range(8)])
            P.op("act", (lambda negA, alog: lambda h: h.activation(out=negA[:], in_=alog[:], func=AF.Exp))(negA, alog),
                 reads=[b("alog")], writes=[b("negA")])
            P.op("dve", (lambda negA: lambda h: h.tensor_scalar(out=negA[:], in0=negA[:], scalar1=-1.0, scalar2=None, op0=ALU.mult))(negA),
                 reads=[b("negA")], writes=[b("negA")])
            if gdn:
                for i in range(4):
                    P.op("dve", (lambda i, ident4: lambda h: h.tensor_copy(out=ident4[:, i, :], in_=identb[:]))(i, ident4),
                         reads=[b("identb")], writes=[b("ident4")])
            def load_wb(lay):
                for k in range(8):
                    for (f0, f1) in [(0, 2048), (2048, 4096), (4096, INW)]:
                        dma("pool", Wb[:, k, f0:f1], LD[lay]["w_in"][k * 128:(k + 1) * 128, f0:f1], [], [b("Wb")])
            if lidx == 0:
                load_wb(layer)
            for kt2 in range(8):
                dma("pool", zs[:].rearrange("p (a c) -> p a c", a=2),
                    D_["w_out"][kt2 * 256:(kt2 + 1) * 256, :].rearrange("(a p) c -> p a c", a=2), [], [b("zs")])
                dma("sp", wout_s[kt2 * 2:(kt2 + 1) * 2].rearrange("a p c -> p a c"),
                    zs[:].rearrange("p (a c) -> p a c", a=2), [b("zs")], [b("wout_s")])
            for c4 in range(8):
                dma("pool", zs[:], D_["diag"][c4], [], [b("zs")])
                dma("sp", diag_s[c4], zs[:], [b("zs")], [b("diag_s")])
            dbg_chunk = NCH
            for ci in range(NCH + 1):
                halo = ci == 0
                xb_ = xt[ci % 2]
                XB = b("xt%d" % (ci % 2))
                dma("sp", xb_[:], src_d[ci * 128:(ci + 1) * 128, :], [] if first_layer else [b("xmid%d" % ci)], [XB])
                memset(ss[:, 0:1], 0.0, [b("ss")])
                act(hid[:], xb_[:], AF.Square, [XB], [b("hid"), b("ss")], accum_out=ss[:, 0:1])
                act(ss[:, 1:2], ss[:, 0:1], AF.Sqrt, [b("ss")], [b("ss")], scale=1.0 / DM, bias=EPS)
                recip(ss[:, 2:3], ss[:, 1:2], [b("ss")], [b("ss")])
                stt(hid[:], xb_[:], ss[:, 2:3], normw[:], ALU.mult, ALU.mult, [XB, b("ss"), b("normw")], [b("hid")])
                for k in range(8):
                    tr(psb[0][:, k * 128:(k + 1) * 128], hid[:, k * 128:(k + 1) * 128], identb[:], [b("hid"), b("identb")], [PSB[0]])
                act(hidT[:].rearrange("p a b -> p (a b)"), psb[0][:, 0:1024], AF.Copy, [PSB[0]], [b("hidT")])
                def inproj_group(c4):
                    pa = 1 + (c4 % 2)
                    for i in range(4):
                        f0 = CO + (c4 * 4 + i) * 128
                        for k in range(8):
                            mm(ps[pa][:, i * 128:(i + 1) * 128], Wb[:, k, f0:f0 + 128], hidT[:, k, :], k == 0, k == 7,
                               [b("Wb"), b("hidT")], [PSB[pa]])

                inproj_group(0)
                for c4 in range(8):
                    pa = 1 + (c4 % 2)
                    pc = 3 + (c4 % 2)
                    pbuf = Pbuf[c4 % 2]
                    PB = b("Pbuf%d" % (c4 % 2))
                    if c4 + 1 < 8:
                        inproj_group(c4 + 1)
                    cp(pbuf[:, :, 0:3], carry[:, c4 * 4:(c4 + 1) * 4, :], [b("carry")], [PB])
                    act(pbuf[:, :, 3:131], G4(ps[pa][:]), AF.Copy, [PSB[pa]], [PB])
                    cp(carry[:, c4 * 4:(c4 + 1) * 4, :], pbuf[:, :, 128:131], [PB], [b("carry")])
                    if halo:
                        continue
                    for i in range(4):
                        ct = c4 * 4 + i
                        di = dg_cnt[0] % NDG
                        dg_cnt[0] += 1
                        dg = diag[di]
                        DG = b("diag%d" % di)
                        dma("sp", dg[:].rearrange("p a b -> p (a b)"), diag_s[c4][:, i * 512:(i + 1) * 512], [b("diag_s")], [DG])
                        for j in range(4):
                            mm(ps[pc][:, i * 128:(i + 1) * 128], dg[:, j, :], pbuf[:, i, j:j + 128], j == 0, j == 3, [DG, PB], [PSB[pc]])
                        if not gdn:
                            act(convT[:, ct, :], ps[pc][:, i * 128:(i + 1) * 128], AF.Silu, [PSB[pc], b("convb")], [b("convT%d" % c4)],
                                bias=convb[:, ct:ct + 1])
                    if gdn:
                        act(convT[:, c4 * 4:(c4 + 1) * 4, :], G4(ps[pc][:]), AF.Silu, [PSB[pc]], [b("convT%d" % c4)])
                if halo:
                    continue
                dbg_out("convT", convT[:].rearrange("p a b -> p (a b)"), [b("convT%d" % i) for i in range(8)], ci)
                for f in range(4):
                    pz = 5 + (f % 2)
                    if gdn:
                        CB = b("convT%d" % f)
                        cv = convT[:, f * 4:(f + 1) * 4, :].rearrange("p a b -> p (a b)")
                        tt(sq4[:], cv, cv, ALU.mult, [CB], [b("sq4")])
                    for k in range(8):
                        mm(ps[pz][:, :], hidT[:, k, :], Wb[:, k, ZO + f * 512:ZO + (f + 1) * 512], k == 0, k == 7,
                           [b("Wb"), b("hidT")], [PSB[pz]])
                    if gdn:
                        mm(ps[1][:, :], onesb[:], sq4[:], True, True, [b("onesb"), b("sq4")], [PSB[1]])
                    act(zs[:, f * 512:(f + 1) * 512], ps[pz][:, :], AF.Silu, [PSB[pz]], [b("zs")])
                    if gdn:
                        act(tmpf[:], ps[1][:, :], AF.Sqrt, [PSB[1]], [b("tmpf")], bias=EPS)
                        recip(tmpf[:], tmpf[:], [b("tmpf")], [b("tmpf")])
                        stt(cv, cv, (128.0 ** -0.5) if f < 2 else 1.0, tmpf[:], ALU.mult, ALU.mult, [CB, b("tmpf")], [CB])
                SM = b("smF")
                if gdn:
                    for k in range(8):
                        mm(ps[7][0:16, 0:128], Wb[:, k, SO:SO + 16], hidT[:, k, :], k == 0, k == 7, [b("Wb"), b("hidT")], [PSB[7]])
                    for k in range(8):
                        mm(ps[7][0:16, 128:256], Wb[:, k, SO + 16:SO + 32], hidT[:, k, :], k == 0, k == 7, [b("Wb"), b("hidT")], [PSB[7]])
                    act(smF[:, 0, :], ps[7][0:16, 0:128], AF.Sigmoid, [PSB[7]], [SM])
                    act(smF[:, 1, :], ps[7][0:16, 128:256], AF.Exp, [PSB[7], b("dtb")], [SM], bias=dtb[:, 0:1])
                else:
                    for k in range(8):
                        mm(ps[7][0:32, 0:128], Wb[:, k, SO:SO + 32], hidT[:, k, :], k == 0, k == 7, [b("Wb"), b("hidT")], [PSB[7]])
                    act(smF[:, 1, :], ps[7][0:32, 0:128], AF.Exp, [PSB[7], b("dtb")], [SM], bias=dtb[:, 0:1])
                act(smF[:, 2, :], smF[:, 1, :], AF.Ln, [SM], [SM], bias=1.0)
                if not gdn:
                    cp(smF[:, 0, :], smF[:, 2, :], [SM], [SM])
                ts(smF[:, 3, :], smF[:, 2, :], negA[:, 0:1], None, ALU.mult, None, [SM, b("negA")], [SM])
                P.op("dve", (lambda smF, onesH: lambda h: h.tensor_tensor_scan(
                    out=smF[:, 4, :], data0=onesH[:], data1=smF[:, 3, :], initial=0.0, op0=ALU.mult, op1=ALU.add))(smF, onesH),
                     reads=[SM, b("onesH")], writes=[SM])
                ts(smF[:, 5, :], smF[:, 4, :], -1.0, smF[:, 4, 127:128], ALU.mult, ALU.add, [SM], [SM])
                gcs, GCS = gc_s[ci % 2], b("gc_s%d" % (ci % 2))
                gls, GLS = gl_s[ci % 2], b("gl_s%d" % (ci % 2))
                dma("pool", gcs[:, :], smF[:, 4, :], [SM], [GCS])
                dma("pool", gls[:, :], smF[:, 4, 127:128], [SM], [GLS])
                dma("sp", glrow[0:1, :], gls.rearrange("h o -> o h"), [GLS], [b("glrow")])
                for t_, src in enumerate([0, 4, 5]):
                    tr(ps[7][:, 256 + t_ * H:256 + (t_ + 1) * H], smF[:, src, :], identf[0:H, 0:H], [SM, b("identf")], [PSB[7]])
                cp(smT[:], ps[7][:, 256:256 + 3 * H], [PSB[7]], [b("smT")])
                TS = b("tokS")
                act(tokS[:, 0, :], smT[:, H:2 * H], AF.Exp, [b("smT")], [TS])
                act(tokS[:, 1, :], smT[:, 2 * H:3 * H], AF.Exp, [b("smT")], [TS])
                if gdn:
                    ts(tokS[:, 3, :], smT[:, 0:H], -1.0, None, ALU.mult, None, [b("smT")], [TS])
                else:
                    tt(tokS[:, 2, :], smT[:, 0:H], tokS[:, 1, :], ALU.mult, [b("smT"), TS], [TS])
                mm(ps[7][:, 384:384 + H], onesrow[0:1, 0:128], glrow[0:1, :], True, True, [b("onesrow"), b("glrow")], [PSB[7]])
                act(glbc[:], ps[7][:, 384:384 + H], AF.Exp, [PSB[7]], [b("glbc")])
                dbg_out("smT", smT[:], [b("smT")], ci)
                dbg_out("qkn", convT[:, 0:16, :].rearrange("p a b -> p (a b)"), [b("convT%d" % i) for i in range(4)], ci)
                for half in range(2):
                    pv = 1 + half
                    for i in range(8):
                        ct = VO + half * 8 + i
                        tr(psb[pv][:, i * 128:(i + 1) * 128], convT[:, ct, :], identb[:], [b("convT%d" % (ct // 4)), b("identb")], [PSB[pv]])
                    act(vtok[:, half * 1024:(half + 1) * 1024], psb[pv][:, 0:1024], AF.Copy, [PSB[pv]], [b("vtok")])
                for i in range(8):
                    ct = KO + i
                    tr(psb[3][:, i * 128:(i + 1) * 128], convT[:, ct, :], identb[:], [b("convT%d" % (ct // 4)), b("identb")], [PSB[3]])
                if not gdn:
                    act(ktok[:].rearrange("p a b -> p (a b)"), psb[3][:, 0:1024], AF.Copy, [PSB[3]], [b("ktok")])
                for hg in range(NHG):
                    h0 = hg * 4
                    par = 0 if gdn else hg % 2
                    if par == 0:
                        stg_p, STG, LT_p, LTB, LTa_p, LAB = stg, b("stg"), LT, b("LT"), LTa, b("LTa")
                        vnew_p, VNB, tmpf_p, TMB, qb = vnew, b("vnew"), tmpf, b("tmpf"), 5
                    else:
                        stg_p, STG, LT_p, LTB, LTa_p, LAB = stgB, b("stgB"), LT_B, b("LT_B"), LTaB, b("LTaB")
                        vnew_p, VNB, tmpf_p, TMB, qb = vnew_b, b("vnewB"), tmpf_b, b("tmpfB"), 0
                    SV = views(stg_p)
                    attnT = SV["attnT"] if gdn else LTa_p
                    gr = gcrow4[hg % 2]
                    GR = b("gcrow4_%d" % (hg % 2))
                    dma("sp", gr[:, :], gcs[h0:h0 + 4, :].rearrange("(o h) c -> o (h c)", o=1).broadcast_to([128, 512]), [GCS], [GR])
                    tt(G4(gr[:, :]), G4(gr[:, :]), smT[:, H + h0:H + h0 + 4].unsqueeze(2).broadcast_to([128, 4, 128]), ALU.subtract,
                       [GR, b("smT")], [GR])
                    tt(gr[:, :], gr[:, :], maskneg[:], ALU.add, [GR, b("maskneg")], [GR])
                    act(LT_p[:].rearrange("p a b -> p (a b)"), gr[:, :], AF.Exp, [GR], [LTB])
                    if gdn:
                        kin = psb[3][:, hg * 256:(hg + 1) * 256].rearrange("p (g d) -> p g d", g=2).unsqueeze(2).broadcast_to([128, 2, 2, 128])
                        for (dst, DB, row) in [(ke, b("ke"), 0), (SV["kd"], b("stg"), 1)]:
                            tt(dst[:].rearrange("p (g r) d -> p g r d", g=2), kin,
                               tokS[:, row, h0:h0 + 4].rearrange("p (g r) -> p g r", g=2).unsqueeze(3).broadcast_to([128, 2, 2, 128]),
                               ALU.mult, [PSB[3], TS], [DB])
                        for qq in range(2):
                            g = hg * 2 + qq
                            mm(ps[5][:, qq * 128:(qq + 1) * 128], convT[:, KO + g, :], convT[:, KO + g, :], True, True,
                               [b("convT%d" % ((KO + g) // 4))], [PSB[5]])
                            mm(ps[5][:, 256 + qq * 128:256 + (qq + 1) * 128], convT[:, KO + g, :], convT[:, QO + g, :], True, True,
                               [b("convT%d" % ((KO + g) // 4)), b("convT%d" % ((QO + g) // 4))], [PSB[5]])
                        tt(LTs[:], LT[:], masksu[:].unsqueeze(1).broadcast_to([128, 4, 128]), ALU.mult, [b("LT"), b("masksu")], [b("LTs")])
                        for hh in range(4):
                            stt(NTm[:, hh, :], ps[5][:, (hh // 2) * 128:(hh // 2 + 1) * 128], tokS[:, 3, h0 + hh:h0 + hh + 1], LTs[:, hh, :],
                                ALU.mult, ALU.mult, [PSB[5], TS, b("LTs")], [b("NTm")])
                        tt(attnT[:].rearrange("p (q r) d -> p q r d", q=2),
                           ps[5][:, 256:512].rearrange("p (q d) -> p q d", q=2).unsqueeze(2).broadcast_to([128, 2, 2, 128]),
                           LT[:].rearrange("p (q r) d -> p q r d", q=2), ALU.mult, [PSB[5], b("LT")], [b("stg") if gdn else b("LTa")])
                    else:
                        g = hg
                        mm(ps[qb][:, 0:128], convT[:, KO + g, :], convT[:, QO + g, :], True, True,
                           [b("convT%d" % ((KO + g) // 4)), b("convT%d" % ((QO + g) // 4))], [PSB[qb]])
                        tt(attnT[:], ps[qb][:, 0:128].unsqueeze(1).broadcast_to([128, 4, 128]), LT_p[:], ALU.mult, [PSB[qb], LTB], [LAB])
                    if gdn:
                        for hh in range(4):
                            tr(psb[6][:, hh * 128:(hh + 1) * 128], NTm[:, hh, :], identb[:], [b("NTm"), b("identb")], [PSB[6]])
                        act(Nn[:].rearrange("p a b -> p (a b)"), psb[6][:, 0:512], AF.Copy, [PSB[6]], [b("Nn")])
                        cur = 0
                        Tc, Yc = ident4, ident4
                        TCB, YCB = b("ident4"), b("ident4")
                        for li in range(7):
                            Tn_, Yn_ = Tm[cur], Ym[cur]
                            TNB, YNB = b("Tm%d" % cur), b("Ym%d" % cur)
                            last = li == 6
                            Ml = masks[:, li, :].unsqueeze(1).broadcast_to([128, 4, 128])
                            MlT = masks[:, 7 + li, :].unsqueeze(1).broadcast_to([128, 4, 128])
                            if li == 0:
                                tt(Tn_[:], Nn[:], Ml, ALU.mult, [b("Nn"), b("masks")], [TNB])
                                tt(Tn_[:], Tn_[:], ident4[:], ALU.add, [TNB, b("ident4")], [TNB])
                                tt(Yn_[:], NTm[:], MlT, ALU.mult, [b("NTm"), b("masks")], [YNB])
                                tt(Yn_[:], Yn_[:], ident4[:], ALU.add, [YNB, b("ident4")], [YNB])
                                Tc, TCB, Yc, YCB = Tn_, TNB, Yn_, YNB
                                cur ^= 1
                                continue
                            if not last:
                                for hh in range(4):
                                    mm(ps[4][:, hh * 128:(hh + 1) * 128], NTm[:, hh, :], Tc[:, hh, :], True, True, [b("NTm"), TCB], [PSB[4]])
                                tt(Zm[:], G4(ps[4][:, :]), Ml, ALU.mult, [PSB[4], b("masks")], [b("Zm")])
                            for hh in range(4):
                                mm(ps[5][:, hh * 128:(hh + 1) * 128], Nn[:, hh, :], Yc[:, hh, :], True, True, [b("Nn"), YCB], [PSB[5]])
                            tt(Zpm[:], G4(ps[5][:, :]), MlT, ALU.mult, [PSB[5], b("masks")], [b("Zpm")])
                            if not last:
                                for hh in range(4):
                                    mm(ps[6][:, hh * 128:(hh + 1) * 128], Yc[:, hh, :], Zm[:, hh, :], True, True, [YCB, b("Zm")], [PSB[6]])
                                tt(Tn_[:], Tc[:], G4(ps[6][:, :]), ALU.add, [PSB[6], TCB], [TNB])
                            for hh in range(4):
                                mm(ps[7][:, hh * 128:(hh + 1) * 128], Tc[:, hh, :], Zpm[:, hh, :], True, True, [TCB, b("Zpm")], [PSB[7]])
                            tt(Yn_[:], Yc[:], G4(ps[7][:, :]), ALU.add, [PSB[7], YCB], [YNB])
                            if not last:
                                Tc, TCB = Tn_, TNB
                            Yc, YCB = Yn_, YNB
                            cur ^= 1
                        for hh in range(4):
                            hd = h0 + hh
                            mm(ps[4][:, hh * 128:(hh + 1) * 128], Yc[:, hh, :], vtok[:, hd * 128:(hd + 1) * 128], True, True,
                               [YCB, b("vtok")], [PSB[4]])
                        tt(G4(SV["ub"]), G4(ps[4][:, :]), smT[:, h0:h0 + 4].unsqueeze(2).broadcast_to([128, 4, 128]), ALU.mult,
                           [PSB[4], b("smT")], [b("stg")])
                        for hh in range(4):
                            mm(ps[5][:, hh * 128:(hh + 1) * 128], ke[:, hh, :], Yc[:, hh, :], True, True, [b("ke"), YCB], [PSB[5]])
                        act(SV["wT"].rearrange("p a b -> p (a b)"), ps[5][:, :], AF.Copy, [PSB[5]], [b("stg")])
                        cp(SV["sm"][:, 0:4], tokS[:, 3, h0:h0 + 4], [TS], [b("stg")])
                        cp(SV["sm"][:, 4:8], glbc[:, h0:h0 + 4], [b("glbc")], [b("stg")])
                        cp(SV["sm"][:, 8:12], tokS[:, 0, h0:h0 + 4], [TS], [b("stg")])
                        dma("pool", blk_d(ci - 1, hg, 0, 2080), stg[:, 0:2080], [b("stg")], [b("prod%d_%d" % (ci - 1, hg))])
                        dma("pool", blk_d(ci - 1, hg, 2080, 2336).rearrange("p (a b) -> p a b", a=2), convT[:, QO + 2 * hg:QO + 2 * hg + 2, :],
                            [b("convT%d" % ((QO + 2 * hg) // 4))], [b("prodq%d_%d" % (ci - 1, hg))])
                    else:
                        xin = G4(vtok[:, h0 * DV:(h0 + 4) * DV])
                        tt(G4(vnew_p[:, 0:FW]), xin, smT[:, h0:h0 + 4].unsqueeze(2).broadcast_to([128, 4, DV]),
                           ALU.mult, [b("vtok"), b("smT")], [VNB])
                        tt(G4(SV["vdec"]), xin, tokS[:, 2, h0:h0 + 4].unsqueeze(2).broadcast_to([128, 4, DV]),
                           ALU.mult, [b("vtok"), TS], [STG])
                        for hh in range(4):
                            mm(ps[qb][:, hh * DV:(hh + 1) * DV], attnT[:, hh, :], vnew_p[:, hh * DV:(hh + 1) * DV], True, True,
                               [LAB, VNB], [PSB[qb]])
                        tt(G4(tmpf_p[:, 0:FW]), xin, dskip[:, h0:h0 + 4].unsqueeze(2).broadcast_to([128, 4, DV]),
                           ALU.mult, [b("vtok"), b("dskip")], [TMB])
                        tt(SV["o0"], tmpf_p[:, 0:FW], ps[qb][:, 0:FW], ALU.add, [TMB, PSB[qb]], [STG])
                        cp(SV["sm"][:, 0:4], glbc[:, h0:h0 + 4], [b("glbc")], [STG])
                        cp(SV["sm"][:, 4:8], tokS[:, 0, h0:h0 + 4], [TS], [STG])
                        dma("pool", blk_d(ci - 1, hg, 128, 912), stg_p[:, 128:912], [STG], [b("prod%d_%d" % (ci - 1, hg))])
                        dma("pool", blk_d(ci - 1, hg, 0, 128), ktok[:, hg, :], [b("ktok")], [b("prodk%d_%d" % (ci - 1, hg))])
                        dma("pool", blk_d(ci - 1, hg, 912, 1040), convT[:, QO + hg, :], [b("convT%d" % ((QO + hg) // 4))],
                            [b("prodq%d_%d" % (ci - 1, hg))])
                dma("pool", zs_s[ci - 1], zs[:], [b("zs")], [b("zs_s%d" % (ci - 1))])

            P.op("dve", lambda h: h.memset(ss[:, 3:4], 0.0), reads=[b(n) for n in PA_BUFS], writes=[b(n) for n in RB_BUFS] + [b("ss")])
            SALL = [b("S%d" % i) for i in range(8)]
            SBALL = [b("Sb%d" % i) for i in range(8)]
            rb_cnt = [0]

            def load_blk(l, hg, width):
                i = rb_cnt[0] % 3
                rb_cnt[0] += 1
                RB = b("rb%d" % i)
                deps = [b("prod%d_%d" % (l, hg))]
                if not gdn:
                    deps.append(b("prodk%d_%d" % (l, hg)))
                if width > SC:
                    deps.append(b("prodq%d_%d" % (l, hg)))
                dma("sp", rbuf[i][:, 0:width], blk_d(l, hg, 0, width), deps, [RB])
                return rbuf[i][:, 0:width], RB

            for l in range(NCH):
                for hg in range(NHG):
                    blk, RB = load_blk(l, hg, SC)
                    s_chain(views(blk), hg, [RB], False, par=hg % 2)
            for rnd in range(3):
                dma("pool", srcS[:, :], S[:], SALL, [b("srcS")])
                P.op("pool", (lambda rnd: lambda h: h.collective_compute("AllGather", ALU.bypass, replica_groups=RGL,
                                                                        ins=[srcS[:, :]], outs=[gatS[rnd][:, :]]))(rnd),
                     reads=[b("srcS")], writes=[b("gatS%d" % rnd)])
                if rnd == 2:
                    break
                dma("sp", S[:], gatS[rnd][rnd * 128:(rnd + 1) * 128, :], [b("gatS%d" % rnd)], SALL)
                act(Sb[:], S[:], AF.Copy, SALL, SBALL)
                for l in range(NCH):
                    for hg in range(NHG):
                        blk, RB = load_blk(l, hg, SC)
                        s_chain(views(blk), hg, [RB], False, par=hg % 2)
            for pc_ in range(4):
                sl = slice(pc_ * 512, (pc_ + 1) * 512)
                SP_ = [b("S%d" % i) for i in range(8) if (i * FW) // 512 == pc_]
                for j in range(3):
                    dma("sp", tmpf[:], gatS[j][j * 128:(j + 1) * 128, sl], [b("gatS%d" % j)], [b("tmpf")])
                    if j == 0:
                        ts(S[:, sl], tmpf[:], mprev[:, 0:1], None, ALU.mult, None, [b("tmpf"), b("mprev")], SP_)
                    else:
                        stt(S[:, sl], tmpf[:], mprev[:, j:j + 1], S[:, sl], ALU.mult, ALU.add, [b("tmpf"), b("mprev")] + SP_, SP_)
            act(Sb[:], S[:], AF.Copy, SALL, SBALL)
            if lidx + 1 < len(layers):
                load_wb(layers[lidx + 1])
            wo_list = [
                (convT[:, 8 * q:8 * q + 8, :].rearrange("p a b -> p (a b)"), [b("convT%d" % (2 * q)), b("convT%d" % (2 * q + 1))])
                for q in range(4)]

            def L_front():
                for half in range(2):
                    pv = 2 + half
                    for i in range(8):
                        kt = half * 8 + i
                        tr(psb[pv][:, i * 128:(i + 1) * 128], ya[:, kt * 128:(kt + 1) * 128], identb[:], [b("ya"), b("identb")], [PSB[pv]])
                    act(yT[:, half * 8:(half + 1) * 8, :].rearrange("p a b -> p (a b)"), psb[pv][:, 0:1024], AF.Copy, [PSB[pv]], [b("yT")])

            def L_mid(kts):
                for kt in kts:
                    wo_ap, WBL = wo_list[wo_cnt[0] % len(wo_list)]
                    wo_cnt[0] += 1
                    dma("sp", wo_ap, wout_s[kt], [b("wout_s")], WBL)
                    for n in range(2):
                        mm(ps[2 + n][:, :], yT[:, kt, :], wo_ap[:, n * 512:(n + 1) * 512], kt == 0, kt == 15, [b("yT")] + WBL, [PSB[2 + n]])

            def L_back(cj):
                xb_ = xt[cj % 2]
                XB = b("xt%d" % (cj % 2))
                for n in range(2):
                    tt(xb_[:, n * 512:(n + 1) * 512], xb_[:, n * 512:(n + 1) * 512], ps[2 + n][:, :], ALU.add, [XB, PSB[2 + n]], [XB])
                if final_norm:
                    memset(ss[:, 0:1], 0.0, [b("ss")])
                    act(hid[:], xb_[:], AF.Square, [XB], [b("hid"), b("ss")], accum_out=ss[:, 0:1])
                    act(ss[:, 1:2], ss[:, 0:1], AF.Sqrt, [b("ss")], [b("ss")], scale=1.0 / DM, bias=EPS)
                    recip(ss[:, 2:3], ss[:, 1:2], [b("ss")], [b("ss")])
                    stt(xb_[:], xb_[:], ss[:, 2:3], fnw[:], ALU.mult, ALU.mult, [XB, b("ss"), b("fnw")], [XB])
                    dma("pool", xo_d[(cj - 1) * 128:cj * 128, :], xb_[:], [XB], [b("xo%d" % cj)])
                else:
                    dma("pool", xmid_s[cj * 128:(cj + 1) * 128, :], xb_[:], [XB], [b("xmid%d" % cj)])
                    if cj == NCH:
                        dma("pool", hal_src[:, :], xb_[:], [XB], [b("hal_src")])

            KPH = 16 // NHG
            for ci in range(1, NCH + 1):
                l = ci - 1
                xb_ = xt[ci % 2]
                XB = b("xt%d" % (ci % 2))
                dma("sp", xb_[:], src_d[ci * 128:(ci + 1) * 128, :], [] if first_layer else [b("xmid%d" % ci)], [XB])
                dma("sp", zs[:], zs_s[l], [b("zs_s%d" % l)], [b("zs")])
                if ci > 1:
                    L_front()
                for hg in range(NHG):
                    h0 = hg * 4
                    oh, OB = (o_t[0], b("o_t0")) if hg % 2 == 0 else (o_t_b, b("o_tB"))
                    blk, RB = load_blk(l, hg, PW)
                    s_chain(views(blk), hg, [RB], True, par=hg % 2)
                    ysl = ya[:, hg * FW:(hg + 1) * FW]
                    zsl = zs[:, hg * FW:(hg + 1) * FW]
                    jk = hid[:, 0:FW]
                    memset(nrm[:, 0:4], 0.0, [b("nrm")])
                    if gdn:
                        for hh in range(4):
                            act(jk[:, hh * 128:(hh + 1) * 128], oh[:, hh * 128:(hh + 1) * 128], AF.Square, [OB], [b("hid"), b("nrm")],
                                accum_out=nrm[:, hh:hh + 1])
                        act(nrm[:, 4:8], nrm[:, 0:4], AF.Sqrt, [b("nrm")], [b("nrm")], scale=1.0 / 128, bias=EPS)
                        recip(nrm[:, 4:8], nrm[:, 4:8], [b("nrm")], [b("nrm")])
                        for hh in range(4):
                            act(ysl[:, hh * 128:(hh + 1) * 128], oh[:, hh * 128:(hh + 1) * 128], AF.Copy, [OB, b("nrm")], [b("ya")],
                                scale=nrm[:, 4 + hh:5 + hh])
                        tt(G4(ysl), G4(ysl), gnw[:].unsqueeze(1).broadcast_to([128, 4, 128]), ALU.mult, [b("ya"), b("gnw")], [b("ya")])
                        tt(ysl, ysl, zsl, ALU.mult, [b("ya"), b("zs")], [b("ya")])
                    else:
                        tt(oh[:, 0:FW], oh[:, 0:FW], zsl, ALU.mult, [OB, b("zs")], [OB])
                        act(jk, oh[:, 0:FW], AF.Square, [OB], [b("hid"), b("nrm")], accum_out=nrm[:, 0:1])
                        act(nrm[:, 4:5], nrm[:, 0:1], AF.Sqrt, [b("nrm")], [b("nrm")], scale=1.0 / 256, bias=EPS)
                        recip(nrm[:, 4:5], nrm[:, 4:5], [b("nrm")], [b("nrm")])
                        act(ysl, oh[:, 0:FW], AF.Copy, [OB, b("nrm")], [b("ya")], scale=nrm[:, 4:5])
                        tt(ysl, ysl, snw[:, hg * FW:(hg + 1) * FW], ALU.mult, [b("ya"), b("snw")], [b("ya")])
                    if ci > 1:
                        L_mid(range(hg * KPH, (hg + 1) * KPH))
                if ci > 1:
                    L_back(ci - 1)
            L_front()
            L_mid(range(16))
            L_back(NCH)
            if not final_norm:
                P.op("pool", lambda h: h.collective_compute("AllGather", ALU.bypass, replica_groups=RGL, ins=[hal_src[:, :]], outs=[hal_gat[:, :]]),
                     reads=[b("hal_src")], writes=[b("hal_gat")])
                for j in range(3):
                    dma("sp", xt[0][:], hal_gat[j * 128:(j + 1) * 128, :], [b("hal_gat")], [b("xt0")])
                    if j == 0:
                        ts(xt[1][:], xt[0][:], mprev[:, 0:1], None, ALU.mult, None, [b("xt0"), b("mprev")], [b("xt1")])
                    else:
                        stt(xt[1][:], xt[0][:], mprev[:, j:j + 1], xt[1][:], ALU.mult, ALU.add, [b("xt0"), b("mprev"), b("xt1")], [b("xt1")])
                dma("sp", xmid_s[0:128, :], xt[1][:], [b("xt1")], [b("xmid0")])
        P.emit()
    return nc


def conv_diag(conv_w):
    cw = np.asarray(conv_w, np.float32).reshape(4, 8, 4, 128)
    d = np.zeros((8, 128, 4, 4, 128), np.float32)
    for p in range(128):
        d[:, p, :, :, p] = np.transpose(cw[:, :, :, p], (1, 2, 0))
    return d.reshape(8, 128, 16 * 128)


def bc(v, n=128):
    v = np.asarray(v, np.float32).reshape(1, -1)
    return np.ascontiguousarray(np.broadcast_to(v, (n, v.shape[1])))


def layer_inputs(layer, x_with_halo, s_in, norm_w, w_in, conv_w, w_out, a_log, dt_bias, gdn_norm_w=None,
                 conv_b=None, d_skip=None, ssd_norm_w=None, final_norm_w=None):
    H = 16 if layer == "gdn" else 32
    m = dict(host_consts())
    m.update(x=np.ascontiguousarray(x_with_halo, dtype=np.float32), w_in=np.ascontiguousarray(w_in, dtype=np.float32),
             w_out=np.ascontiguousarray(w_out, dtype=np.float32), normw_bc=bc(norm_w), diag=conv_diag(conv_w),
             s_in=np.ascontiguousarray(s_in, dtype=np.float32),
             a_log=np.asarray(a_log, np.float32).reshape(H, 1).copy(), dt_bias=np.asarray(dt_bias, np.float32).reshape(H, 1).copy())
    if layer == "gdn":
        m["gnw_bc"] = bc(gdn_norm_w)
    else:
        m["conv_b"] = np.ascontiguousarray(np.asarray(conv_b, np.float32).reshape(32, 128).T)
        m["dskip_bc"] = bc(d_skip)
        m["snw_bc"] = bc(ssd_norm_w)
    if final_norm_w is not None:
        m["fnw_bc"] = bc(final_norm_w)
    return m


def fused_inputs(x_with_halo, p):
    m = dict(host_consts())
    m["x"] = np.ascontiguousarray(x_with_halo, dtype=np.float32)
    f32 = lambda a: np.ascontiguousarray(np.asarray(a, np.float32))
    m.update(g_w_in=f32(p["gdn_w_in"][0]), g_w_out=f32(p["gdn_w_out"][0]), g_normw_bc=bc(p["norm_w"][0]),
             g_diag=conv_diag(p["gdn_conv_w"][0]), g_a_log=f32(p["gdn_a_log"][0]).reshape(16, 1),
             g_dt_bias=f32(p["gdn_dt_bias"][0]).reshape(16, 1), g_gnw_bc=bc(p["gdn_norm_w"][0]))
    m.update(s_w_in=f32(p["ssd_w_in"][0]), s_w_out=f32(p["ssd_w_out"][0]), s_normw_bc=bc(p["norm_w"][1]),
             s_diag=conv_diag(p["ssd_conv_w"][0]), s_a_log=f32(p["ssd_a_log"][0]).reshape(32, 1),
             s_dt_bias=f32(p["ssd_dt_bias"][0]).reshape(32, 1),
             s_conv_b=np.ascontiguousarray(f32(p["ssd_conv_b"][0]).reshape(32, 128).T), s_dskip_bc=bc(p["ssd_d"][0]),
             s_snw_bc=bc(p["ssd_norm_w"][0]))
    m["fnw_bc"] = bc(p["final_norm_w"])
    return m


def par_inputs(x_with_halo, r, p):
    m = fused_inputs(x_with_halo, p)
    mp = np.zeros((128, 4), np.float32)
    if r >= 1:
        mp[:, r - 1] = 1.0
    m["mprev"] = mp
    return m


def kernel(**inputs):
    x = np.asarray(inputs["x"], np.float32)
    Bn, T, _ = x.shape
    NSEG = 4
    SEG = T // NSEG
    NCH = SEG // 128
    RG = tuple(tuple(range(bi * NSEG, (bi + 1) * NSEG)) for bi in range(Bn))
    nc = build_par(NCH, RG=RG)
    z128 = np.zeros((128, DM), np.float32)
    maps = []
    for bi in range(Bn):
        for r in range(NSEG):
            halo = z128 if r == 0 else x[bi, r * SEG - 128:r * SEG]
            maps.append(par_inputs(np.concatenate([halo, x[bi, r * SEG:(r + 1) * SEG]], axis=0), r, inputs))
    res = run_bass_kernel_spmd(nc, maps, core_ids=list(range(Bn * NSEG)))
    out = np.empty_like(x)
    for bi in range(Bn):
        for r in range(NSEG):
            out[bi, r * SEG:(r + 1) * SEG] = res.results[bi * NSEG + r]["xo"]
    return out
```

```python
import numpy as np
from contextlib import ExitStack
import concourse.bass as bass
import concourse.mybir as mybir
from concourse.bass_utils import run_bass_kernel_spmd

F32 = mybir.dt.float32
BF16 = mybir.dt.bfloat16
AF = mybir.ActivationFunctionType
ALU = mybir.AluOpType

NDMASEM = 24
PROFILE_LINES = None
PROFILE_NAMES = {}
EPS = 1e-6
C = 128
DM = 1024
INW = 6176


class Buf:
    __slots__ = ("name", "lw", "rd")

    def __init__(self, name):
        self.name = name
        self.lw = None
        self.rd = []


class Prog:
    ENGS = ("pe", "act", "dve", "pool", "sp")

    def __init__(self, nc):
        self.nc = nc
        self.ops = []
        self.ndma = 0

    def op(self, eng, fn, reads=(), writes=(), dma=False):
        idx = len(self.ops)
        deps = set()
        for b in reads:
            if b.lw is not None:
                deps.add(b.lw)
        for b in writes:
            if b.lw is not None:
                deps.add(b.lw)
            deps.update(b.rd)
        key = None if dma else eng
        for b in reads:
            if key is not None:
                b.rd = [r for r in b.rd if self.ops[r]["dma"] or self.ops[r]["eng"] != key]
            b.rd.append(idx)
        for b in writes:
            b.lw = idx
            b.rd = []
        d = dict(eng=eng, fn=fn, deps=deps, dma=dma, dmaidx=None)
        if PROFILE_LINES is not None:
            import sys as _sys
            f = _sys._getframe(1)
            while f is not None and f.f_code.co_name not in ("build_par", "build_fused", "build"):
                f = f.f_back
            PROFILE_LINES.append((eng, dma, f.f_lineno if f is not None else 0))
            d["line"] = f.f_lineno if f is not None else 0
        if dma:
            d["dmaidx"] = self.ndma
            self.ndma += 1
        self.ops.append(d)
        return idx

    def emit(self):
        nc = self.nc
        ops = self.ops
        needed = [False] * len(ops)
        for i, o in enumerate(ops):
            for d in o["deps"]:
                po = ops[d]
                if po["dma"]:
                    continue
                if po["eng"] == "pe" and o["eng"] == "pe" and not o["dma"]:
                    continue
                needed[d] = True
        with ExitStack() as es:
            esem = {e: es.enter_context(nc.semaphore("s_" + e)) for e in self.ENGS}
            dsem = [es.enter_context(nc.semaphore("d_%d" % i)) for i in range(NDMASEM)]
            cnt = {e: 0 for e in self.ENGS}
            ev = [None] * len(ops)
            dma_by_idx = {}
            for i, o in enumerate(ops):
                if o["dma"]:
                    k = o["dmaidx"]
                    ev[i] = (dsem[k % NDMASEM], 16 * (k // NDMASEM + 1))
                    dma_by_idx[k] = i
                elif needed[i]:
                    cnt[o["eng"]] += 1
                    ev[i] = (esem[o["eng"]], cnt[o["eng"]])
            per_eng = {e: [] for e in self.ENGS}
            for i, o in enumerate(ops):
                per_eng[o["eng"]].append(i)
            block = es.enter_context(nc.Block())

            def make(ename, handle_name):
                lst = per_eng[ename]
                if not lst:
                    return

                def body(h):
                    waited = {}
                    for i in lst:
                        o = ops[i]
                        evs = []
                        for d in sorted(o["deps"]):
                            po = ops[d]
                            if (not po["dma"]) and po["eng"] == "pe" and ename == "pe" and not o["dma"]:
                                continue
                            evs.append(ev[d])
                        if o["dma"] and o["dmaidx"] >= NDMASEM:
                            evs.append(ev[dma_by_idx[o["dmaidx"] - NDMASEM]])
                        for (s, v) in evs:
                            key = id(s)
                            if waited.get(key, 0) >= v:
                                continue
                            waited[key] = v
                            h.wait_ge(s, v)
                        ins = o["fn"](h)
                        if PROFILE_LINES is not None:
                            try:
                                PROFILE_NAMES[ins.ins.name] = o.get("line", 0)
                            except Exception:
                                pass
                        if ev[i] is not None:
                            s, v = ev[i]
                            ins.then_inc(s, 16 if o["dma"] else 1)
                    for i in lst:
                        o = ops[i]
                        if o["dma"]:
                            s, v = ev[i]
                            if waited.get(id(s), 0) < v:
                                waited[id(s)] = v
                                h.wait_ge(s, v)
                getattr(block, handle_name)(body)

            make("sp", "sync")
            make("pe", "tensor")
            make("act", "scalar")
            make("dve", "vector")
            make("pool", "gpsimd")


LEVELS = [1, 2, 4, 8, 16, 32, 64]


def host_consts():
    i = np.arange(128)
    ident = np.eye(128, dtype=np.float32)
    masks = np.zeros((128, 14, 128), np.float32)
    for li, l in enumerate(LEVELS):
        blk = i // (2 * l)
        half = (i // l) % 2
        M = (blk[:, None] == blk[None, :]) & (half[:, None] == 1) & (half[None, :] == 0)
        masks[:, li, :] = M
        masks[:, 7 + li, :] = M.T
    maskneg = np.where(i[None, :] >= i[:, None], 0.0, -30000.0).astype(np.float32)
    maskneg4 = np.tile(maskneg, (1, 4))
    masksu = (i[None, :] > i[:, None]).astype(np.float32)
    return dict(c_ident=ident, c_masks=masks.reshape(128, 14 * 128), c_maskneg4=maskneg4, c_masksu=masksu)


def build_par(NCH, layers=("gdn", "ssd"), dbg=None, RG=((0, 1, 2, 3), (4, 5, 6, 7))):
    nc = bass.Bass("TRN2", target_bir_lowering=False)

    def din(name, shape, dt=F32):
        return nc.dram_tensor(name, shape, dt, kind="ExternalInput").ap()

    def dout(name, shape, dt=F32):
        return nc.dram_tensor(name, shape, dt, kind="ExternalOutput").ap()

    x_d = din("x", [(NCH + 1) * 128, DM])
    ident_d = din("c_ident", [128, 128])
    masks_d = din("c_masks", [128, 14 * 128])
    maskneg_d = din("c_maskneg4", [128, 512])
    masksu_d = din("c_masksu", [128, 128])
    LD = {}
    for layer in layers:
        pf = layer[0] + "_"
        Hh = 16 if layer == "gdn" else 32
        d = dict(w_in=din(pf + "w_in", [DM, INW]), w_out=din(pf + "w_out", [2048, DM]), normw=din(pf + "normw_bc", [128, DM]),
                 diag=din(pf + "diag", [8, 128, 16 * 128]), a_log=din(pf + "a_log", [Hh, 1]), dt_bias=din(pf + "dt_bias", [Hh, 1]))
        if layer == "gdn":
            d["gnw"] = din(pf + "gnw_bc", [128, 128])
        else:
            d["convb"] = din(pf + "conv_b", [128, 32])
            d["dskip"] = din(pf + "dskip_bc", [128, 32])
            d["snw"] = din(pf + "snw_bc", [128, 2048])
        LD[layer] = d
    fnw_d = din("fnw_bc", [128, DM])
    mprev_d = din("mprev", [128, 4])
    RGL = [list(g) for g in RG]
    xo_d = dout("xo", [NCH * 128, DM])
    dbg_d = {}
    if dbg:
        for nm, (shp, dt_) in dbg.items():
            dbg_d[nm] = dout("dbg_" + nm, shp, dt_)
    diag_s = nc.dram_tensor("diag_s", [8, 128, 16 * 128], BF16, kind="Internal").ap()
    wout_s = nc.dram_tensor("wout_s", [16, 128, DM], BF16, kind="Internal").ap()
    gc_s_full = [nc.dram_tensor("gc_s%d" % i, [32, 128], F32, kind="Internal").ap() for i in range(2)]
    gl_s_full = [nc.dram_tensor("gl_s%d" % i, [32, 1], F32, kind="Internal").ap() for i in range(2)]
    xmid_s = nc.dram_tensor("xmid_s", [(NCH + 1) * 128, DM], F32, kind="Internal").ap()
    PWG, PWS = 2336, 1040
    prod_s = nc.dram_tensor("prod_s", [NCH, 128, 4 * PWG], BF16, kind="Internal").ap()
    zs_s = nc.dram_tensor("zs_s", [NCH, 128, 2048], BF16, kind="Internal").ap()
    srcS = nc.dram_tensor("srcS", [128, 2048], F32, kind="Internal").ap()
    gatS = [nc.dram_tensor("gatS%d" % i, [512, 2048], F32, kind="Internal").ap() for i in range(3)]
    gsrc = nc.dram_tensor("gsrc", [128, DM], F32, kind="Internal").ap()
    ggat = nc.dram_tensor("ggat", [512, DM], F32, kind="Internal").ap()
    hal_src = nc.dram_tensor("hal_src", [128, DM], F32, kind="Internal").ap()
    hal_gat = nc.dram_tensor("hal_gat", [512, DM], F32, kind="Internal").ap()

    P = Prog(nc)
    es = ExitStack()
    with es:
        def sb(name, shape, dt=F32):
            return es.enter_context(nc.sbuf_tensor(name, shape, dt))

        Wb = sb("Wb", [128, 8, INW], BF16)
        xt = [sb("xt%d" % i, [128, DM]) for i in range(2)]
        hid = sb("hid", [128, DM], BF16)
        hidT = sb("hidT", [128, 8, 128], BF16)
        normw = sb("normw", [128, DM])
        fnw = sb("fnw", [128, DM])
        Pbuf = [sb("Pbuf%d" % i, [128, 4, 131], BF16) for i in range(2)]
        NDG = 6
        diag = [sb("diag%d" % i, [128, 4, 128], BF16) for i in range(NDG)]
        convT = sb("convT", [128, 32, 128], BF16)
        zs = sb("zs", [128, 2048], BF16)
        carry = sb("carry", [128, 32, 3], BF16)
        identf = sb("identf", [128, 128])
        identb = sb("identb", [128, 128], BF16)
        onesb = sb("onesb", [128, 128], BF16)
        maskneg = sb("maskneg", [128, 512], BF16)
        onesrow = sb("onesrow", [1, 128])
        negonesrow = sb("negonesrow", [1, 128])
        onesH_f = sb("onesH", [32, 128])
        alog_f = sb("alog", [32, 1])
        negA_f = sb("negA", [32, 1])
        dtb_f = sb("dtb", [32, 1])
        ss = sb("ss", [128, 4])
        smF_f = sb("smF", [32, 6, 128])
        gcrow4 = [sb("gcrow4_%d" % i, [128, 512]) for i in range(2)]
        glrow_f = sb("glrow", [1, 32])
        smT_f = sb("smT", [128, 96])
        tokS_f = sb("tokS", [128, 4, 32])
        glbc_f = sb("glbc", [128, 32])
        vtok = sb("vtok", [128, 2048], BF16)
        S = sb("S", [128, 2048])
        Sb = sb("Sb", [128, 2048], BF16)
        o_t = [sb("o_t%d" % i, [128, 512]) for i in range(1)]
        ya = sb("ya", [128, 2048], BF16)
        yT = sb("yT", [128, 16, 128], BF16)
        nrm = sb("nrm", [128, 8])
        Gtot = sb("Gtot", [128, 32])
        LT = sb("LT", [128, 4, 128], BF16)
        LT2 = sb("LT2", [128, 4, 128], BF16)
        tmpf = sb("tmpf", [128, 512])
        vnew = sb("vnew", [128, 512], BF16)
        UN = 23232 // 2
        mprev = sb("mprev_sb", [128, 4])
        U = sb("U", [128, UN], BF16)
        ps = [es.enter_context(nc.psum_tensor("ps%d" % i, [128, 512], F32)) for i in range(8)]
        psb = [p[:].bitcast(BF16) for p in ps]

        B = {}

        def b(n):
            if n not in B:
                B[n] = Buf(n)
            return B[n]

        PSB = [b("ps%d" % i) for i in range(8)]

        def dma(eng, out, in_, reads, writes):
            P.op(eng, lambda h: h.dma_start(out=out, in_=in_), reads=reads, writes=writes, dma=True)


        dma("sp", identf[:], ident_d[:, :], [], [b("identf")])
        dma("pool", identb[:], ident_d[:, :], [], [b("identb")])
        dma("pool", maskneg[:], maskneg_d[:, :], [], [b("maskneg")])
        dma("sp", fnw[:], fnw_d[:, :], [], [b("fnw")])
        dma("sp", mprev[:], mprev_d[:, :], [], [b("mprev")])
        P.op("dve", lambda h: h.memset(onesb[:], 1.0), writes=[b("onesb")])
        P.op("dve", lambda h: h.memset(onesrow[:], 1.0), writes=[b("onesrow")])
        P.op("dve", lambda h: h.memset(negonesrow[:], -1.0), writes=[b("negonesrow")])
        P.op("dve", lambda h: h.memset(onesH_f[:], 1.0), writes=[b("onesH")])
        P.op("dve", lambda h: h.memset(ss[:], 0.0), writes=[b("ss")])
        P.op("dve", lambda h: h.memset(nrm[:], 0.0), writes=[b("nrm")])

        def mm(out, lhsT, rhs, start, stop, reads, writes):
            P.op("pe", lambda h: h.matmul(out, lhsT=lhsT, rhs=rhs, start=start, stop=stop), reads=reads, writes=writes)

        def tr(out, in_, ident, reads, writes):
            P.op("pe", lambda h: h.transpose(out=out, in_=in_, identity=ident), reads=reads, writes=writes)

        def act(out, in_, func, reads, writes, **kw):
            P.op("act", lambda h: h.activation(out=out, in_=in_, func=func, **kw), reads=reads, writes=writes)

        def tt(out, in0, in1, op, reads, writes, eng="dve"):
            P.op(eng, lambda h: h.tensor_tensor(out=out, in0=in0, in1=in1, op=op), reads=reads, writes=writes)

        def ts(out, in0, s1, s2, op0, op1, reads, writes, eng="dve"):
            if op1 is None:
                P.op(eng, lambda h: h.tensor_scalar(out=out, in0=in0, scalar1=s1, scalar2=None, op0=op0), reads=reads, writes=writes)
            else:
                P.op(eng, lambda h: h.tensor_scalar(out=out, in0=in0, scalar1=s1, scalar2=s2, op0=op0, op1=op1), reads=reads, writes=writes)

        def stt(out, in0, scalar, in1, op0, op1, reads, writes):
            P.op("dve", lambda h: h.scalar_tensor_tensor(out=out, in0=in0, scalar=scalar, in1=in1, op0=op0, op1=op1),
                 reads=reads, writes=writes)

        def memset(ap, val, writes):
            P.op("dve", lambda h: h.memset(ap, val), writes=writes)

        def recip(out, in_, reads, writes):
            P.op("dve", lambda h: h.reciprocal(out=out, in_=in_), reads=reads, writes=writes)

        def cp(out, in_, reads, writes):
            P.op("dve", lambda h: h.tensor_copy(out=out, in_=in_), reads=reads, writes=writes)

        def dbg_out(name, src_ap, reads, ci):
            if dbg and name in dbg_d and ci == dbg_chunk:
                dma("sp", dbg_d[name][:, :], src_ap, reads, [b("dbgo_" + name)])

        dbg_chunk = NCH
        wo_cnt = [0]
        dg_cnt = [0]
        G4 = lambda ap: ap.rearrange("p (a b) -> p a b", a=4)


        GDN_BUFS = ["masks", "masksu", "gnw", "sq4", "ke", "kd", "LTs", "NTm", "Nn", "Tm0", "Tm1", "Ym0", "Ym1", "Zm", "Zpm",
                    "ident4", "ub", "wT"]
        SSD_BUFS = ["ktok", "vdec", "convb", "dskip", "snw"]
        wo_cnt = [0]
        dg_cnt = [0]
        G4 = lambda ap: ap.rearrange("p (a b) -> p a b", a=4)

        def carve_factory():
            off = [0]

            def carve(nbytes, dt=BF16):
                n = nbytes // 2
                ap = U[:, off[0]:off[0] + n]
                off[0] += n
                assert off[0] <= UN
                if dt == F32:
                    ap = ap.bitcast(F32)
                return ap
            return carve

        for lidx, layer in enumerate(layers):
            gdn = layer == "gdn"
            first_layer = lidx == 0
            final_norm = lidx == len(layers) - 1
            H = 16 if gdn else 32
            DV = 2048 // H
            NHG = H // 4
            FW = 4 * DV
            CO = 0 if gdn else 2048
            ZO = 4096 if gdn else 0
            SO = 6144
            QO, KO, VO = (0, 8, 16) if gdn else (24, 16, 0)
            D_ = LD[layer]
            src_d = x_d if first_layer else xmid_s
            smF = smF_f[0:H]
            onesH = onesH_f[0:H]
            alog, negA, dtb = alog_f[0:H], negA_f[0:H], dtb_f[0:H]
            glrow = glrow_f[:, 0:H]
            smT = smT_f[:, 0:3 * H]
            tokS = tokS_f[:, :, 0:H]
            glbc = glbc_f[:, 0:H]
            gc_s = [g[0:H] for g in gc_s_full]
            gl_s = [g[0:H] for g in gl_s_full]
            carve = carve_factory()
            PA_BUFS = ["masks", "masksu", "sq4", "ke", "LTs", "NTm", "Nn", "Tm0", "Tm1", "Ym0", "Ym1", "Zm", "Zpm", "ident4", "stg",
                       "ktok", "convb", "dskip", "LTa", "stgB", "LTaB", "LT_B"]
            RB_BUFS = ["rb0", "rb1", "rb2", "tmpfB", "vnewB", "o_tB"]
            if lidx > 0:
                P.op("dve", lambda h: h.memset(ss[:, 3:4], 0.0), reads=[b(n) for n in RB_BUFS + ["gnw", "snw"]],
                     writes=[b(n) for n in PA_BUFS + ["gnw", "snw"]] + [b("ss")])
            PW = PWG if gdn else PWS
            SC = 1568 if gdn else 400
            if gdn:
                gnw = carve(512, F32)
            else:
                snw = carve(4096)
            carve_rb = carve_factory()
            carve_rb(512 if gdn else 4096)
            rbuf = [carve_rb(4672) for i in range(3)]
            tmpf_b = carve_rb(2048, F32)
            vnew_b = carve_rb(1024)
            o_t_b = carve_rb(2048, F32)
            tmpf_a, vnew_a = tmpf, vnew
            if gdn:
                masks = carve(3584).rearrange("p (a b) -> p a b", a=14)
                masksu = carve(256)
                sq4 = carve(1024)
                ke = G4(carve(1024))
                LTs = G4(carve(1024))
                NTm = G4(carve(1024))
                Nn = G4(carve(1024))
                Tm = [G4(carve(1024)) for i in range(2)]
                Ym = [G4(carve(1024)) for i in range(2)]
                Zm = G4(carve(1024))
                Zpm = G4(carve(1024))
                ident4 = G4(carve(1024))
                stg = carve(4672)
            else:
                ktok = carve(2048).rearrange("p (a b) -> p a b", a=8)
                convb = carve(128, F32)
                dskip = carve(128, F32)
                stg = carve(2080)
                LTa = G4(carve(1024))
                stgB = carve(2080)
                LTaB = G4(carve(1024))
                LT_B = G4(carve(1024))
            if gdn:
                LTa = None

            def views(blk):
                w_ = blk.shape[1]
                if gdn:
                    return dict(wT=G4(blk[:, 0:512]), kd=G4(blk[:, 512:1024]), ub=blk[:, 1024:1536],
                                sm=blk[:, 1536:1560].bitcast(F32), attnT=G4(blk[:, 1568:2080]) if w_ >= 2080 else None,
                                qT=blk[:, 2080:2336].rearrange("p (a b) -> p a b", a=2) if w_ >= 2336 else None)
                return dict(ktok=blk[:, 0:128], vdec=blk[:, 128:384], sm=blk[:, 384:400].bitcast(F32),
                            o0=blk[:, 400:912].bitcast(F32) if w_ >= 912 else None, CT=blk[:, 912:1040] if w_ >= 1040 else None)

            def s_chain(v, hg, VB, full, par=0):
                h0 = hg * 4
                SB_, SBb = b("S%d" % hg), b("Sb%d" % hg)
                ps_a, ps_b, ps_c = (4, 5, 6) if par == 0 else (7, 0, 1)
                tmpf, TMB = (tmpf_a, b("tmpf")) if par == 0 else (tmpf_b, b("tmpfB"))
                vnew, VNB = (vnew_a, b("vnew")) if par == 0 else (vnew_b, b("vnewB"))
                oh, OB = (o_t[0], b("o_t0")) if par == 0 else (o_t_b, b("o_tB"))
                sm = v["sm"]
                if gdn:
                    for hh in range(4):
                        hd = h0 + hh
                        mm(ps[ps_c][:, hh * 128:(hh + 1) * 128], v["wT"][:, hh, :], Sb[:, hd * 128:(hd + 1) * 128], True, True, VB + [SBb], [PSB[ps_c]])
                    tt(G4(tmpf[:]), G4(ps[ps_c][:, :]), sm[:, 0:4].unsqueeze(2).broadcast_to([128, 4, 128]), ALU.mult,
                       [PSB[ps_c]] + VB, [TMB])
                    tt(vnew[:], tmpf[:], v["ub"], ALU.add, [TMB] + VB, [VNB])
                    if full:
                        for hh in range(4):
                            hd = h0 + hh
                            mm(ps[ps_a][:, hh * 128:(hh + 1) * 128], v["qT"][:, hh // 2, :], Sb[:, hd * 128:(hd + 1) * 128], True, True,
                               VB + [SBb], [PSB[ps_a]])
                        for hh in range(4):
                            mm(ps[ps_b][:, hh * 128:(hh + 1) * 128], v["attnT"][:, hh, :], vnew[:, hh * 128:(hh + 1) * 128], True, True,
                               VB + [VNB], [PSB[ps_b]])
                        tt(G4(tmpf[:]), G4(ps[ps_a][:, :]), sm[:, 8:12].unsqueeze(2).broadcast_to([128, 4, 128]), ALU.mult,
                           [PSB[ps_a]] + VB, [TMB])
                        tt(oh[:, 0:FW], tmpf[:, 0:FW], ps[ps_b][:, 0:FW], ALU.add, [TMB, PSB[ps_b]], [OB])
                    for hh in range(4):
                        mm(ps[ps_c][:, hh * 128:(hh + 1) * 128], v["kd"][:, hh, :], vnew[:, hh * 128:(hh + 1) * 128], True, True,
                           VB + [VNB], [PSB[ps_c]])
                    gl = sm[:, 4:8]
                else:
                    if full:
                        mm(ps[ps_a][:, 0:FW], v["CT"], Sb[:, h0 * DV:(h0 + 4) * DV], True, True, VB + [SBb], [PSB[ps_a]])
                        tt(G4(tmpf[:, 0:FW]), G4(ps[ps_a][:, 0:FW]), sm[:, 4:8].unsqueeze(2).broadcast_to([128, 4, DV]), ALU.mult,
                           [PSB[ps_a]] + VB, [TMB])
                        tt(oh[:, 0:FW], tmpf[:, 0:FW], v["o0"], ALU.add, [TMB] + VB, [OB])
                    mm(ps[ps_c][:, 0:FW], v["ktok"], v["vdec"], True, True, VB, [PSB[ps_c]])
                    gl = sm[:, 0:4]
                tt(G4(tmpf[:, 0:FW]), G4(S[:, hg * FW:(hg + 1) * FW]), gl.unsqueeze(2).broadcast_to([128, 4, DV]),
                   ALU.mult, [SB_] + VB, [TMB])
                tt(S[:, hg * FW:(hg + 1) * FW], tmpf[:, 0:FW], ps[ps_c][:, 0:FW], ALU.add, [TMB, PSB[ps_c]], [SB_])
                act(Sb[:, hg * FW:(hg + 1) * FW], S[:, hg * FW:(hg + 1) * FW], AF.Copy, [SB_], [SBb])

            def blk_d(l, hg, lo, hi):
                return prod_s[l][:, hg * PW + lo:hg * PW + hi]

            dma("sp", normw[:], D_["normw"][:, :], [], [b("normw")])
            dma("sp", alog[:], D_["a_log"][:, :], [], [b("alog")])
            dma("sp", dtb[:], D_["dt_bias"][:, :], [], [b("dtb")])
            if gdn:
                dma("pool", masks[:].rearrange("p a b -> p (a b)"), masks_d[:, :], [], [b("masks")])
                dma("pool", masksu[:], masksu_d[:, :], [], [b("masksu")])
                dma("sp", gnw[:], D_["gnw"][:, :], [], [b("gnw")])
            else:
                dma("sp", convb[:], D_["convb"][:, :], [], [b("convb")])
                dma("sp", dskip[:], D_["dskip"][:, :], [], [b("dskip")])
                dma("pool", snw[:], D_["snw"][:, :], [], [b("snw")])
            P.op("dve", lambda h: h.memset(Gtot[:], 1.0), writes=[b("Gtot")])
            P.op("dve", lambda h: h.memset(carry[:], 0.0), writes=[b("carry")])
            P.op("dve", lambda h: h.memset(S[:], 0.0), writes=[b("S%d" % i) for i in range(8)])
            P.op("dve", lambda h: h.memset(Sb[:], 0.0), writes=[b("Sb%d" % i) for i in range(8)])
            P.op("act", (lambda negA, alog: lambda h: h.activation(out=negA[:], in_=alog[:], func=AF.Exp))(negA, alog),
                 reads=[b("alog")], writes=[b("negA")])
            P.op("dve", (lambda negA: lambda h: h.tensor_scalar(out=negA[:], in0=negA[:], scalar1=-1.0, scalar2=None, op0=ALU.mult))(negA),
                 reads=[b("negA")], writes=[b("negA")])
            if gdn:
                for i in range(4):
                    P.op("dve", (lambda i, ident4: lambda h: h.tensor_copy(out=ident4[:, i, :], in_=identb[:]))(i, ident4),
                         reads=[b("identb")], writes=[b("ident4")])
            def load_wb(lay):
                for k in range(8):
                    for (f0, f1) in [(0, 2048), (2048, 4096), (4096, INW)]:
                        dma("pool", Wb[:, k, f0:f1], LD[lay]["w_in"][k * 128:(k + 1) * 128, f0:f1], [], [b("Wb")])
            if lidx == 0:
                load_wb(layer)
            for kt2 in range(8):
                dma("pool", zs[:].rearrange("p (a c) -> p a c", a=2),
                    D_["w_out"][kt2 * 256:(kt2 + 1) * 256, :].rearrange("(a p) c -> p a c", a=2), [], [b("zs")])
                dma("sp", wout_s[kt2 * 2:(kt2 + 1) * 2].rearrange("a p c -> p a c"),
                    zs[:].rearrange("p (a c) -> p a c", a=2), [b("zs")], [b("wout_s")])
            for c4 in range(8):
                dma("pool", zs[:], D_["diag"][c4], [], [b("zs")])
                dma("sp", diag_s[c4], zs[:], [b("zs")], [b("diag_s")])
            dbg_chunk = NCH
            def A_step(cj):
                xb_ = xt[cj % 2]
                XB = b("xt%d" % (cj % 2))
                dma("sp", xb_[:], src_d[cj * 128:(cj + 1) * 128, :], [] if first_layer else [b("xmid%d" % cj)], [XB])
                memset(ss[:, 0:1], 0.0, [b("ss")])
                act(hid[:], xb_[:], AF.Square, [XB], [b("hid"), b("ss")], accum_out=ss[:, 0:1])
                act(ss[:, 1:2], ss[:, 0:1], AF.Sqrt, [b("ss")], [b("ss")], scale=1.0 / DM, bias=EPS)
                recip(ss[:, 2:3], ss[:, 1:2], [b("ss")], [b("ss")])
                stt(hid[:], xb_[:], ss[:, 2:3], normw[:], ALU.mult, ALU.mult, [XB, b("ss"), b("normw")], [b("hid")])

            def A2_step():
                for k in range(8):
                    tr(psb[0][:, k * 128:(k + 1) * 128], hid[:, k * 128:(k + 1) * 128], identb[:], [b("hid"), b("identb")], [PSB[0]])
                act(hidT[:].rearrange("p a b -> p (a b)"), psb[0][:, 0:1024], AF.Copy, [PSB[0]], [b("hidT")])

            A_step(0)
            A2_step()
            for ci in range(NCH + 1):
                halo = ci == 0
                def inproj_group(c4):
                    pa = 1 + (c4 % 2)
                    for i in range(4):
                        f0 = CO + (c4 * 4 + i) * 128
                        for k in range(8):
                            mm(ps[pa][:, i * 128:(i + 1) * 128], Wb[:, k, f0:f0 + 128], hidT[:, k, :], k == 0, k == 7,
                               [b("Wb"), b("hidT")], [PSB[pa]])

                inproj_group(0)
                for c4 in range(8):
                    pa = 1 + (c4 % 2)
                    pc = 3 + (c4 % 2)
                    pbuf = Pbuf[c4 % 2]
                    PB = b("Pbuf%d" % (c4 % 2))
                    if c4 + 1 < 8:
                        inproj_group(c4 + 1)
                    cp(pbuf[:, :, 0:3], carry[:, c4 * 4:(c4 + 1) * 4, :], [b("carry")], [PB])
                    act(pbuf[:, :, 3:131], G4(ps[pa][:]), AF.Copy, [PSB[pa]], [PB])
                    cp(carry[:, c4 * 4:(c4 + 1) * 4, :], pbuf[:, :, 128:131], [PB], [b("carry")])
                    if halo:
                        continue
                    for i in range(4):
                        ct = c4 * 4 + i
                        di = dg_cnt[0] % NDG
                        dg_cnt[0] += 1
                        dg = diag[di]
                        DG = b("diag%d" % di)
                        dma("sp", dg[:].rearrange("p a b -> p (a b)"), diag_s[c4][:, i * 512:(i + 1) * 512], [b("diag_s")], [DG])
                        for j in range(4):
                            mm(ps[pc][:, i * 128:(i + 1) * 128], dg[:, j, :], pbuf[:, i, j:j + 128], j == 0, j == 3, [DG, PB], [PSB[pc]])
                        if not gdn:
                            act(convT[:, ct, :], ps[pc][:, i * 128:(i + 1) * 128], AF.Silu, [PSB[pc], b("convb")], [b("convT%d" % c4)],
                                bias=convb[:, ct:ct + 1])
                    if gdn:
                        act(convT[:, c4 * 4:(c4 + 1) * 4, :], G4(ps[pc][:]), AF.Silu, [PSB[pc]], [b("convT%d" % c4)])
                if halo:
                    A_step(1)
                    A2_step()
                    continue
                dbg_out("convT", convT[:].rearrange("p a b -> p (a b)"), [b("convT%d" % i) for i in range(8)], ci)
                for f in range(4):
                    pz = 5 + (f % 2)
                    if gdn:
                        CB = b("convT%d" % f)
                        cv = convT[:, f * 4:(f + 1) * 4, :].rearrange("p a b -> p (a b)")
                        tt(sq4[:], cv, cv, ALU.mult, [CB], [b("sq4")])
                    for k in range(8):
                        mm(ps[pz][:, :], hidT[:, k, :], Wb[:, k, ZO + f * 512:ZO + (f + 1) * 512], k == 0, k == 7,
                           [b("Wb"), b("hidT")], [PSB[pz]])
                    if gdn:
                        mm(ps[1][:, :], onesb[:], sq4[:], True, True, [b("onesb"), b("sq4")], [PSB[1]])
                    act(zs[:, f * 512:(f + 1) * 512], ps[pz][:, :], AF.Silu, [PSB[pz]], [b("zs")])
                    if gdn:
                        act(tmpf[:], ps[1][:, :], AF.Sqrt, [PSB[1]], [b("tmpf")], bias=EPS)
                        recip(tmpf[:], tmpf[:], [b("tmpf")], [b("tmpf")])
                        stt(cv, cv, (128.0 ** -0.5) if f < 2 else 1.0, tmpf[:], ALU.mult, ALU.mult, [CB, b("tmpf")], [CB])
                SM = b("smF")
                if gdn:
                    for k in range(8):
                        mm(ps[7][0:16, 0:128], Wb[:, k, SO:SO + 16], hidT[:, k, :], k == 0, k == 7, [b("Wb"), b("hidT")], [PSB[7]])
                    for k in range(8):
                        mm(ps[7][0:16, 128:256], Wb[:, k, SO + 16:SO + 32], hidT[:, k, :], k == 0, k == 7, [b("Wb"), b("hidT")], [PSB[7]])
                    act(smF[:, 0, :], ps[7][0:16, 0:128], AF.Sigmoid, [PSB[7]], [SM])
                    act(smF[:, 1, :], ps[7][0:16, 128:256], AF.Exp, [PSB[7], b("dtb")], [SM], bias=dtb[:, 0:1])
                else:
                    for k in range(8):
                        mm(ps[7][0:32, 0:128], Wb[:, k, SO:SO + 32], hidT[:, k, :], k == 0, k == 7, [b("Wb"), b("hidT")], [PSB[7]])
                    act(smF[:, 1, :], ps[7][0:32, 0:128], AF.Exp, [PSB[7], b("dtb")], [SM], bias=dtb[:, 0:1])
                act(smF[:, 2, :], smF[:, 1, :], AF.Ln, [SM], [SM], bias=1.0)
                if not gdn:
                    cp(smF[:, 0, :], smF[:, 2, :], [SM], [SM])
                ts(smF[:, 3, :], smF[:, 2, :], negA[:, 0:1], None, ALU.mult, None, [SM, b("negA")], [SM])
                P.op("dve", (lambda smF, onesH: lambda h: h.tensor_tensor_scan(
                    out=smF[:, 4, :], data0=onesH[:], data1=smF[:, 3, :], initial=0.0, op0=ALU.mult, op1=ALU.add))(smF, onesH),
                     reads=[SM, b("onesH")], writes=[SM])
                ts(smF[:, 5, :], smF[:, 4, :], -1.0, smF[:, 4, 127:128], ALU.mult, ALU.add, [SM], [SM])
                gcs, GCS = gc_s[ci % 2], b("gc_s%d" % (ci % 2))
                gls, GLS = gl_s[ci % 2], b("gl_s%d" % (ci % 2))
                dma("sp", gcs[:, :], smF[:, 4, :], [SM], [GCS])
                dma("sp", gls[:, :], smF[:, 4, 127:128], [SM], [GLS])
                dma("sp", glrow[0:1, :], gls.rearrange("h o -> o h"), [GLS], [b("glrow")])
                for t_, src in enumerate([0, 4, 5]):
                    tr(ps[7][:, 256 + t_ * H:256 + (t_ + 1) * H], smF[:, src, :], identf[0:H, 0:H], [SM, b("identf")], [PSB[7]])
                cp(smT[:], ps[7][:, 256:256 + 3 * H], [PSB[7]], [b("smT")])
                TS = b("tokS")
                act(tokS[:, 0, :], smT[:, H:2 * H], AF.Exp, [b("smT")], [TS])
                act(tokS[:, 1, :], smT[:, 2 * H:3 * H], AF.Exp, [b("smT")], [TS])
                if gdn:
                    ts(tokS[:, 3, :], smT[:, 0:H], -1.0, None, ALU.mult, None, [b("smT")], [TS])
                else:
                    tt(tokS[:, 2, :], smT[:, 0:H], tokS[:, 1, :], ALU.mult, [b("smT"), TS], [TS])
                mm(ps[7][:, 384:384 + H], onesrow[0:1, 0:128], glrow[0:1, :], True, True, [b("onesrow"), b("glrow")], [PSB[7]])
                act(glbc[:], ps[7][:, 384:384 + H], AF.Exp, [PSB[7]], [b("glbc")])
                if not gdn:
                    tt(Gtot[:, 0:H], Gtot[:, 0:H], glbc[:], ALU.mult, [b("Gtot"), b("glbc")], [b("Gtot")])
                dbg_out("smT", smT[:], [b("smT")], ci)
                dbg_out("qkn", convT[:, 0:16, :].rearrange("p a b -> p (a b)"), [b("convT%d" % i) for i in range(4)], ci)
                for half in range(2):
                    pv = 1 + half
                    for i in range(8):
                        ct = VO + half * 8 + i
                        tr(psb[pv][:, i * 128:(i + 1) * 128], convT[:, ct, :], identb[:], [b("convT%d" % (ct // 4)), b("identb")], [PSB[pv]])
                    act(vtok[:, half * 1024:(half + 1) * 1024], psb[pv][:, 0:1024], AF.Copy, [PSB[pv]], [b("vtok")])
                for i in range(8):
                    ct = KO + i
                    tr(psb[3][:, i * 128:(i + 1) * 128], convT[:, ct, :], identb[:], [b("convT%d" % (ct // 4)), b("identb")], [PSB[3]])
                if not gdn:
                    act(ktok[:].rearrange("p a b -> p (a b)"), psb[3][:, 0:1024], AF.Copy, [PSB[3]], [b("ktok")])
                if ci + 1 <= NCH:
                    A_step(ci + 1)
                def lt_of(hg_):
                    if hg_ % 2 == 0:
                        return LT, b("LT")
                    return (LT2, b("LT2")) if gdn else (LT_B, b("LT_B"))

                def stage1_dma(hg_):
                    gr_, GR_ = gcrow4[hg_ % 2], b("gcrow4_%d" % (hg_ % 2))
                    dma("sp", gr_[:, :], gcs[hg_ * 4:hg_ * 4 + 4, :].rearrange("(o h) c -> o (h c)", o=1).broadcast_to([128, 512]), [GCS], [GR_])

                def stage1_compute(hg_):
                    gr_, GR_ = gcrow4[hg_ % 2], b("gcrow4_%d" % (hg_ % 2))
                    lt_, LTB_ = lt_of(hg_)
                    tt(G4(gr_[:, :]), G4(gr_[:, :]), smT[:, H + hg_ * 4:H + hg_ * 4 + 4].unsqueeze(2).broadcast_to([128, 4, 128]), ALU.subtract,
                       [GR_, b("smT")], [GR_])
                    tt(gr_[:, :], gr_[:, :], maskneg[:], ALU.add, [GR_, b("maskneg")], [GR_])
                    act(lt_[:].rearrange("p a b -> p (a b)"), gr_[:, :], AF.Exp, [GR_], [LTB_])

                stage1_dma(0)
                stage1_compute(0)
                for hg in range(NHG):
                    h0 = hg * 4
                    if hg + 1 < NHG:
                        stage1_dma(hg + 1)
                    par = 0 if gdn else hg % 2
                    if par == 0:
                        stg_p, STG, LT_p, LTB, LTa_p, LAB = stg, b("stg"), LT, b("LT"), LTa, b("LTa")
                        vnew_p, VNB, tmpf_p, TMB, qb = vnew, b("vnew"), tmpf, b("tmpf"), 5
                    else:
                        stg_p, STG, LT_p, LTB, LTa_p, LAB = stgB, b("stgB"), LT_B, b("LT_B"), LTaB, b("LTaB")
                        vnew_p, VNB, tmpf_p, TMB, qb = vnew_b, b("vnewB"), tmpf_b, b("tmpfB"), 0
                    SV = views(stg_p)
                    attnT = SV["attnT"] if gdn else LTa_p
                    if hg == 2 and ci + 1 <= NCH:
                        A2_step()
                    LT_p, LTB = lt_of(hg)
                    if gdn:
                        for qq in range(2):
                            g = hg * 2 + qq
                            mm(ps[5][:, qq * 128:(qq + 1) * 128], convT[:, KO + g, :], convT[:, KO + g, :], True, True,
                               [b("convT%d" % ((KO + g) // 4))], [PSB[5]])
                            mm(ps[5][:, 256 + qq * 128:256 + (qq + 1) * 128], convT[:, KO + g, :], convT[:, QO + g, :], True, True,
                               [b("convT%d" % ((KO + g) // 4)), b("convT%d" % ((QO + g) // 4))], [PSB[5]])
                        tt(LTs[:], LT_p[:], masksu[:].unsqueeze(1).broadcast_to([128, 4, 128]), ALU.mult, [LTB, b("masksu")], [b("LTs")])
                        for hh in range(4):
                            stt(NTm[:, hh, :], ps[5][:, (hh // 2) * 128:(hh // 2 + 1) * 128], tokS[:, 3, h0 + hh:h0 + hh + 1], LTs[:, hh, :],
                                ALU.mult, ALU.mult, [PSB[5], TS, b("LTs")], [b("NTm")])
                        kin = psb[3][:, hg * 256:(hg + 1) * 256].rearrange("p (g d) -> p g d", g=2).unsqueeze(2).broadcast_to([128, 2, 2, 128])
                        for (dst, DB, row) in [(ke, b("ke"), 0), (SV["kd"], b("stg"), 1)]:
                            tt(dst[:].rearrange("p (g r) d -> p g r d", g=2), kin,
                               tokS[:, row, h0:h0 + 4].rearrange("p (g r) -> p g r", g=2).unsqueeze(3).broadcast_to([128, 2, 2, 128]),
                               ALU.mult, [PSB[3], TS], [DB])
                        tt(attnT[:].rearrange("p (q r) d -> p q r d", q=2),
                           ps[5][:, 256:512].rearrange("p (q d) -> p q d", q=2).unsqueeze(2).broadcast_to([128, 2, 2, 128]),
                           LT_p[:].rearrange("p (q r) d -> p q r d", q=2), ALU.mult, [PSB[5], LTB], [b("stg")])
                    else:
                        g = hg
                        mm(ps[qb][:, 0:128], convT[:, KO + g, :], convT[:, QO + g, :], True, True,
                           [b("convT%d" % ((KO + g) // 4)), b("convT%d" % ((QO + g) // 4))], [PSB[qb]])
                        tt(attnT[:], ps[qb][:, 0:128].unsqueeze(1).broadcast_to([128, 4, 128]), LT_p[:], ALU.mult, [PSB[qb], LTB], [LAB])
                    if gdn:
                        for hh in range(4):
                            tr(psb[6][:, hh * 128:(hh + 1) * 128], NTm[:, hh, :], identb[:], [b("NTm"), b("identb")], [PSB[6]])
                        act(Nn[:].rearrange("p a b -> p (a b)"), psb[6][:, 0:512], AF.Copy, [PSB[6]], [b("Nn")])
                        cur = 0
                        Tc, Yc = ident4, ident4
                        TCB, YCB = b("ident4"), b("ident4")
                        for li in range(7):
                            Tn_, Yn_ = Tm[cur], Ym[cur]
                            TNB, YNB = b("Tm%d" % cur), b("Ym%d" % cur)
                            last = li == 6
                            Ml = masks[:, li, :].unsqueeze(1).broadcast_to([128, 4, 128])
                            MlT = masks[:, 7 + li, :].unsqueeze(1).broadcast_to([128, 4, 128])
                            if li == 3 and hg + 1 < NHG:
                                stage1_compute(hg + 1)
                            if li == 0:
                                tt(Tn_[:], Nn[:], Ml, ALU.mult, [b("Nn"), b("masks")], [TNB])
                                tt(Tn_[:], Tn_[:], ident4[:], ALU.add, [TNB, b("ident4")], [TNB])
                                tt(Yn_[:], NTm[:], MlT, ALU.mult, [b("NTm"), b("masks")], [YNB])
                                tt(Yn_[:], Yn_[:], ident4[:], ALU.add, [YNB, b("ident4")], [YNB])
                                Tc, TCB, Yc, YCB = Tn_, TNB, Yn_, YNB
                                cur ^= 1
                                continue
                            if not last:
                                for hh in range(4):
                                    mm(ps[4][:, hh * 128:(hh + 1) * 128], NTm[:, hh, :], Tc[:, hh, :], True, True, [b("NTm"), TCB], [PSB[4]])
                                tt(Zm[:], G4(ps[4][:, :]), Ml, ALU.mult, [PSB[4], b("masks")], [b("Zm")])
                            for hh in range(4):
                                mm(ps[5][:, hh * 128:(hh + 1) * 128], Nn[:, hh, :], Yc[:, hh, :], True, True, [b("Nn"), YCB], [PSB[5]])
                            tt(Zpm[:], G4(ps[5][:, :]), MlT, ALU.mult, [PSB[5], b("masks")], [b("Zpm")])
                            if not last:
                                for hh in range(4):
                                    mm(ps[6][:, hh * 128:(hh + 1) * 128], Yc[:, hh, :], Zm[:, hh, :], True, True, [YCB, b("Zm")], [PSB[6]])
                                tt(Tn_[:], Tc[:], G4(ps[6][:, :]), ALU.add, [PSB[6], TCB], [TNB])
                            for hh in range(4):
                                mm(ps[7][:, hh * 128:(hh + 1) * 128], Tc[:, hh, :], Zpm[:, hh, :], True, True, [TCB, b("Zpm")], [PSB[7]])
                            tt(Yn_[:], Yc[:], G4(ps[7][:, :]), ALU.add, [PSB[7], YCB], [YNB])
                            if not last:
                                Tc, TCB = Tn_, TNB
                            Yc, YCB = Yn_, YNB
                            cur ^= 1
                        for hh in range(4):
                            hd = h0 + hh
                            mm(ps[4][:, hh * 128:(hh + 1) * 128], Yc[:, hh, :], vtok[:, hd * 128:(hd + 1) * 128], True, True,
                               [YCB, b("vtok")], [PSB[4]])
                        tt(G4(SV["ub"]), G4(ps[4][:, :]), smT[:, h0:h0 + 4].unsqueeze(2).broadcast_to([128, 4, 128]), ALU.mult,
                           [PSB[4], b("smT")], [b("stg")])
                        for hh in range(4):
                            mm(ps[5][:, hh * 128:(hh + 1) * 128], ke[:, hh, :], Yc[:, hh, :], True, True, [b("ke"), YCB], [PSB[5]])
                        act(SV["wT"].rearrange("p a b -> p (a b)"), ps[5][:, :], AF.Copy, [PSB[5]], [b("stg")])
                        cp(SV["sm"][:, 0:4], tokS[:, 3, h0:h0 + 4], [TS], [b("stg")])
                        cp(SV["sm"][:, 4:8], glbc[:, h0:h0 + 4], [b("glbc")], [b("stg")])
                        cp(SV["sm"][:, 8:12], tokS[:, 0, h0:h0 + 4], [TS], [b("stg")])
                        dma("pool", blk_d(ci - 1, hg, 0, 2080), stg[:, 0:2080], [b("stg")], [b("prod%d_%d" % (ci - 1, hg))])
                        dma("pool", blk_d(ci - 1, hg, 2080, 2336).rearrange("p (a b) -> p a b", a=2), convT[:, QO + 2 * hg:QO + 2 * hg + 2, :],
                            [b("convT%d" % ((QO + 2 * hg) // 4))], [b("prodq%d_%d" % (ci - 1, hg))])
                    else:
                        xin = G4(vtok[:, h0 * DV:(h0 + 4) * DV])
                        tt(G4(vnew_p[:, 0:FW]), xin, smT[:, h0:h0 + 4].unsqueeze(2).broadcast_to([128, 4, DV]),
                           ALU.mult, [b("vtok"), b("smT")], [VNB])
                        tt(G4(SV["vdec"]), xin, tokS[:, 2, h0:h0 + 4].unsqueeze(2).broadcast_to([128, 4, DV]),
                           ALU.mult, [b("vtok"), TS], [STG])
                        for hh in range(4):
                            mm(ps[qb][:, hh * DV:(hh + 1) * DV], attnT[:, hh, :], vnew_p[:, hh * DV:(hh + 1) * DV], True, True,
                               [LAB, VNB], [PSB[qb]])
                        if hg + 1 < NHG:
                            stage1_compute(hg + 1)
                        tt(G4(tmpf_p[:, 0:FW]), xin, dskip[:, h0:h0 + 4].unsqueeze(2).broadcast_to([128, 4, DV]),
                           ALU.mult, [b("vtok"), b("dskip")], [TMB])
                        tt(SV["o0"], tmpf_p[:, 0:FW], ps[qb][:, 0:FW], ALU.add, [TMB, PSB[qb]], [STG])
                        cp(SV["sm"][:, 0:4], glbc[:, h0:h0 + 4], [b("glbc")], [STG])
                        cp(SV["sm"][:, 4:8], tokS[:, 0, h0:h0 + 4], [TS], [STG])
                        dma("pool", blk_d(ci - 1, hg, 128, 912), stg_p[:, 128:912], [STG], [b("prod%d_%d" % (ci - 1, hg))])
                        dma("pool", blk_d(ci - 1, hg, 0, 128), ktok[:, hg, :], [b("ktok")], [b("prodk%d_%d" % (ci - 1, hg))])
                        dma("pool", blk_d(ci - 1, hg, 912, 1040), convT[:, QO + hg, :], [b("convT%d" % ((QO + hg) // 4))],
                            [b("prodq%d_%d" % (ci - 1, hg))])
                dma("pool", zs_s[ci - 1], zs[:], [b("zs")], [b("zs_s%d" % (ci - 1))])

            P.op("dve", lambda h: h.memset(ss[:, 3:4], 0.0), reads=[b(n) for n in PA_BUFS], writes=[b(n) for n in RB_BUFS] + [b("ss")])
            SALL = [b("S%d" % i) for i in range(8)]
            SBALL = [b("Sb%d" % i) for i in range(8)]
            rb_cnt = [0]

            def load_blk(l, hg, width):
                i = rb_cnt[0] % 3
                rb_cnt[0] += 1
                RB = b("rb%d" % i)
                deps = [b("prod%d_%d" % (l, hg))]
                if not gdn:
                    deps.append(b("prodk%d_%d" % (l, hg)))
                if width > SC:
                    deps.append(b("prodq%d_%d" % (l, hg)))
                dma("sp", rbuf[i][:, 0:width], blk_d(l, hg, 0, width), deps, [RB])
                return rbuf[i][:, 0:width], RB

            for l in range(NCH):
                for hg in range(NHG):
                    blk, RB = load_blk(l, hg, SC)
                    s_chain(views(blk), hg, [RB], False, par=hg % 2)
            if gdn:
                for rnd in range(3):
                    dma("pool", srcS[:, :], S[:], SALL, [b("srcS")])
                    P.op("pool", (lambda rnd: lambda h: h.collective_compute("AllGather", ALU.bypass, replica_groups=RGL,
                                                                            ins=[srcS[:, :]], outs=[gatS[rnd][:, :]]))(rnd),
                         reads=[b("srcS")], writes=[b("gatS%d" % rnd)])
                    if rnd == 2:
                        break
                    dma("sp", S[:], gatS[rnd][rnd * 128:(rnd + 1) * 128, :], [b("gatS%d" % rnd)], SALL)
                    act(Sb[:], S[:], AF.Copy, SALL, SBALL)
                    for l in range(NCH):
                        for hg in range(NHG):
                            blk, RB = load_blk(l, hg, SC)
                            s_chain(views(blk), hg, [RB], False, par=hg % 2)
                for pc_ in range(4):
                    sl = slice(pc_ * 512, (pc_ + 1) * 512)
                    SP_ = [b("S%d" % i) for i in range(8) if (i * FW) // 512 == pc_]
                    for j in range(3):
                        dma("sp", tmpf[:], gatS[j][j * 128:(j + 1) * 128, sl], [b("gatS%d" % j)], [b("tmpf")])
                        if j == 0:
                            ts(S[:, sl], tmpf[:], mprev[:, 0:1], None, ALU.mult, None, [b("tmpf"), b("mprev")], SP_)
                        else:
                            stt(S[:, sl], tmpf[:], mprev[:, j:j + 1], S[:, sl], ALU.mult, ALU.add, [b("tmpf"), b("mprev")] + SP_, SP_)
            else:
                dma("pool", srcS[:, :], S[:], SALL, [b("srcS")])
                P.op("pool", lambda h: h.collective_compute("AllGather", ALU.bypass, replica_groups=RGL, ins=[srcS[:, :]], outs=[gatS[0][:, :]]),
                     reads=[b("srcS")], writes=[b("gatS0")])
                memset(xt[0][:], 0.0, [b("xt0")])
                cp(xt[0][:, 0:32], Gtot[:], [b("Gtot")], [b("xt0")])
                dma("pool", gsrc[:, :], xt[0][:], [b("xt0")], [b("gsrc")])
                P.op("pool", lambda h: h.collective_compute("AllGather", ALU.bypass, replica_groups=RGL, ins=[gsrc[:, :]], outs=[ggat[:, :]]),
                     reads=[b("gsrc")], writes=[b("ggat")])
                for pc_ in range(4):
                    sl = slice(pc_ * 512, (pc_ + 1) * 512)
                    SP_ = [b("S%d" % i) for i in range(8) if (i * FW) // 512 == pc_]
                    memset(tmpf[:], 0.0, [b("tmpf")])
                    memset(S[:, sl], 0.0, SP_)
                    for j in range(3):
                        dma("sp", tmpf_b[:], gatS[0][j * 128:(j + 1) * 128, sl], [b("gatS0")], [b("tmpfB")])
                        dma("sp", nrm[:, 0:8], ggat[j * 128:(j + 1) * 128, pc_ * 8:pc_ * 8 + 8], [b("ggat")], [b("nrm")])
                        tt(tmpf[:].rearrange("p (a b) -> p a b", a=8), tmpf[:].rearrange("p (a b) -> p a b", a=8),
                           nrm[:, 0:8].unsqueeze(2).broadcast_to([128, 8, 64]), ALU.mult, [b("tmpf"), b("nrm")], [b("tmpf")])
                        tt(tmpf[:], tmpf[:], tmpf_b[:], ALU.add, [b("tmpf"), b("tmpfB")], [b("tmpf")])
                        stt(S[:, sl], tmpf[:], mprev[:, j:j + 1], S[:, sl], ALU.mult, ALU.add, [b("tmpf"), b("mprev")] + SP_, SP_)
            act(Sb[:], S[:], AF.Copy, SALL, SBALL)
            if lidx + 1 < len(layers):
                load_wb(layers[lidx + 1])
            wo_list = [
                (convT[:, 8 * q:8 * q + 8, :].rearrange("p a b -> p (a b)"), [b("convT%d" % (2 * q)), b("convT%d" % (2 * q + 1))])
                for q in range(4)]

            def L_front():
                for half in range(2):
                    pv = 2 + half
                    for i in range(8):
                        kt = half * 8 + i
                        tr(psb[pv][:, i * 128:(i + 1) * 128], ya[:, kt * 128:(kt + 1) * 128], identb[:], [b("ya"), b("identb")], [PSB[pv]])
                    act(yT[:, half * 8:(half + 1) * 8, :].rearrange("p a b -> p (a b)"), psb[pv][:, 0:1024], AF.Copy, [PSB[pv]], [b("yT")])

            def L_mid(kts):
                for kt in kts:
                    wo_ap, WBL = wo_list[wo_cnt[0] % len(wo_list)]
                    wo_cnt[0] += 1
                    dma("sp", wo_ap, wout_s[kt], [b("wout_s")], WBL)
                    for n in range(2):
                        mm(ps[2 + n][:, :], yT[:, kt, :], wo_ap[:, n * 512:(n + 1) * 512], kt == 0, kt == 15, [b("yT")] + WBL, [PSB[2 + n]])

            def L_back(cj):
                xb_ = xt[cj % 2]
                XB = b("xt%d" % (cj % 2))
                for n in range(2):
                    tt(xb_[:, n * 512:(n + 1) * 512], xb_[:, n * 512:(n + 1) * 512], ps[2 + n][:, :], ALU.add, [XB, PSB[2 + n]], [XB])
                if final_norm:
                    memset(ss[:, 0:1], 0.0, [b("ss")])
                    act(hid[:], xb_[:], AF.Square, [XB], [b("hid"), b("ss")], accum_out=ss[:, 0:1])
                    act(ss[:, 1:2], ss[:, 0:1], AF.Sqrt, [b("ss")], [b("ss")], scale=1.0 / DM, bias=EPS)
                    recip(ss[:, 2:3], ss[:, 1:2], [b("ss")], [b("ss")])
                    stt(xb_[:], xb_[:], ss[:, 2:3], fnw[:], ALU.mult, ALU.mult, [XB, b("ss"), b("fnw")], [XB])
                    dma("pool", xo_d[(cj - 1) * 128:cj * 128, :], xb_[:], [XB], [b("xo%d" % cj)])
                else:
                    dma("pool", xmid_s[cj * 128:(cj + 1) * 128, :], xb_[:], [XB], [b("xmid%d" % cj)])
                    if cj == NCH:
                        dma("pool", hal_src[:, :], xb_[:], [XB], [b("hal_src")])

            KPH = 16 // NHG
            fsteps = [(l_, hg_) for l_ in range(NCH) for hg_ in range(NHG)]
            floaded = {}
            fnext = [0]

            def prefetch_upto(n):
                while fnext[0] <= min(n, len(fsteps) - 1):
                    floaded[fsteps[fnext[0]]] = load_blk(fsteps[fnext[0]][0], fsteps[fnext[0]][1], PW)
                    fnext[0] += 1

            prefetch_upto(1)
            for ci in range(1, NCH + 1):
                l = ci - 1
                xb_ = xt[ci % 2]
                XB = b("xt%d" % (ci % 2))
                dma("sp", xb_[:], src_d[ci * 128:(ci + 1) * 128, :], [] if first_layer else [b("xmid%d" % ci)], [XB])
                dma("sp", zs[:], zs_s[l], [b("zs_s%d" % l)], [b("zs")])
                for hg in range(NHG):
                    h0 = hg * 4
                    oh, OB = (o_t[0], b("o_t0")) if hg % 2 == 0 else (o_t_b, b("o_tB"))
                    prefetch_upto(l * NHG + hg + 2)
                    blk, RB = floaded.pop((l, hg))
                    s_chain(views(blk), hg, [RB], True, par=hg % 2)
                    if hg == 0 and ci > 1:
                        L_front()
                    ysl = ya[:, hg * FW:(hg + 1) * FW]
                    zsl = zs[:, hg * FW:(hg + 1) * FW]
                    jk = hid[:, 0:FW]
                    memset(nrm[:, 0:4], 0.0, [b("nrm")])
                    if gdn:
                        for hh in range(4):
                            act(jk[:, hh * 128:(hh + 1) * 128], oh[:, hh * 128:(hh + 1) * 128], AF.Square, [OB], [b("hid"), b("nrm")],
                                accum_out=nrm[:, hh:hh + 1])
                        act(nrm[:, 4:8], nrm[:, 0:4], AF.Sqrt, [b("nrm")], [b("nrm")], scale=1.0 / 128, bias=EPS)
                        recip(nrm[:, 4:8], nrm[:, 4:8], [b("nrm")], [b("nrm")])
                        for hh in range(4):
                            act(ysl[:, hh * 128:(hh + 1) * 128], oh[:, hh * 128:(hh + 1) * 128], AF.Copy, [OB, b("nrm")], [b("ya")],
                                scale=nrm[:, 4 + hh:5 + hh])
                        tt(G4(ysl), G4(ysl), gnw[:].unsqueeze(1).broadcast_to([128, 4, 128]), ALU.mult, [b("ya"), b("gnw")], [b("ya")])
                        tt(ysl, ysl, zsl, ALU.mult, [b("ya"), b("zs")], [b("ya")])
                    else:
                        tt(oh[:, 0:FW], oh[:, 0:FW], zsl, ALU.mult, [OB, b("zs")], [OB])
                        act(jk, oh[:, 0:FW], AF.Square, [OB], [b("hid"), b("nrm")], accum_out=nrm[:, 0:1])
                        act(nrm[:, 4:5], nrm[:, 0:1], AF.Sqrt, [b("nrm")], [b("nrm")], scale=1.0 / 256, bias=EPS)
                        recip(nrm[:, 4:5], nrm[:, 4:5], [b("nrm")], [b("nrm")])
                        act(ysl, oh[:, 0:FW], AF.Copy, [OB, b("nrm")], [b("ya")], scale=nrm[:, 4:5])
                        tt(ysl, ysl, snw[:, hg * FW:(hg + 1) * FW], ALU.mult, [b("ya"), b("snw")], [b("ya")])
                    if ci > 1:
                        L_mid(range(hg * KPH, (hg + 1) * KPH))
                if ci > 1:
                    L_back(ci - 1)
            L_front()
            L_mid(range(16))
            L_back(NCH)
            if not final_norm:
                P.op("pool", lambda h: h.collective_compute("AllGather", ALU.bypass, replica_groups=RGL, ins=[hal_src[:, :]], outs=[hal_gat[:, :]]),
                     reads=[b("hal_src")], writes=[b("hal_gat")])
                for j in range(3):
                    dma("sp", xt[0][:], hal_gat[j * 128:(j + 1) * 128, :], [b("hal_gat")], [b("xt0")])
                    if j == 0:
                        ts(xt[1][:], xt[0][:], mprev[:, 0:1], None, ALU.mult, None, [b("xt0"), b("mprev")], [b("xt1")])
                    else:
                        stt(xt[1][:], xt[0][:], mprev[:, j:j + 1], xt[1][:], ALU.mult, ALU.add, [b("xt0"), b("mprev"), b("xt1")], [b("xt1")])
                dma("sp", xmid_s[0:128, :], xt[1][:], [b("xt1")], [b("xmid0")])
        P.emit()
    return nc


def conv_diag(conv_w):
    cw = np.asarray(conv_w, np.float32).reshape(4, 8, 4, 128)
    d = np.zeros((8, 128, 4, 4, 128), np.float32)
    for p in range(128):
        d[:, p, :, :, p] = np.transpose(cw[:, :, :, p], (1, 2, 0))
    return d.reshape(8, 128, 16 * 128)


def bc(v, n=128):
    v = np.asarray(v, np.float32).reshape(1, -1)
    return np.ascontiguousarray(np.broadcast_to(v, (n, v.shape[1])))


def layer_inputs(layer, x_with_halo, s_in, norm_w, w_in, conv_w, w_out, a_log, dt_bias, gdn_norm_w=None,
                 conv_b=None, d_skip=None, ssd_norm_w=None, final_norm_w=None):
    H = 16 if layer == "gdn" else 32
    m = dict(host_consts())
    m.update(x=np.ascontiguousarray(x_with_halo, dtype=np.float32), w_in=np.ascontiguousarray(w_in, dtype=np.float32),
             w_out=np.ascontiguousarray(w_out, dtype=np.float32), normw_bc=bc(norm_w), diag=conv_diag(conv_w),
             s_in=np.ascontiguousarray(s_in, dtype=np.float32),
             a_log=np.asarray(a_log, np.float32).reshape(H, 1).copy(), dt_bias=np.asarray(dt_bias, np.float32).reshape(H, 1).copy())
    if layer == "gdn":
        m["gnw_bc"] = bc(gdn_norm_w)
    else:
        m["conv_b"] = np.ascontiguousarray(np.asarray(conv_b, np.float32).reshape(32, 128).T)
        m["dskip_bc"] = bc(d_skip)
        m["snw_bc"] = bc(ssd_norm_w)
    if final_norm_w is not None:
        m["fnw_bc"] = bc(final_norm_w)
    return m


def fused_inputs(x_with_halo, p):
    m = dict(host_consts())
    m["x"] = np.ascontiguousarray(x_with_halo, dtype=np.float32)
    f32 = lambda a: np.ascontiguousarray(np.asarray(a, np.float32))
    m.update(g_w_in=f32(p["gdn_w_in"][0]), g_w_out=f32(p["gdn_w_out"][0]), g_normw_bc=bc(p["norm_w"][0]),
             g_diag=conv_diag(p["gdn_conv_w"][0]), g_a_log=f32(p["gdn_a_log"][0]).reshape(16, 1),
             g_dt_bias=f32(p["gdn_dt_bias"][0]).reshape(16, 1), g_gnw_bc=bc(p["gdn_norm_w"][0]))
    m.update(s_w_in=f32(p["ssd_w_in"][0]), s_w_out=f32(p["ssd_w_out"][0]), s_normw_bc=bc(p["norm_w"][1]),
             s_diag=conv_diag(p["ssd_conv_w"][0]), s_a_log=f32(p["ssd_a_log"][0]).reshape(32, 1),
             s_dt_bias=f32(p["ssd_dt_bias"][0]).reshape(32, 1),
             s_conv_b=np.ascontiguousarray(f32(p["ssd_conv_b"][0]).reshape(32, 128).T), s_dskip_bc=bc(p["ssd_d"][0]),
             s_snw_bc=bc(p["ssd_norm_w"][0]))
    m["fnw_bc"] = bc(p["final_norm_w"])
    return m


def par_inputs(x_with_halo, r, p):
    m = fused_inputs(x_with_halo, p)
    mp = np.zeros((128, 4), np.float32)
    if r >= 1:
        mp[:, r - 1] = 1.0
    m["mprev"] = mp
    return m


def kernel(**inputs):
    x = np.asarray(inputs["x"], np.float32)
    Bn, T, _ = x.shape
    NSEG = 4
    SEG = T // NSEG
    NCH = SEG // 128
    RG = tuple(tuple(range(bi * NSEG, (bi + 1) * NSEG)) for bi in range(Bn))
    nc = build_par(NCH, RG=RG)
    z128 = np.zeros((128, DM), np.float32)
    maps = []
    for bi in range(Bn):
        for r in range(NSEG):
            halo = z128 if r == 0 else x[bi, r * SEG - 128:r * SEG]
            maps.append(par_inputs(np.concatenate([halo, x[bi, r * SEG:(r + 1) * SEG]], axis=0), r, inputs))
    res = run_bass_kernel_spmd(nc, maps, core_ids=list(range(Bn * NSEG)))
    out = np.empty_like(x)
    for bi in range(Bn):
        for r in range(NSEG):
            out[bi, r * SEG:(r + 1) * SEG] = res.results[bi * NSEG + r]["xo"]
    return out
```
